# Optimizing a Trainium2 kernel written in Bass

```python
import math
import jax
import jax.numpy as jnp
from jax import lax
import numpy as np

D_MODEL = 1024
BATCH = 16
SEQ = 256
DEPTH = 2
DEC_BATCH = 2
DEC_SEQ = 2048
PAST_LEN = 256

GRID_W = 64
GROUP_W = D_MODEL // 4
HEAD_DIM = 64
DA_HEADS = GROUP_W // HEAD_DIM
DA_QK = HEAD_DIM // 2
DA_V = HEAD_DIM
MLA_HEADS = GROUP_W // HEAD_DIM
MLA_NOPE = HEAD_DIM
MLA_ROPE = HEAD_DIM // 2
MLA_V = HEAD_DIM
MLA_Q_LORA = GROUP_W
MLA_KV_LORA = GROUP_W // 2
NA_HEADS = GROUP_W // HEAD_DIM
NA_DIM = HEAD_DIM
NA_KR = 8
NA_KC = 16
CONV_CH = GROUP_W
CONV_W = 31
D_FF = 4 * D_MODEL
ROPE_BASE = 10000.0
QBLK = 128
EPS = 1e-6
N_MOD = 6

IN_SIZES = (2 * DA_HEADS * DA_QK, 2 * DA_HEADS * DA_QK, DA_HEADS * DA_V,
            MLA_Q_LORA, MLA_KV_LORA, MLA_ROPE,
            NA_HEADS * NA_DIM, NA_HEADS * NA_DIM, NA_HEADS * NA_DIM,
            2 * CONV_CH)
IN_COLS = sum(IN_SIZES)
IN_OFFSETS = tuple(int(o) for o in np.cumsum(IN_SIZES)[:-1])
MIX_OUT = DA_HEADS * DA_V + MLA_HEADS * MLA_V + NA_HEADS * NA_DIM + CONV_CH

kernel_name = 'hybrid_dit_prefix_ctx_step'


def rmsnorm(x, g):
    x32 = x.astype(jnp.float32)
    y = x32 * lax.rsqrt(jnp.mean(x32 * x32, axis=-1, keepdims=True) + EPS)
    return y.astype(x.dtype) * g


def layernorm(x, g, b):
    x32 = x.astype(jnp.float32)
    mu = jnp.mean(x32, axis=-1, keepdims=True)
    var = jnp.mean(jnp.square(x32 - mu), axis=-1, keepdims=True)
    return ((x32 - mu) * lax.rsqrt(var + EPS)).astype(x.dtype) * g + b


def heads(t, n):
    B, S, _ = t.shape
    return t.reshape(B, S, n, -1).transpose(0, 2, 1, 3)


def merge_heads(t):
    B, n, S, d = t.shape
    return t.transpose(0, 2, 1, 3).reshape(B, S, n * d)


def rope1d(x, pos):
    d = x.shape[-1]
    half = d // 2
    freqs = ROPE_BASE ** (-jnp.arange(half, dtype=jnp.float32) * 2.0 / d)
    ang = pos[:, None] * freqs[None, :]
    cos = jnp.cos(ang).astype(x.dtype)
    sin = jnp.sin(ang).astype(x.dtype)
    x1, x2 = x[..., :half], x[..., half:]
    return jnp.concatenate([x1 * cos - x2 * sin, x1 * sin + x2 * cos], axis=-1)


def rope2d(x, rows, cols):
    h = x.shape[-1] // 2
    return jnp.concatenate([rope1d(x[..., :h], rows), rope1d(x[..., h:], cols)], axis=-1)


def sweep_query_blocks(fn, qs):
    B, H, S = qs[0].shape[:3]
    blk = math.gcd(S, QBLK)
    nb = S // blk
    blocks = tuple(jnp.moveaxis(q.reshape(B, H, nb, blk, q.shape[-1]), 2, 0) for q in qs)
    out = lax.map(fn, blocks)
    return jnp.moveaxis(out, 0, 2).reshape(B, H, S, out.shape[-1])


def attn_core(q, k, v):
    scale = q.shape[-1] ** -0.5
    def blk(qb):
        s = jnp.einsum('bhqd,bhkd->bhqk', qb[0], k).astype(jnp.float32) * scale
        p = jax.nn.softmax(s, axis=-1).astype(v.dtype)
        return jnp.einsum('bhqk,bhkd->bhqd', p, v)
    return sweep_query_blocks(blk, (q,))


def diff_lambda(lq1, lk1, lq2, lk2, layer):
    lam_init = 0.8 - 0.6 * math.exp(-0.3 * layer)
    f = lambda a, b: jnp.exp(jnp.sum(a.astype(jnp.float32) * b.astype(jnp.float32)))
    return f(lq1, lk1) - f(lq2, lk2) + lam_init, lam_init


def diff_attn_core(q1, q2, k1, k2, v, lam):
    scale = DA_QK ** -0.5
    def blk(qb):
        a, b = qb
        s1 = jnp.einsum('bhqd,bhkd->bhqk', a, k1).astype(jnp.float32) * scale
        s2 = jnp.einsum('bhqd,bhkd->bhqk', b, k2).astype(jnp.float32) * scale
        p = jax.nn.softmax(s1, axis=-1) - lam * jax.nn.softmax(s2, axis=-1)
        return jnp.einsum('bhqk,bhkd->bhqd', p.astype(v.dtype), v)
    return sweep_query_blocks(blk, (q1, q2))


def neighbourhood_attn_latent(q, k, v, kc, vc, rpb):
    B, H, S, d = q.shape
    rows = S // GRID_W
    kr = min(NA_KR, rows)
    scale = d ** -0.5
    qg = q.reshape(B, H, rows, GRID_W, d)
    kg = k.reshape(B, H, rows, GRID_W, d)
    vg = v.reshape(B, H, rows, GRID_W, d)
    r = np.arange(rows)
    row_idx = np.clip(r - kr // 2, 0, rows - kr)[:, None] + np.arange(kr)[None, :]
    kb = jnp.take(kg, row_idx, axis=2)
    vb = jnp.take(vg, row_idx, axis=2)
    w = np.arange(GRID_W)
    col_start = np.clip(w - NA_KC // 2, 0, GRID_W - NA_KC)
    col_ok = (w[None, :] >= col_start[:, None]) & (w[None, :] < col_start[:, None] + NA_KC)
    dr = row_idx - r[:, None] + NA_KR - 1
    dc = np.clip(w[None, :] - w[:, None], -(NA_KC - 1), NA_KC - 1) + NA_KC - 1
    bias = rpb[:, dr[:, None, :, None], dc[None, :, None, :]]
    s_loc = jnp.einsum('bhrqd,bhrjwd->bhrqjw', qg, kb).astype(jnp.float32) * scale + bias.astype(jnp.float32)
    s_loc = jnp.where(col_ok[:, None, :], s_loc, -jnp.inf)
    n_loc = kr * GRID_W
    s_loc = s_loc.reshape(B, H, rows, GRID_W, n_loc)
    s_ctx = jnp.einsum('bhrqd,bhkd->bhrqk', qg, kc).astype(jnp.float32) * scale
    p = jax.nn.softmax(jnp.concatenate([s_loc, s_ctx], axis=-1), axis=-1).astype(v.dtype)
    p_loc = p[..., :n_loc].reshape(B, H, rows, GRID_W, kr, GRID_W)
    o = (jnp.einsum('bhrqjw,bhrjwd->bhrqd', p_loc, vb)
         + jnp.einsum('bhrqk,bhkd->bhrqd', p[..., n_loc:], vc))
    return o.reshape(B, H, S, d)


def conformer_conv(u, dw, db, ln_g, ln_b):
    a, b = jnp.split(u, 2, axis=-1)
    g = a * jax.nn.sigmoid(b)
    y = lax.conv_general_dilated(g, dw[:, None, :], window_strides=(1,),
                                 padding=[(CONV_W // 2, CONV_W // 2)],
                                 dimension_numbers=('NWC', 'WIO', 'NWC'),
                                 feature_group_count=CONV_CH) + db
    return jax.nn.silu(layernorm(y, ln_g, ln_b))


def token_mixers(h, lw, layer, ctx):
    B, S, _ = h.shape
    latent = ctx is not None
    (da_q, da_k, da_v, mla_qd, mla_kvd, mla_kr,
     na_q, na_k, na_v, conv_u) = jnp.split(h @ lw['w_in'], IN_OFFSETS, axis=-1)
    if latent:
        c_dak, c_dav, c_ckv, c_kr, c_nak, c_nav = ctx
        t = jnp.arange(S)
        rows = (t // GRID_W).astype(jnp.float32)
        cols = (t % GRID_W).astype(jnp.float32)
        rot = lambda z: rope2d(z, rows, cols)
    qa, ka, va = heads(da_q, DA_HEADS), heads(da_k, DA_HEADS), heads(da_v, DA_HEADS)
    q1, q2 = qa[..., :DA_QK], qa[..., DA_QK:]
    k1, k2 = ka[..., :DA_QK], ka[..., DA_QK:]
    if latent:
        q1, q2 = rot(q1), rot(q2)
        k1 = jnp.concatenate([rot(k1), c_dak[..., :DA_QK]], axis=2)
        k2 = jnp.concatenate([rot(k2), c_dak[..., DA_QK:]], axis=2)
        va_all = jnp.concatenate([va, c_dav], axis=2)
    else:
        va_all = va
    lam, lam_init = diff_lambda(lw['lq1'], lw['lk1'], lw['lq2'], lw['lk2'], layer)
    o_da = rmsnorm(diff_attn_core(q1, q2, k1, k2, va_all, lam), lw['g_da_subln']) * (1.0 - lam_init)
    qm = heads(rmsnorm(mla_qd, lw['g_mla_q']) @ lw['w_mla_qup'], MLA_HEADS)
    q_nope, q_rope = qm[..., :MLA_NOPE], qm[..., MLA_NOPE:]
    ckv = rmsnorm(mla_kvd, lw['g_mla_kv'])
    if latent:
        q_rope = rot(q_rope)
        ckv_all = jnp.concatenate([ckv, c_ckv], axis=1)
        kr_all = jnp.concatenate([rot(mla_kr), c_kr], axis=1)
    else:
        ckv_all, kr_all = ckv, mla_kr
    kvm = heads(ckv_all @ lw['w_mla_kvup'], MLA_HEADS)
    k_nope, v_m = kvm[..., :MLA_NOPE], kvm[..., MLA_NOPE:]
    Sk = ckv_all.shape[1]
    k_m = jnp.concatenate([k_nope, jnp.broadcast_to(kr_all[:, None], (B, MLA_HEADS, Sk, MLA_ROPE))], axis=-1)
    o_mla = attn_core(jnp.concatenate([q_nope, q_rope], axis=-1), k_m, v_m)
    qn, kn, vn = heads(na_q, NA_HEADS), heads(na_k, NA_HEADS), heads(na_v, NA_HEADS)
    if latent:
        o_na = neighbourhood_attn_latent(qn, kn, vn, c_nak, c_nav, lw['na_rpb'])
    else:
        o_na = attn_core(qn, kn, vn)
    o_conv = conformer_conv(conv_u, lw['conv_dw'], lw['conv_b'], lw['conv_ln_g'], lw['conv_ln_b'])
    mixed = jnp.concatenate([merge_heads(o_da), merge_heads(o_mla), merge_heads(o_na), o_conv], axis=-1) @ lw['w_out']
    new_ctx = None if latent else (ka, va, ckv, mla_kr, kn, vn)
    return mixed, new_ctx


def modulation(cvec, w_mod, b_mod):
    m = jax.nn.silu(cvec) @ w_mod + b_mod
    return jnp.split(m[:, None, :], N_MOD, axis=-1)


def trunk_layer(x, cvec, lw, layer, ctx):
    sh1, sc1, g1, sh2, sc2, g2 = modulation(cvec, lw['w_mod'], lw['b_mod'])
    h = rmsnorm(x, lw['g_mix']) * (1.0 + sc1) + sh1
    mixed, new_ctx = token_mixers(h, lw, layer, ctx)
    x = x + g1 * mixed
    h = rmsnorm(x, lw['g_ff']) * (1.0 + sc2) + sh2
    x = x + g2 * (jnp.square(jax.nn.relu(h @ lw['w_ff1'])) @ lw['w_ff2'])
    return x, new_ctx


def setup_inputs(seed: int = 0) -> dict:
    key = jax.random.key(seed)
    keys = jax.random.split(key, 40)
    it = iter(range(40))
    nrm = lambda shape, s: jax.random.normal(keys[next(it)], shape, jnp.float32) * s
    gain = lambda shape: 1.0 + nrm(shape, 0.02)
    L = DEPTH
    return {
        'x_prompt': nrm((BATCH, SEQ, D_MODEL), 1.0),
        'x_sample': nrm((DEC_BATCH, DEC_SEQ, D_MODEL), 1.0),
        'c': nrm((DEC_BATCH, D_MODEL), 1.0),
        'cache_da_k': nrm((DEC_BATCH, L, DA_HEADS, PAST_LEN, 2 * DA_QK), 1.0),
        'cache_da_v': nrm((DEC_BATCH, L, DA_HEADS, PAST_LEN, DA_V), 1.0),
        'cache_mla_ckv': nrm((DEC_BATCH, L, PAST_LEN, MLA_KV_LORA), 1.0),
        'cache_mla_krope': nrm((DEC_BATCH, L, PAST_LEN, MLA_ROPE), 1.0),
        'cache_na_k': nrm((DEC_BATCH, L, NA_HEADS, PAST_LEN, NA_DIM), 1.0),
        'cache_na_v': nrm((DEC_BATCH, L, NA_HEADS, PAST_LEN, NA_DIM), 1.0),
        'c_ctx': nrm((D_MODEL,), 1.0),
        'w_mod': nrm((L, D_MODEL, N_MOD * D_MODEL), 0.5 * D_MODEL ** -0.5),
        'b_mod': nrm((L, N_MOD * D_MODEL), 0.01),
        'g_norm_mix': gain((L, D_MODEL)),
        'g_norm_ff': gain((L, D_MODEL)),
        'w_in': nrm((L, D_MODEL, IN_COLS), D_MODEL ** -0.5),
        'da_lambda_q1': nrm((L, DA_QK), 0.1),
        'da_lambda_k1': nrm((L, DA_QK), 0.1),
        'da_lambda_q2': nrm((L, DA_QK), 0.1),
        'da_lambda_k2': nrm((L, DA_QK), 0.1),
        'g_da_subln': gain((L, DA_V)),
        'g_mla_q': gain((L, MLA_Q_LORA)),
        'w_mla_qup': nrm((L, MLA_Q_LORA, MLA_HEADS * (MLA_NOPE + MLA_ROPE)), MLA_Q_LORA ** -0.5),
        'g_mla_kv': gain((L, MLA_KV_LORA)),
        'w_mla_kvup': nrm((L, MLA_KV_LORA, MLA_HEADS * (MLA_NOPE + MLA_V)), MLA_KV_LORA ** -0.5),
        'na_rpb': nrm((L, NA_HEADS, 2 * NA_KR - 1, 2 * NA_KC - 1), 0.1),
        'conv_dw': nrm((L, CONV_W, CONV_CH), CONV_W ** -0.5),
        'conv_b': nrm((L, CONV_CH), 0.01),
        'conv_ln_g': gain((L, CONV_CH)),
        'conv_ln_b': nrm((L, CONV_CH), 0.01),
        'w_out': nrm((L, MIX_OUT, D_MODEL), MIX_OUT ** -0.5),
        'w_ff1': nrm((L, D_MODEL, D_FF), D_MODEL ** -0.5),
        'w_ff2': nrm((L, D_FF, D_MODEL), D_FF ** -0.5),
        'g_final': gain((D_MODEL,)),
    }


def reference(x_prompt, x_sample, c, cache_da_k, cache_da_v, cache_mla_ckv, cache_mla_krope,
              cache_na_k, cache_na_v, c_ctx, w_mod, b_mod, g_norm_mix, g_norm_ff, w_in,
              da_lambda_q1, da_lambda_k1, da_lambda_q2, da_lambda_k2, g_da_subln,
              g_mla_q, w_mla_qup, g_mla_kv, w_mla_kvup, na_rpb,
              conv_dw, conv_b, conv_ln_g, conv_ln_b, w_out, w_ff1, w_ff2, g_final):
    xp, xs = x_prompt, x_sample
    ctx_lists = ([], [], [], [], [], [])
    for l in range(DEPTH):
        lw = {'w_mod': w_mod[l], 'b_mod': b_mod[l], 'g_mix': g_norm_mix[l], 'g_ff': g_norm_ff[l],
              'w_in': w_in[l], 'lq1': da_lambda_q1[l], 'lk1': da_lambda_k1[l],
              'lq2': da_lambda_q2[l], 'lk2': da_lambda_k2[l], 'g_da_subln': g_da_subln[l],
              'g_mla_q': g_mla_q[l], 'w_mla_qup': w_mla_qup[l], 'g_mla_kv': g_mla_kv[l],
              'w_mla_kvup': w_mla_kvup[l], 'na_rpb': na_rpb[l], 'conv_dw': conv_dw[l],
              'conv_b': conv_b[l], 'conv_ln_g': conv_ln_g[l], 'conv_ln_b': conv_ln_b[l],
              'w_out': w_out[l], 'w_ff1': w_ff1[l], 'w_ff2': w_ff2[l]}
        xp, new_ctx = trunk_layer(xp, c_ctx[None, :], lw, l, None)
        for lst, t in zip(ctx_lists, new_ctx):
            lst.append(t)
        cached = (cache_da_k[:, l], cache_da_v[:, l], cache_mla_ckv[:, l], cache_mla_krope[:, l],
                  cache_na_k[:, l], cache_na_v[:, l])
        xs, _ = trunk_layer(xs, c, lw, l, cached)
    y_prompt = rmsnorm(xp, g_final)
    y_sample = rmsnorm(xs, g_final)
    new_da_k = jnp.stack(ctx_lists[0], axis=1)
    new_da_v = jnp.stack(ctx_lists[1], axis=1)
    new_mla_ckv = jnp.stack(ctx_lists[2], axis=1)
    new_mla_krope = jnp.stack(ctx_lists[3], axis=1)
    new_na_k = jnp.stack(ctx_lists[4], axis=1)
    new_na_v = jnp.stack(ctx_lists[5], axis=1)
    return (y_prompt, y_sample, new_da_k, new_da_v, new_mla_ckv, new_mla_krope, new_na_k, new_na_v)
```

```python
import contextlib
import math
import numpy as np
import concourse.bass as bass
import concourse.mybir as mybir
from concourse.bass_utils import run_bass_kernel_spmd

F32 = mybir.dt.float32
BF16 = mybir.dt.bfloat16
ALU = mybir.AluOpType
AF = mybir.ActivationFunctionType

ENGINES = ("pe", "act", "dve", "pool", "sp")

D = 1024
L = 2
NB = 16
SEQ = 256
DEC_B = 2
DEC_S = 2048
PAST = 256
GRID_W = 64
NROWS = DEC_S // GRID_W
EPS = 1e-6
T = 512
NCORE = 8
BIG = 30000.0
A2 = 46
TZL = 127
PA = 672
PB = 527
R_DAK, R_CKV, R_KR, R_NAK = 0, 256, 384, 416
R_V, R_HALO = 0, 512
P32 = [8, 9, 10, 11, 12, 13, 14, 15, 0, 1, 2, 3, 4, 5, 6, 7,
       24, 25, 26, 27, 28, 29, 30, 31, 16, 17, 18, 19, 20, 21, 22, 23]


class Buf:
    __slots__ = ("name", "last_w", "readers", "excl")

    def __init__(self, name, excl=False):
        self.name = name
        self.last_w = None
        self.readers = []
        self.excl = excl


class Op:
    __slots__ = ("eng", "fn", "deps", "dma", "sig", "sem", "val")

    def __init__(self, eng, fn, dma):
        self.eng = eng
        self.fn = fn
        self.dma = dma
        self.deps = []
        self.sig = False
        self.sem = None
        self.val = 0


class Sched:
    def __init__(self, nc, n_dma_slots=16):
        self.nc = nc
        self.ops = []
        self.n_dma_slots = n_dma_slots

    def add(self, eng, fn, reads=(), writes=(), dma=False):
        op = Op(eng, fn, dma)
        ex = [b for b in reads if b.excl]
        if ex:
            reads = [b for b in reads if not b.excl]
            writes = list(writes) + ex
        deps = {}
        for b in reads:
            if b.last_w is not None:
                deps[id(b.last_w)] = b.last_w
        for b in writes:
            if b.last_w is not None:
                deps[id(b.last_w)] = b.last_w
            for r in b.readers:
                deps[id(r)] = r
        for b in reads:
            b.readers.append(op)
        for b in writes:
            b.last_w = op
            b.readers = []
        for d in deps.values():
            if d.eng == "pe" and eng == "pe" and not d.dma and not dma:
                continue
            op.deps.append(d)
            d.sig = True
        self.ops.append(op)
        return op

    def emit(self, stack):
        nc = self.nc
        eng_sem = {e: stack.enter_context(nc.semaphore("s_" + e)) for e in ENGINES}
        dma_engs = sorted({op.eng for op in self.ops if op.dma is True})
        dma_slots = {e: [stack.enter_context(nc.semaphore("d_%s_%d" % (e, i)))
                         for i in range(self.n_dma_slots)] for e in dma_engs}
        cnt = {e: 0 for e in ENGINES}
        dcnt = {e: 0 for e in dma_engs}
        slot_uses = {e: [0] * self.n_dma_slots for e in dma_engs}
        slot_prev = {}
        ncc = 0
        for op in self.ops:
            if op.dma == "cc":
                op.sem = stack.enter_context(nc.semaphore("cc_%d" % ncc))
                ncc += 1
                op.val = 1
            elif op.dma:
                k = dcnt[op.eng] % self.n_dma_slots
                dcnt[op.eng] += 1
                slot_uses[op.eng][k] += 1
                op.sem = dma_slots[op.eng][k]
                op.val = 16 * slot_uses[op.eng][k]
                slot_prev[id(op)] = (op.sem, op.val - 16)
            elif op.sig:
                cnt[op.eng] += 1
                op.sem = eng_sem[op.eng]
                op.val = cnt[op.eng]
        block = stack.enter_context(nc.Block())
        handles = {"pe": block.tensor, "act": block.scalar, "dve": block.vector,
                   "pool": block.gpsimd, "sp": block.sync}
        all_async = [op for op in self.ops if op.dma]

        def run_engine(ename):
            def body(eng):
                waited = {}
                for op in self.ops:
                    if op.eng != ename:
                        continue
                    need = {}
                    for d in op.deps:
                        key = id(d.sem)
                        if key not in need or need[key][1] < d.val:
                            need[key] = (d.sem, d.val)
                    if op.dma is True:
                        s, v = slot_prev[id(op)]
                        if v > 0:
                            key = id(s)
                            if key not in need or need[key][1] < v:
                                need[key] = (s, v)
                    for key, (s, v) in need.items():
                        if waited.get(key, 0) >= v:
                            continue
                        eng.wait_ge(s, v)
                        waited[key] = v
                    ins = op.fn(eng)
                    if op.dma == "cc":
                        ins.then_inc(op.sem)
                    elif op.dma:
                        ins.then_inc(op.sem, 16)
                    elif op.sig:
                        ins.then_inc(op.sem, 1)
                if ename == "sp":
                    last = {}
                    for op in all_async:
                        last[id(op.sem)] = (op.sem, op.val)
                    for key, (s, v) in last.items():
                        if waited.get(key, 0) < v:
                            eng.wait_ge(s, v)
                    for e in ENGINES:
                        if cnt[e] > 0:
                            eng.wait_ge(eng_sem[e], cnt[e])
            handles[ename](body)

        for e in ENGINES:
            run_engine(e)


def build(dbg=(), nlayers=L, do_lat=True, stages=None):
    if stages is None:
        stages = {"mod", "norm1", "proj", "ctxattn", "conv", "latattn", "outproj", "norm2", "ffn", "final"}
    nc = bass.Bass("TRN2", target_bir_lowering=False)
    st = contextlib.ExitStack()
    S = Sched(nc)
    dbg_out = {}

    def din(name, shape, dt=F32):
        return nc.dram_tensor(name, list(shape), dt, kind="ExternalInput").ap()

    def dout(name, shape, dt=F32):
        return nc.dram_tensor(name, list(shape), dt, kind="ExternalOutput").ap()

    xin = din("xin", [2 * T, D])
    cvT = din("cvT", [128, 8, 2])
    bmodT = din("bmodT", [128, L, 48])
    gmixT = din("gmixT", [128, L, 8])
    gffT = din("gffT", [128, L, 8])
    gfinT = din("gfinT", [128, 8])
    w_mod = din("w_mod", [L, D, 6 * D])
    w_in = din("w_in", [L, D, 2464])
    w_inp = din("w_inp", [L, D, 608])
    w_out = din("w_out", [L, D, D])
    w_ff1 = din("w_ff1", [L, D, 4 * D])
    w_ff2 = din("w_ff2", [L, 4 * D, D])
    wqup = din("wqup", [L, 256, 384])
    wqupp = din("wqupp", [L, 256, 384])
    wkvup = din("wkvup", [L, 128, 512])
    gqT = din("gqT", [128, L, 2])
    gkvc = din("gkvc", [128, L])
    gkvr = din("gkvr", [128, L, 128])
    gsubc = din("gsubc", [128, L])
    lamv = din("lamv", [128, L, 4, 32])
    tz_all = din("tz_rep", [L * 4 * (A2 + 1), 64, TZL]).tensor
    dwT = din("dwT", [128, L, 2, 31])
    cbT = din("cbT", [128, L, 2])
    lngT = din("lngT", [128, L, 2])
    lnbT = din("lnbT", [128, L, 2])
    c_dak = din("c_dak", [L, 4, PAST, 64])
    c_dav = din("c_dav", [L, 4, PAST, 64])
    c_ckv = din("c_ckv", [L, PAST, 128])
    c_kr = din("c_kr", [L, PAST, 32])
    c_nak = din("c_nak", [L, 4, PAST, 64])
    c_nav = din("c_nav", [L, 4, PAST, 64])
    cosT_d = din("cosT", [128, T])
    sinT_d = din("sinT", [128, T])
    colmask_d = din("colmask", [128, 64])
    rowsel_d = din("rowsel", [32, T])
    rowind_d = din("rowind", [32, DEC_S])
    halsel_d = din("halsel", [128, 8])
    ident_d = din("ident", [128, 128])
    y_out = dout("y_out", [2 * T, D])
    o_dak = dout("o_dak", [2, L, 4, SEQ, 64])
    o_dav = dout("o_dav", [2, L, 4, SEQ, 64])
    o_ckv = dout("o_ckv", [2, L, SEQ, 128])
    o_kr = dout("o_kr", [2, L, SEQ, 32])
    o_nak = dout("o_nak", [2, L, 4, SEQ, 64])
    o_nav = dout("o_nav", [2, L, 4, SEQ, 64])
    LROWS = 6 * PA + 6 * PB
    pay_all = nc.dram_tensor("pay_all", [L * LROWS, T], BF16)
    O_AIN, O_AOUT, O_BIN, O_BOUT = 0, PA, 5 * PA, 5 * PA + PB

    class _Sub:
        def __init__(self, row0, nrows):
            self.row0, self.nrows = row0, nrows

        def ap(self):
            return pay_all.ap()[self.row0:self.row0 + self.nrows, :]

    payA_in = [_Sub(l * LROWS + O_AIN, PA) for l in range(L)]
    payA_out = [_Sub(l * LROWS + O_AOUT, 4 * PA) for l in range(L)]
    payB_in = [_Sub(l * LROWS + O_BIN, PB) for l in range(L)]
    payB_out = [_Sub(l * LROWS + O_BOUT, 4 * PB) for l in range(L)]
    TZ_SZ = 4 * (A2 + 1) * 64 * TZL

    def dram_ap(base, off, dims):
        if isinstance(base, _Sub):
            return bass.AP(pay_all, base.row0 * T + off, dims)
        return bass.AP(base, off, dims)

    def sb(name, shape, dt):
        return st.enter_context(nc.sbuf_tensor("sb_" + name, list(shape), dt))

    add = S.add

    banks = []
    for i in range(8):
        banks.append((st.enter_context(nc.psum_tensor("ps%d" % i, [128, 512], F32)), Buf("ps%d" % i, excl=True)))
    gp_i = [0]
    ac_i = [0]

    def gp():
        b = banks[(0, 1, 2, 7)[gp_i[0] % 4]]
        gp_i[0] += 1
        return b

    def acb():
        b = banks[4 + ac_i[0] % 3]
        ac_i[0] += 1
        return b

    ev_i = [0]

    def evac(dst, src, r, w, scale=None):
        ev_i[0] += 1
        if ev_i[0] % 2 == 0:
            if scale is None:
                add("act", lambda e: e.activation(out=dst, in_=src, func=AF.Copy), r, w)
            else:
                add("act", lambda e: e.activation(out=dst, in_=src, func=AF.Identity, scale=scale), r, w)
        else:
            if scale is None:
                add("dve", lambda e: e.tensor_copy(out=dst, in_=src), r, w)
            else:
                add("dve", lambda e: e.tensor_scalar(out=dst, in0=src, scalar1=scale, scalar2=None, op0=ALU.mult), r, w)

    ident = sb("ident", [128, 128], F32); b_ident = Buf("ident")
    identb = sb("identb", [128, 128], BF16); b_identb = Buf("identb")
    ones_f = sb("ones_f", [128, 128], F32); b_ones_f = Buf("ones_f")
    ones_b = sb("ones_b", [128, 128], BF16); b_ones_b = Buf("ones_b")
    epsc = sb("epsc", [128, 1], F32); b_epsc = Buf("epsc")
    add("sp", lambda e: e.dma_start(out=ident[:], in_=ident_d), (), [b_ident], dma=True)
    add("pool", lambda e: e.memset(ones_f[:], 1.0), (), [b_ones_f])
    add("pool", lambda e: e.memset(ones_b[:], 1.0), (), [b_ones_b])
    add("pool", lambda e: e.memset(epsc[:], EPS), (), [b_epsc])
    add("dve", lambda e: e.tensor_copy(out=identb[:], in_=ident[:]), [b_ident], [b_identb])

    prm = {}
    b_prm = Buf("prm")
    b_small = []

    def load_small(name, src, shape):
        t = sb(name, shape, F32)
        bt = Buf("ld_" + name)
        b_small.append(bt)
        add("sp", lambda e: e.dma_start(out=t[:], in_=src), (), [bt], dma=True)
        prm[name] = t
        return t

    cv_s = load_small("cv_s", cvT, [128, 8, 2])
    bmod_s = load_small("bmod_s", bmodT, [128, L, 48])
    gmix_s = load_small("gmix_s", gmixT, [128, L, 8])
    gff_s = load_small("gff_s", gffT, [128, L, 8])
    gfin_s = load_small("gfin_s", gfinT, [128, 8])
    gq_s = load_small("gq_s", gqT, [128, L, 2])
    gkvc_s = load_small("gkvc_s", gkvc, [128, L])
    gkvr_s = sb("gkvr_s", [128, L, 128], BF16)
    b_small.append(Buf("ld_gkvr"))
    add("pool", lambda e: e.dma_start(out=gkvr_s[:], in_=gkvr), (), [b_small[-1]], dma=True)
    gsub_s = load_small("gsub_s", gsubc, [128, L])
    lam_s = load_small("lam_s", lamv, [128, L, 4, 32])
    dw_s = load_small("dw_s", dwT, [128, L, 2, 31])
    cb_s = load_small("cb_s", cbT, [128, L, 2])
    lng_s = load_small("lng_s", lngT, [128, L, 2])
    lnb_s = load_small("lnb_s", lnbT, [128, L, 2])
    cos_s = load_small("cos_s", cosT_d, [128, T])
    sin_s = load_small("sin_s", sinT_d, [128, T])
    colm_s = load_small("colm_s", colmask_d, [128, 64])
    halsel_s = load_small("halsel_s", halsel_d, [128, 8])
    joinc = sb("joinc", [128, 1], F32)
    add("dve", lambda e: e.memset(joinc[:], 0.0), list(b_small), [b_prm])

    lams = sb("lams", [128, L, 2], F32)
    neglam = sb("neglam", [128, L], F32)
    gsub2 = sb("gsub2", [128, L], F32)
    for l in range(L):
        lam_init = 0.8 - 0.6 * math.exp(-0.3 * l)
        for m in range(2):
            add("dve", lambda e, l=l, m=m: e.tensor_tensor(out=lam_s[:, l, 2 * m, :], in0=lam_s[:, l, 2 * m, :],
                                                          in1=lam_s[:, l, 2 * m + 1, :], op=ALU.mult), [b_prm], [b_prm])
            add("dve", lambda e, l=l, m=m: e.reduce_sum(out=lams[:, l, m:m + 1], in_=lam_s[:, l, 2 * m, :],
                                                       axis=mybir.AxisListType.X), [b_prm], [b_prm])
        add("act", lambda e, l=l: e.activation(out=lams[:, l, :], in_=lams[:, l, :], func=AF.Exp), [b_prm], [b_prm])
        add("dve", lambda e, l=l: e.tensor_tensor(out=neglam[:, l:l + 1], in0=lams[:, l, 1:2], in1=lams[:, l, 0:1],
                                                 op=ALU.subtract), [b_prm], [b_prm])
        add("dve", lambda e, l=l, li=lam_init: e.tensor_scalar(out=neglam[:, l:l + 1], in0=neglam[:, l:l + 1],
                                                              scalar1=-li, scalar2=None, op0=ALU.add), [b_prm], [b_prm])
        add("dve", lambda e, l=l, li=lam_init: e.tensor_scalar(out=gsub2[:, l:l + 1], in0=gsub_s[:, l:l + 1],
                                                              scalar1=1.0 - li, scalar2=None, op0=ALU.mult), [b_prm], [b_prm])

    xT = [sb("xT%d" % g, [128, 8, T], F32) for g in range(2)]
    b_xT = [[Buf("xT%d_%d" % (g, j)) for j in range(8)] for g in range(2)]
    hT = [sb("hT%d" % g, [128, 8, T], BF16) for g in range(2)]
    b_hT = [Buf("hT%d" % g) for g in range(2)]
    mixT = [sb("mixT%d" % g, [128, 8, T], BF16) for g in range(2)]
    b_mix = [[Buf("mix%d_%d" % (g, j)) for j in range(8)] for g in range(2)]
    NW = 3
    wpool = [sb("wp%d" % i, [128, 8, 512], BF16) for i in range(NW)]
    b_wp = [Buf("wp%d" % i) for i in range(NW)]
    wp_i = [0]

    def wtile(src_ap, ncols, nk=8):
        i = wp_i[0] % NW
        wp_i[0] += 1
        t, b = wpool[i], b_wp[i]
        add("pool", lambda e: e.dma_start(out=t[:, 0:nk, 0:ncols], in_=src_ap.rearrange("(k p) c -> p k c", p=128)),
            (), [b], dma=True)
        return t, b

    qd = sb("qd", [128, 2, T], F32); b_qd = Buf("qd_cacc_stage")
    stage1 = qd[:, :, :].rearrange("p a b -> p (a b)")

    class _Stage:
        def __getitem__(self, key):
            p, sl, c = key
            return stage1[p, c]
    stage = _Stage()
    b_stage = [b_qd, b_qd]
    rstd = sb("rstd", [128, T], F32); b_rstd = Buf("rstd")
    tmpf = [sb("tmpf%d" % i, [128, T], F32) for i in range(4)]
    b_tmpf = [Buf("tmpf%d" % i) for i in range(4)]
    tf_i = [0]

    def tmp():
        i = tf_i[0] % 4
        tf_i[0] += 1
        return tmpf[i], b_tmpf[i]

    sqb = [sb("sqb%d" % i, [128, T], BF16) for i in range(2)]
    b_sqb = [Buf("sqb%d" % i) for i in range(2)]
    sq_i = [0]

    def sqt():
        i = sq_i[0] % 2
        sq_i[0] += 1
        return sqb[i], b_sqb[i]

    for g in range(2):
        for tb in range(4):
            sl = (g * 4 + tb) % 2
            add("sp", lambda e, g=g, tb=tb, sl=sl: e.dma_start(out=stage[:, sl, :], in_=xin[g * T + tb * 128: g * T + (tb + 1) * 128, :]),
                (), [b_stage[sl]], dma=True)
            for half in range(2):
                pt, pb = gp()
                for jj in range(4):
                    j = half * 4 + jj
                    add("pe", lambda e, pt=pt, sl=sl, j=j, jj=jj: e.transpose(out=pt[:, jj * 128:(jj + 1) * 128],
                                                                              in_=stage[:, sl, j * 128:(j + 1) * 128], identity=ident[:]),
                        [b_stage[sl], b_ident], [pb])
                evac(xT[g][:, half * 4:half * 4 + 4, tb * 128:(tb + 1) * 128],
                     pt[:, :].rearrange("p (j t) -> p j t", j=4), [pb], [b_xT[g][half * 4 + jj] for jj in range(4)])

    sil = sb("sil", [128, 8, 2], BF16); b_sil = Buf("sil")
    add("act", lambda e: e.activation(out=sil[:], in_=cv_s[:], func=AF.Silu), [b_prm], [b_sil])
    modv = sb("modv", [128, L, 48, 2], F32)
    gsc = sb("gsc", [128, L, 2, 8, 2], F32)
    b_mod = [Buf("mod%d" % l) for l in range(L)]

    def modulation(l):
        pt, pb = banks[3]
        for ti in range(12):
            wt, wb = wtile(w_mod[l, :, ti * 512:(ti + 1) * 512], 512)
            for cc in range(4):
                n = ti * 4 + cc
                for k in range(8):
                    add("pe", lambda e, wt=wt, cc=cc, k=k, n=n, pt=pt: e.matmul(pt[:, n * 2:n * 2 + 2], lhsT=wt[:, k, cc * 128:(cc + 1) * 128],
                                                                             rhs=sil[:, k, :], start=(k == 0), stop=(k == 7)),
                        [wb, b_sil], [pb])
            yield
        add("dve", lambda e, pt=pt: e.tensor_tensor(out=modv[:, l, :, :], in0=pt[:, 0:96].rearrange("p (n v) -> p n v", v=2),
                                                   in1=bmod_s[:, l, :].unsqueeze(2).to_broadcast([128, 48, 2]), op=ALU.add),
            [pb, b_prm], [b_mod[l]])
        for ni, (gsrc, off) in enumerate(((gmix_s, 8), (gff_s, 32))):
            add("dve", lambda e, ni=ni, gsrc=gsrc, off=off: e.scalar_tensor_tensor(
                out=gsc[:, l, ni, :, :], in0=modv[:, l, off:off + 8, :], scalar=1.0,
                in1=gsrc[:, l, :].unsqueeze(2).to_broadcast([128, 8, 2]), op0=ALU.add, op1=ALU.mult),
                [b_mod[l], b_prm], [b_mod[l]])

    def mcol(l, which, j, v):
        off = {"sh1": 0, "sc1": 8, "g1": 16, "sh2": 24, "sc2": 32, "g2": 40}[which]
        return modv[:, l, off + j, v:v + 1]

    def rmsnorm_mod(l, g, ni):
        pt, pb = gp()
        for j in range(8):
            sq, bq = sqt()
            add("act", lambda e, sq=sq, j=j: e.activation(out=sq[:], in_=xT[g][:, j, :], func=AF.Square), [b_xT[g][j]], [bq])
            add("pe", lambda e, sq=sq, j=j, pt=pt: e.matmul(pt[:, :], lhsT=ones_b[:, :], rhs=sq[:], start=(j == 0), stop=(j == 7)),
                [bq, b_ones_b], [pb])
        add("act", lambda e, pt=pt: e.activation(out=rstd[:], in_=pt[:, :], func=AF.Ln, bias=epsc[:, 0:1], scale=1.0 / D),
            [pb, b_epsc], [b_rstd])
        add("act", lambda e: e.activation(out=rstd[:], in_=rstd[:], func=AF.Exp, scale=-0.5), [b_rstd], [b_rstd])
        for j in range(8):
            tt, tb_ = tmp()
            add("dve", lambda e, tt=tt, j=j: e.scalar_tensor_tensor(out=tt[:], in0=xT[g][:, j, :], scalar=gsc[:, l, ni, j, g:g + 1],
                                                                    in1=rstd[:], op0=ALU.mult, op1=ALU.mult),
                [b_xT[g][j], b_rstd, b_mod[l]], [tb_])
            add("act", lambda e, tt=tt, j=j: e.activation(out=hT[g][:, j, :], in_=tt[:], func=AF.Identity,
                                                          bias=mcol(l, "sh1" if ni == 0 else "sh2", j, g), scale=1.0),
                [tb_, b_mod[l]], [b_hT[g]])

    def stat_rstd(src_list, nfeat, dst, b_dst, reads, K=128, n=T):
        pt, pb = gp()
        for i, src in enumerate(src_list):
            tt, tb_ = tmp()
            add("act", lambda e, tt=tt, src=src: e.activation(out=tt[0:K, 0:n], in_=src, func=AF.Square), reads, [tb_])
            add("pe", lambda e, tt=tt, i=i, pt=pt: e.matmul(pt[0:K, 0:n], lhsT=ones_f[0:K, 0:K], rhs=tt[0:K, 0:n],
                                                          start=(i == 0), stop=(i == len(src_list) - 1)),
                [tb_, b_ones_f], [pb])
        add("act", lambda e, pt=pt: e.activation(out=dst, in_=pt[0:K, 0:n], func=AF.Ln, bias=epsc[0:K, 0:1], scale=1.0 / nfeat),
            [pb, b_epsc], [b_dst])
        add("act", lambda e: e.activation(out=dst, in_=dst, func=AF.Exp, scale=-0.5), [b_dst], [b_dst])

    KTb = sb("KTb", [128, 4, DEC_S + PAST], BF16); b_KTh = [Buf("KT%d" % h) for h in range(4)]
    VAb = sb("VAb", [128, 18, 4, 72], BF16); b_VAk = [Buf("VA%d" % k) for k in range(18)]
    KTc2 = [sb("KTc%d" % m, [128, 4, T], BF16) for m in range(2)]
    b_KTc = [Buf("KTc%d" % m) for m in range(3)]
    VAc = [sb("VAc%d" % m, [128, 4, 4, 72], BF16) for m in range(3)]; b_VAc = [Buf("VAc%d" % m) for m in range(3)]
    QT2 = [[sb("QT%d_%d" % (g, m), [128, 4, T], BF16) for m in range(2)] for g in range(2)]
    b_QT = [[Buf("QT%d_%d" % (g, m)) for m in range(3)] for g in range(2)]
    add("pool", lambda e: e.memset(VAb[:, :, :, 64:72], 1.0), (), b_VAk)
    for m in range(3):
        add("pool", lambda e, m=m: e.memset(VAc[m][:, :, :, 64:72], 1.0), (), [b_VAc[m]])

    def q_ap(g, m, h, p_lo, p_hi, c0=0, n=T):
        if m == 0:
            return QT2[g][0][p_lo:p_hi, h, c0:c0 + n]
        if m == 1:
            return QT2[g][1][p_lo:p_hi, h, c0:c0 + n]
        return QT2[g][0][64 + p_lo:64 + p_hi, h, c0:c0 + n]

    def kc_ap(m, h, p_lo, p_hi, c0, n):
        if m == 0:
            return KTc2[0][p_lo:p_hi, h, c0:c0 + n]
        if m == 1:
            return KTc2[1][p_lo:p_hi, h, c0:c0 + n]
        return KTc2[0][64 + p_lo:64 + p_hi, h, c0:c0 + n]
    Eb = [sb("E%d" % i, [128, T], BF16) for i in range(3)]
    b_E = [Buf("E%d" % i) for i in range(3)]
    e_i = [0]
    NSET = 2
    rsum_s = [sb("rsum0", [128, T], F32)] * 2; b_rsum_s = [Buf("rsum0")] * 2
    rsb0 = sb("rsb0", [128, T], BF16)
    rsb_row = [64, 64]
    b_rsb_s = [Buf("rsb0")] * 2
    dao = sb("dao", [128, T], F32); b_dao = Buf("dao")
    set_i = [0]
    ckvT = sb("ckvT", [128, DEC_S + PAST], BF16); b_ckvT = Buf("ckvT")
    ckvTc = ckvT; b_ckvTc = b_ckvT
    qdn = sb("qdn", [128, 2, T], BF16); b_qdn = Buf("qdn")
    wq_s = sb("wq_s", [128, 2, 384], BF16); wqp_s = sb("wqp_s", [128, 2, 384], BF16); wkv_s = sb("wkv_s", [128, 512], BF16)
    b_wsm = Buf("wsmall")
    gpad = [sb("gpad%d" % g, [128, 2, 2, 15 + 256 + 15], BF16) for g in range(2)]
    b_gpad = [Buf("gpad%d" % g) for g in range(2)]
    for g in range(2):
        add("pool", lambda e, g=g: e.memset(gpad[g][:], 0.0), (), [b_gpad[g]])
    cacc = qd; b_cacc = b_qd
    halo = sb("halo", [128, 2, 4, 30], BF16); b_halo = Buf("halo")
    NTBT = 2
    tbt = [sb("tbt%d" % i, [128, A2, 64], BF16) for i in range(NTBT)] * (2 // NTBT)
    b_tbt = [Buf("tbt%d" % i) for i in range(NTBT)] * (2 // NTBT)
    b_payA = [Buf("payA%d" % l) for l in range(L)]
    b_payB = [Buf("payB%d" % l) for l in range(L)]
    b_payoA = [Buf("payoA%d" % l) for l in range(L)]
    b_payoB = [Buf("payoB%d" % l) for l in range(L)]
    b_tz = [Buf("tz%d" % l) for l in range(L)]

    pending_post = [None]
    import os as _os2
    WARM_LAT = int(_os2.environ.get("WARM_LAT", "0"))
    WARM_CTX = int(_os2.environ.get("WARM_CTX", "0"))
    WARM_N = int(_os2.environ.get("WARM_N", "256"))

    def warm(n):
        for _ in range(n):
            add("pe", lambda e: e.matmul(banks[7][0][:, 512 - WARM_N:512], lhsT=identb[:, 0:128], rhs=identb[:, 0:128], start=True, stop=True), (), ())

    def flush_post():
        if pending_post[0] is not None:
            f = pending_post[0]
            pending_post[0] = None
            f()

    def attention(kt_fn, q_ap, va_fn, nkb, nq, scale, reads, extra_fn=None):
        at, ab = acb()
        pend = []

        def score(kb):
            pt, pb = gp()
            ex = extra_fn(kb) if extra_fn is not None else []
            add("pe", lambda e, pt=pt, kb=kb: e.matmul(pt[:, 0:nq], lhsT=kt_fn(kb), rhs=q_ap, start=True, stop=(len(ex) == 0)),
                reads, [pb])
            for i, (lh, rh, rd) in enumerate(ex):
                add("pe", lambda e, pt=pt, lh=lh, rh=rh, i=i: e.matmul(pt[:, 0:nq], lhsT=lh, rhs=rh, start=False, stop=(i == len(ex) - 1)),
                    rd, [pb])
            i = e_i[0] % 3
            e_i[0] += 1
            add("act", lambda e, pt=pt, i=i: e.activation(out=Eb[i][:, 0:nq], in_=pt[:, 0:nq], func=AF.Exp, scale=scale), [pb], [b_E[i]])
            return i

        for kb in range(min(2, nkb)):
            pend.append(score(kb))
        flush_post()
        for kb in range(nkb):
            if kb + 2 < nkb:
                pend.append(score(kb + 2))
            warm(WARM_LAT)
            i = pend[kb]
            add("pe", lambda e, at=at, kb=kb, i=i: e.matmul(at[0:65, 0:nq], lhsT=va_fn(kb), rhs=Eb[i][:, 0:nq], start=(kb == 0), stop=(kb == nkb - 1)),
                reads + [b_E[i]], [ab])
        return at, ab

    def normalize_rep(at, ab, nq, dst, b_dst_list):
        r, br = tmp()
        add("act", lambda e: e.activation(out=r[0:64, 0:nq], in_=at[0:64, 256:256 + nq], func=AF.Ln), [ab], [br])
        add("act", lambda e: e.activation(out=r[0:64, 0:nq], in_=r[0:64, 0:nq], func=AF.Exp, scale=-1.0), [br], [br])
        add("dve", lambda e: e.tensor_tensor(out=dst, in0=at[0:64, 0:nq], in1=r[0:64, 0:nq], op=ALU.mult), [ab, br], b_dst_list)

    def normalize(at, ab, nq, dst, b_dst_list, k=0):
        rsum, b_rsum, b_rsb, rr = rsum_s[k], b_rsum_s[k], b_rsb_s[k], rsb_row[k]
        add("act", lambda e: e.activation(out=rsum[64:65, 0:nq], in_=at[64:65, 0:nq], func=AF.Ln), [ab], [b_rsum])
        add("act", lambda e: e.activation(out=rsb0[rr:rr + 1, 0:nq], in_=rsum[64:65, 0:nq], func=AF.Exp, scale=-1.0), [b_rsum], [b_rsb])
        pt, pb = gp()
        add("pe", lambda e: e.matmul(pt[0:64, 0:nq], lhsT=ones_b[rr:rr + 1, 0:64], rhs=rsb0[rr:rr + 1, 0:nq], start=True, stop=True),
            [b_rsb, b_ones_b], [pb])
        bc, bbc = tmp()
        add("act", lambda e: e.activation(out=bc[0:64, 0:nq], in_=pt[0:64, 0:nq], func=AF.Copy), [pb], [bbc])
        add("dve", lambda e: e.tensor_tensor(out=dst, in0=at[0:64, 0:nq], in1=bc[0:64, 0:nq], op=ALU.mult), [ab, bbc], b_dst_list)

    def stream_attention(jobs, nkb=18, nq=T, side_cb=None):
        blocks = [(ji, kb) for ji in range(len(jobs)) for kb in range(nkb)]
        N = len(blocks)
        pend = {}
        posts = {}

        def do_score(idx):
            ji, kb = blocks[idx]
            J = jobs[ji]
            if kb == 0 and J.get("pre") is not None:
                J["pre"]()
            pt, pb = gp()
            ex = J["extra_fn"](kb) if J.get("extra_fn") is not None else []
            add("pe", lambda e, pt=pt, kb=kb, J=J: e.matmul(pt[:, 0:nq], lhsT=J["kt_fn"](kb), rhs=J["q_ap"], start=True, stop=(len(ex) == 0)),
                J["reads"], [pb])
            for i2, (lh, rh, rd) in enumerate(ex):
                add("pe", lambda e, pt=pt, lh=lh, rh=rh, i2=i2: e.matmul(pt[:, 0:nq], lhsT=lh, rhs=rh, start=False, stop=(i2 == len(ex) - 1)),
                    rd, [pb])
            i = e_i[0] % 3
            e_i[0] += 1
            add("act", lambda e, pt=pt, i=i, J=J: e.activation(out=Eb[i][:, 0:nq], in_=pt[:, 0:nq], func=AF.Exp, scale=J["scale"]), [pb], [b_E[i]])
            pend[idx] = i

        for idx in range(min(2, N)):
            do_score(idx)
        for idx in range(N):
            if idx + 2 < N:
                do_score(idx + 2)
            ji, kb = blocks[idx]
            J = jobs[ji]
            if kb == 0:
                J["acc"] = acb()
            at, ab = J["acc"]
            i = pend.pop(idx)
            add("pe", lambda e, at=at, kb=kb, i=i, J=J: e.matmul(at[0:65, 0:nq], lhsT=J["va_fn"](kb), rhs=Eb[i][:, 0:nq],
                                                              start=(kb == 0), stop=(kb == nkb - 1)), J["reads"] + [b_E[i]], [ab])
            if kb == nkb - 1:
                if J.get("post") is not None:
                    posts.setdefault(min(idx + 2, N - 1), []).append(J["post"])
                if side_cb is not None:
                    side_cb()
            for f in posts.pop(idx, []):
                f()

    def input_proj(l):
        res = {}
        add("pool", lambda e: e.dma_start(out=wq_s[:], in_=wqup[l].rearrange("(k p) c -> p k c", p=128)), (), [b_wsm], dma=True)
        add("pool", lambda e: e.dma_start(out=wqp_s[:], in_=wqupp[l].rearrange("(k p) c -> p k c", p=128)), (), [b_wsm], dma=True)
        add("pool", lambda e: e.dma_start(out=wkv_s[:], in_=wkvup[l]), (), [b_wsm], dma=True)

        def fm(wt, wb, c0, M, g, n0=0, n=T):
            pt, pb = gp()
            for k in range(8):
                add("pe", lambda e, pt=pt, k=k: e.matmul(pt[0:M, 0:n], lhsT=wt[:, k, c0:c0 + M], rhs=hT[g][:, k, n0:n0 + n],
                                                        start=(k == 0), stop=(k == 7)), [wb, b_hT[g]], [pb])
            return pt, pb

        def tm(wt, wb, c0, N, g, tb):
            pt, pb = gp()
            for k in range(8):
                add("pe", lambda e, pt=pt, k=k: e.matmul(pt[:, 0:N], lhsT=hT[g][:, k, tb * 128:(tb + 1) * 128], rhs=wt[:, k, c0:c0 + N],
                                                        start=(k == 0), stop=(k == 7)), [wb, b_hT[g]], [pb])
            return pt, pb

        def rope_evac(ptA, pbA, ptB, pbB, p0, p1, dst, wlist):
            t1, tb1 = tmp()
            t2, tb2 = tmp()
            add("dve", lambda e: e.tensor_tensor(out=t1[p0:p1, :], in0=ptA[p0:p1, :], in1=cos_s[p0:p1, :], op=ALU.mult), [pbA, b_prm], [tb1])
            add("dve", lambda e: e.tensor_tensor(out=t2[p0:p1, :], in0=ptB[p0:p1, :], in1=sin_s[p0:p1, :], op=ALU.mult), [pbB, b_prm], [tb2])
            add("dve", lambda e: e.tensor_tensor(out=dst, in0=t1[p0:p1, :], in1=t2[p0:p1, :], op=ALU.add), [tb1, tb2], wlist)

        groups = [0, 1] if do_lat else [0]
        wA, bA = wtile(w_in[l, :, 0:512], 512)
        if do_lat:
            wF1, bF1 = wtile(w_inp[l, :, 0:512], 512)
        for h in range(4):
            pt, pb = fm(wA, bA, h * 64, 64, 0)
            evac(QT2[0][0][0:64, h, :], pt[0:64, :], [pb], [b_QT[0][0]])
            pt, pb = fm(wA, bA, 256 + h * 64, 64, 0)
            evac(KTc2[0][0:64, h, :], pt[0:64, :], [pb], [b_KTc[0]])
        if do_lat:
            for h in range(4):
                ptA, pbA = fm(wA, bA, h * 64, 64, 1)
                ptB, pbB = fm(wF1, bF1, h * 64, 64, 1)
                rope_evac(ptA, pbA, ptB, pbB, 0, 64, QT2[1][0][0:64, h, :], [b_QT[1][0]])
                ptA, pbA = fm(wA, bA, 256 + h * 64, 64, 1)
                ptB, pbB = fm(wF1, bF1, 256 + h * 64, 64, 1)
                tk, tkb = sqt()
                rope_evac(ptA, pbA, ptB, pbB, 0, 64, tk[0:64, :], [tkb])
                add("sp", lambda e, tk=tk, h=h: e.dma_start(out=payA_in[l].ap()[R_DAK + h * 64:R_DAK + (h + 1) * 64, :], in_=tk[0:64, :]),
                    [tkb], [b_payA[l]], dma=True)
        wB, bB = wtile(w_in[l, :, 512:1024], 512)
        for g in groups:
            for c in range(2):
                pt, pb = fm(wB, bB, 256 + c * 128, 128, g)
                evac(qd[:, c, :], pt[:, :], [pb], [b_qd])
            stat_rstd([qd[:, 0, :], qd[:, 1, :]], 256, rstd[:], b_rstd, [b_qd])
            for c in range(2):
                add("dve", lambda e, c=c: e.scalar_tensor_tensor(out=qdn[:, c, :], in0=qd[:, c, :], scalar=gq_s[:, l, c:c + 1], in1=rstd[:],
                                                               op0=ALU.mult, op1=ALU.mult), [b_qd, b_rstd, b_prm], [b_qdn])
            for h in range(4):
                pt, pb = gp()
                for c in range(2):
                    add("pe", lambda e, pt=pt, c=c, h=h: e.matmul(pt[0:96, :], lhsT=wq_s[:, c, h * 96:(h + 1) * 96], rhs=qdn[:, c, :],
                                                                 start=(c == 0), stop=(c == 1)), [b_wsm, b_qdn], [pb])
                if g == 0:
                    evac(QT2[0][1][0:96, h, :], pt[0:96, :], [pb], [b_QT[0][1]])
                else:
                    pt2, pb2 = gp()
                    for c in range(2):
                        add("pe", lambda e, pt2=pt2, c=c, h=h: e.matmul(pt2[0:96, :], lhsT=wqp_s[:, c, h * 96:(h + 1) * 96], rhs=qdn[:, c, :],
                                                                       start=(c == 0), stop=(c == 1)), [b_wsm, b_qdn], [pb2])
                    evac(QT2[1][1][0:64, h, :], pt[0:64, :], [pb], [b_QT[1][1]])
                    rope_evac(pt, pb, pt2, pb2, 64, 96, QT2[1][1][64:96, h, :], [b_QT[1][1]])
        for g in groups:
            for tb in range(4):
                pt, pb = tm(wB, bB, 0, 256, g, tb)
                if g == 0:
                    ostage, b_ostage = tmp()
                    add("act", lambda e, pt=pt, ostage=ostage: e.activation(out=ostage[:, 0:256], in_=pt[:, 0:256], func=AF.Copy), [pb], [b_ostage])
                    s, tl = tb // 2, (tb % 2) * 128
                    add("sp", lambda e, s=s, tl=tl, ostage=ostage: e.dma_start(out=o_dav[s, l, :, tl:tl + 128, :].rearrange("h t d -> t h d"),
                                                               in_=ostage[:, 0:256].rearrange("p (h d) -> p h d", h=4)), [b_ostage], (), dma=True)
                    add("dve", lambda e, pt=pt, tb=tb: e.tensor_copy(out=VAc[0][:, tb, :, 0:64], in_=pt[:, 0:256].rearrange("p (h d) -> p h d", h=4)),
                        [pb], [b_VAc[0]])
                else:
                    tv, tvb = sqt()
                    evac(tv[:, 0:256], pt[:, 0:256], [pb], [tvb])
                    add("sp", lambda e, tv=tv, tb=tb: e.dma_start(out=payB_in[l].ap()[R_V + tb * 128:R_V + (tb + 1) * 128, 0:256], in_=tv[:, 0:256]),
                        [tvb], [b_payB[l]], dma=True)
        wC, bC = wtile(w_in[l, :, 1024:1440], 416)
        if do_lat:
            wF2, bF2 = wtile(w_inp[l, :, 512:608], 96)
        for g in groups:
            pt, pb = fm(wC, bC, 0, 128, g)
            kvd, b_kvd = tmp()
            evac(kvd[:, :], pt[:, :], [pb], [b_kvd])
            stat_rstd([kvd[:, :]], 128, rstd[:], b_rstd, [b_kvd])
            dstc = ckvTc if g == 0 else sqb[0]
            if g == 0:
                add("dve", lambda e, kvd=kvd: e.scalar_tensor_tensor(out=ckvTc[:, 0:T], in0=kvd[:, :], scalar=gkvc_s[:, l:l + 1], in1=rstd[:],
                                                            op0=ALU.mult, op1=ALU.mult), [b_kvd, b_rstd, b_prm], [b_ckvTc])
            else:
                tk, tkb = sqt()
                add("dve", lambda e, tk=tk, kvd=kvd: e.scalar_tensor_tensor(out=tk[:, :], in0=kvd[:, :], scalar=gkvc_s[:, l:l + 1], in1=rstd[:],
                                                                   op0=ALU.mult, op1=ALU.mult), [b_kvd, b_rstd, b_prm], [tkb])
                add("sp", lambda e, tk=tk: e.dma_start(out=payA_in[l].ap()[R_CKV:R_CKV + 128, :], in_=tk[:, :]), [tkb], [b_payA[l]], dma=True)
            pt, pb = fm(wC, bC, 64, 96, g)
            if g == 0:
                for h in range(4):
                    evac(KTc2[1][64:96, h, :], pt[64:96, :], [pb], [b_KTc[1]])
            else:
                ptB, pbB = fm(wF2, bF2, 0, 96, 1)
                tk, tkb = sqt()
                rope_evac(pt, pb, ptB, pbB, 64, 96, tk[64:96, :], [tkb])
                add("sp", lambda e, tk=tk: e.dma_start(out=payA_in[l].ap()[R_KR:R_KR + 32, :], in_=tk[64:96, :]), [tkb], [b_payA[l]], dma=True)
            for h in range(4):
                pt, pb = fm(wC, bC, 160 + h * 64, 64, g)
                evac(QT2[g][0][64:128, h, :], pt[0:64, :], [pb], [b_QT[g][2]])
        for tb in range(4):
            pt, pb = tm(wC, bC, 0, 160, 0, tb)
            s, tl = tb // 2, (tb % 2) * 128
            tt, tb_ = tmp()
            add("act", lambda e, pt=pt, tt=tt: e.activation(out=tt[:, 0:128], in_=pt[:, 0:128], func=AF.Square), [pb], [tb_])
            add("dve", lambda e, tt=tt: e.reduce_sum(out=tt[:, 200:201], in_=tt[:, 0:128], axis=mybir.AxisListType.X), [tb_], [tb_])
            add("act", lambda e, tt=tt: e.activation(out=tt[:, 201:202], in_=tt[:, 200:201], func=AF.Ln, bias=epsc[:, 0:1], scale=1.0 / 128),
                [tb_, b_epsc], [tb_])
            add("act", lambda e, tt=tt: e.activation(out=tt[:, 202:203], in_=tt[:, 201:202], func=AF.Exp, scale=-0.5), [tb_], [tb_])
            add("dve", lambda e, pt=pt, tt=tt: e.scalar_tensor_tensor(out=tt[:, 256:384], in0=pt[:, 0:128], scalar=tt[:, 202:203],
                                                                      in1=gkvr_s[:, l, :], op0=ALU.mult, op1=ALU.mult),
                [pb, tb_, b_prm], [tb_])
            add("act", lambda e, pt=pt, tt=tt: e.activation(out=tt[:, 384:416], in_=pt[:, 128:160], func=AF.Copy), [pb, tb_], [tb_])
            add("sp", lambda e, s=s, tl=tl, tt=tt: e.dma_start(out=o_ckv[s, l, tl:tl + 128, :], in_=tt[:, 256:384]), [tb_], (), dma=True)
            add("sp", lambda e, s=s, tl=tl, tt=tt: e.dma_start(out=o_kr[s, l, tl:tl + 128, :], in_=tt[:, 384:416]), [tb_], (), dma=True)
        for h in range(4):
            pt, pb = gp()
            add("pe", lambda e, pt=pt, h=h: e.matmul(pt[0:64, :], lhsT=wkv_s[:, h * 128:h * 128 + 64], rhs=ckvTc[:, 0:T], start=True, stop=True),
                [b_wsm, b_ckvTc], [pb])
            evac(KTc2[1][0:64, h, :], pt[0:64, :], [pb], [b_KTc[1]])
        for tb in range(4):
            pt, pb = gp()
            add("pe", lambda e, pt=pt, tb=tb: e.matmul(pt[:, 0:256], lhsT=ckvTc[:, tb * 128:(tb + 1) * 128],
                                                      rhs=wkv_s[:, :].rearrange("p (h c) -> p h c", h=4)[:, :, 64:128], start=True, stop=True),
                [b_wsm, b_ckvTc], [pb])
            evac(VAc[1][:, tb, :, 0:64], pt[:, 0:256].rearrange("p (h d) -> p h d", h=4), [pb], [b_VAc[1]])
        wD, bD = wtile(w_in[l, :, 1440:1952], 512)
        for h in range(4):
            pt, pb = fm(wD, bD, h * 64, 64, 0)
            evac(KTc2[0][64:128, h, :], pt[0:64, :], [pb], [b_KTc[2]])
            if do_lat:
                pt, pb = fm(wD, bD, h * 64, 64, 1)
                tk, tkb = sqt()
                evac(tk[0:64, :], pt[0:64, :], [pb], [tkb])
                add("sp", lambda e, tk=tk, h=h: e.dma_start(out=payA_in[l].ap()[R_NAK + h * 64:R_NAK + (h + 1) * 64, :], in_=tk[0:64, :]),
                    [tkb], [b_payA[l]], dma=True)
        for tb in range(4):
            pt, pb = tm(wD, bD, 0, 512, 0, tb)
            s, tl = tb // 2, (tb % 2) * 128
            ostage, b_ostage = tmp()
            add("act", lambda e, pt=pt, ostage=ostage: e.activation(out=ostage[:, :], in_=pt[:, :], func=AF.Copy), [pb], [b_ostage])
            add("sp", lambda e, s=s, tl=tl, ostage=ostage: e.dma_start(out=o_nak[s, l, :, tl:tl + 128, :].rearrange("h t d -> t h d"),
                                                       in_=ostage[:, 0:256].rearrange("p (h d) -> p h d", h=4)), [b_ostage], (), dma=True)
            add("sp", lambda e, s=s, tl=tl, ostage=ostage: e.dma_start(out=o_nav[s, l, :, tl:tl + 128, :].rearrange("h t d -> t h d"),
                                                       in_=ostage[:, 256:512].rearrange("p (h d) -> p h d", h=4)), [b_ostage], (), dma=True)
            add("dve", lambda e, pt=pt, tb=tb: e.tensor_copy(out=VAc[2][:, tb, :, 0:64], in_=pt[:, 256:512].rearrange("p (h d) -> p h d", h=4)),
                [pb], [b_VAc[2]])
            if do_lat:
                pt, pb = tm(wD, bD, 256, 256, 1, tb)
                tv, tvb = sqt()
                evac(tv[:, 0:256], pt[:, 0:256], [pb], [tvb])
                add("sp", lambda e, tv=tv, tb=tb: e.dma_start(out=payB_in[l].ap()[R_V + tb * 128:R_V + (tb + 1) * 128, 256:512], in_=tv[:, 0:256]),
                    [tvb], [b_payB[l]], dma=True)
        wA2, bA2 = wtile(w_in[l, :, 256:512], 256)
        for tb in range(4):
            pt, pb = tm(wA2, bA2, 0, 256, 0, tb)
            s, tl = tb // 2, (tb % 2) * 128
            ostage, b_ostage = tmp()
            add("act", lambda e, pt=pt, ostage=ostage: e.activation(out=ostage[:, 0:256], in_=pt[:, 0:256], func=AF.Copy), [pb], [b_ostage])
            add("sp", lambda e, s=s, tl=tl, ostage=ostage: e.dma_start(out=o_dak[s, l, :, tl:tl + 128, :].rearrange("h t d -> t h d"),
                                                       in_=ostage[:, 0:256].rearrange("p (h d) -> p h d", h=4)), [b_ostage], (), dma=True)
        wE, bE = wtile(w_in[l, :, 1952:2464], 512)
        for g in groups:
            for c in range(2):
                pa, pba = fm(wE, bE, c * 128, 128, g)
                pbm, pbb = fm(wE, bE, 256 + c * 128, 128, g)
                tt, tb_ = tmp()
                add("act", lambda e, tt=tt, pbm=pbm: e.activation(out=tt[:], in_=pbm[:, :], func=AF.Sigmoid), [pbb], [tb_])
                add("dve", lambda e, tt=tt, pa=pa, c=c, g=g: e.tensor_tensor(out=gpad[g][:, c, :, 15:15 + 256],
                                                                           in0=pa[:, :].rearrange("p (s t) -> p s t", s=2),
                                                                           in1=tt[:].rearrange("p (s t) -> p s t", s=2), op=ALU.mult),
                    [pba, tb_], [b_gpad[g]])
        if do_lat:
            add("dve", lambda e: e.tensor_copy(out=halo[:, :, 0, 0:15], in_=gpad[1][:, :, 0, 15:30]), [b_gpad[1]], [b_halo])
            add("dve", lambda e: e.tensor_copy(out=halo[:, :, 0, 15:30], in_=gpad[1][:, :, 1, 256:271]), [b_gpad[1]], [b_halo])
            hdst = dram_ap(payB_in[l], R_HALO * T, [[30, 128], [128 * 30, 2], [1, 30]])
            add("sp", lambda e: e.dma_start(out=hdst, in_=halo[:, :, 0, :]), [b_halo], [b_payB[l]], dma=True)
            add("pool", lambda e: e.collective_compute("AllGather", ALU.bypass, replica_groups=[[0, 1, 2, 3], [4, 5, 6, 7]],
                                                       ins=[payA_in[l].ap().opt()], outs=[payA_out[l].ap().opt()]),
                [b_payA[l]], [b_payoA[l]], dma="cc")
            add("pool", lambda e: e.collective_compute("AllGather", ALU.bypass, replica_groups=[[0, 1, 2, 3], [4, 5, 6, 7]],
                                                       ins=[payB_in[l].ap().opt()], outs=[payB_out[l].ap().opt()]),
                [b_payB[l]], [b_payoB[l]], dma="cc")

    def da_post(l, g, h, at1, ab1, at2, ab2, nq=T, q0=0, repl=False):
        o1, bo1 = dao, b_dao
        o2, bo2 = rsum_s[0], b_rsum_s[0]
        if repl:
            normalize_rep(at1, ab1, nq, o1[0:64, 0:nq], [bo1])
            normalize_rep(at2, ab2, nq, o2[0:64, 0:nq], [bo2])
        else:
            normalize(at1, ab1, nq, o1[0:64, 0:nq], [bo1], 0)
            normalize(at2, ab2, nq, o2[0:64, 0:nq], [bo2], 1 if nq <= 256 else 0)
        add("dve", lambda e: e.scalar_tensor_tensor(out=o1[0:64, 0:nq], in0=o2[0:64, 0:nq], scalar=neglam[0:64, l:l + 1], in1=o1[0:64, 0:nq],
                                                    op0=ALU.mult, op1=ALU.add), [bo1, bo2, b_prm], [bo1])
        stat_rstd([o1[0:64, 0:nq]], 64, rstd[0:64, 0:nq], b_rstd, [bo1], K=64, n=nq)
        p0 = (h % 2) * 64
        add("dve", lambda e: e.scalar_tensor_tensor(out=mixT[g][p0:p0 + 64, h // 2, q0:q0 + nq], in0=o1[0:64, 0:nq], scalar=gsub2[0:64, l:l + 1],
                                                    in1=rstd[0:64, 0:nq], op0=ALU.mult, op1=ALU.mult),
            [bo1, b_rstd, b_prm], [b_mix[g][h // 2]])

    def plain_post(g, m, h, at, ab, nq=T, q0=0, repl=False):
        if repl:
            p0 = (h % 2) * 64
            ch = 2 * m + h // 2
            normalize_rep(at, ab, nq, mixT[g][p0:p0 + 64, ch, q0:q0 + nq], [b_mix[g][ch]])
            return
        k = (set_i[0] % NSET) if nq <= 256 else 0
        set_i[0] += 1
        o1, bo1 = tmp()
        normalize(at, ab, nq, o1[0:64, 0:nq], [bo1], k)
        p0 = (h % 2) * 64
        ch = 2 * m + h // 2
        add("act", lambda e: e.activation(out=mixT[g][p0:p0 + 64, ch, q0:q0 + nq], in_=o1[0:64, 0:nq], func=AF.Copy), [bo1], [b_mix[g][ch]])

    def ctx_attention(l, side_gen=None):
        scales = (32 ** -0.5, 96 ** -0.5, 64 ** -0.5)
        rows = {0: None, 1: (0, 96), 2: (0, 64)}
        jobs = []
        for s in range(2):
            for m in range(3):
                for h in range(4):
                    for mp in ((0, 1) if m == 0 else (0,)):
                        jobs.append(dict(s=s, m=m, h=h, mp=mp))
        n = len(jobs)

        def stage_a(j):
            s_, m, h, mp = j["s"], j["m"], j["h"], j["mp"]
            q0 = s_ * 256
            lo, hi = (mp * 32, mp * 32 + 32) if m == 0 else rows[m]
            pt, pb = gp()
            rd = [b_KTc[m], b_QT[0][m]]
            for kb in range(2):
                add("pe", lambda e, pt=pt, kb=kb, m=m, h=h, lo=lo, hi=hi, q0=q0: e.matmul(
                    pt[:, kb * 256:(kb + 1) * 256], lhsT=kc_ap(m, h, lo, hi, q0 + kb * 128, 128), rhs=q_ap(0, m, h, lo, hi, q0, 256),
                    start=True, stop=True), rd, [pb])
            i = e_i[0] % 3
            e_i[0] += 1
            add("act", lambda e, pt=pt, i=i, m=m: e.activation(out=Eb[i][:, :], in_=pt[:, :], func=AF.Exp, scale=scales[m]), [pb], [b_E[i]])
            j["e"] = i

        def stage_b(j):
            s_, m, h = j["s"], j["m"], j["h"]
            at, ab = acb()
            i = j["e"]
            for kb in range(2):
                add("pe", lambda e, at=at, kb=kb, i=i, m=m, h=h, s_=s_: e.matmul(at[0:65, 0:256], lhsT=VAc[m][:, s_ * 2 + kb, h, 0:65],
                                                                               rhs=Eb[i][:, kb * 256:(kb + 1) * 256], start=(kb == 0), stop=(kb == 1)),
                    [b_VAc[m], b_E[i]], [ab])
            for kb in range(2):
                add("pe", lambda e, at=at, kb=kb, i=i: e.matmul(at[0:64, 256:512], lhsT=ones_b[:, 0:64], rhs=Eb[i][:, kb * 256:(kb + 1) * 256],
                                                               start=(kb == 0), stop=(kb == 1)), [b_ones_b, b_E[i]], [ab])
            j["acc"] = (at, ab)

        def stage_c(idx):
            j = jobs[idx]
            s_, m, h, mp = j["s"], j["m"], j["h"], j["mp"]
            q0 = s_ * 256
            if m == 0:
                if mp == 1:
                    a1 = jobs[idx - 1]["acc"]
                    da_post(l, 0, h, a1[0], a1[1], j["acc"][0], j["acc"][1], nq=256, q0=q0, repl=True)
            else:
                plain_post(0, m, h, j["acc"][0], j["acc"][1], nq=256, q0=q0, repl=True)

        for i in range(n + 2):
            if i < n:
                stage_a(jobs[i])
            warm(WARM_CTX)
            if 0 <= i - 1 < n:
                stage_b(jobs[i - 1])
            if 0 <= i - 2 < n:
                stage_c(i - 2)
            if side_gen is not None and i % 2 == 1:
                next(side_gen, None)

    def cache_T(src_ap, ncol, dst_fn, wlist):
        stg, bstg = tmp()
        add("sp", lambda e: e.dma_start(out=stg[:, 0:2 * ncol].rearrange("p (t c) -> p t c", t=2),
                                        in_=src_ap.rearrange("(t p) c -> p t c", p=128)), (), [bstg], dma=True)
        for tb in range(2):
            pt, pb = gp()
            add("pe", lambda e, pt=pt, tb=tb: e.transpose(out=pt[0:ncol, 0:128], in_=stg[:, tb * ncol:(tb + 1) * ncol], identity=ident[:]),
                [bstg, b_ident], [pb])
            evac(dst_fn(tb), pt[0:ncol, 0:128], [pb], wlist)

    def load_V(l, c0, cache_ap):
        for rk in range(4):
            for tb in range(4):
                src = dram_ap(payB_out[l], (rk * PB + R_V + tb * 128) * T + c0, [[T, 128], [64, 4], [1, 64]])
                add("sp", lambda e, rk=rk, tb=tb, src=src: e.dma_start(out=VAb[:, rk * 4 + tb, :, 0:64], in_=src), [b_payoB[l]], [b_VAk[rk * 4 + tb]], dma=True)
        for tb in range(2):
            add("pool", lambda e, tb=tb: e.dma_start(out=VAb[:, 16 + tb, :, 0:64], in_=cache_ap[:, tb * 128:(tb + 1) * 128, :].rearrange("h t d -> t h d")),
                (), [b_VAk[16 + tb]], dma=True)

    def load_KT(l, row0, nrows, p0, heads=True):
        for h in range(4):
            r = row0 + (h * nrows if heads else 0)
            src = dram_ap(payA_out[l], r * T, [[T, nrows], [PA * T, 4], [1, T]])
            add("sp", lambda e, h=h, src=src: e.dma_start(out=KTb[p0:p0 + nrows, h, 0:DEC_S].rearrange("p (r t) -> p r t", r=4), in_=src),
                [b_payoA[l]], [b_KTh[h]], dma=True)

    def lat_attention(l, side_gen=None, mod_gen=None):
        rdv = list(b_VAk)

        def side_step(n=1):
            if side_gen is not None:
                for _ in range(n):
                    next(side_gen, None)
            if mod_gen is not None:
                next(mod_gen, None)

        load_KT(l, R_DAK, 64, 0)
        for h in range(4):
            cache_T(c_dak[l, h], 64, lambda tb, h=h: KTb[0:64, h, DEC_S + tb * 128:DEC_S + (tb + 1) * 128], [b_KTh[h]])
        load_V(l, 0, c_dav[l])
        jobs = []
        for h in range(4):
            for mp in range(2):
                r0 = mp * 32
                J = dict(kt_fn=(lambda kb, h=h, r0=r0: KTb[r0:r0 + 32, h, kb * 128:(kb + 1) * 128]), q_ap=q_ap(1, 0, h, r0, r0 + 32),
                         va_fn=(lambda kb, h=h: VAb[:, kb, h, 0:65]), scale=32 ** -0.5, reads=rdv + [b_KTh[h], b_QT[1][0]])
                jobs.append(J)
                if mp == 1:
                    J1, J2 = jobs[-2], jobs[-1]
                    J["post"] = (lambda h=h, J1=J1, J2=J2: da_post(l, 1, h, J1["acc"][0], J1["acc"][1], J2["acc"][0], J2["acc"][1]))
        stream_attention(jobs, side_cb=lambda: side_step(1))
        src = dram_ap(payA_out[l], R_CKV * T, [[T, 128], [PA * T, 4], [1, T]])
        add("sp", lambda e, src=src: e.dma_start(out=ckvT[:, 0:DEC_S].rearrange("p (r t) -> p r t", r=4), in_=src), [b_payoA[l]], [b_ckvT], dma=True)
        cache_T(c_ckv[l], 128, lambda tb: ckvT[:, DEC_S + tb * 128:DEC_S + (tb + 1) * 128], [b_ckvT])
        load_KT(l, R_KR, 32, 64, heads=False)
        for h in range(4):
            cache_T(c_kr[l], 32, lambda tb, h=h: KTb[64:96, h, DEC_S + tb * 128:DEC_S + (tb + 1) * 128], [b_KTh[h]])
        for h in range(4):
            for cb in range(5):
                n0 = cb * 512
                n = min(512, DEC_S + PAST - n0)
                pt, pb = gp()
                add("pe", lambda e, pt=pt, h=h, n0=n0, n=n: e.matmul(pt[0:64, 0:n], lhsT=wkv_s[:, h * 128:h * 128 + 64], rhs=ckvT[:, n0:n0 + n],
                                                                    start=True, stop=True), [b_wsm, b_ckvT], [pb])
                evac(KTb[0:64, h, n0:n0 + n], pt[0:64, 0:n], [pb], [b_KTh[h]])
        for kb in range(18):
            pt, pb = gp()
            add("pe", lambda e, pt=pt, kb=kb: e.matmul(pt[:, 0:256], lhsT=ckvT[:, kb * 128:(kb + 1) * 128],
                                                      rhs=wkv_s[:, :].rearrange("p (h c) -> p h c", h=4)[:, :, 64:128], start=True, stop=True),
                [b_wsm, b_ckvT], [pb])
            evac(VAb[:, kb, :, 0:64], pt[:, 0:256].rearrange("p (h d) -> p h d", h=4), [pb], [b_VAk[kb]])
        jobs = []
        for h in range(4):
            J = dict(kt_fn=(lambda kb, h=h: KTb[0:96, h, kb * 128:(kb + 1) * 128]), q_ap=q_ap(1, 1, h, 0, 96),
                     va_fn=(lambda kb, h=h: VAb[:, kb, h, 0:65]), scale=96 ** -0.5, reads=rdv + [b_KTh[h], b_QT[1][1]])
            J["post"] = (lambda h=h, J=J: plain_post(1, 1, h, J["acc"][0], J["acc"][1]))
            jobs.append(J)
        stream_attention(jobs, side_cb=lambda: side_step(2))
        load_KT(l, R_NAK, 64, 0)
        for h in range(4):
            cache_T(c_nak[l, h], 64, lambda tb, h=h: KTb[0:64, h, DEC_S + tb * 128:DEC_S + (tb + 1) * 128], [b_KTh[h]])
            add("pool", lambda e, h=h: e.dma_start(out=KTb[64:96, h, 0:DEC_S], in_=rowind_d), (), [b_KTh[h]], dma=True)
            add("pool", lambda e, h=h: e.memset(KTb[64:96, h, DEC_S:DEC_S + PAST], 0.0), (), [b_KTh[h]])
            evac(QT2[1][0][0:64, h, :], QT2[1][0][64:128, h, :], [b_QT[1][2], b_QT[1][0]], [b_QT[1][0], b_QT[1][2]])
            add("pool", lambda e, h=h: e.dma_start(out=QT2[1][0][64:96, h, :], in_=rowsel_d), (), [b_QT[1][0], b_QT[1][2]], dma=True)
        load_V(l, 256, c_nav[l])
        jobs = []
        for h in range(4):
            ti = h % 2

            def pre(h=h, ti=ti):
                for j in range(2):
                    src = bass.AP(tz_all, l * TZ_SZ + (h * (A2 + 1) + (1 - j)) * 64 * TZL + 63, [[TZL - 1, 64], [64 * TZL, A2], [1, 64]])
                    add("pool", lambda e, j=j, src=src, ti=ti: e.dma_start(out=tbt[ti][j * 64:(j + 1) * 64, :, :], in_=src), (), [b_tbt[ti]], dma=True)
                add("dve", lambda e, ti=ti: e.scalar_tensor_tensor(out=tbt[ti][:], in0=tbt[ti][:], scalar=8.0,
                                                                   in1=colm_s[:, :].unsqueeze(1).to_broadcast([128, A2, 64]),
                                                                   op0=ALU.mult, op1=ALU.add), [b_tbt[ti], b_prm], [b_tbt[ti]])

            def extra(kb, ti=ti):
                if kb >= 16:
                    return []
                a0 = 37 - 2 * kb
                return [(identb[:, :], tbt[ti][:, a0:a0 + 8, :], [b_identb, b_tbt[ti]])]

            J = dict(kt_fn=(lambda kb, h=h: KTb[0:96, h, kb * 128:(kb + 1) * 128]), q_ap=QT2[1][0][0:96, h, :],
                     va_fn=(lambda kb, h=h: VAb[:, kb, h, 0:65]), scale=64 ** -0.5, reads=rdv + [b_KTh[h], b_QT[1][2]],
                     extra_fn=extra, pre=pre)
            J["post"] = (lambda h=h, J=J: plain_post(1, 2, h, J["acc"][0], J["acc"][1]))
            jobs.append(J)
        stream_attention(jobs, side_cb=lambda: side_step(2))

    def conv_module(l, g):
        if g == 1:
            for c in range(2):
                src = dram_ap(payB_out[l], R_HALO * T + c * 128 * 30, [[30, 128], [PB * T, 4], [1, 30]])
                add("sp", lambda e, src=src, c=c: e.dma_start(out=halo[:, c, :, :], in_=src), [b_payoB[l]], [b_halo], dma=True)
            for side in range(2):
                dst = gpad[1][:, :, 0, 0:15] if side == 0 else gpad[1][:, :, 1, 271:286]
                for rk in range(4):
                    srcv = halo[:, :, rk, 15:30] if side == 0 else halo[:, :, rk, 0:15]
                    sc = halsel_s[:, side * 4 + rk:side * 4 + rk + 1]
                    if rk == 0:
                        add("dve", lambda e, dst=dst, srcv=srcv, sc=sc: e.tensor_scalar(out=dst, in0=srcv, scalar1=sc, scalar2=None, op0=ALU.mult),
                            [b_halo, b_prm, b_gpad[1]], [b_gpad[1]])
                    else:
                        add("dve", lambda e, dst=dst, srcv=srcv, sc=sc: e.scalar_tensor_tensor(out=dst, in0=srcv, scalar=sc, in1=dst,
                                                                                              op0=ALU.mult, op1=ALU.add),
                            [b_halo, b_prm, b_gpad[1]], [b_gpad[1]])
        for c in range(2):
            for j in range(31):
                if g == 0:
                    src = gpad[0][:, c, :, j:j + 256]
                    dst = cacc[:, c, :].rearrange("p (s t) -> p s t", s=2)
                    ops = [(src, dst)]
                else:
                    ops = []
                    n_a = max(0, min(256, 271 - j))
                    if n_a > 0:
                        ops.append((gpad[1][:, c, 0, j:j + n_a], cacc[:, c, 0:n_a]))
                    if n_a < 256:
                        ops.append((gpad[1][:, c, 1, 15 + (n_a + j - 271):15 + (256 + j - 271)], cacc[:, c, n_a:256]))
                    n_b = max(0, min(256, 271 - (256 + j)))
                    if n_b > 0:
                        ops.append((gpad[1][:, c, 0, 256 + j:256 + j + n_b], cacc[:, c, 256:256 + n_b]))
                    ops.append((gpad[1][:, c, 1, 15 + (256 + n_b + j - 271):15 + (512 + j - 271)], cacc[:, c, 256 + n_b:512]))
                if j % 4 == 3:
                    yield
                for (src, dst) in ops:
                    if j == 0:
                        add("dve", lambda e, src=src, dst=dst, c=c: e.tensor_scalar(out=dst, in0=src, scalar1=dw_s[:, l, c, 0:1], scalar2=cb_s[:, l, c:c + 1],
                                                                                    op0=ALU.mult, op1=ALU.add), [b_gpad[g], b_prm, b_cacc], [b_cacc])
                    else:
                        add("dve", lambda e, src=src, dst=dst, c=c, j=j: e.scalar_tensor_tensor(out=dst, in0=src, scalar=dw_s[:, l, c, j:j + 1], in1=dst,
                                                                                                op0=ALU.mult, op1=ALU.add), [b_gpad[g], b_prm, b_cacc], [b_cacc])
        p1, pb1 = gp()
        p2, pb2 = gp()
        for c in range(2):
            tt, tb_ = tmp()
            add("act", lambda e, tt=tt, c=c: e.activation(out=tt[:], in_=cacc[:, c, :], func=AF.Square), [b_cacc], [tb_])
            add("pe", lambda e, c=c: e.matmul(p1[:, :], lhsT=ones_f[:, :], rhs=cacc[:, c, :], start=(c == 0), stop=(c == 1)), [b_cacc, b_ones_f], [pb1])
            add("pe", lambda e, tt=tt, c=c: e.matmul(p2[:, :], lhsT=ones_f[:, :], rhs=tt[:], start=(c == 0), stop=(c == 1)), [tb_, b_ones_f], [pb2])
        mean, bmean = tmp()
        var, bvar = tmp()
        add("act", lambda e: e.activation(out=mean[:], in_=p1[:, :], func=AF.Identity, scale=1.0 / 256), [pb1], [bmean])
        add("dve", lambda e: e.tensor_tensor(out=var[:], in0=mean[:], in1=mean[:], op=ALU.mult), [bmean], [bvar])
        add("dve", lambda e: e.scalar_tensor_tensor(out=var[:], in0=p2[:, :], scalar=1.0 / 256, in1=var[:], op0=ALU.mult, op1=ALU.subtract),
            [pb2, bvar], [bvar])
        add("act", lambda e: e.activation(out=var[:], in_=var[:], func=AF.Ln, bias=epsc[:, 0:1], scale=1.0), [bvar, b_epsc], [bvar])
        add("act", lambda e: e.activation(out=var[:], in_=var[:], func=AF.Exp, scale=-0.5), [bvar], [bvar])
        for c in range(2):
            add("dve", lambda e, c=c: e.tensor_tensor(out=cacc[:, c, :], in0=cacc[:, c, :], in1=mean[:], op=ALU.subtract), [b_cacc, bmean], [b_cacc])
            add("dve", lambda e, c=c: e.tensor_tensor(out=cacc[:, c, :], in0=cacc[:, c, :], in1=var[:], op=ALU.mult), [b_cacc, bvar], [b_cacc])
            add("act", lambda e, c=c: e.activation(out=cacc[:, c, :], in_=cacc[:, c, :], func=AF.Identity, bias=lnb_s[:, l, c:c + 1],
                                                   scale=lng_s[:, l, c:c + 1]), [b_cacc, b_prm], [b_cacc])
            tt, tb_ = tmp()
            add("act", lambda e, tt=tt, c=c: e.activation(out=tt[:], in_=cacc[:, c, :], func=AF.Sigmoid), [b_cacc], [tb_])
            add("dve", lambda e, tt=tt, c=c: e.tensor_tensor(out=mixT[g][:, 6 + c, :], in0=cacc[:, c, :], in1=tt[:], op=ALU.mult),
                [b_cacc, tb_], [b_mix[g][6 + c]])

    def out_proj(l, groups):
        for ti in range(2):
            wt, wb = wtile(w_out[l, :, ti * 512:(ti + 1) * 512], 512)
            for cc in range(4):
                oc = ti * 4 + cc
                for g in groups:
                    pt, pb = gp()
                    for k in range(8):
                        add("pe", lambda e, pt=pt, k=k, cc=cc, wt=wt, g=g: e.matmul(pt[:, :], lhsT=wt[:, k, cc * 128:(cc + 1) * 128], rhs=mixT[g][:, k, :],
                                                                                   start=(k == 0), stop=(k == 7)), [wb, b_mix[g][k]], [pb])
                    add("dve", lambda e, pt=pt, g=g, oc=oc: e.scalar_tensor_tensor(out=xT[g][:, oc, :], in0=pt[:, :], scalar=mcol(l, "g1", oc, g),
                                                                                 in1=xT[g][:, oc, :], op0=ALU.mult, op1=ALU.add),
                        [pb, b_mod[l], b_xT[g][oc]], [b_xT[g][oc]])

    def ffn(l, groups):
        for blk in range(4):
            for ti in range(2):
                wt, wb = wtile(w_ff1[l, :, blk * 1024 + ti * 512: blk * 1024 + (ti + 1) * 512], 512)
                for cc in range(4):
                    fc = ti * 4 + cc
                    for g in groups:
                        pt, pb = gp()
                        for k in range(8):
                            add("pe", lambda e, pt=pt, k=k, cc=cc, wt=wt, g=g: e.matmul(pt[:, :], lhsT=wt[:, k, cc * 128:(cc + 1) * 128], rhs=hT[g][:, k, :],
                                                                                       start=(k == 0), stop=(k == 7)), [wb, b_hT[g]], [pb])
                        sq, bq = sqt()
                        add("act", lambda e, pt=pt, sq=sq: e.activation(out=sq[:], in_=pt[:, :], func=AF.Relu), [pb], [bq])
                        add("dve", lambda e, sq=sq, g=g, fc=fc: e.tensor_tensor(out=mixT[g][:, fc, :], in0=sq[:], in1=sq[:], op=ALU.mult),
                            [bq], [b_mix[g][fc]])
            for ti in range(2):
                wt, wb = wtile(w_ff2[l, blk * 1024:(blk + 1) * 1024, ti * 512:(ti + 1) * 512], 512)
                for cc in range(4):
                    oc = ti * 4 + cc
                    for g in groups:
                        pt, pb = gp()
                        for k in range(8):
                            add("pe", lambda e, pt=pt, k=k, cc=cc, wt=wt, g=g: e.matmul(pt[:, :], lhsT=wt[:, k, cc * 128:(cc + 1) * 128], rhs=mixT[g][:, k, :],
                                                                                       start=(k == 0), stop=(k == 7)), [wb, b_mix[g][k]], [pb])
                        add("dve", lambda e, pt=pt, g=g, oc=oc: e.scalar_tensor_tensor(out=xT[g][:, oc, :], in0=pt[:, :], scalar=mcol(l, "g2", oc, g),
                                                                                     in1=xT[g][:, oc, :], op0=ALU.mult, op1=ALU.add),
                            [pb, b_mod[l], b_xT[g][oc]], [b_xT[g][oc]])

    def dump8(name, t, bufs):
        if name not in dbg:
            return
        o = dout("dbg_" + name, [128, 8, T])
        for j in range(8):
            tt, tb_ = tmp()
            add("dve", lambda e, tt=tt, j=j: e.tensor_copy(out=tt[:], in_=t[:, j, :]), [bufs[j]], [tb_])
            add("sp", lambda e, tt=tt, j=j: e.dma_start(out=o[:, j, :], in_=tt[:]), [tb_], (), dma=True)
        dbg_out[name] = [128, 8, T]

    groups = [0, 1] if do_lat else [0]
    import os as _os
    _l1 = _os.environ.get("L1STAGES")
    _stages0 = stages
    for l in range(nlayers):
        stages = _stages0 if (l == 0 or _l1 is None) else set(_l1.split(","))
        if "mod" in stages and l == 0:
            for _ in modulation(0):
                pass
        if "norm1" in stages:
            for g in groups:
                rmsnorm_mod(l, g, 0)
        if "proj" in stages:
            input_proj(l)
        side = modulation(l + 1) if ("mod" in stages and l + 1 < nlayers) else None
        if "ctxattn" in stages:
            ctx_attention(l, None if (do_lat and "latattn" in stages) else side)
        if "conv" in stages:
            for _ in conv_module(l, 0):
                pass
        dump8("mix0%d" % l, mixT[0], b_mix[0])
        if do_lat:
            side2 = conv_module(l, 1) if "conv" in stages else None
            if "latattn" in stages:
                lat_attention(l, side2, side)
            if side2 is not None:
                for _ in side2:
                    pass
        if side is not None:
            for _ in side:
                pass
            dump8("mix1%d" % l, mixT[1], b_mix[1])
        if "outproj" in stages:
            out_proj(l, groups)
        for g in groups:
            dump8("xattn%d%d" % (g, l), xT[g], b_xT[g])
        if "norm2" in stages:
            for g in groups:
                rmsnorm_mod(l, g, 1)
        if "ffn" in stages:
            ffn(l, groups)
        for g in groups:
            dump8("xffn%d%d" % (g, l), xT[g], b_xT[g])
    for g in groups:
        if "final" not in stages:
            continue
        pt, pb = gp()
        for j in range(8):
            sq, bq = sqt()
            add("act", lambda e, sq=sq, j=j, g=g: e.activation(out=sq[:], in_=xT[g][:, j, :], func=AF.Square), [b_xT[g][j]], [bq])
            add("pe", lambda e, sq=sq, j=j, pt=pt: e.matmul(pt[:, :], lhsT=ones_b[:, :], rhs=sq[:], start=(j == 0), stop=(j == 7)), [bq, b_ones_b], [pb])
        add("act", lambda e, pt=pt: e.activation(out=rstd[:], in_=pt[:, :], func=AF.Ln, bias=epsc[:, 0:1], scale=1.0 / D), [pb, b_epsc], [b_rstd])
        add("act", lambda e: e.activation(out=rstd[:], in_=rstd[:], func=AF.Exp, scale=-0.5), [b_rstd], [b_rstd])
        for j in range(8):
            add("dve", lambda e, j=j, g=g: e.scalar_tensor_tensor(out=xT[g][:, j, :], in0=xT[g][:, j, :], scalar=gfin_s[:, j:j + 1], in1=rstd[:],
                                                                  op0=ALU.mult, op1=ALU.mult), [b_xT[g][j], b_rstd, b_prm], [b_xT[g][j]])
    for g in groups:
        for tb in range(4):
            sl = tb % 2
            for half in range(2):
                pt, pb = gp()
                for jj in range(4):
                    j = half * 4 + jj
                    add("pe", lambda e, pt=pt, j=j, jj=jj, tb=tb, g=g: e.transpose(out=pt[:, jj * 128:(jj + 1) * 128],
                                                                                   in_=xT[g][:, j, tb * 128:(tb + 1) * 128], identity=ident[:]),
                        [b_xT[g][j], b_ident], [pb])
                evac(stage[:, sl, half * 512:(half + 1) * 512], pt[:, :], [pb], [b_stage[sl]])
            add("sp", lambda e, g=g, tb=tb, sl=sl: e.dma_start(out=y_out[g * T + tb * 128:g * T + (tb + 1) * 128, :], in_=stage[:, sl, :]),
                [b_stage[sl]], (), dma=True)
    S.emit(st)
    st.close()
    return nc, dbg_out


def _colT(v, nchunk):
    return np.ascontiguousarray(v.reshape(nchunk, 128).T)


def prepare_inputs(inp):
    f = lambda a: np.ascontiguousarray(np.asarray(a, dtype=np.float32))
    x_prompt, x_sample, c = f(inp["x_prompt"]), f(inp["x_sample"]), f(inp["c"])
    w_in = f(inp["w_in"])
    shared = {}
    shared["bmodT"] = np.ascontiguousarray(np.stack([_colT(f(inp["b_mod"])[l], 48) for l in range(L)], 1))
    shared["gmixT"] = np.ascontiguousarray(np.stack([_colT(f(inp["g_norm_mix"])[l], 8) for l in range(L)], 1))
    shared["gffT"] = np.ascontiguousarray(np.stack([_colT(f(inp["g_norm_ff"])[l], 8) for l in range(L)], 1))
    shared["gfinT"] = _colT(f(inp["g_final"]), 8)
    shared["w_mod"] = f(inp["w_mod"])
    shared["w_in"] = w_in
    idx = []
    for base in (0, 256):
        for h in range(4):
            for m in range(2):
                idx += [base + h * 64 + m * 32 + P32[j] for j in range(32)]
    idx += list(range(1088, 1152))
    idx += [1152 + P32[j] for j in range(32)]
    shared["w_inp"] = np.ascontiguousarray(w_in[:, :, idx])
    shared["w_out"] = f(inp["w_out"])
    shared["w_ff1"] = f(inp["w_ff1"])
    shared["w_ff2"] = f(inp["w_ff2"])
    wq = f(inp["w_mla_qup"])
    shared["wqup"] = wq
    wqp = np.zeros_like(wq)
    for h in range(4):
        for j in range(32):
            wqp[:, :, h * 96 + 64 + j] = wq[:, :, h * 96 + 64 + P32[j]]
    shared["wqupp"] = wqp
    shared["wkvup"] = f(inp["w_mla_kvup"])
    shared["gqT"] = np.ascontiguousarray(np.stack([_colT(f(inp["g_mla_q"])[l], 2) for l in range(L)], 1))
    shared["gkvc"] = np.ascontiguousarray(f(inp["g_mla_kv"]).T)
    shared["gkvr"] = np.ascontiguousarray(np.broadcast_to(f(inp["g_mla_kv"])[None], (128, L, 128)))
    gs = f(inp["g_da_subln"])
    shared["gsubc"] = np.ascontiguousarray(np.concatenate([gs.T, gs.T], 0))
    lamv = np.stack([f(inp["da_lambda_q1"]), f(inp["da_lambda_k1"]), f(inp["da_lambda_q2"]), f(inp["da_lambda_k2"])], 1)
    shared["lamv"] = np.ascontiguousarray(np.broadcast_to(lamv[None], (128, L, 4, 32)))
    dw = f(inp["conv_dw"])
    shared["dwT"] = np.ascontiguousarray(dw.reshape(L, 31, 2, 128).transpose(3, 0, 2, 1))
    for nm, key in (("cbT", "conv_b"), ("lngT", "conv_ln_g"), ("lnbT", "conv_ln_b")):
        shared[nm] = np.ascontiguousarray(f(inp[key]).reshape(L, 2, 128).transpose(2, 0, 1))
    shared["ident"] = np.eye(128, dtype=np.float32)
    w = np.arange(GRID_W)
    cs = np.clip(w - 8, 0, GRID_W - 16)
    col_ok = (w[None, :] >= cs[:, None]) & (w[None, :] < cs[:, None] + 16)
    cm = np.where(col_ok.T, 0.0, -BIG * 8).astype(np.float32)
    shared["colmask"] = np.ascontiguousarray(np.concatenate([cm, cm], 0))
    ri = np.zeros((32, DEC_S), np.float32)
    ri[np.arange(DEC_S) // 64, np.arange(DEC_S)] = 1.0
    shared["rowind"] = ri
    rpb = f(inp["na_rpb"])
    half = 8
    freqs = (10000.0 ** (-np.arange(half, dtype=np.float32) * 2.0 / 16)).astype(np.float32)
    maps = []
    for core in range(NCORE):
        b, r = core // 4, core % 4
        m = dict(shared)
        m["xin"] = np.ascontiguousarray(np.concatenate([x_prompt[2 * core], x_prompt[2 * core + 1], x_sample[b, r * T:(r + 1) * T]], 0))
        cv = np.stack([f(inp["c_ctx"]), c[b]], 1)
        m["cvT"] = np.ascontiguousarray(cv.reshape(8, 128, 2).transpose(1, 0, 2))
        m["c_dak"] = f(inp["cache_da_k"])[b]
        m["c_dav"] = f(inp["cache_da_v"])[b]
        m["c_ckv"] = f(inp["cache_mla_ckv"])[b]
        m["c_kr"] = f(inp["cache_mla_krope"])[b]
        m["c_nak"] = f(inp["cache_na_k"])[b]
        m["c_nav"] = f(inp["cache_na_v"])[b]
        t = r * T + np.arange(T)
        rows = (t // GRID_W).astype(np.float32)
        cols = (t % GRID_W).astype(np.float32)
        ang = np.concatenate([rows[None, :] * freqs[:, None], rows[None, :] * freqs[:, None],
                              cols[None, :] * freqs[:, None], cols[None, :] * freqs[:, None]], 0)
        cos32 = np.cos(ang).astype(np.float32)
        sin32 = np.sin(ang).astype(np.float32)
        sgn = np.concatenate([-np.ones(8), np.ones(8), -np.ones(8), np.ones(8)]).astype(np.float32)[:, None]
        m["cosT"] = np.ascontiguousarray(np.tile(cos32, (4, 1)))
        m["sinT"] = np.ascontiguousarray(np.tile(sin32 * sgn, (4, 1)))
        r0 = r * 8
        qrow = r0 + np.arange(T) // 64
        start = np.clip(qrow - 4, 0, NROWS - 8)
        rk = np.arange(32)
        ok = (rk[:, None] >= start[None, :]) & (rk[:, None] < start[None, :] + 8)
        m["rowsel"] = np.where(ok, 0.0, -BIG * 8).astype(np.float32)
        P = np.zeros((L, 4, A2 + 1, TZL), np.float32)
        for a2 in range(A2):
            a = 45 - a2 - r0
            if 0 <= a <= 14:
                P[:, :, a2, 48:79] = rpb[:, :, a, ::-1]
        m["tz_rep"] = np.ascontiguousarray(np.broadcast_to(P.reshape(L * 4 * (A2 + 1), 1, TZL), (L * 4 * (A2 + 1), 64, TZL)))
        hs = np.zeros((128, 8), np.float32)
        if r > 0:
            hs[:, r - 1] = 1.0
        if r < 3:
            hs[:, 4 + r + 1] = 1.0
        m["halsel"] = hs
        maps.append(m)
    return maps


_NC_CACHE = {}


def kernel(**inputs):
    maps = prepare_inputs(inputs)
    if "nc" not in _NC_CACHE:
        _NC_CACHE["nc"] = build()[0]
    nc = _NC_CACHE["nc"]
    res = run_bass_kernel_spmd(nc, maps, core_ids=list(range(NCORE)))
    R = res.results
    y_prompt = np.zeros((NB, SEQ, D), np.float32)
    y_sample = np.zeros((DEC_B, DEC_S, D), np.float32)
    outs = {k: [] for k in ("o_dak", "o_dav", "o_ckv", "o_kr", "o_nak", "o_nav")}
    for core in range(NCORE):
        b, r = core // 4, core % 4
        y = R[core]["y_out"]
        y_prompt[2 * core] = y[0:256]
        y_prompt[2 * core + 1] = y[256:512]
        y_sample[b, r * T:(r + 1) * T] = y[512:1024]
        for k in outs:
            outs[k].append(R[core][k])
    cat = lambda k: np.ascontiguousarray(np.concatenate(outs[k], 0).astype(np.float32))
    return (y_prompt, y_sample, cat("o_dak"), cat("o_dav"), cat("o_ckv"), cat("o_kr"), cat("o_nak"), cat("o_nav"))
```

```python
import contextlib
import math
import numpy as np
import concourse.bass as bass
import concourse.mybir as mybir
from concourse.bass_utils import run_bass_kernel_spmd

F32 = mybir.dt.float32
BF16 = mybir.dt.bfloat16
ALU = mybir.AluOpType
AF = mybir.ActivationFunctionType

ENGINES = ("pe", "act", "dve", "pool", "sp")

D = 1024
L = 2
NB = 16
SEQ = 256
DEC_B = 2
DEC_S = 2048
PAST = 256
GRID_W = 64
NROWS = DEC_S // GRID_W
EPS = 1e-6
T = 512
NCORE = 8
BIG = 30000.0
A2 = 46
TZL = 127
PA = 672
PB = 527
R_DAK, R_CKV, R_KR, R_NAK = 0, 256, 384, 416
R_V, R_HALO = 0, 512
P32 = [8, 9, 10, 11, 12, 13, 14, 15, 0, 1, 2, 3, 4, 5, 6, 7,
       24, 25, 26, 27, 28, 29, 30, 31, 16, 17, 18, 19, 20, 21, 22, 23]


class Buf:
    __slots__ = ("name", "last_w", "readers", "excl")

    def __init__(self, name, excl=False):
        self.name = name
        self.last_w = None
        self.readers = []
        self.excl = excl


class Op:
    __slots__ = ("eng", "fn", "deps", "dma", "sig", "sem", "val")

    def __init__(self, eng, fn, dma):
        self.eng = eng
        self.fn = fn
        self.dma = dma
        self.deps = []
        self.sig = False
        self.sem = None
        self.val = 0


class Sched:
    def __init__(self, nc, n_dma_slots=16):
        self.nc = nc
        self.ops = []
        self.n_dma_slots = n_dma_slots

    def add(self, eng, fn, reads=(), writes=(), dma=False):
        op = Op(eng, fn, dma)
        ex = [b for b in reads if b.excl]
        if ex:
            reads = [b for b in reads if not b.excl]
            writes = list(writes) + ex
        deps = {}
        for b in reads:
            if b.last_w is not None:
                deps[id(b.last_w)] = b.last_w
        for b in writes:
            if b.last_w is not None:
                deps[id(b.last_w)] = b.last_w
            for r in b.readers:
                deps[id(r)] = r
        for b in reads:
            b.readers.append(op)
        for b in writes:
            b.last_w = op
            b.readers = []
        for d in deps.values():
            if d.eng == "pe" and eng == "pe" and not d.dma and not dma:
                continue
            op.deps.append(d)
            d.sig = True
        self.ops.append(op)
        return op

    def emit(self, stack):
        nc = self.nc
        eng_sem = {e: stack.enter_context(nc.semaphore("s_" + e)) for e in ENGINES}
        dma_engs = sorted({op.eng for op in self.ops if op.dma is True})
        dma_slots = {e: [stack.enter_context(nc.semaphore("d_%s_%d" % (e, i)))
                         for i in range(self.n_dma_slots)] for e in dma_engs}
        cnt = {e: 0 for e in ENGINES}
        dcnt = {e: 0 for e in dma_engs}
        slot_uses = {e: [0] * self.n_dma_slots for e in dma_engs}
        slot_prev = {}
        ncc = 0
        for op in self.ops:
            if op.dma == "cc":
                op.sem = stack.enter_context(nc.semaphore("cc_%d" % ncc))
                ncc += 1
                op.val = 1
            elif op.dma:
                k = dcnt[op.eng] % self.n_dma_slots
                dcnt[op.eng] += 1
                slot_uses[op.eng][k] += 1
                op.sem = dma_slots[op.eng][k]
                op.val = 16 * slot_uses[op.eng][k]
                slot_prev[id(op)] = (op.sem, op.val - 16)
            elif op.sig:
                cnt[op.eng] += 1
                op.sem = eng_sem[op.eng]
                op.val = cnt[op.eng]
        block = stack.enter_context(nc.Block())
        handles = {"pe": block.tensor, "act": block.scalar, "dve": block.vector,
                   "pool": block.gpsimd, "sp": block.sync}
        all_async = [op for op in self.ops if op.dma]

        def run_engine(ename):
            def body(eng):
                waited = {}
                for op in self.ops:
                    if op.eng != ename:
                        continue
                    need = {}
                    for d in op.deps:
                        key = id(d.sem)
                        if key not in need or need[key][1] < d.val:
                            need[key] = (d.sem, d.val)
                    if op.dma is True:
                        s, v = slot_prev[id(op)]
                        if v > 0:
                            key = id(s)
                            if key not in need or need[key][1] < v:
                                need[key] = (s, v)
                    for key, (s, v) in need.items():
                        if waited.get(key, 0) >= v:
                            continue
                        eng.wait_ge(s, v)
                        waited[key] = v
                    ins = op.fn(eng)
                    if op.dma == "cc":
                        ins.then_inc(op.sem)
                    elif op.dma:
                        ins.then_inc(op.sem, 16)
                    elif op.sig:
                        ins.then_inc(op.sem, 1)
                if ename == "sp":
                    last = {}
                    for op in all_async:
                        last[id(op.sem)] = (op.sem, op.val)
                    for key, (s, v) in last.items():
                        if waited.get(key, 0) < v:
                            eng.wait_ge(s, v)
                    for e in ENGINES:
                        if cnt[e] > 0:
                            eng.wait_ge(eng_sem[e], cnt[e])
            handles[ename](body)

        for e in ENGINES:
            run_engine(e)


def build(dbg=(), nlayers=L, do_lat=True, stages=None):
    if stages is None:
        stages = {"mod", "norm1", "proj", "ctxattn", "conv", "latattn", "outproj", "norm2", "ffn", "final"}
    nc = bass.Bass("TRN2", target_bir_lowering=False)
    st = contextlib.ExitStack()
    S = Sched(nc)
    dbg_out = {}

    def din(name, shape, dt=F32):
        return nc.dram_tensor(name, list(shape), dt, kind="ExternalInput").ap()

    def dout(name, shape, dt=F32):
        return nc.dram_tensor(name, list(shape), dt, kind="ExternalOutput").ap()

    xin = din("xin", [2 * T, D])
    cvT = din("cvT", [128, 8, 2])
    bmodT = din("bmodT", [128, L, 48])
    gmixT = din("gmixT", [128, L, 8])
    gffT = din("gffT", [128, L, 8])
    gfinT = din("gfinT", [128, 8])
    w_mod = din("w_mod", [L, D, 6 * D])
    w_in = din("w_in", [L, D, 2464])
    w_inp = din("w_inp", [L, D, 608])
    w_out = din("w_out", [L, D, D])
    w_ff1 = din("w_ff1", [L, D, 4 * D])
    w_ff2 = din("w_ff2", [L, 4 * D, D])
    wqup = din("wqup", [L, 256, 384])
    wqupp = din("wqupp", [L, 256, 384])
    wkvup = din("wkvup", [L, 128, 512])
    gqT = din("gqT", [128, L, 2])
    gkvc = din("gkvc", [128, L])
    gkvr = din("gkvr", [128, L, 128])
    gsubc = din("gsubc", [128, L])
    lamv = din("lamv", [128, L, 4, 32])
    tz_all = din("tz_rep", [L * 4 * (A2 + 1), 64, TZL]).tensor
    dwT = din("dwT", [128, L, 2, 31])
    cbT = din("cbT", [128, L, 2])
    lngT = din("lngT", [128, L, 2])
    lnbT = din("lnbT", [128, L, 2])
    c_dak = din("c_dak", [L, 4, PAST, 64])
    c_dav = din("c_dav", [L, 4, PAST, 64])
    c_ckv = din("c_ckv", [L, PAST, 128])
    c_kr = din("c_kr", [L, PAST, 32])
    c_nak = din("c_nak", [L, 4, PAST, 64])
    c_nav = din("c_nav", [L, 4, PAST, 64])
    cosT_d = din("cosT", [128, T])
    sinT_d = din("sinT", [128, T])
    colmask_d = din("colmask", [128, 64])
    rowsel_d = din("rowsel", [32, T])
    rowind_d = din("rowind", [32, DEC_S])
    halsel_d = din("halsel", [128, 8])
    ident_d = din("ident", [128, 128])
    y_out = dout("y_out", [2 * T, D])
    o_dak = dout("o_dak", [2, L, 4, SEQ, 64])
    o_dav = dout("o_dav", [2, L, 4, SEQ, 64])
    o_ckv = dout("o_ckv", [2, L, SEQ, 128])
    o_kr = dout("o_kr", [2, L, SEQ, 32])
    o_nak = dout("o_nak", [2, L, 4, SEQ, 64])
    o_nav = dout("o_nav", [2, L, 4, SEQ, 64])
    LROWS = 6 * PA + 6 * PB
    pay_all = nc.dram_tensor("pay_all", [L * LROWS, T], BF16)
    O_AIN, O_AOUT, O_BIN, O_BOUT = 0, PA, 5 * PA, 5 * PA + PB

    class _Sub:
        def __init__(self, row0, nrows):
            self.row0, self.nrows = row0, nrows

        def ap(self):
            return pay_all.ap()[self.row0:self.row0 + self.nrows, :]

    payA_in = [_Sub(l * LROWS + O_AIN, PA) for l in range(L)]
    payA_out = [_Sub(l * LROWS + O_AOUT, 4 * PA) for l in range(L)]
    payB_in = [_Sub(l * LROWS + O_BIN, PB) for l in range(L)]
    payB_out = [_Sub(l * LROWS + O_BOUT, 4 * PB) for l in range(L)]
    TZ_SZ = 4 * (A2 + 1) * 64 * TZL

    def dram_ap(base, off, dims):
        if isinstance(base, _Sub):
            return bass.AP(pay_all, base.row0 * T + off, dims)
        return bass.AP(base, off, dims)

    def sb(name, shape, dt):
        return st.enter_context(nc.sbuf_tensor("sb_" + name, list(shape), dt))

    add = S.add

    banks = []
    for i in range(8):
        banks.append((st.enter_context(nc.psum_tensor("ps%d" % i, [128, 512], F32)), Buf("ps%d" % i, excl=True)))
    gp_i = [0]
    ac_i = [0]

    def gp():
        b = banks[(0, 1, 2, 7)[gp_i[0] % 4]]
        gp_i[0] += 1
        return b

    def acb():
        b = banks[4 + ac_i[0] % 3]
        ac_i[0] += 1
        return b

    ev_i = [0]

    def evac(dst, src, r, w, scale=None):
        ev_i[0] += 1
        if ev_i[0] % 2 == 0:
            if scale is None:
                add("act", lambda e: e.activation(out=dst, in_=src, func=AF.Copy), r, w)
            else:
                add("act", lambda e: e.activation(out=dst, in_=src, func=AF.Identity, scale=scale), r, w)
        else:
            if scale is None:
                add("dve", lambda e: e.tensor_copy(out=dst, in_=src), r, w)
            else:
                add("dve", lambda e: e.tensor_scalar(out=dst, in0=src, scalar1=scale, scalar2=None, op0=ALU.mult), r, w)

    ident = sb("ident", [128, 128], F32); b_ident = Buf("ident")
    identb = sb("identb", [128, 128], BF16); b_identb = Buf("identb")
    ones_f = sb("ones_f", [128, 128], F32); b_ones_f = Buf("ones_f")
    ones_b = sb("ones_b", [128, 128], BF16); b_ones_b = Buf("ones_b")
    epsc = sb("epsc", [128, 1], F32); b_epsc = Buf("epsc")
    add("sp", lambda e: e.dma_start(out=ident[:], in_=ident_d), (), [b_ident], dma=True)
    add("pool", lambda e: e.memset(ones_f[:], 1.0), (), [b_ones_f])
    add("pool", lambda e: e.memset(ones_b[:], 1.0), (), [b_ones_b])
    add("pool", lambda e: e.memset(epsc[:], EPS), (), [b_epsc])
    add("dve", lambda e: e.tensor_copy(out=identb[:], in_=ident[:]), [b_ident], [b_identb])

    prm = {}
    b_prm = Buf("prm")
    b_small = []

    def load_small(name, src, shape):
        t = sb(name, shape, F32)
        bt = Buf("ld_" + name)
        b_small.append(bt)
        add("sp", lambda e: e.dma_start(out=t[:], in_=src), (), [bt], dma=True)
        prm[name] = t
        return t

    cv_s = load_small("cv_s", cvT, [128, 8, 2])
    bmod_s = load_small("bmod_s", bmodT, [128, L, 48])
    gmix_s = load_small("gmix_s", gmixT, [128, L, 8])
    gff_s = load_small("gff_s", gffT, [128, L, 8])
    gfin_s = load_small("gfin_s", gfinT, [128, 8])
    gq_s = load_small("gq_s", gqT, [128, L, 2])
    gkvc_s = load_small("gkvc_s", gkvc, [128, L])
    gkvr_s = sb("gkvr_s", [128, L, 128], BF16)
    b_small.append(Buf("ld_gkvr"))
    add("pool", lambda e: e.dma_start(out=gkvr_s[:], in_=gkvr), (), [b_small[-1]], dma=True)
    gsub_s = load_small("gsub_s", gsubc, [128, L])
    lam_s = load_small("lam_s", lamv, [128, L, 4, 32])
    dw_s = load_small("dw_s", dwT, [128, L, 2, 31])
    cb_s = load_small("cb_s", cbT, [128, L, 2])
    lng_s = load_small("lng_s", lngT, [128, L, 2])
    lnb_s = load_small("lnb_s", lnbT, [128, L, 2])
    cos_s = load_small("cos_s", cosT_d, [128, T])
    sin_s = load_small("sin_s", sinT_d, [128, T])
    colm_s = load_small("colm_s", colmask_d, [128, 64])
    halsel_s = load_small("halsel_s", halsel_d, [128, 8])
    joinc = sb("joinc", [128, 1], F32)
    add("dve", lambda e: e.memset(joinc[:], 0.0), list(b_small), [b_prm])

    lams = sb("lams", [128, L, 2], F32)
    neglam = sb("neglam", [128, L], F32)
    gsub2 = sb("gsub2", [128, L], F32)
    for l in range(L):
        lam_init = 0.8 - 0.6 * math.exp(-0.3 * l)
        for m in range(2):
            add("dve", lambda e, l=l, m=m: e.tensor_tensor(out=lam_s[:, l, 2 * m, :], in0=lam_s[:, l, 2 * m, :],
                                                          in1=lam_s[:, l, 2 * m + 1, :], op=ALU.mult), [b_prm], [b_prm])
            add("dve", lambda e, l=l, m=m: e.reduce_sum(out=lams[:, l, m:m + 1], in_=lam_s[:, l, 2 * m, :],
                                                       axis=mybir.AxisListType.X), [b_prm], [b_prm])
        add("act", lambda e, l=l: e.activation(out=lams[:, l, :], in_=lams[:, l, :], func=AF.Exp), [b_prm], [b_prm])
        add("dve", lambda e, l=l: e.tensor_tensor(out=neglam[:, l:l + 1], in0=lams[:, l, 1:2], in1=lams[:, l, 0:1],
                                                 op=ALU.subtract), [b_prm], [b_prm])
        add("dve", lambda e, l=l, li=lam_init: e.tensor_scalar(out=neglam[:, l:l + 1], in0=neglam[:, l:l + 1],
                                                              scalar1=-li, scalar2=None, op0=ALU.add), [b_prm], [b_prm])
        add("dve", lambda e, l=l, li=lam_init: e.tensor_scalar(out=gsub2[:, l:l + 1], in0=gsub_s[:, l:l + 1],
                                                              scalar1=1.0 - li, scalar2=None, op0=ALU.mult), [b_prm], [b_prm])

    xT = [sb("xT%d" % g, [128, 8, T], F32) for g in range(2)]
    b_xT = [[Buf("xT%d_%d" % (g, j)) for j in range(8)] for g in range(2)]
    hT = [sb("hT%d" % g, [128, 8, T], BF16) for g in range(2)]
    b_hT = [Buf("hT%d" % g) for g in range(2)]
    mixT = [sb("mixT%d" % g, [128, 8, T], BF16) for g in range(2)]
    b_mix = [[Buf("mix%d_%d" % (g, j)) for j in range(8)] for g in range(2)]
    NW = 3
    wpool = [sb("wp%d" % i, [128, 8, 512], BF16) for i in range(NW)]
    b_wp = [Buf("wp%d" % i) for i in range(NW)]
    wp_i = [0]

    def wtile(src_ap, ncols, nk=8):
        i = wp_i[0] % NW
        wp_i[0] += 1
        t, b = wpool[i], b_wp[i]
        add("pool", lambda e: e.dma_start(out=t[:, 0:nk, 0:ncols], in_=src_ap.rearrange("(k p) c -> p k c", p=128)),
            (), [b], dma=True)
        return t, b

    qd = sb("qd", [128, 2, T], F32); b_qd = Buf("qd_cacc_stage")
    stage1 = qd[:, :, :].rearrange("p a b -> p (a b)")

    class _Stage:
        def __getitem__(self, key):
            p, sl, c = key
            return stage1[p, c]
    stage = _Stage()
    b_stage = [b_qd, b_qd]
    rstd = sb("rstd", [128, T], F32); b_rstd = Buf("rstd")
    tmpf = [sb("tmpf%d" % i, [128, T], F32) for i in range(4)]
    b_tmpf = [Buf("tmpf%d" % i) for i in range(4)]
    tf_i = [0]

    def tmp():
        i = tf_i[0] % 4
        tf_i[0] += 1
        return tmpf[i], b_tmpf[i]

    sqb = [sb("sqb%d" % i, [128, T], BF16) for i in range(2)]
    b_sqb = [Buf("sqb%d" % i) for i in range(2)]
    sq_i = [0]

    def sqt():
        i = sq_i[0] % 2
        sq_i[0] += 1
        return sqb[i], b_sqb[i]

    for g in range(2):
        for tb in range(4):
            for half in range(2):
                stg, bstg = tmp()
                add("sp", lambda e, g=g, tb=tb, half=half, stg=stg: e.dma_start(
                    out=stg[:, :], in_=xin[g * T + tb * 128: g * T + (tb + 1) * 128, half * 512:(half + 1) * 512]), (), [bstg], dma=True)
                pt, pb = gp()
                for jj in range(4):
                    add("pe", lambda e, pt=pt, stg=stg, jj=jj: e.transpose(out=pt[:, jj * 128:(jj + 1) * 128],
                                                                           in_=stg[:, jj * 128:(jj + 1) * 128], identity=ident[:]),
                        [bstg, b_ident], [pb])
                evac(xT[g][:, half * 4:half * 4 + 4, tb * 128:(tb + 1) * 128],
                     pt[:, :].rearrange("p (j t) -> p j t", j=4), [pb], [b_xT[g][half * 4 + jj] for jj in range(4)])

    sil = sb("sil", [128, 8, 2], BF16); b_sil = Buf("sil")
    add("act", lambda e: e.activation(out=sil[:], in_=cv_s[:], func=AF.Silu), [b_prm], [b_sil])
    modv = sb("modv", [128, L, 48, 2], F32)
    gsc = sb("gsc", [128, L, 2, 8, 2], F32)
    b_mod = [Buf("mod%d" % l) for l in range(L)]

    def modulation(l):
        pt, pb = banks[3]
        for ti in range(12):
            wt, wb = wtile(w_mod[l, :, ti * 512:(ti + 1) * 512], 512)
            for cc in range(4):
                n = ti * 4 + cc
                for k in range(8):
                    add("pe", lambda e, wt=wt, cc=cc, k=k, n=n, pt=pt: e.matmul(pt[:, n * 2:n * 2 + 2], lhsT=wt[:, k, cc * 128:(cc + 1) * 128],
                                                                             rhs=sil[:, k, :], start=(k == 0), stop=(k == 7)),
                        [wb, b_sil], [pb])
            yield
        add("dve", lambda e, pt=pt: e.tensor_tensor(out=modv[:, l, :, :], in0=pt[:, 0:96].rearrange("p (n v) -> p n v", v=2),
                                                   in1=bmod_s[:, l, :].unsqueeze(2).to_broadcast([128, 48, 2]), op=ALU.add),
            [pb, b_prm], [b_mod[l]])
        for ni, (gsrc, off) in enumerate(((gmix_s, 8), (gff_s, 32))):
            add("dve", lambda e, ni=ni, gsrc=gsrc, off=off: e.scalar_tensor_tensor(
                out=gsc[:, l, ni, :, :], in0=modv[:, l, off:off + 8, :], scalar=1.0,
                in1=gsrc[:, l, :].unsqueeze(2).to_broadcast([128, 8, 2]), op0=ALU.add, op1=ALU.mult),
                [b_mod[l], b_prm], [b_mod[l]])

    def mcol(l, which, j, v):
        off = {"sh1": 0, "sc1": 8, "g1": 16, "sh2": 24, "sc2": 32, "g2": 40}[which]
        return modv[:, l, off + j, v:v + 1]

    def rmsnorm_mod(l, g, ni):
        pt, pb = gp()
        for j in range(8):
            sq, bq = sqt()
            add("act", lambda e, sq=sq, j=j: e.activation(out=sq[:], in_=xT[g][:, j, :], func=AF.Square), [b_xT[g][j]], [bq])
            add("pe", lambda e, sq=sq, j=j, pt=pt: e.matmul(pt[:, :], lhsT=ones_b[:, :], rhs=sq[:], start=(j == 0), stop=(j == 7)),
                [bq, b_ones_b], [pb])
        add("act", lambda e, pt=pt: e.activation(out=rstd[:], in_=pt[:, :], func=AF.Ln, bias=epsc[:, 0:1], scale=1.0 / D),
            [pb, b_epsc], [b_rstd])
        add("act", lambda e: e.activation(out=rstd[:], in_=rstd[:], func=AF.Exp, scale=-0.5), [b_rstd], [b_rstd])
        for j in range(8):
            tt, tb_ = tmp()
            add("dve", lambda e, tt=tt, j=j: e.scalar_tensor_tensor(out=tt[:], in0=xT[g][:, j, :], scalar=gsc[:, l, ni, j, g:g + 1],
                                                                    in1=rstd[:], op0=ALU.mult, op1=ALU.mult),
                [b_xT[g][j], b_rstd, b_mod[l]], [tb_])
            add("act", lambda e, tt=tt, j=j: e.activation(out=hT[g][:, j, :], in_=tt[:], func=AF.Identity,
                                                          bias=mcol(l, "sh1" if ni == 0 else "sh2", j, g), scale=1.0),
                [tb_, b_mod[l]], [b_hT[g]])

    def stat_rstd(src_list, nfeat, dst, b_dst, reads, K=128, n=T):
        pt, pb = gp()
        for i, src in enumerate(src_list):
            tt, tb_ = tmp()
            add("act", lambda e, tt=tt, src=src: e.activation(out=tt[0:K, 0:n], in_=src, func=AF.Square), reads, [tb_])
            add("pe", lambda e, tt=tt, i=i, pt=pt: e.matmul(pt[0:K, 0:n], lhsT=ones_f[0:K, 0:K], rhs=tt[0:K, 0:n],
                                                          start=(i == 0), stop=(i == len(src_list) - 1)),
                [tb_, b_ones_f], [pb])
        add("act", lambda e, pt=pt: e.activation(out=dst, in_=pt[0:K, 0:n], func=AF.Ln, bias=epsc[0:K, 0:1], scale=1.0 / nfeat),
            [pb, b_epsc], [b_dst])
        add("act", lambda e: e.activation(out=dst, in_=dst, func=AF.Exp, scale=-0.5), [b_dst], [b_dst])

    KTb = sb("KTb", [128, 4, DEC_S + PAST], BF16); b_KTh = [Buf("KT%d" % h) for h in range(4)]
    VAb = sb("VAb", [128, 18, 4, 72], BF16); b_VAk = [Buf("VA%d" % k) for k in range(18)]
    KTc2 = [sb("KTc%d" % m, [128, 4, T], BF16) for m in range(2)]
    b_KTc = [Buf("KTc%d" % m) for m in range(3)]
    VAc = [sb("VAc%d" % m, [128, 4, 4, 72], BF16) for m in range(3)]; b_VAc = [Buf("VAc%d" % m) for m in range(3)]
    QT2 = [[sb("QT%d_%d" % (g, m), [128, 4, T], BF16) for m in range(2)] for g in range(2)]
    b_QT = [[Buf("QT%d_%d" % (g, m)) for m in range(3)] for g in range(2)]
    add("pool", lambda e: e.memset(VAb[:, :, :, 64:72], 1.0), (), b_VAk)
    for m in range(3):
        add("pool", lambda e, m=m: e.memset(VAc[m][:, :, :, 64:72], 1.0), (), [b_VAc[m]])

    def q_ap(g, m, h, p_lo, p_hi, c0=0, n=T):
        if m == 0:
            return QT2[g][0][p_lo:p_hi, h, c0:c0 + n]
        if m == 1:
            return QT2[g][1][p_lo:p_hi, h, c0:c0 + n]
        return QT2[g][0][64 + p_lo:64 + p_hi, h, c0:c0 + n]

    def kc_ap(m, h, p_lo, p_hi, c0, n):
        if m == 0:
            return KTc2[0][p_lo:p_hi, h, c0:c0 + n]
        if m == 1:
            return KTc2[1][p_lo:p_hi, h, c0:c0 + n]
        return KTc2[0][64 + p_lo:64 + p_hi, h, c0:c0 + n]
    Eb = [sb("E%d" % i, [128, T], BF16) for i in range(3)]
    b_E = [Buf("E%d" % i) for i in range(3)]
    e_i = [0]
    NSET = 2
    rsum_s = [sb("rsum0", [128, T], F32)] * 2; b_rsum_s = [Buf("rsum0")] * 2
    rsb0 = sb("rsb0", [128, T], BF16)
    rsb_row = [64, 64]
    b_rsb_s = [Buf("rsb0")] * 2
    dao = sb("dao", [128, T], F32); b_dao = Buf("dao")
    set_i = [0]
    ckvT = sb("ckvT", [128, DEC_S + PAST], BF16); b_ckvT = Buf("ckvT")
    ckvTc = ckvT; b_ckvTc = b_ckvT
    qdn = sb("qdn", [128, 2, T], BF16); b_qdn = Buf("qdn")
    wq_s = sb("wq_s", [128, 2, 384], BF16); wqp_s = sb("wqp_s", [128, 2, 384], BF16); wkv_s = sb("wkv_s", [128, 512], BF16)
    b_wsm = Buf("wsmall")
    gpad = [sb("gpad%d" % g, [128, 2, 2, 15 + 256 + 15], BF16) for g in range(2)]
    b_gpad = [Buf("gpad%d" % g) for g in range(2)]
    for g in range(2):
        add("pool", lambda e, g=g: e.memset(gpad[g][:], 0.0), (), [b_gpad[g]])
    cacc = qd; b_cacc = b_qd
    halo = sb("halo", [128, 2, 4, 30], BF16); b_halo = Buf("halo")
    NTBT = 2
    tbt = [sb("tbt%d" % i, [128, A2, 64], BF16) for i in range(NTBT)] * (2 // NTBT)
    b_tbt = [Buf("tbt%d" % i) for i in range(NTBT)] * (2 // NTBT)
    b_payA = [Buf("payA%d" % l) for l in range(L)]
    b_payB = [Buf("payB%d" % l) for l in range(L)]
    b_payoA = [Buf("payoA%d" % l) for l in range(L)]
    b_payoB = [Buf("payoB%d" % l) for l in range(L)]
    b_tz = [Buf("tz%d" % l) for l in range(L)]

    pending_post = [None]
    import os as _os2
    WARM_LAT = int(_os2.environ.get("WARM_LAT", "0"))
    WARM_CTX = int(_os2.environ.get("WARM_CTX", "0"))
    WARM_N = int(_os2.environ.get("WARM_N", "256"))

    def warm(n):
        for _ in range(n):
            add("pe", lambda e: e.matmul(banks[7][0][:, 512 - WARM_N:512], lhsT=identb[:, 0:128], rhs=identb[:, 0:128], start=True, stop=True), (), ())

    def flush_post():
        if pending_post[0] is not None:
            f = pending_post[0]
            pending_post[0] = None
            f()

    def attention(kt_fn, q_ap, va_fn, nkb, nq, scale, reads, extra_fn=None):
        at, ab = acb()
        pend = []

        def score(kb):
            pt, pb = gp()
            ex = extra_fn(kb) if extra_fn is not None else []
            add("pe", lambda e, pt=pt, kb=kb: e.matmul(pt[:, 0:nq], lhsT=kt_fn(kb), rhs=q_ap, start=True, stop=(len(ex) == 0)),
                reads, [pb])
            for i, (lh, rh, rd) in enumerate(ex):
                add("pe", lambda e, pt=pt, lh=lh, rh=rh, i=i: e.matmul(pt[:, 0:nq], lhsT=lh, rhs=rh, start=False, stop=(i == len(ex) - 1)),
                    rd, [pb])
            i = e_i[0] % 3
            e_i[0] += 1
            add("act", lambda e, pt=pt, i=i: e.activation(out=Eb[i][:, 0:nq], in_=pt[:, 0:nq], func=AF.Exp, scale=scale), [pb], [b_E[i]])
            return i

        for kb in range(min(2, nkb)):
            pend.append(score(kb))
        flush_post()
        for kb in range(nkb):
            if kb + 2 < nkb:
                pend.append(score(kb + 2))
            warm(WARM_LAT)
            i = pend[kb]
            add("pe", lambda e, at=at, kb=kb, i=i: e.matmul(at[0:65, 0:nq], lhsT=va_fn(kb), rhs=Eb[i][:, 0:nq], start=(kb == 0), stop=(kb == nkb - 1)),
                reads + [b_E[i]], [ab])
        return at, ab

    def normalize_rep(at, ab, nq, dst, b_dst_list):
        r, br = tmp()
        add("act", lambda e: e.activation(out=r[0:64, 0:nq], in_=at[0:64, 256:256 + nq], func=AF.Ln), [ab], [br])
        add("act", lambda e: e.activation(out=r[0:64, 0:nq], in_=r[0:64, 0:nq], func=AF.Exp, scale=-1.0), [br], [br])
        add("dve", lambda e: e.tensor_tensor(out=dst, in0=at[0:64, 0:nq], in1=r[0:64, 0:nq], op=ALU.mult), [ab, br], b_dst_list)

    def normalize(at, ab, nq, dst, b_dst_list, k=0, c0=0):
        rsum, b_rsum, b_rsb, rr = rsum_s[k], b_rsum_s[k], b_rsb_s[k], rsb_row[k]
        add("act", lambda e: e.activation(out=rsum[64:65, 0:nq], in_=at[64:65, c0:c0 + nq], func=AF.Ln), [ab], [b_rsum])
        add("act", lambda e: e.activation(out=rsb0[rr:rr + 1, 0:nq], in_=rsum[64:65, 0:nq], func=AF.Exp, scale=-1.0), [b_rsum], [b_rsb])
        pt, pb = gp()
        add("pe", lambda e: e.matmul(pt[0:64, 0:nq], lhsT=ones_b[rr:rr + 1, 0:64], rhs=rsb0[rr:rr + 1, 0:nq], start=True, stop=True),
            [b_rsb, b_ones_b], [pb])
        bc, bbc = tmp()
        add("act", lambda e: e.activation(out=bc[0:64, 0:nq], in_=pt[0:64, 0:nq], func=AF.Copy), [pb], [bbc])
        add("dve", lambda e: e.tensor_tensor(out=dst, in0=at[0:64, c0:c0 + nq], in1=bc[0:64, 0:nq], op=ALU.mult), [ab, bbc], b_dst_list)

    def stream_attention(jobs, nkb=18, nq=T, side_cb=None):
        blocks = [(ji, kb) for ji in range(len(jobs)) for kb in range(nkb)]
        N = len(blocks)
        pend = {}
        posts = {}

        def do_score(idx):
            ji, kb = blocks[idx]
            J = jobs[ji]
            if kb == 0 and J.get("pre") is not None:
                J["pre"]()
            pt, pb = gp()
            ex = J["extra_fn"](kb) if J.get("extra_fn") is not None else []
            add("pe", lambda e, pt=pt, kb=kb, J=J: e.matmul(pt[:, 0:nq], lhsT=J["kt_fn"](kb), rhs=J["q_ap"], start=True, stop=(len(ex) == 0)),
                J["reads"], [pb])
            for i2, (lh, rh, rd) in enumerate(ex):
                add("pe", lambda e, pt=pt, lh=lh, rh=rh, i2=i2: e.matmul(pt[:, 0:nq], lhsT=lh, rhs=rh, start=False, stop=(i2 == len(ex) - 1)),
                    rd, [pb])
            i = e_i[0] % 3
            e_i[0] += 1
            add("act", lambda e, pt=pt, i=i, J=J: e.activation(out=Eb[i][:, 0:nq], in_=pt[:, 0:nq], func=AF.Exp, scale=J["scale"]), [pb], [b_E[i]])
            pend[idx] = i

        for idx in range(min(2, N)):
            do_score(idx)
        for idx in range(N):
            if idx + 2 < N:
                do_score(idx + 2)
            ji, kb = blocks[idx]
            J = jobs[ji]
            if kb == 0:
                J["acc"] = acb()
            at, ab = J["acc"]
            i = pend.pop(idx)
            add("pe", lambda e, at=at, kb=kb, i=i, J=J: e.matmul(at[0:65, 0:nq], lhsT=J["va_fn"](kb), rhs=Eb[i][:, 0:nq],
                                                              start=(kb == 0), stop=(kb == nkb - 1)), J["reads"] + [b_E[i]], [ab])
            if kb == nkb - 1:
                if J.get("post") is not None:
                    posts.setdefault(min(idx + 2, N - 1), []).append(J["post"])
                if side_cb is not None:
                    side_cb()
            for f in posts.pop(idx, []):
                f()

    def input_proj(l):
        res = {}
        add("pool", lambda e: e.dma_start(out=wq_s[:], in_=wqup[l].rearrange("(k p) c -> p k c", p=128)), (), [b_wsm], dma=True)
        add("pool", lambda e: e.dma_start(out=wqp_s[:], in_=wqupp[l].rearrange("(k p) c -> p k c", p=128)), (), [b_wsm], dma=True)
        add("pool", lambda e: e.dma_start(out=wkv_s[:], in_=wkvup[l]), (), [b_wsm], dma=True)

        def fm(wt, wb, c0, M, g, n0=0, n=T):
            pt, pb = gp()
            for k in range(8):
                add("pe", lambda e, pt=pt, k=k: e.matmul(pt[0:M, 0:n], lhsT=wt[:, k, c0:c0 + M], rhs=hT[g][:, k, n0:n0 + n],
                                                        start=(k == 0), stop=(k == 7)), [wb, b_hT[g]], [pb])
            return pt, pb

        def tm(wt, wb, c0, N, g, tb):
            pt, pb = gp()
            for k in range(8):
                add("pe", lambda e, pt=pt, k=k: e.matmul(pt[:, 0:N], lhsT=hT[g][:, k, tb * 128:(tb + 1) * 128], rhs=wt[:, k, c0:c0 + N],
                                                        start=(k == 0), stop=(k == 7)), [wb, b_hT[g]], [pb])
            return pt, pb

        def rope_evac(ptA, pbA, ptB, pbB, p0, p1, dst, wlist):
            t1, tb1 = tmp()
            t2, tb2 = tmp()
            add("dve", lambda e: e.tensor_tensor(out=t1[p0:p1, :], in0=ptA[p0:p1, :], in1=cos_s[p0:p1, :], op=ALU.mult), [pbA, b_prm], [tb1])
            add("dve", lambda e: e.tensor_tensor(out=t2[p0:p1, :], in0=ptB[p0:p1, :], in1=sin_s[p0:p1, :], op=ALU.mult), [pbB, b_prm], [tb2])
            add("dve", lambda e: e.tensor_tensor(out=dst, in0=t1[p0:p1, :], in1=t2[p0:p1, :], op=ALU.add), [tb1, tb2], wlist)

        groups = [0, 1] if do_lat else [0]
        wA, bA = wtile(w_in[l, :, 0:512], 512)
        if do_lat:
            wF1, bF1 = wtile(w_inp[l, :, 0:512], 512)
        for h in range(4):
            pt, pb = fm(wA, bA, h * 64, 64, 0)
            evac(QT2[0][0][0:64, h, :], pt[0:64, :], [pb], [b_QT[0][0]])
            pt, pb = fm(wA, bA, 256 + h * 64, 64, 0)
            evac(KTc2[0][0:64, h, :], pt[0:64, :], [pb], [b_KTc[0]])
        if do_lat:
            for h in range(4):
                ptA, pbA = fm(wA, bA, h * 64, 64, 1)
                ptB, pbB = fm(wF1, bF1, h * 64, 64, 1)
                rope_evac(ptA, pbA, ptB, pbB, 0, 64, QT2[1][0][0:64, h, :], [b_QT[1][0]])
                ptA, pbA = fm(wA, bA, 256 + h * 64, 64, 1)
                ptB, pbB = fm(wF1, bF1, 256 + h * 64, 64, 1)
                tk, tkb = sqt()
                rope_evac(ptA, pbA, ptB, pbB, 0, 64, tk[0:64, :], [tkb])
                add("sp", lambda e, tk=tk, h=h: e.dma_start(out=payA_in[l].ap()[R_DAK + h * 64:R_DAK + (h + 1) * 64, :], in_=tk[0:64, :]),
                    [tkb], [b_payA[l]], dma=True)
        wB, bB = wtile(w_in[l, :, 512:1024], 512)
        for g in groups:
            for c in range(2):
                pt, pb = fm(wB, bB, 256 + c * 128, 128, g)
                evac(qd[:, c, :], pt[:, :], [pb], [b_qd])
            stat_rstd([qd[:, 0, :], qd[:, 1, :]], 256, rstd[:], b_rstd, [b_qd])
            for c in range(2):
                add("dve", lambda e, c=c: e.scalar_tensor_tensor(out=qdn[:, c, :], in0=qd[:, c, :], scalar=gq_s[:, l, c:c + 1], in1=rstd[:],
                                                               op0=ALU.mult, op1=ALU.mult), [b_qd, b_rstd, b_prm], [b_qdn])
            for h in range(4):
                pt, pb = gp()
                for c in range(2):
                    add("pe", lambda e, pt=pt, c=c, h=h: e.matmul(pt[0:96, :], lhsT=wq_s[:, c, h * 96:(h + 1) * 96], rhs=qdn[:, c, :],
                                                                 start=(c == 0), stop=(c == 1)), [b_wsm, b_qdn], [pb])
                if g == 0:
                    evac(QT2[0][1][0:96, h, :], pt[0:96, :], [pb], [b_QT[0][1]])
                else:
                    pt2, pb2 = gp()
                    for c in range(2):
                        add("pe", lambda e, pt2=pt2, c=c, h=h: e.matmul(pt2[0:96, :], lhsT=wqp_s[:, c, h * 96:(h + 1) * 96], rhs=qdn[:, c, :],
                                                                       start=(c == 0), stop=(c == 1)), [b_wsm, b_qdn], [pb2])
                    evac(QT2[1][1][0:64, h, :], pt[0:64, :], [pb], [b_QT[1][1]])
                    rope_evac(pt, pb, pt2, pb2, 64, 96, QT2[1][1][64:96, h, :], [b_QT[1][1]])
        for g in groups:
            for tb in range(4):
                pt, pb = tm(wB, bB, 0, 256, g, tb)
                if g == 0:
                    ostage, b_ostage = tmp()
                    add("act", lambda e, pt=pt, ostage=ostage: e.activation(out=ostage[:, 0:256], in_=pt[:, 0:256], func=AF.Copy), [pb], [b_ostage])
                    s, tl = tb // 2, (tb % 2) * 128
                    add("sp", lambda e, s=s, tl=tl, ostage=ostage: e.dma_start(out=o_dav[s, l, :, tl:tl + 128, :].rearrange("h t d -> t h d"),
                                                               in_=ostage[:, 0:256].rearrange("p (h d) -> p h d", h=4)), [b_ostage], (), dma=True)
                    add("dve", lambda e, pt=pt, tb=tb: e.tensor_copy(out=VAc[0][:, tb, :, 0:64], in_=pt[:, 0:256].rearrange("p (h d) -> p h d", h=4)),
                        [pb], [b_VAc[0]])
                else:
                    tv, tvb = sqt()
                    evac(tv[:, 0:256], pt[:, 0:256], [pb], [tvb])
                    add("sp", lambda e, tv=tv, tb=tb: e.dma_start(out=payB_in[l].ap()[R_V + tb * 128:R_V + (tb + 1) * 128, 0:256], in_=tv[:, 0:256]),
                        [tvb], [b_payB[l]], dma=True)
        wC, bC = wtile(w_in[l, :, 1024:1440], 416)
        if do_lat:
            wF2, bF2 = wtile(w_inp[l, :, 512:608], 96)
        for g in groups:
            pt, pb = fm(wC, bC, 0, 128, g)
            kvd, b_kvd = tmp()
            evac(kvd[:, :], pt[:, :], [pb], [b_kvd])
            stat_rstd([kvd[:, :]], 128, rstd[:], b_rstd, [b_kvd])
            dstc = ckvTc if g == 0 else sqb[0]
            if g == 0:
                add("dve", lambda e, kvd=kvd: e.scalar_tensor_tensor(out=ckvTc[:, 0:T], in0=kvd[:, :], scalar=gkvc_s[:, l:l + 1], in1=rstd[:],
                                                            op0=ALU.mult, op1=ALU.mult), [b_kvd, b_rstd, b_prm], [b_ckvTc])
            else:
                tk, tkb = sqt()
                add("dve", lambda e, tk=tk, kvd=kvd: e.scalar_tensor_tensor(out=tk[:, :], in0=kvd[:, :], scalar=gkvc_s[:, l:l + 1], in1=rstd[:],
                                                                   op0=ALU.mult, op1=ALU.mult), [b_kvd, b_rstd, b_prm], [tkb])
                add("sp", lambda e, tk=tk: e.dma_start(out=payA_in[l].ap()[R_CKV:R_CKV + 128, :], in_=tk[:, :]), [tkb], [b_payA[l]], dma=True)
            pt, pb = fm(wC, bC, 64, 96, g)
            if g == 0:
                for h in range(4):
                    evac(KTc2[1][64:96, h, :], pt[64:96, :], [pb], [b_KTc[1]])
            else:
                ptB, pbB = fm(wF2, bF2, 0, 96, 1)
                tk, tkb = sqt()
                rope_evac(pt, pb, ptB, pbB, 64, 96, tk[64:96, :], [tkb])
                add("sp", lambda e, tk=tk: e.dma_start(out=payA_in[l].ap()[R_KR:R_KR + 32, :], in_=tk[64:96, :]), [tkb], [b_payA[l]], dma=True)
            for h in range(4):
                pt, pb = fm(wC, bC, 160 + h * 64, 64, g)
                evac(QT2[g][0][64:128, h, :], pt[0:64, :], [pb], [b_QT[g][2]])
        for tb in range(4):
            pt, pb = tm(wC, bC, 0, 160, 0, tb)
            s, tl = tb // 2, (tb % 2) * 128
            tt, tb_ = tmp()
            add("act", lambda e, pt=pt, tt=tt: e.activation(out=tt[:, 0:128], in_=pt[:, 0:128], func=AF.Square), [pb], [tb_])
            add("dve", lambda e, tt=tt: e.reduce_sum(out=tt[:, 200:201], in_=tt[:, 0:128], axis=mybir.AxisListType.X), [tb_], [tb_])
            add("act", lambda e, tt=tt: e.activation(out=tt[:, 201:202], in_=tt[:, 200:201], func=AF.Ln, bias=epsc[:, 0:1], scale=1.0 / 128),
                [tb_, b_epsc], [tb_])
            add("act", lambda e, tt=tt: e.activation(out=tt[:, 202:203], in_=tt[:, 201:202], func=AF.Exp, scale=-0.5), [tb_], [tb_])
            add("dve", lambda e, pt=pt, tt=tt: e.scalar_tensor_tensor(out=tt[:, 256:384], in0=pt[:, 0:128], scalar=tt[:, 202:203],
                                                                      in1=gkvr_s[:, l, :], op0=ALU.mult, op1=ALU.mult),
                [pb, tb_, b_prm], [tb_])
            add("act", lambda e, pt=pt, tt=tt: e.activation(out=tt[:, 384:416], in_=pt[:, 128:160], func=AF.Copy), [pb, tb_], [tb_])
            add("sp", lambda e, s=s, tl=tl, tt=tt: e.dma_start(out=o_ckv[s, l, tl:tl + 128, :], in_=tt[:, 256:384]), [tb_], (), dma=True)
            add("sp", lambda e, s=s, tl=tl, tt=tt: e.dma_start(out=o_kr[s, l, tl:tl + 128, :], in_=tt[:, 384:416]), [tb_], (), dma=True)
        for h in range(4):
            pt, pb = gp()
            add("pe", lambda e, pt=pt, h=h: e.matmul(pt[0:64, :], lhsT=wkv_s[:, h * 128:h * 128 + 64], rhs=ckvTc[:, 0:T], start=True, stop=True),
                [b_wsm, b_ckvTc], [pb])
            evac(KTc2[1][0:64, h, :], pt[0:64, :], [pb], [b_KTc[1]])
        for tb in range(4):
            pt, pb = gp()
            add("pe", lambda e, pt=pt, tb=tb: e.matmul(pt[:, 0:256], lhsT=ckvTc[:, tb * 128:(tb + 1) * 128],
                                                      rhs=wkv_s[:, :].rearrange("p (h c) -> p h c", h=4)[:, :, 64:128], start=True, stop=True),
                [b_wsm, b_ckvTc], [pb])
            evac(VAc[1][:, tb, :, 0:64], pt[:, 0:256].rearrange("p (h d) -> p h d", h=4), [pb], [b_VAc[1]])
        wD, bD = wtile(w_in[l, :, 1440:1952], 512)
        for h in range(4):
            pt, pb = fm(wD, bD, h * 64, 64, 0)
            evac(KTc2[0][64:128, h, :], pt[0:64, :], [pb], [b_KTc[2]])
            if do_lat:
                pt, pb = fm(wD, bD, h * 64, 64, 1)
                tk, tkb = sqt()
                evac(tk[0:64, :], pt[0:64, :], [pb], [tkb])
                add("sp", lambda e, tk=tk, h=h: e.dma_start(out=payA_in[l].ap()[R_NAK + h * 64:R_NAK + (h + 1) * 64, :], in_=tk[0:64, :]),
                    [tkb], [b_payA[l]], dma=True)
        for tb in range(4):
            pt, pb = tm(wD, bD, 0, 512, 0, tb)
            s, tl = tb // 2, (tb % 2) * 128
            ostage, b_ostage = tmp()
            add("act", lambda e, pt=pt, ostage=ostage: e.activation(out=ostage[:, :], in_=pt[:, :], func=AF.Copy), [pb], [b_ostage])
            add("sp", lambda e, s=s, tl=tl, ostage=ostage: e.dma_start(out=o_nak[s, l, :, tl:tl + 128, :].rearrange("h t d -> t h d"),
                                                       in_=ostage[:, 0:256].rearrange("p (h d) -> p h d", h=4)), [b_ostage], (), dma=True)
            add("sp", lambda e, s=s, tl=tl, ostage=ostage: e.dma_start(out=o_nav[s, l, :, tl:tl + 128, :].rearrange("h t d -> t h d"),
                                                       in_=ostage[:, 256:512].rearrange("p (h d) -> p h d", h=4)), [b_ostage], (), dma=True)
            add("dve", lambda e, pt=pt, tb=tb: e.tensor_copy(out=VAc[2][:, tb, :, 0:64], in_=pt[:, 256:512].rearrange("p (h d) -> p h d", h=4)),
                [pb], [b_VAc[2]])
            if do_lat:
                pt, pb = tm(wD, bD, 256, 256, 1, tb)
                tv, tvb = sqt()
                evac(tv[:, 0:256], pt[:, 0:256], [pb], [tvb])
                add("sp", lambda e, tv=tv, tb=tb: e.dma_start(out=payB_in[l].ap()[R_V + tb * 128:R_V + (tb + 1) * 128, 256:512], in_=tv[:, 0:256]),
                    [tvb], [b_payB[l]], dma=True)
        wA2, bA2 = wtile(w_in[l, :, 256:512], 256)
        for tb in range(4):
            pt, pb = tm(wA2, bA2, 0, 256, 0, tb)
            s, tl = tb // 2, (tb % 2) * 128
            ostage, b_ostage = tmp()
            add("act", lambda e, pt=pt, ostage=ostage: e.activation(out=ostage[:, 0:256], in_=pt[:, 0:256], func=AF.Copy), [pb], [b_ostage])
            add("sp", lambda e, s=s, tl=tl, ostage=ostage: e.dma_start(out=o_dak[s, l, :, tl:tl + 128, :].rearrange("h t d -> t h d"),
                                                       in_=ostage[:, 0:256].rearrange("p (h d) -> p h d", h=4)), [b_ostage], (), dma=True)
        wE, bE = wtile(w_in[l, :, 1952:2464], 512)
        for g in groups:
            for c in range(2):
                pa, pba = fm(wE, bE, c * 128, 128, g)
                pbm, pbb = fm(wE, bE, 256 + c * 128, 128, g)
                tt, tb_ = tmp()
                add("act", lambda e, tt=tt, pbm=pbm: e.activation(out=tt[:], in_=pbm[:, :], func=AF.Sigmoid), [pbb], [tb_])
                add("dve", lambda e, tt=tt, pa=pa, c=c, g=g: e.tensor_tensor(out=gpad[g][:, c, :, 15:15 + 256],
                                                                           in0=pa[:, :].rearrange("p (s t) -> p s t", s=2),
                                                                           in1=tt[:].rearrange("p (s t) -> p s t", s=2), op=ALU.mult),
                    [pba, tb_], [b_gpad[g]])
        if do_lat:
            add("dve", lambda e: e.tensor_copy(out=halo[:, :, 0, 0:15], in_=gpad[1][:, :, 0, 15:30]), [b_gpad[1]], [b_halo])
            add("dve", lambda e: e.tensor_copy(out=halo[:, :, 0, 15:30], in_=gpad[1][:, :, 1, 256:271]), [b_gpad[1]], [b_halo])
            hdst = dram_ap(payB_in[l], R_HALO * T, [[30, 128], [128 * 30, 2], [1, 30]])
            add("sp", lambda e: e.dma_start(out=hdst, in_=halo[:, :, 0, :]), [b_halo], [b_payB[l]], dma=True)
            add("pool", lambda e: e.collective_compute("AllGather", ALU.bypass, replica_groups=[[0, 1, 2, 3], [4, 5, 6, 7]],
                                                       ins=[payA_in[l].ap().opt()], outs=[payA_out[l].ap().opt()]),
                [b_payA[l]], [b_payoA[l]], dma="cc")
            add("pool", lambda e: e.collective_compute("AllGather", ALU.bypass, replica_groups=[[0, 1, 2, 3], [4, 5, 6, 7]],
                                                       ins=[payB_in[l].ap().opt()], outs=[payB_out[l].ap().opt()]),
                [b_payB[l]], [b_payoB[l]], dma="cc")

    def da_post(l, g, h, at1, ab1, at2, ab2, nq=T, q0=0, repl=False, c1=0, c2=0):
        o1, bo1 = dao, b_dao
        o2, bo2 = rsum_s[0], b_rsum_s[0]
        if repl:
            normalize_rep(at1, ab1, nq, o1[0:64, 0:nq], [bo1])
            normalize_rep(at2, ab2, nq, o2[0:64, 0:nq], [bo2])
        else:
            normalize(at1, ab1, nq, o1[0:64, 0:nq], [bo1], 0, c1)
            normalize(at2, ab2, nq, o2[0:64, 0:nq], [bo2], 1 if nq <= 256 else 0, c2)
        add("dve", lambda e: e.scalar_tensor_tensor(out=o1[0:64, 0:nq], in0=o2[0:64, 0:nq], scalar=neglam[0:64, l:l + 1], in1=o1[0:64, 0:nq],
                                                    op0=ALU.mult, op1=ALU.add), [bo1, bo2, b_prm], [bo1])
        stat_rstd([o1[0:64, 0:nq]], 64, rstd[0:64, 0:nq], b_rstd, [bo1], K=64, n=nq)
        p0 = (h % 2) * 64
        add("dve", lambda e: e.scalar_tensor_tensor(out=mixT[g][p0:p0 + 64, h // 2, q0:q0 + nq], in0=o1[0:64, 0:nq], scalar=gsub2[0:64, l:l + 1],
                                                    in1=rstd[0:64, 0:nq], op0=ALU.mult, op1=ALU.mult),
            [bo1, b_rstd, b_prm], [b_mix[g][h // 2]])

    def plain_post(g, m, h, at, ab, nq=T, q0=0, repl=False):
        if repl:
            p0 = (h % 2) * 64
            ch = 2 * m + h // 2
            normalize_rep(at, ab, nq, mixT[g][p0:p0 + 64, ch, q0:q0 + nq], [b_mix[g][ch]])
            return
        k = (set_i[0] % NSET) if nq <= 256 else 0
        set_i[0] += 1
        o1, bo1 = tmp()
        normalize(at, ab, nq, o1[0:64, 0:nq], [bo1], k)
        p0 = (h % 2) * 64
        ch = 2 * m + h // 2
        add("act", lambda e: e.activation(out=mixT[g][p0:p0 + 64, ch, q0:q0 + nq], in_=o1[0:64, 0:nq], func=AF.Copy), [bo1], [b_mix[g][ch]])

    def ctx_attention(l, side_gen=None):
        scales = (32 ** -0.5, 96 ** -0.5, 64 ** -0.5)
        rows = {0: None, 1: (0, 96), 2: (0, 64)}
        jobs = []
        for s in range(2):
            for m in range(3):
                for h in range(4):
                    for mp in ((0, 1) if m == 0 else (0,)):
                        jobs.append(dict(s=s, m=m, h=h, mp=mp))
        n = len(jobs)

        def stage_a(j):
            s_, m, h, mp = j["s"], j["m"], j["h"], j["mp"]
            q0 = s_ * 256
            lo, hi = (mp * 32, mp * 32 + 32) if m == 0 else rows[m]
            pt, pb = gp()
            rd = [b_KTc[m], b_QT[0][m]]
            for kb in range(2):
                add("pe", lambda e, pt=pt, kb=kb, m=m, h=h, lo=lo, hi=hi, q0=q0: e.matmul(
                    pt[:, kb * 256:(kb + 1) * 256], lhsT=kc_ap(m, h, lo, hi, q0 + kb * 128, 128), rhs=q_ap(0, m, h, lo, hi, q0, 256),
                    start=True, stop=True), rd, [pb])
            i = e_i[0] % 3
            e_i[0] += 1
            add("act", lambda e, pt=pt, i=i, m=m: e.activation(out=Eb[i][:, :], in_=pt[:, :], func=AF.Exp, scale=scales[m]), [pb], [b_E[i]])
            j["e"] = i

        def stage_b(j):
            s_, m, h = j["s"], j["m"], j["h"]
            at, ab = acb()
            i = j["e"]
            for kb in range(2):
                add("pe", lambda e, at=at, kb=kb, i=i, m=m, h=h, s_=s_: e.matmul(at[0:65, 0:256], lhsT=VAc[m][:, s_ * 2 + kb, h, 0:65],
                                                                               rhs=Eb[i][:, kb * 256:(kb + 1) * 256], start=(kb == 0), stop=(kb == 1)),
                    [b_VAc[m], b_E[i]], [ab])
            for kb in range(2):
                add("pe", lambda e, at=at, kb=kb, i=i: e.matmul(at[0:64, 256:512], lhsT=ones_b[:, 0:64], rhs=Eb[i][:, kb * 256:(kb + 1) * 256],
                                                               start=(kb == 0), stop=(kb == 1)), [b_ones_b, b_E[i]], [ab])
            j["acc"] = (at, ab)

        def stage_c(idx):
            j = jobs[idx]
            s_, m, h, mp = j["s"], j["m"], j["h"], j["mp"]
            q0 = s_ * 256
            if m == 0:
                if mp == 1:
                    a1 = jobs[idx - 1]["acc"]
                    da_post(l, 0, h, a1[0], a1[1], j["acc"][0], j["acc"][1], nq=256, q0=q0, repl=True)
            else:
                plain_post(0, m, h, j["acc"][0], j["acc"][1], nq=256, q0=q0, repl=True)

        for i in range(n + 2):
            if i < n:
                stage_a(jobs[i])
            warm(WARM_CTX)
            if 0 <= i - 1 < n:
                stage_b(jobs[i - 1])
            if 0 <= i - 2 < n:
                stage_c(i - 2)
            if side_gen is not None and i % 2 == 1:
                next(side_gen, None)

    def cache_T(src_ap, ncol, dst_fn, wlist):
        stg, bstg = tmp()
        add("sp", lambda e: e.dma_start(out=stg[:, 0:2 * ncol].rearrange("p (t c) -> p t c", t=2),
                                        in_=src_ap.rearrange("(t p) c -> p t c", p=128)), (), [bstg], dma=True)
        for tb in range(2):
            pt, pb = gp()
            add("pe", lambda e, pt=pt, tb=tb: e.transpose(out=pt[0:ncol, 0:128], in_=stg[:, tb * ncol:(tb + 1) * ncol], identity=ident[:]),
                [bstg, b_ident], [pb])
            evac(dst_fn(tb), pt[0:ncol, 0:128], [pb], wlist)

    def load_V(l, c0, cache_ap):
        for rk in range(4):
            for tb in range(4):
                src = dram_ap(payB_out[l], (rk * PB + R_V + tb * 128) * T + c0, [[T, 128], [64, 4], [1, 64]])
                add("sp", lambda e, rk=rk, tb=tb, src=src: e.dma_start(out=VAb[:, rk * 4 + tb, :, 0:64], in_=src), [b_payoB[l]], [b_VAk[rk * 4 + tb]], dma=True)
        for tb in range(2):
            add("pool", lambda e, tb=tb: e.dma_start(out=VAb[:, 16 + tb, :, 0:64], in_=cache_ap[:, tb * 128:(tb + 1) * 128, :].rearrange("h t d -> t h d")),
                (), [b_VAk[16 + tb]], dma=True)

    def load_KT(l, row0, nrows, p0, heads=True):
        for h in range(4):
            r = row0 + (h * nrows if heads else 0)
            src = dram_ap(payA_out[l], r * T, [[T, nrows], [PA * T, 4], [1, T]])
            add("sp", lambda e, h=h, src=src: e.dma_start(out=KTb[p0:p0 + nrows, h, 0:DEC_S].rearrange("p (r t) -> p r t", r=4), in_=src),
                [b_payoA[l]], [b_KTh[h]], dma=True)

    def lat_attention(l, side_gen=None, mod_gen=None):
        rdv = list(b_VAk)

        def side_step(n=1):
            if side_gen is not None:
                for _ in range(n):
                    next(side_gen, None)
            if mod_gen is not None:
                next(mod_gen, None)

        load_KT(l, R_DAK, 64, 0)
        for h in range(4):
            cache_T(c_dak[l, h], 64, lambda tb, h=h: KTb[0:64, h, DEC_S + tb * 128:DEC_S + (tb + 1) * 128], [b_KTh[h]])
        load_V(l, 0, c_dav[l])
        jobs = []
        for h in range(4):
            for qh in range(2):
                idx = h * 2 + qh
                tq, btq = tbt[idx // 5], b_tbt[idx // 5]
                qb = tq[0:64, :, :].rearrange("p a w -> p (a w)")[:, (idx % 5) * 512:(idx % 5 + 1) * 512]
                add("pool", lambda e, qb=qb: e.memset(qb, 0.0), (), [btq])
                add("pool", lambda e, qb=qb, h=h, qh=qh: e.tensor_copy(out=qb[0:32, 0:256], in_=QT2[1][0][0:32, h, qh * 256:(qh + 1) * 256]),
                    [b_QT[1][0]], [btq])
                add("pool", lambda e, qb=qb, h=h, qh=qh: e.tensor_copy(out=qb[32:64, 256:512], in_=QT2[1][0][32:64, h, qh * 256:(qh + 1) * 256]),
                    [b_QT[1][0]], [btq])
                J = dict(kt_fn=(lambda kb, h=h: KTb[0:64, h, kb * 128:(kb + 1) * 128]), q_ap=qb,
                         va_fn=(lambda kb, h=h: VAb[:, kb, h, 0:65]), scale=32 ** -0.5, reads=rdv + [b_KTh[h], btq])
                J["post"] = (lambda h=h, qh=qh, J=J: da_post(l, 1, h, J["acc"][0], J["acc"][1], J["acc"][0], J["acc"][1],
                                                             nq=256, q0=qh * 256, c1=0, c2=256))
                jobs.append(J)
        stream_attention(jobs, side_cb=lambda: side_step(1))
        src = dram_ap(payA_out[l], R_CKV * T, [[T, 128], [PA * T, 4], [1, T]])
        add("sp", lambda e, src=src: e.dma_start(out=ckvT[:, 0:DEC_S].rearrange("p (r t) -> p r t", r=4), in_=src), [b_payoA[l]], [b_ckvT], dma=True)
        cache_T(c_ckv[l], 128, lambda tb: ckvT[:, DEC_S + tb * 128:DEC_S + (tb + 1) * 128], [b_ckvT])
        load_KT(l, R_KR, 32, 64, heads=False)
        for h in range(4):
            cache_T(c_kr[l], 32, lambda tb, h=h: KTb[64:96, h, DEC_S + tb * 128:DEC_S + (tb + 1) * 128], [b_KTh[h]])
        for h in range(4):
            for cb in range(5):
                n0 = cb * 512
                n = min(512, DEC_S + PAST - n0)
                pt, pb = gp()
                add("pe", lambda e, pt=pt, h=h, n0=n0, n=n: e.matmul(pt[0:64, 0:n], lhsT=wkv_s[:, h * 128:h * 128 + 64], rhs=ckvT[:, n0:n0 + n],
                                                                    start=True, stop=True), [b_wsm, b_ckvT], [pb])
                evac(KTb[0:64, h, n0:n0 + n], pt[0:64, 0:n], [pb], [b_KTh[h]])
        for kb in range(18):
            pt, pb = gp()
            add("pe", lambda e, pt=pt, kb=kb: e.matmul(pt[:, 0:256], lhsT=ckvT[:, kb * 128:(kb + 1) * 128],
                                                      rhs=wkv_s[:, :].rearrange("p (h c) -> p h c", h=4)[:, :, 64:128], start=True, stop=True),
                [b_wsm, b_ckvT], [pb])
            evac(VAb[:, kb, :, 0:64], pt[:, 0:256].rearrange("p (h d) -> p h d", h=4), [pb], [b_VAk[kb]])
        jobs = []
        for h in range(4):
            J = dict(kt_fn=(lambda kb, h=h: KTb[0:96, h, kb * 128:(kb + 1) * 128]), q_ap=q_ap(1, 1, h, 0, 96),
                     va_fn=(lambda kb, h=h: VAb[:, kb, h, 0:65]), scale=96 ** -0.5, reads=rdv + [b_KTh[h], b_QT[1][1]])
            J["post"] = (lambda h=h, J=J: plain_post(1, 1, h, J["acc"][0], J["acc"][1]))
            jobs.append(J)
        stream_attention(jobs, side_cb=lambda: side_step(2))
        load_KT(l, R_NAK, 64, 0)
        for h in range(4):
            cache_T(c_nak[l, h], 64, lambda tb, h=h: KTb[0:64, h, DEC_S + tb * 128:DEC_S + (tb + 1) * 128], [b_KTh[h]])
            add("pool", lambda e, h=h: e.dma_start(out=KTb[64:96, h, 0:DEC_S], in_=rowind_d), (), [b_KTh[h]], dma=True)
            add("pool", lambda e, h=h: e.memset(KTb[64:96, h, DEC_S:DEC_S + PAST], 0.0), (), [b_KTh[h]])
            evac(QT2[1][0][0:64, h, :], QT2[1][0][64:128, h, :], [b_QT[1][2], b_QT[1][0]], [b_QT[1][0], b_QT[1][2]])
            add("pool", lambda e, h=h: e.dma_start(out=QT2[1][0][64:96, h, :], in_=rowsel_d), (), [b_QT[1][0], b_QT[1][2]], dma=True)
        load_V(l, 256, c_nav[l])
        jobs = []
        for h in range(4):
            ti = h % 2

            def pre(h=h, ti=ti):
                for j in range(2):
                    src = bass.AP(tz_all, l * TZ_SZ + (h * (A2 + 1) + (1 - j)) * 64 * TZL + 63, [[TZL - 1, 64], [64 * TZL, A2], [1, 64]])
                    add("pool", lambda e, j=j, src=src, ti=ti: e.dma_start(out=tbt[ti][j * 64:(j + 1) * 64, :, :], in_=src), (), [b_tbt[ti]], dma=True)
                add("dve", lambda e, ti=ti: e.scalar_tensor_tensor(out=tbt[ti][:], in0=tbt[ti][:], scalar=8.0,
                                                                   in1=colm_s[:, :].unsqueeze(1).to_broadcast([128, A2, 64]),
                                                                   op0=ALU.mult, op1=ALU.add), [b_tbt[ti], b_prm], [b_tbt[ti]])

            def extra(kb, ti=ti):
                if kb >= 16:
                    return []
                a0 = 37 - 2 * kb
                return [(identb[:, :], tbt[ti][:, a0:a0 + 8, :], [b_identb, b_tbt[ti]])]

            J = dict(kt_fn=(lambda kb, h=h: KTb[0:96, h, kb * 128:(kb + 1) * 128]), q_ap=QT2[1][0][0:96, h, :],
                     va_fn=(lambda kb, h=h: VAb[:, kb, h, 0:65]), scale=64 ** -0.5, reads=rdv + [b_KTh[h], b_QT[1][2]],
                     extra_fn=extra, pre=pre)
            J["post"] = (lambda h=h, J=J: plain_post(1, 2, h, J["acc"][0], J["acc"][1]))
            jobs.append(J)
        stream_attention(jobs, side_cb=lambda: side_step(2))

    def conv_module(l, g):
        if g == 1:
            for c in range(2):
                src = dram_ap(payB_out[l], R_HALO * T + c * 128 * 30, [[30, 128], [PB * T, 4], [1, 30]])
                add("sp", lambda e, src=src, c=c: e.dma_start(out=halo[:, c, :, :], in_=src), [b_payoB[l]], [b_halo], dma=True)
            for side in range(2):
                dst = gpad[1][:, :, 0, 0:15] if side == 0 else gpad[1][:, :, 1, 271:286]
                for rk in range(4):
                    srcv = halo[:, :, rk, 15:30] if side == 0 else halo[:, :, rk, 0:15]
                    sc = halsel_s[:, side * 4 + rk:side * 4 + rk + 1]
                    if rk == 0:
                        add("dve", lambda e, dst=dst, srcv=srcv, sc=sc: e.tensor_scalar(out=dst, in0=srcv, scalar1=sc, scalar2=None, op0=ALU.mult),
                            [b_halo, b_prm, b_gpad[1]], [b_gpad[1]])
                    else:
                        add("dve", lambda e, dst=dst, srcv=srcv, sc=sc: e.scalar_tensor_tensor(out=dst, in0=srcv, scalar=sc, in1=dst,
                                                                                              op0=ALU.mult, op1=ALU.add),
                            [b_halo, b_prm, b_gpad[1]], [b_gpad[1]])
        for c in range(2):
            for j in range(31):
                if g == 0:
                    src = gpad[0][:, c, :, j:j + 256]
                    dst = cacc[:, c, :].rearrange("p (s t) -> p s t", s=2)
                    ops = [(src, dst)]
                else:
                    ops = []
                    n_a = max(0, min(256, 271 - j))
                    if n_a > 0:
                        ops.append((gpad[1][:, c, 0, j:j + n_a], cacc[:, c, 0:n_a]))
                    if n_a < 256:
                        ops.append((gpad[1][:, c, 1, 15 + (n_a + j - 271):15 + (256 + j - 271)], cacc[:, c, n_a:256]))
                    n_b = max(0, min(256, 271 - (256 + j)))
                    if n_b > 0:
                        ops.append((gpad[1][:, c, 0, 256 + j:256 + j + n_b], cacc[:, c, 256:256 + n_b]))
                    ops.append((gpad[1][:, c, 1, 15 + (256 + n_b + j - 271):15 + (512 + j - 271)], cacc[:, c, 256 + n_b:512]))
                if j % 4 == 3:
                    yield
                for (src, dst) in ops:
                    if j == 0:
                        add("dve", lambda e, src=src, dst=dst, c=c: e.tensor_scalar(out=dst, in0=src, scalar1=dw_s[:, l, c, 0:1], scalar2=cb_s[:, l, c:c + 1],
                                                                                    op0=ALU.mult, op1=ALU.add), [b_gpad[g], b_prm, b_cacc], [b_cacc])
                    else:
                        add("dve", lambda e, src=src, dst=dst, c=c, j=j: e.scalar_tensor_tensor(out=dst, in0=src, scalar=dw_s[:, l, c, j:j + 1], in1=dst,
                                                                                                op0=ALU.mult, op1=ALU.add), [b_gpad[g], b_prm, b_cacc], [b_cacc])
        p1, pb1 = gp()
        p2, pb2 = gp()
        for c in range(2):
            tt, tb_ = tmp()
            add("act", lambda e, tt=tt, c=c: e.activation(out=tt[:], in_=cacc[:, c, :], func=AF.Square), [b_cacc], [tb_])
            add("pe", lambda e, c=c: e.matmul(p1[:, :], lhsT=ones_f[:, :], rhs=cacc[:, c, :], start=(c == 0), stop=(c == 1)), [b_cacc, b_ones_f], [pb1])
            add("pe", lambda e, tt=tt, c=c: e.matmul(p2[:, :], lhsT=ones_f[:, :], rhs=tt[:], start=(c == 0), stop=(c == 1)), [tb_, b_ones_f], [pb2])
        mean, bmean = tmp()
        var, bvar = tmp()
        add("act", lambda e: e.activation(out=mean[:], in_=p1[:, :], func=AF.Identity, scale=1.0 / 256), [pb1], [bmean])
        add("dve", lambda e: e.tensor_tensor(out=var[:], in0=mean[:], in1=mean[:], op=ALU.mult), [bmean], [bvar])
        add("dve", lambda e: e.scalar_tensor_tensor(out=var[:], in0=p2[:, :], scalar=1.0 / 256, in1=var[:], op0=ALU.mult, op1=ALU.subtract),
            [pb2, bvar], [bvar])
        add("act", lambda e: e.activation(out=var[:], in_=var[:], func=AF.Ln, bias=epsc[:, 0:1], scale=1.0), [bvar, b_epsc], [bvar])
        add("act", lambda e: e.activation(out=var[:], in_=var[:], func=AF.Exp, scale=-0.5), [bvar], [bvar])
        for c in range(2):
            add("dve", lambda e, c=c: e.tensor_tensor(out=cacc[:, c, :], in0=cacc[:, c, :], in1=mean[:], op=ALU.subtract), [b_cacc, bmean], [b_cacc])
            add("dve", lambda e, c=c: e.tensor_tensor(out=cacc[:, c, :], in0=cacc[:, c, :], in1=var[:], op=ALU.mult), [b_cacc, bvar], [b_cacc])
            add("act", lambda e, c=c: e.activation(out=cacc[:, c, :], in_=cacc[:, c, :], func=AF.Identity, bias=lnb_s[:, l, c:c + 1],
                                                   scale=lng_s[:, l, c:c + 1]), [b_cacc, b_prm], [b_cacc])
            tt, tb_ = tmp()
            add("act", lambda e, tt=tt, c=c: e.activation(out=tt[:], in_=cacc[:, c, :], func=AF.Sigmoid), [b_cacc], [tb_])
            add("dve", lambda e, tt=tt, c=c: e.tensor_tensor(out=mixT[g][:, 6 + c, :], in0=cacc[:, c, :], in1=tt[:], op=ALU.mult),
                [b_cacc, tb_], [b_mix[g][6 + c]])

    def out_proj(l, groups):
        for ti in range(2):
            wt, wb = wtile(w_out[l, :, ti * 512:(ti + 1) * 512], 512)
            for cc in range(4):
                oc = ti * 4 + cc
                for g in groups:
                    pt, pb = gp()
                    for k in range(8):
                        add("pe", lambda e, pt=pt, k=k, cc=cc, wt=wt, g=g: e.matmul(pt[:, :], lhsT=wt[:, k, cc * 128:(cc + 1) * 128], rhs=mixT[g][:, k, :],
                                                                                   start=(k == 0), stop=(k == 7)), [wb, b_mix[g][k]], [pb])
                    add("dve", lambda e, pt=pt, g=g, oc=oc: e.scalar_tensor_tensor(out=xT[g][:, oc, :], in0=pt[:, :], scalar=mcol(l, "g1", oc, g),
                                                                                 in1=xT[g][:, oc, :], op0=ALU.mult, op1=ALU.add),
                        [pb, b_mod[l], b_xT[g][oc]], [b_xT[g][oc]])

    def ffn(l, groups):
        for blk in range(4):
            for ti in range(2):
                wt, wb = wtile(w_ff1[l, :, blk * 1024 + ti * 512: blk * 1024 + (ti + 1) * 512], 512)
                for cc in range(4):
                    fc = ti * 4 + cc
                    for g in groups:
                        pt, pb = gp()
                        for k in range(8):
                            add("pe", lambda e, pt=pt, k=k, cc=cc, wt=wt, g=g: e.matmul(pt[:, :], lhsT=wt[:, k, cc * 128:(cc + 1) * 128], rhs=hT[g][:, k, :],
                                                                                       start=(k == 0), stop=(k == 7)), [wb, b_hT[g]], [pb])
                        sq, bq = sqt()
                        add("act", lambda e, pt=pt, sq=sq: e.activation(out=sq[:], in_=pt[:, :], func=AF.Relu), [pb], [bq])
                        add("dve", lambda e, sq=sq, g=g, fc=fc: e.tensor_tensor(out=mixT[g][:, fc, :], in0=sq[:], in1=sq[:], op=ALU.mult),
                            [bq], [b_mix[g][fc]])
            for ti in range(2):
                wt, wb = wtile(w_ff2[l, blk * 1024:(blk + 1) * 1024, ti * 512:(ti + 1) * 512], 512)
                for cc in range(4):
                    oc = ti * 4 + cc
                    for g in groups:
                        pt, pb = gp()
                        for k in range(8):
                            add("pe", lambda e, pt=pt, k=k, cc=cc, wt=wt, g=g: e.matmul(pt[:, :], lhsT=wt[:, k, cc * 128:(cc + 1) * 128], rhs=mixT[g][:, k, :],
                                                                                       start=(k == 0), stop=(k == 7)), [wb, b_mix[g][k]], [pb])
                        add("dve", lambda e, pt=pt, g=g, oc=oc: e.scalar_tensor_tensor(out=xT[g][:, oc, :], in0=pt[:, :], scalar=mcol(l, "g2", oc, g),
                                                                                     in1=xT[g][:, oc, :], op0=ALU.mult, op1=ALU.add),
                            [pb, b_mod[l], b_xT[g][oc]], [b_xT[g][oc]])

    def dump8(name, t, bufs):
        if name not in dbg:
            return
        o = dout("dbg_" + name, [128, 8, T])
        for j in range(8):
            tt, tb_ = tmp()
            add("dve", lambda e, tt=tt, j=j: e.tensor_copy(out=tt[:], in_=t[:, j, :]), [bufs[j]], [tb_])
            add("sp", lambda e, tt=tt, j=j: e.dma_start(out=o[:, j, :], in_=tt[:]), [tb_], (), dma=True)
        dbg_out[name] = [128, 8, T]

    groups = [0, 1] if do_lat else [0]
    import os as _os
    _l1 = _os.environ.get("L1STAGES")
    _stages0 = stages
    for l in range(nlayers):
        stages = _stages0 if (l == 0 or _l1 is None) else set(_l1.split(","))
        if "mod" in stages and l == 0:
            for _ in modulation(0):
                pass
        if "norm1" in stages:
            for g in groups:
                rmsnorm_mod(l, g, 0)
        if "proj" in stages:
            input_proj(l)
        side = modulation(l + 1) if ("mod" in stages and l + 1 < nlayers) else None
        if "ctxattn" in stages:
            ctx_attention(l, None if (do_lat and "latattn" in stages) else side)
        if "conv" in stages:
            for _ in conv_module(l, 0):
                pass
        dump8("mix0%d" % l, mixT[0], b_mix[0])
        if do_lat:
            side2 = conv_module(l, 1) if "conv" in stages else None
            if "latattn" in stages:
                lat_attention(l, side2, side)
            if side2 is not None:
                for _ in side2:
                    pass
        if side is not None:
            for _ in side:
                pass
            dump8("mix1%d" % l, mixT[1], b_mix[1])
        if "outproj" in stages:
            out_proj(l, groups)
        for g in groups:
            dump8("xattn%d%d" % (g, l), xT[g], b_xT[g])
        if "norm2" in stages:
            for g in groups:
                rmsnorm_mod(l, g, 1)
        if "ffn" in stages:
            ffn(l, groups)
        for g in groups:
            dump8("xffn%d%d" % (g, l), xT[g], b_xT[g])
    for g in groups:
        if "final" not in stages:
            continue
        pt, pb = gp()
        for j in range(8):
            sq, bq = sqt()
            add("act", lambda e, sq=sq, j=j, g=g: e.activation(out=sq[:], in_=xT[g][:, j, :], func=AF.Square), [b_xT[g][j]], [bq])
            add("pe", lambda e, sq=sq, j=j, pt=pt: e.matmul(pt[:, :], lhsT=ones_b[:, :], rhs=sq[:], start=(j == 0), stop=(j == 7)), [bq, b_ones_b], [pb])
        add("act", lambda e, pt=pt: e.activation(out=rstd[:], in_=pt[:, :], func=AF.Ln, bias=epsc[:, 0:1], scale=1.0 / D), [pb, b_epsc], [b_rstd])
        add("act", lambda e: e.activation(out=rstd[:], in_=rstd[:], func=AF.Exp, scale=-0.5), [b_rstd], [b_rstd])
        for j in range(8):
            add("dve", lambda e, j=j, g=g: e.scalar_tensor_tensor(out=xT[g][:, j, :], in0=xT[g][:, j, :], scalar=gfin_s[:, j:j + 1], in1=rstd[:],
                                                                  op0=ALU.mult, op1=ALU.mult), [b_xT[g][j], b_rstd, b_prm], [b_xT[g][j]])
    for g in groups:
        for tb in range(4):
            for half in range(2):
                pt, pb = gp()
                for jj in range(4):
                    j = half * 4 + jj
                    add("pe", lambda e, pt=pt, j=j, jj=jj, tb=tb, g=g: e.transpose(out=pt[:, jj * 128:(jj + 1) * 128],
                                                                                   in_=xT[g][:, j, tb * 128:(tb + 1) * 128], identity=ident[:]),
                        [b_xT[g][j], b_ident], [pb])
                stg, bstg = tmp()
                evac(stg[:, :], pt[:, :], [pb], [bstg])
                add("sp", lambda e, g=g, tb=tb, half=half, stg=stg: e.dma_start(
                    out=y_out[g * T + tb * 128:g * T + (tb + 1) * 128, half * 512:(half + 1) * 512], in_=stg[:, :]), [bstg], (), dma=True)
    S.emit(st)
    st.close()
    return nc, dbg_out


def _colT(v, nchunk):
    return np.ascontiguousarray(v.reshape(nchunk, 128).T)


def prepare_inputs(inp):
    f = lambda a: np.ascontiguousarray(np.asarray(a, dtype=np.float32))
    x_prompt, x_sample, c = f(inp["x_prompt"]), f(inp["x_sample"]), f(inp["c"])
    w_in = f(inp["w_in"])
    shared = {}
    shared["bmodT"] = np.ascontiguousarray(np.stack([_colT(f(inp["b_mod"])[l], 48) for l in range(L)], 1))
    shared["gmixT"] = np.ascontiguousarray(np.stack([_colT(f(inp["g_norm_mix"])[l], 8) for l in range(L)], 1))
    shared["gffT"] = np.ascontiguousarray(np.stack([_colT(f(inp["g_norm_ff"])[l], 8) for l in range(L)], 1))
    shared["gfinT"] = _colT(f(inp["g_final"]), 8)
    shared["w_mod"] = f(inp["w_mod"])
    shared["w_in"] = w_in
    idx = []
    for base in (0, 256):
        for h in range(4):
            for m in range(2):
                idx += [base + h * 64 + m * 32 + P32[j] for j in range(32)]
    idx += list(range(1088, 1152))
    idx += [1152 + P32[j] for j in range(32)]
    shared["w_inp"] = np.ascontiguousarray(w_in[:, :, idx])
    shared["w_out"] = f(inp["w_out"])
    shared["w_ff1"] = f(inp["w_ff1"])
    shared["w_ff2"] = f(inp["w_ff2"])
    wq = f(inp["w_mla_qup"])
    shared["wqup"] = wq
    wqp = np.zeros_like(wq)
    for h in range(4):
        for j in range(32):
            wqp[:, :, h * 96 + 64 + j] = wq[:, :, h * 96 + 64 + P32[j]]
    shared["wqupp"] = wqp
    shared["wkvup"] = f(inp["w_mla_kvup"])
    shared["gqT"] = np.ascontiguousarray(np.stack([_colT(f(inp["g_mla_q"])[l], 2) for l in range(L)], 1))
    shared["gkvc"] = np.ascontiguousarray(f(inp["g_mla_kv"]).T)
    shared["gkvr"] = np.ascontiguousarray(np.broadcast_to(f(inp["g_mla_kv"])[None], (128, L, 128)))
    gs = f(inp["g_da_subln"])
    shared["gsubc"] = np.ascontiguousarray(np.concatenate([gs.T, gs.T], 0))
    lamv = np.stack([f(inp["da_lambda_q1"]), f(inp["da_lambda_k1"]), f(inp["da_lambda_q2"]), f(inp["da_lambda_k2"])], 1)
    shared["lamv"] = np.ascontiguousarray(np.broadcast_to(lamv[None], (128, L, 4, 32)))
    dw = f(inp["conv_dw"])
    shared["dwT"] = np.ascontiguousarray(dw.reshape(L, 31, 2, 128).transpose(3, 0, 2, 1))
    for nm, key in (("cbT", "conv_b"), ("lngT", "conv_ln_g"), ("lnbT", "conv_ln_b")):
        shared[nm] = np.ascontiguousarray(f(inp[key]).reshape(L, 2, 128).transpose(2, 0, 1))
    shared["ident"] = np.eye(128, dtype=np.float32)
    w = np.arange(GRID_W)
    cs = np.clip(w - 8, 0, GRID_W - 16)
    col_ok = (w[None, :] >= cs[:, None]) & (w[None, :] < cs[:, None] + 16)
    cm = np.where(col_ok.T, 0.0, -BIG * 8).astype(np.float32)
    shared["colmask"] = np.ascontiguousarray(np.concatenate([cm, cm], 0))
    ri = np.zeros((32, DEC_S), np.float32)
    ri[np.arange(DEC_S) // 64, np.arange(DEC_S)] = 1.0
    shared["rowind"] = ri
    rpb = f(inp["na_rpb"])
    half = 8
    freqs = (10000.0 ** (-np.arange(half, dtype=np.float32) * 2.0 / 16)).astype(np.float32)
    maps = []
    for core in range(NCORE):
        b, r = core // 4, core % 4
        m = dict(shared)
        m["xin"] = np.ascontiguousarray(np.concatenate([x_prompt[2 * core], x_prompt[2 * core + 1], x_sample[b, r * T:(r + 1) * T]], 0))
        cv = np.stack([f(inp["c_ctx"]), c[b]], 1)
        m["cvT"] = np.ascontiguousarray(cv.reshape(8, 128, 2).transpose(1, 0, 2))
        m["c_dak"] = f(inp["cache_da_k"])[b]
        m["c_dav"] = f(inp["cache_da_v"])[b]
        m["c_ckv"] = f(inp["cache_mla_ckv"])[b]
        m["c_kr"] = f(inp["cache_mla_krope"])[b]
        m["c_nak"] = f(inp["cache_na_k"])[b]
        m["c_nav"] = f(inp["cache_na_v"])[b]
        t = r * T + np.arange(T)
        rows = (t // GRID_W).astype(np.float32)
        cols = (t % GRID_W).astype(np.float32)
        ang = np.concatenate([rows[None, :] * freqs[:, None], rows[None, :] * freqs[:, None],
                              cols[None, :] * freqs[:, None], cols[None, :] * freqs[:, None]], 0)
        cos32 = np.cos(ang).astype(np.float32)
        sin32 = np.sin(ang).astype(np.float32)
        sgn = np.concatenate([-np.ones(8), np.ones(8), -np.ones(8), np.ones(8)]).astype(np.float32)[:, None]
        m["cosT"] = np.ascontiguousarray(np.tile(cos32, (4, 1)))
        m["sinT"] = np.ascontiguousarray(np.tile(sin32 * sgn, (4, 1)))
        r0 = r * 8
        qrow = r0 + np.arange(T) // 64
        start = np.clip(qrow - 4, 0, NROWS - 8)
        rk = np.arange(32)
        ok = (rk[:, None] >= start[None, :]) & (rk[:, None] < start[None, :] + 8)
        m["rowsel"] = np.where(ok, 0.0, -BIG * 8).astype(np.float32)
        P = np.zeros((L, 4, A2 + 1, TZL), np.float32)
        for a2 in range(A2):
            a = 45 - a2 - r0
            if 0 <= a <= 14:
                P[:, :, a2, 48:79] = rpb[:, :, a, ::-1]
        m["tz_rep"] = np.ascontiguousarray(np.broadcast_to(P.reshape(L * 4 * (A2 + 1), 1, TZL), (L * 4 * (A2 + 1), 64, TZL)))
        hs = np.zeros((128, 8), np.float32)
        if r > 0:
            hs[:, r - 1] = 1.0
        if r < 3:
            hs[:, 4 + r + 1] = 1.0
        m["halsel"] = hs
        maps.append(m)
    return maps


_NC_CACHE = {}


def kernel(**inputs):
    maps = prepare_inputs(inputs)
    if "nc" not in _NC_CACHE:
        _NC_CACHE["nc"] = build()[0]
    nc = _NC_CACHE["nc"]
    res = run_bass_kernel_spmd(nc, maps, core_ids=list(range(NCORE)))
    R = res.results
    y_prompt = np.zeros((NB, SEQ, D), np.float32)
    y_sample = np.zeros((DEC_B, DEC_S, D), np.float32)
    outs = {k: [] for k in ("o_dak", "o_dav", "o_ckv", "o_kr", "o_nak", "o_nav")}
    for core in range(NCORE):
        b, r = core // 4, core % 4
        y = R[core]["y_out"]
        y_prompt[2 * core] = y[0:256]
        y_prompt[2 * core + 1] = y[256:512]
        y_sample[b, r * T:(r + 1) * T] = y[512:1024]
        for k in outs:
            outs[k].append(R[core][k])
    cat = lambda k: np.ascontiguousarray(np.concatenate(outs[k], 0).astype(np.float32))
    return (y_prompt, y_sample, cat("o_dak"), cat("o_dav"), cat("o_ckv"), cat("o_kr"), cat("o_nak"), cat("o_nav"))
```

```python
import contextlib
import math
import numpy as np
import concourse.bass as bass
import concourse.mybir as mybir
from concourse.bass_utils import run_bass_kernel_spmd

F32 = mybir.dt.float32
BF16 = mybir.dt.bfloat16
ALU = mybir.AluOpType
AF = mybir.ActivationFunctionType

ENGINES = ("pe", "act", "dve", "pool", "sp")

D = 1024
L = 2
NB = 16
SEQ = 256
DEC_B = 2
DEC_S = 2048
PAST = 256
GRID_W = 64
NROWS = DEC_S // GRID_W
EPS = 1e-6
T = 512
NCORE = 8
BIG = 30000.0
A2 = 46
TZL = 127
PA = 672
PB = 527
R_DAK, R_CKV, R_KR, R_NAK = 0, 256, 384, 416
R_V, R_HALO = 0, 512
P32 = [8, 9, 10, 11, 12, 13, 14, 15, 0, 1, 2, 3, 4, 5, 6, 7,
       24, 25, 26, 27, 28, 29, 30, 31, 16, 17, 18, 19, 20, 21, 22, 23]


class Buf:
    __slots__ = ("name", "last_w", "readers", "excl")

    def __init__(self, name, excl=False):
        self.name = name
        self.last_w = None
        self.readers = []
        self.excl = excl


class Op:
    __slots__ = ("eng", "fn", "deps", "dma", "sig", "sem", "val")

    def __init__(self, eng, fn, dma):
        self.eng = eng
        self.fn = fn
        self.dma = dma
        self.deps = []
        self.sig = False
        self.sem = None
        self.val = 0


class Sched:
    def __init__(self, nc, n_dma_slots=16):
        self.nc = nc
        self.ops = []
        self.n_dma_slots = n_dma_slots

    def add(self, eng, fn, reads=(), writes=(), dma=False):
        op = Op(eng, fn, dma)
        ex = [b for b in reads if b.excl]
        if ex:
            reads = [b for b in reads if not b.excl]
            writes = list(writes) + ex
        deps = {}
        for b in reads:
            if b.last_w is not None:
                deps[id(b.last_w)] = b.last_w
        for b in writes:
            if b.last_w is not None:
                deps[id(b.last_w)] = b.last_w
            for r in b.readers:
                deps[id(r)] = r
        for b in reads:
            b.readers.append(op)
        for b in writes:
            b.last_w = op
            b.readers = []
        for d in deps.values():
            if d.eng == "pe" and eng == "pe" and not d.dma and not dma:
                continue
            op.deps.append(d)
            d.sig = True
        self.ops.append(op)
        return op

    def emit(self, stack):
        nc = self.nc
        eng_sem = {e: stack.enter_context(nc.semaphore("s_" + e)) for e in ENGINES}
        dma_engs = sorted({op.eng for op in self.ops if op.dma is True})
        dma_slots = {e: [stack.enter_context(nc.semaphore("d_%s_%d" % (e, i)))
                         for i in range(self.n_dma_slots)] for e in dma_engs}
        cnt = {e: 0 for e in ENGINES}
        dcnt = {e: 0 for e in dma_engs}
        slot_uses = {e: [0] * self.n_dma_slots for e in dma_engs}
        slot_prev = {}
        ncc = 0
        for op in self.ops:
            if op.dma == "cc":
                op.sem = stack.enter_context(nc.semaphore("cc_%d" % ncc))
                ncc += 1
                op.val = 1
            elif op.dma:
                k = dcnt[op.eng] % self.n_dma_slots
                dcnt[op.eng] += 1
                slot_uses[op.eng][k] += 1
                op.sem = dma_slots[op.eng][k]
                op.val = 16 * slot_uses[op.eng][k]
                slot_prev[id(op)] = (op.sem, op.val - 16)
            elif op.sig:
                cnt[op.eng] += 1
                op.sem = eng_sem[op.eng]
                op.val = cnt[op.eng]
        block = stack.enter_context(nc.Block())
        handles = {"pe": block.tensor, "act": block.scalar, "dve": block.vector,
                   "pool": block.gpsimd, "sp": block.sync}
        all_async = [op for op in self.ops if op.dma]

        def run_engine(ename):
            def body(eng):
                waited = {}
                for op in self.ops:
                    if op.eng != ename:
                        continue
                    need = {}
                    for d in op.deps:
                        key = id(d.sem)
                        if key not in need or need[key][1] < d.val:
                            need[key] = (d.sem, d.val)
                    if op.dma is True:
                        s, v = slot_prev[id(op)]
                        if v > 0:
                            key = id(s)
                            if key not in need or need[key][1] < v:
                                need[key] = (s, v)
                    for key, (s, v) in need.items():
                        if waited.get(key, 0) >= v:
                            continue
                        eng.wait_ge(s, v)
                        waited[key] = v
                    ins = op.fn(eng)
                    if op.dma == "cc":
                        ins.then_inc(op.sem)
                    elif op.dma:
                        ins.then_inc(op.sem, 16)
                    elif op.sig:
                        ins.then_inc(op.sem, 1)
                if ename == "sp":
                    last = {}
                    for op in all_async:
                        last[id(op.sem)] = (op.sem, op.val)
                    for key, (s, v) in last.items():
                        if waited.get(key, 0) < v:
                            eng.wait_ge(s, v)
                    for e in ENGINES:
                        if cnt[e] > 0:
                            eng.wait_ge(eng_sem[e], cnt[e])
            handles[ename](body)

        for e in ENGINES:
            run_engine(e)


def build(dbg=(), nlayers=L, do_lat=True, stages=None):
    if stages is None:
        stages = {"mod", "norm1", "proj", "ctxattn", "conv", "latattn", "outproj", "norm2", "ffn", "final"}
    nc = bass.Bass("TRN2", target_bir_lowering=False)
    st = contextlib.ExitStack()
    S = Sched(nc)
    dbg_out = {}

    def din(name, shape, dt=F32):
        return nc.dram_tensor(name, list(shape), dt, kind="ExternalInput").ap()

    def dout(name, shape, dt=F32):
        return nc.dram_tensor(name, list(shape), dt, kind="ExternalOutput").ap()

    xin = din("xin", [2 * T, D])
    cvT = din("cvT", [128, 8, 2])
    bmodT = din("bmodT", [128, L, 48])
    gmixT = din("gmixT", [128, L, 8])
    gffT = din("gffT", [128, L, 8])
    gfinT = din("gfinT", [128, 8])
    w_mod = din("w_mod", [L, D, 6 * D])
    w_in = din("w_in", [L, D, 2464])
    w_inp = din("w_inp", [L, D, 608])
    w_out = din("w_out", [L, D, D])
    w_ff1 = din("w_ff1", [L, D, 4 * D])
    w_ff2 = din("w_ff2", [L, 4 * D, D])
    wqup = din("wqup", [L, 256, 384])
    wqupp = din("wqupp", [L, 256, 384])
    wkvup = din("wkvup", [L, 128, 512])
    gqT = din("gqT", [128, L, 2])
    gkvc = din("gkvc", [128, L])
    gkvr = din("gkvr", [128, L, 128])
    gsubc = din("gsubc", [128, L])
    lamv = din("lamv", [128, L, 4, 32])
    tz_all = din("tz_rep", [L * 4 * (A2 + 1), 64, TZL]).tensor
    dwT = din("dwT", [128, L, 2, 31])
    cbT = din("cbT", [128, L, 2])
    lngT = din("lngT", [128, L, 2])
    lnbT = din("lnbT", [128, L, 2])
    c_dak = din("c_dak", [L, 4, PAST, 64])
    c_dav = din("c_dav", [L, 4, PAST, 64])
    c_ckv = din("c_ckv", [L, PAST, 128])
    c_kr = din("c_kr", [L, PAST, 32])
    c_nak = din("c_nak", [L, 4, PAST, 64])
    c_nav = din("c_nav", [L, 4, PAST, 64])
    cosT_d = din("cosT", [128, T])
    sinT_d = din("sinT", [128, T])
    colmask_d = din("colmask", [128, 64])
    rowsel_d = din("rowsel", [32, T])
    rowind_d = din("rowind", [32, DEC_S])
    halsel_d = din("halsel", [128, 8])
    ident_d = din("ident", [128, 128])
    y_out = dout("y_out", [2 * T, D])
    o_dak = dout("o_dak", [2, L, 4, SEQ, 64])
    o_dav = dout("o_dav", [2, L, 4, SEQ, 64])
    o_ckv = dout("o_ckv", [2, L, SEQ, 128])
    o_kr = dout("o_kr", [2, L, SEQ, 32])
    o_nak = dout("o_nak", [2, L, 4, SEQ, 64])
    o_nav = dout("o_nav", [2, L, 4, SEQ, 64])
    LROWS = 6 * PA + 6 * PB
    pay_all = nc.dram_tensor("pay_all", [L * LROWS, T], BF16)
    O_AIN, O_AOUT, O_BIN, O_BOUT = 0, PA, 5 * PA, 5 * PA + PB

    class _Sub:
        def __init__(self, row0, nrows):
            self.row0, self.nrows = row0, nrows

        def ap(self):
            return pay_all.ap()[self.row0:self.row0 + self.nrows, :]

    payA_in = [_Sub(l * LROWS + O_AIN, PA) for l in range(L)]
    payA_out = [_Sub(l * LROWS + O_AOUT, 4 * PA) for l in range(L)]
    payB_in = [_Sub(l * LROWS + O_BIN, PB) for l in range(L)]
    payB_out = [_Sub(l * LROWS + O_BOUT, 4 * PB) for l in range(L)]
    TZ_SZ = 4 * (A2 + 1) * 64 * TZL

    def dram_ap(base, off, dims):
        if isinstance(base, _Sub):
            return bass.AP(pay_all, base.row0 * T + off, dims)
        return bass.AP(base, off, dims)

    def sb(name, shape, dt):
        return st.enter_context(nc.sbuf_tensor("sb_" + name, list(shape), dt))

    add = S.add

    banks = []
    for i in range(8):
        banks.append((st.enter_context(nc.psum_tensor("ps%d" % i, [128, 512], F32)), Buf("ps%d" % i, excl=True)))
    gp_i = [0]
    ac_i = [0]

    def gp():
        b = banks[(0, 1, 2, 7)[gp_i[0] % 4]]
        gp_i[0] += 1
        return b

    def acb():
        b = banks[4 + ac_i[0] % 3]
        ac_i[0] += 1
        return b

    ev_i = [0]

    def evac(dst, src, r, w, scale=None):
        ev_i[0] += 1
        if ev_i[0] % 2 == 0:
            if scale is None:
                add("act", lambda e: e.activation(out=dst, in_=src, func=AF.Copy), r, w)
            else:
                add("act", lambda e: e.activation(out=dst, in_=src, func=AF.Identity, scale=scale), r, w)
        else:
            if scale is None:
                add("dve", lambda e: e.tensor_copy(out=dst, in_=src), r, w)
            else:
                add("dve", lambda e: e.tensor_scalar(out=dst, in0=src, scalar1=scale, scalar2=None, op0=ALU.mult), r, w)

    ident = sb("ident", [128, 128], F32); b_ident = Buf("ident")
    identb = sb("identb", [128, 128], BF16); b_identb = Buf("identb")
    ones_f = sb("ones_f", [128, 128], F32); b_ones_f = Buf("ones_f")
    ones_b = sb("ones_b", [128, 128], BF16); b_ones_b = Buf("ones_b")
    epsc = sb("epsc", [128, 1], F32); b_epsc = Buf("epsc")
    add("sp", lambda e: e.dma_start(out=ident[:], in_=ident_d), (), [b_ident], dma=True)
    add("pool", lambda e: e.memset(ones_f[:], 1.0), (), [b_ones_f])
    add("pool", lambda e: e.memset(ones_b[:], 1.0), (), [b_ones_b])
    add("pool", lambda e: e.memset(epsc[:], EPS), (), [b_epsc])
    add("dve", lambda e: e.tensor_copy(out=identb[:], in_=ident[:]), [b_ident], [b_identb])

    prm = {}
    b_prm = Buf("prm")
    b_small = []

    def load_small(name, src, shape):
        t = sb(name, shape, F32)
        bt = Buf("ld_" + name)
        b_small.append(bt)
        add("sp", lambda e: e.dma_start(out=t[:], in_=src), (), [bt], dma=True)
        prm[name] = t
        return t

    cv_s = load_small("cv_s", cvT, [128, 8, 2])
    bmod_s = load_small("bmod_s", bmodT, [128, L, 48])
    gmix_s = load_small("gmix_s", gmixT, [128, L, 8])
    gff_s = load_small("gff_s", gffT, [128, L, 8])
    gfin_s = load_small("gfin_s", gfinT, [128, 8])
    gq_s = load_small("gq_s", gqT, [128, L, 2])
    gkvc_s = load_small("gkvc_s", gkvc, [128, L])
    gkvr_s = sb("gkvr_s", [128, L, 128], BF16)
    b_small.append(Buf("ld_gkvr"))
    add("pool", lambda e: e.dma_start(out=gkvr_s[:], in_=gkvr), (), [b_small[-1]], dma=True)
    gsub_s = load_small("gsub_s", gsubc, [128, L])
    lam_s = load_small("lam_s", lamv, [128, L, 4, 32])
    dw_s = load_small("dw_s", dwT, [128, L, 2, 31])
    cb_s = load_small("cb_s", cbT, [128, L, 2])
    lng_s = load_small("lng_s", lngT, [128, L, 2])
    lnb_s = load_small("lnb_s", lnbT, [128, L, 2])
    cos_s = load_small("cos_s", cosT_d, [128, T])
    sin_s = load_small("sin_s", sinT_d, [128, T])
    colm_s = load_small("colm_s", colmask_d, [128, 64])
    halsel_s = load_small("halsel_s", halsel_d, [128, 8])
    joinc = sb("joinc", [128, 1], F32)
    add("dve", lambda e: e.memset(joinc[:], 0.0), list(b_small), [b_prm])

    lams = sb("lams", [128, L, 2], F32)
    neglam = sb("neglam", [128, L], F32)
    gsub2 = sb("gsub2", [128, L], F32)
    for l in range(L):
        lam_init = 0.8 - 0.6 * math.exp(-0.3 * l)
        for m in range(2):
            add("dve", lambda e, l=l, m=m: e.tensor_tensor(out=lam_s[:, l, 2 * m, :], in0=lam_s[:, l, 2 * m, :],
                                                          in1=lam_s[:, l, 2 * m + 1, :], op=ALU.mult), [b_prm], [b_prm])
            add("dve", lambda e, l=l, m=m: e.reduce_sum(out=lams[:, l, m:m + 1], in_=lam_s[:, l, 2 * m, :],
                                                       axis=mybir.AxisListType.X), [b_prm], [b_prm])
        add("act", lambda e, l=l: e.activation(out=lams[:, l, :], in_=lams[:, l, :], func=AF.Exp), [b_prm], [b_prm])
        add("dve", lambda e, l=l: e.tensor_tensor(out=neglam[:, l:l + 1], in0=lams[:, l, 1:2], in1=lams[:, l, 0:1],
                                                 op=ALU.subtract), [b_prm], [b_prm])
        add("dve", lambda e, l=l, li=lam_init: e.tensor_scalar(out=neglam[:, l:l + 1], in0=neglam[:, l:l + 1],
                                                              scalar1=-li, scalar2=None, op0=ALU.add), [b_prm], [b_prm])
        add("dve", lambda e, l=l, li=lam_init: e.tensor_scalar(out=gsub2[:, l:l + 1], in0=gsub_s[:, l:l + 1],
                                                              scalar1=1.0 - li, scalar2=None, op0=ALU.mult), [b_prm], [b_prm])

    xT = [sb("xT%d" % g, [128, 8, T], F32) for g in range(2)]
    b_xT = [[Buf("xT%d_%d" % (g, j)) for j in range(8)] for g in range(2)]
    hT = [sb("hT%d" % g, [128, 8, T], BF16) for g in range(2)]
    b_hT = [Buf("hT%d" % g) for g in range(2)]
    mixT = [sb("mixT%d" % g, [128, 8, T], BF16) for g in range(2)]
    b_mix = [[Buf("mix%d_%d" % (g, j)) for j in range(8)] for g in range(2)]
    NW = 3
    wpool = [sb("wp%d" % i, [128, 8, 512], BF16) for i in range(NW)]
    b_wp = [Buf("wp%d" % i) for i in range(NW)]
    wp_i = [0]

    def wtile(src_ap, ncols, nk=8):
        i = wp_i[0] % NW
        wp_i[0] += 1
        t, b = wpool[i], b_wp[i]
        add("pool", lambda e: e.dma_start(out=t[:, 0:nk, 0:ncols], in_=src_ap.rearrange("(k p) c -> p k c", p=128)),
            (), [b], dma=True)
        return t, b

    qd = sb("qd", [128, 2, T], F32); b_qd = Buf("qd_cacc_stage")
    stage1 = qd[:, :, :].rearrange("p a b -> p (a b)")

    class _Stage:
        def __getitem__(self, key):
            p, sl, c = key
            return stage1[p, c]
    stage = _Stage()
    b_stage = [b_qd, b_qd]
    rstd = sb("rstd", [128, T], F32); b_rstd = Buf("rstd")
    tmpf = [sb("tmpf%d" % i, [128, T], F32) for i in range(4)]
    b_tmpf = [Buf("tmpf%d" % i) for i in range(4)]
    tf_i = [0]

    def tmp():
        i = tf_i[0] % 4
        tf_i[0] += 1
        return tmpf[i], b_tmpf[i]

    sqb = [sb("sqb%d" % i, [128, T], BF16) for i in range(2)]
    b_sqb = [Buf("sqb%d" % i) for i in range(2)]
    sq_i = [0]

    def sqt():
        i = sq_i[0] % 2
        sq_i[0] += 1
        return sqb[i], b_sqb[i]

    for g in range(2):
        for tb in range(4):
            for half in range(2):
                stg, bstg = tmp()
                add("sp", lambda e, g=g, tb=tb, half=half, stg=stg: e.dma_start(
                    out=stg[:, :], in_=xin[g * T + tb * 128: g * T + (tb + 1) * 128, half * 512:(half + 1) * 512]), (), [bstg], dma=True)
                pt, pb = gp()
                for jj in range(4):
                    add("pe", lambda e, pt=pt, stg=stg, jj=jj: e.transpose(out=pt[:, jj * 128:(jj + 1) * 128],
                                                                           in_=stg[:, jj * 128:(jj + 1) * 128], identity=ident[:]),
                        [bstg, b_ident], [pb])
                evac(xT[g][:, half * 4:half * 4 + 4, tb * 128:(tb + 1) * 128],
                     pt[:, :].rearrange("p (j t) -> p j t", j=4), [pb], [b_xT[g][half * 4 + jj] for jj in range(4)])

    sil = sb("sil", [128, 8, 2], BF16); b_sil = Buf("sil")
    add("act", lambda e: e.activation(out=sil[:], in_=cv_s[:], func=AF.Silu), [b_prm], [b_sil])
    modv = sb("modv", [128, L, 48, 2], F32)
    gsc = sb("gsc", [128, L, 2, 8, 2], F32)
    b_mod = [Buf("mod%d" % l) for l in range(L)]

    def modulation(l):
        pt, pb = banks[3]
        for ti in range(12):
            wt, wb = wtile(w_mod[l, :, ti * 512:(ti + 1) * 512], 512)
            for cc in range(4):
                n = ti * 4 + cc
                for k in range(8):
                    add("pe", lambda e, wt=wt, cc=cc, k=k, n=n, pt=pt: e.matmul(pt[:, n * 2:n * 2 + 2], lhsT=wt[:, k, cc * 128:(cc + 1) * 128],
                                                                             rhs=sil[:, k, :], start=(k == 0), stop=(k == 7)),
                        [wb, b_sil], [pb])
            yield
        add("dve", lambda e, pt=pt: e.tensor_tensor(out=modv[:, l, :, :], in0=pt[:, 0:96].rearrange("p (n v) -> p n v", v=2),
                                                   in1=bmod_s[:, l, :].unsqueeze(2).to_broadcast([128, 48, 2]), op=ALU.add),
            [pb, b_prm], [b_mod[l]])
        for ni, (gsrc, off) in enumerate(((gmix_s, 8), (gff_s, 32))):
            add("dve", lambda e, ni=ni, gsrc=gsrc, off=off: e.scalar_tensor_tensor(
                out=gsc[:, l, ni, :, :], in0=modv[:, l, off:off + 8, :], scalar=1.0,
                in1=gsrc[:, l, :].unsqueeze(2).to_broadcast([128, 8, 2]), op0=ALU.add, op1=ALU.mult),
                [b_mod[l], b_prm], [b_mod[l]])

    def mcol(l, which, j, v):
        off = {"sh1": 0, "sc1": 8, "g1": 16, "sh2": 24, "sc2": 32, "g2": 40}[which]
        return modv[:, l, off + j, v:v + 1]

    def rmsnorm_mod(l, g, ni):
        pt, pb = gp()
        for j in range(8):
            sq, bq = sqt()
            add("act", lambda e, sq=sq, j=j: e.activation(out=sq[:], in_=xT[g][:, j, :], func=AF.Square), [b_xT[g][j]], [bq])
            add("pe", lambda e, sq=sq, j=j, pt=pt: e.matmul(pt[:, :], lhsT=ones_b[:, :], rhs=sq[:], start=(j == 0), stop=(j == 7)),
                [bq, b_ones_b], [pb])
        add("act", lambda e, pt=pt: e.activation(out=rstd[:], in_=pt[:, :], func=AF.Ln, bias=epsc[:, 0:1], scale=1.0 / D),
            [pb, b_epsc], [b_rstd])
        add("act", lambda e: e.activation(out=rstd[:], in_=rstd[:], func=AF.Exp, scale=-0.5), [b_rstd], [b_rstd])
        for j in range(8):
            tt, tb_ = tmp()
            add("dve", lambda e, tt=tt, j=j: e.scalar_tensor_tensor(out=tt[:], in0=xT[g][:, j, :], scalar=gsc[:, l, ni, j, g:g + 1],
                                                                    in1=rstd[:], op0=ALU.mult, op1=ALU.mult),
                [b_xT[g][j], b_rstd, b_mod[l]], [tb_])
            add("act", lambda e, tt=tt, j=j: e.activation(out=hT[g][:, j, :], in_=tt[:], func=AF.Identity,
                                                          bias=mcol(l, "sh1" if ni == 0 else "sh2", j, g), scale=1.0),
                [tb_, b_mod[l]], [b_hT[g]])

    def stat_rstd(src_list, nfeat, dst, b_dst, reads, K=128, n=T):
        pt, pb = gp()
        for i, src in enumerate(src_list):
            tt, tb_ = tmp()
            add("act", lambda e, tt=tt, src=src: e.activation(out=tt[0:K, 0:n], in_=src, func=AF.Square), reads, [tb_])
            add("pe", lambda e, tt=tt, i=i, pt=pt: e.matmul(pt[0:K, 0:n], lhsT=ones_f[0:K, 0:K], rhs=tt[0:K, 0:n],
                                                          start=(i == 0), stop=(i == len(src_list) - 1)),
                [tb_, b_ones_f], [pb])
        add("act", lambda e, pt=pt: e.activation(out=dst, in_=pt[0:K, 0:n], func=AF.Ln, bias=epsc[0:K, 0:1], scale=1.0 / nfeat),
            [pb, b_epsc], [b_dst])
        add("act", lambda e: e.activation(out=dst, in_=dst, func=AF.Exp, scale=-0.5), [b_dst], [b_dst])

    KTb = sb("KTb", [128, 4, DEC_S + PAST], BF16); b_KTh = [Buf("KT%d" % h) for h in range(4)]
    VAb = sb("VAb", [128, 18, 4, 72], BF16); b_VAk = [Buf("VA%d" % k) for k in range(18)]
    KTc2 = [sb("KTc%d" % m, [128, 4, T], BF16) for m in range(2)]
    b_KTc = [Buf("KTc%d" % m) for m in range(3)]
    VAc = [sb("VAc%d" % m, [128, 4, 4, 72], BF16) for m in range(3)]; b_VAc = [Buf("VAc%d" % m) for m in range(3)]
    QT2 = [[sb("QT%d_%d" % (g, m), [128, 4, T], BF16) for m in range(2)] for g in range(2)]
    b_QT = [[Buf("QT%d_%d" % (g, m)) for m in range(3)] for g in range(2)]
    add("pool", lambda e: e.memset(VAb[:, :, :, 64:72], 1.0), (), b_VAk)
    for m in range(3):
        add("pool", lambda e, m=m: e.memset(VAc[m][:, :, :, 64:72], 1.0), (), [b_VAc[m]])

    def q_ap(g, m, h, p_lo, p_hi, c0=0, n=T):
        if m == 0:
            return QT2[g][0][p_lo:p_hi, h, c0:c0 + n]
        if m == 1:
            return QT2[g][1][p_lo:p_hi, h, c0:c0 + n]
        return QT2[g][0][64 + p_lo:64 + p_hi, h, c0:c0 + n]

    def kc_ap(m, h, p_lo, p_hi, c0, n):
        if m == 0:
            return KTc2[0][p_lo:p_hi, h, c0:c0 + n]
        if m == 1:
            return KTc2[1][p_lo:p_hi, h, c0:c0 + n]
        return KTc2[0][64 + p_lo:64 + p_hi, h, c0:c0 + n]
    Eb = [sb("E%d" % i, [128, T], BF16) for i in range(3)]
    b_E = [Buf("E%d" % i) for i in range(3)]
    e_i = [0]
    NSET = 2
    rsum_s = [sb("rsum0", [128, T], F32)] * 2; b_rsum_s = [Buf("rsum0")] * 2
    rsb0 = sb("rsb0", [128, T], BF16)
    rsb_row = [64, 64]
    b_rsb_s = [Buf("rsb0")] * 2
    dao = sb("dao", [128, T], F32); b_dao = Buf("dao")
    set_i = [0]
    ckvT = sb("ckvT", [128, DEC_S + PAST], BF16); b_ckvT = Buf("ckvT")
    ckvTc = ckvT; b_ckvTc = b_ckvT
    qdn = sb("qdn", [128, 2, T], BF16); b_qdn = Buf("qdn")
    wq_s = sb("wq_s", [128, 2, 384], BF16); wqp_s = sb("wqp_s", [128, 2, 384], BF16); wkv_s = sb("wkv_s", [128, 512], BF16)
    b_wsm = Buf("wsmall")
    gpad = [sb("gpad%d" % g, [128, 2, 2, 15 + 256 + 15], BF16) for g in range(2)]
    b_gpad = [Buf("gpad%d" % g) for g in range(2)]
    for g in range(2):
        add("pool", lambda e, g=g: e.memset(gpad[g][:], 0.0), (), [b_gpad[g]])
    cacc = qd; b_cacc = b_qd
    halo = sb("halo", [128, 2, 4, 30], BF16); b_halo = Buf("halo")
    NTBT = 2
    tbt = [sb("tbt%d" % i, [128, A2, 64], BF16) for i in range(NTBT)] * (2 // NTBT)
    b_tbt = [Buf("tbt%d" % i) for i in range(NTBT)] * (2 // NTBT)
    b_payA = [Buf("payA%d" % l) for l in range(L)]
    b_payB = [Buf("payB%d" % l) for l in range(L)]
    b_payoA = [Buf("payoA%d" % l) for l in range(L)]
    b_payoB = [Buf("payoB%d" % l) for l in range(L)]
    b_tz = [Buf("tz%d" % l) for l in range(L)]

    pending_post = [None]
    import os as _os2
    WARM_LAT = int(_os2.environ.get("WARM_LAT", "0"))
    WARM_CTX = int(_os2.environ.get("WARM_CTX", "0"))
    WARM_N = int(_os2.environ.get("WARM_N", "256"))
    BURST = int(_os2.environ.get("BURST", "20"))

    def warm(n):
        for _ in range(n):
            add("pe", lambda e: e.matmul(banks[7][0][:, 512 - WARM_N:512], lhsT=identb[:, 0:128], rhs=identb[:, 0:128], start=True, stop=True), (), ())

    def flush_post():
        if pending_post[0] is not None:
            f = pending_post[0]
            pending_post[0] = None
            f()

    def attention(kt_fn, q_ap, va_fn, nkb, nq, scale, reads, extra_fn=None):
        at, ab = acb()
        pend = []

        def score(kb):
            pt, pb = gp()
            ex = extra_fn(kb) if extra_fn is not None else []
            add("pe", lambda e, pt=pt, kb=kb: e.matmul(pt[:, 0:nq], lhsT=kt_fn(kb), rhs=q_ap, start=True, stop=(len(ex) == 0)),
                reads, [pb])
            for i, (lh, rh, rd) in enumerate(ex):
                add("pe", lambda e, pt=pt, lh=lh, rh=rh, i=i: e.matmul(pt[:, 0:nq], lhsT=lh, rhs=rh, start=False, stop=(i == len(ex) - 1)),
                    rd, [pb])
            i = e_i[0] % 3
            e_i[0] += 1
            add("act", lambda e, pt=pt, i=i: e.activation(out=Eb[i][:, 0:nq], in_=pt[:, 0:nq], func=AF.Exp, scale=scale), [pb], [b_E[i]])
            return i

        for kb in range(min(2, nkb)):
            pend.append(score(kb))
        flush_post()
        for kb in range(nkb):
            if kb + 2 < nkb:
                pend.append(score(kb + 2))
            warm(WARM_LAT)
            i = pend[kb]
            add("pe", lambda e, at=at, kb=kb, i=i: e.matmul(at[0:65, 0:nq], lhsT=va_fn(kb), rhs=Eb[i][:, 0:nq], start=(kb == 0), stop=(kb == nkb - 1)),
                reads + [b_E[i]], [ab])
        return at, ab

    def normalize_rep(at, ab, nq, dst, b_dst_list):
        r, br = tmp()
        add("act", lambda e: e.activation(out=r[0:64, 0:nq], in_=at[0:64, 256:256 + nq], func=AF.Ln), [ab], [br])
        add("act", lambda e: e.activation(out=r[0:64, 0:nq], in_=r[0:64, 0:nq], func=AF.Exp, scale=-1.0), [br], [br])
        add("dve", lambda e: e.tensor_tensor(out=dst, in0=at[0:64, 0:nq], in1=r[0:64, 0:nq], op=ALU.mult), [ab, br], b_dst_list)

    def normalize(at, ab, nq, dst, b_dst_list, k=0):
        rsum, b_rsum, b_rsb, rr = rsum_s[k], b_rsum_s[k], b_rsb_s[k], rsb_row[k]
        add("act", lambda e: e.activation(out=rsum[64:65, 0:nq], in_=at[64:65, 0:nq], func=AF.Ln), [ab], [b_rsum])
        add("act", lambda e: e.activation(out=rsb0[rr:rr + 1, 0:nq], in_=rsum[64:65, 0:nq], func=AF.Exp, scale=-1.0), [b_rsum], [b_rsb])
        pt, pb = gp()
        add("pe", lambda e: e.matmul(pt[0:64, 0:nq], lhsT=ones_b[rr:rr + 1, 0:64], rhs=rsb0[rr:rr + 1, 0:nq], start=True, stop=True),
            [b_rsb, b_ones_b], [pb])
        bc, bbc = tmp()
        add("act", lambda e: e.activation(out=bc[0:64, 0:nq], in_=pt[0:64, 0:nq], func=AF.Copy), [pb], [bbc])
        add("dve", lambda e: e.tensor_tensor(out=dst, in0=at[0:64, 0:nq], in1=bc[0:64, 0:nq], op=ALU.mult), [ab, bbc], b_dst_list)

    def stream_attention(jobs, nkb=18, nq=T, side_cb=None, burst=0):
        blocks = [(ji, kb) for ji in range(len(jobs)) for kb in range(nkb)]
        N = len(blocks)
        pend = {}
        posts = {}

        def do_score(idx):
            ji, kb = blocks[idx]
            J = jobs[ji]
            if kb == 0 and J.get("pre") is not None:
                J["pre"]()
            pt, pb = gp()
            ex = J["extra_fn"](kb) if J.get("extra_fn") is not None else []
            add("pe", lambda e, pt=pt, kb=kb, J=J: e.matmul(pt[:, 0:nq], lhsT=J["kt_fn"](kb), rhs=J["q_ap"], start=True, stop=(len(ex) == 0)),
                J["reads"], [pb])
            for i2, (lh, rh, rd) in enumerate(ex):
                add("pe", lambda e, pt=pt, lh=lh, rh=rh, i2=i2: e.matmul(pt[:, 0:nq], lhsT=lh, rhs=rh, start=False, stop=(i2 == len(ex) - 1)),
                    rd, [pb])
            i = e_i[0] % 3
            e_i[0] += 1
            add("act", lambda e, pt=pt, i=i, J=J: e.activation(out=Eb[i][:, 0:nq], in_=pt[:, 0:nq], func=AF.Exp, scale=J["scale"]), [pb], [b_E[i]])
            pend[idx] = i

        for idx in range(min(2, N)):
            do_score(idx)
        if burst:
            wpt, wpb = gp()
            for _ in range(burst):
                add("pe", lambda e, wpt=wpt: e.matmul(wpt[:, :], lhsT=identb[:, :], rhs=hT[1][:, 0, :], start=True, stop=True),
                    [b_identb, b_hT[1]], [wpb])
        for idx in range(N):
            if idx + 2 < N:
                do_score(idx + 2)
            ji, kb = blocks[idx]
            J = jobs[ji]
            if kb == 0:
                J["acc"] = acb()
            at, ab = J["acc"]
            i = pend.pop(idx)
            add("pe", lambda e, at=at, kb=kb, i=i, J=J: e.matmul(at[0:65, 0:nq], lhsT=J["va_fn"](kb), rhs=Eb[i][:, 0:nq],
                                                              start=(kb == 0), stop=(kb == nkb - 1)), J["reads"] + [b_E[i]], [ab])
            if kb == nkb - 1:
                if J.get("post") is not None:
                    posts.setdefault(min(idx + 2, N - 1), []).append(J["post"])
                if side_cb is not None:
                    side_cb()
            for f in posts.pop(idx, []):
                f()

    def input_proj(l):
        res = {}
        add("pool", lambda e: e.dma_start(out=wq_s[:], in_=wqup[l].rearrange("(k p) c -> p k c", p=128)), (), [b_wsm], dma=True)
        add("pool", lambda e: e.dma_start(out=wqp_s[:], in_=wqupp[l].rearrange("(k p) c -> p k c", p=128)), (), [b_wsm], dma=True)
        add("pool", lambda e: e.dma_start(out=wkv_s[:], in_=wkvup[l]), (), [b_wsm], dma=True)

        def fm(wt, wb, c0, M, g, n0=0, n=T):
            pt, pb = gp()
            for k in range(8):
                add("pe", lambda e, pt=pt, k=k: e.matmul(pt[0:M, 0:n], lhsT=wt[:, k, c0:c0 + M], rhs=hT[g][:, k, n0:n0 + n],
                                                        start=(k == 0), stop=(k == 7)), [wb, b_hT[g]], [pb])
            return pt, pb

        def tm(wt, wb, c0, N, g, tb):
            pt, pb = gp()
            for k in range(8):
                add("pe", lambda e, pt=pt, k=k: e.matmul(pt[:, 0:N], lhsT=hT[g][:, k, tb * 128:(tb + 1) * 128], rhs=wt[:, k, c0:c0 + N],
                                                        start=(k == 0), stop=(k == 7)), [wb, b_hT[g]], [pb])
            return pt, pb

        def rope_evac(ptA, pbA, ptB, pbB, p0, p1, dst, wlist):
            t1, tb1 = tmp()
            t2, tb2 = tmp()
            add("dve", lambda e: e.tensor_tensor(out=t1[p0:p1, :], in0=ptA[p0:p1, :], in1=cos_s[p0:p1, :], op=ALU.mult), [pbA, b_prm], [tb1])
            add("dve", lambda e: e.tensor_tensor(out=t2[p0:p1, :], in0=ptB[p0:p1, :], in1=sin_s[p0:p1, :], op=ALU.mult), [pbB, b_prm], [tb2])
            add("dve", lambda e: e.tensor_tensor(out=dst, in0=t1[p0:p1, :], in1=t2[p0:p1, :], op=ALU.add), [tb1, tb2], wlist)

        groups = [0, 1] if do_lat else [0]
        wA, bA = wtile(w_in[l, :, 0:512], 512)
        if do_lat:
            wF1, bF1 = wtile(w_inp[l, :, 0:512], 512)
        for h in range(4):
            pt, pb = fm(wA, bA, h * 64, 64, 0)
            evac(QT2[0][0][0:64, h, :], pt[0:64, :], [pb], [b_QT[0][0]])
            pt, pb = fm(wA, bA, 256 + h * 64, 64, 0)
            evac(KTc2[0][0:64, h, :], pt[0:64, :], [pb], [b_KTc[0]])
        if do_lat:
            for h in range(4):
                ptA, pbA = fm(wA, bA, h * 64, 64, 1)
                ptB, pbB = fm(wF1, bF1, h * 64, 64, 1)
                rope_evac(ptA, pbA, ptB, pbB, 0, 64, QT2[1][0][0:64, h, :], [b_QT[1][0]])
                ptA, pbA = fm(wA, bA, 256 + h * 64, 64, 1)
                ptB, pbB = fm(wF1, bF1, 256 + h * 64, 64, 1)
                tk, tkb = sqt()
                rope_evac(ptA, pbA, ptB, pbB, 0, 64, tk[0:64, :], [tkb])
                add("sp", lambda e, tk=tk, h=h: e.dma_start(out=payA_in[l].ap()[R_DAK + h * 64:R_DAK + (h + 1) * 64, :], in_=tk[0:64, :]),
                    [tkb], [b_payA[l]], dma=True)
        wB, bB = wtile(w_in[l, :, 512:1024], 512)
        for g in groups:
            for c in range(2):
                pt, pb = fm(wB, bB, 256 + c * 128, 128, g)
                evac(qd[:, c, :], pt[:, :], [pb], [b_qd])
            stat_rstd([qd[:, 0, :], qd[:, 1, :]], 256, rstd[:], b_rstd, [b_qd])
            for c in range(2):
                add("dve", lambda e, c=c: e.scalar_tensor_tensor(out=qdn[:, c, :], in0=qd[:, c, :], scalar=gq_s[:, l, c:c + 1], in1=rstd[:],
                                                               op0=ALU.mult, op1=ALU.mult), [b_qd, b_rstd, b_prm], [b_qdn])
            for h in range(4):
                pt, pb = gp()
                for c in range(2):
                    add("pe", lambda e, pt=pt, c=c, h=h: e.matmul(pt[0:96, :], lhsT=wq_s[:, c, h * 96:(h + 1) * 96], rhs=qdn[:, c, :],
                                                                 start=(c == 0), stop=(c == 1)), [b_wsm, b_qdn], [pb])
                if g == 0:
                    evac(QT2[0][1][0:96, h, :], pt[0:96, :], [pb], [b_QT[0][1]])
                else:
                    pt2, pb2 = gp()
                    for c in range(2):
                        add("pe", lambda e, pt2=pt2, c=c, h=h: e.matmul(pt2[0:96, :], lhsT=wqp_s[:, c, h * 96:(h + 1) * 96], rhs=qdn[:, c, :],
                                                                       start=(c == 0), stop=(c == 1)), [b_wsm, b_qdn], [pb2])
                    evac(QT2[1][1][0:64, h, :], pt[0:64, :], [pb], [b_QT[1][1]])
                    rope_evac(pt, pb, pt2, pb2, 64, 96, QT2[1][1][64:96, h, :], [b_QT[1][1]])
        for g in groups:
            for tb in range(4):
                pt, pb = tm(wB, bB, 0, 256, g, tb)
                if g == 0:
                    ostage, b_ostage = tmp()
                    add("act", lambda e, pt=pt, ostage=ostage: e.activation(out=ostage[:, 0:256], in_=pt[:, 0:256], func=AF.Copy), [pb], [b_ostage])
                    s, tl = tb // 2, (tb % 2) * 128
                    add("sp", lambda e, s=s, tl=tl, ostage=ostage: e.dma_start(out=o_dav[s, l, :, tl:tl + 128, :].rearrange("h t d -> t h d"),
                                                               in_=ostage[:, 0:256].rearrange("p (h d) -> p h d", h=4)), [b_ostage], (), dma=True)
                    add("dve", lambda e, pt=pt, tb=tb: e.tensor_copy(out=VAc[0][:, tb, :, 0:64], in_=pt[:, 0:256].rearrange("p (h d) -> p h d", h=4)),
                        [pb], [b_VAc[0]])
                else:
                    tv, tvb = sqt()
                    evac(tv[:, 0:256], pt[:, 0:256], [pb], [tvb])
                    add("sp", lambda e, tv=tv, tb=tb: e.dma_start(out=payB_in[l].ap()[R_V + tb * 128:R_V + (tb + 1) * 128, 0:256], in_=tv[:, 0:256]),
                        [tvb], [b_payB[l]], dma=True)
        wC, bC = wtile(w_in[l, :, 1024:1440], 416)
        if do_lat:
            wF2, bF2 = wtile(w_inp[l, :, 512:608], 96)
        for g in groups:
            pt, pb = fm(wC, bC, 0, 128, g)
            kvd, b_kvd = tmp()
            evac(kvd[:, :], pt[:, :], [pb], [b_kvd])
            stat_rstd([kvd[:, :]], 128, rstd[:], b_rstd, [b_kvd])
            dstc = ckvTc if g == 0 else sqb[0]
            if g == 0:
                add("dve", lambda e, kvd=kvd: e.scalar_tensor_tensor(out=ckvTc[:, 0:T], in0=kvd[:, :], scalar=gkvc_s[:, l:l + 1], in1=rstd[:],
                                                            op0=ALU.mult, op1=ALU.mult), [b_kvd, b_rstd, b_prm], [b_ckvTc])
            else:
                tk, tkb = sqt()
                add("dve", lambda e, tk=tk, kvd=kvd: e.scalar_tensor_tensor(out=tk[:, :], in0=kvd[:, :], scalar=gkvc_s[:, l:l + 1], in1=rstd[:],
                                                                   op0=ALU.mult, op1=ALU.mult), [b_kvd, b_rstd, b_prm], [tkb])
                add("sp", lambda e, tk=tk: e.dma_start(out=payA_in[l].ap()[R_CKV:R_CKV + 128, :], in_=tk[:, :]), [tkb], [b_payA[l]], dma=True)
            pt, pb = fm(wC, bC, 64, 96, g)
            if g == 0:
                for h in range(4):
                    evac(KTc2[1][64:96, h, :], pt[64:96, :], [pb], [b_KTc[1]])
            else:
                ptB, pbB = fm(wF2, bF2, 0, 96, 1)
                tk, tkb = sqt()
                rope_evac(pt, pb, ptB, pbB, 64, 96, tk[64:96, :], [tkb])
                add("sp", lambda e, tk=tk: e.dma_start(out=payA_in[l].ap()[R_KR:R_KR + 32, :], in_=tk[64:96, :]), [tkb], [b_payA[l]], dma=True)
            for h in range(4):
                pt, pb = fm(wC, bC, 160 + h * 64, 64, g)
                evac(QT2[g][0][64:128, h, :], pt[0:64, :], [pb], [b_QT[g][2]])
        for tb in range(4):
            pt, pb = tm(wC, bC, 0, 160, 0, tb)
            s, tl = tb // 2, (tb % 2) * 128
            tt, tb_ = tmp()
            add("act", lambda e, pt=pt, tt=tt: e.activation(out=tt[:, 0:128], in_=pt[:, 0:128], func=AF.Square), [pb], [tb_])
            add("dve", lambda e, tt=tt: e.reduce_sum(out=tt[:, 200:201], in_=tt[:, 0:128], axis=mybir.AxisListType.X), [tb_], [tb_])
            add("act", lambda e, tt=tt: e.activation(out=tt[:, 201:202], in_=tt[:, 200:201], func=AF.Ln, bias=epsc[:, 0:1], scale=1.0 / 128),
                [tb_, b_epsc], [tb_])
            add("act", lambda e, tt=tt: e.activation(out=tt[:, 202:203], in_=tt[:, 201:202], func=AF.Exp, scale=-0.5), [tb_], [tb_])
            add("dve", lambda e, pt=pt, tt=tt: e.scalar_tensor_tensor(out=tt[:, 256:384], in0=pt[:, 0:128], scalar=tt[:, 202:203],
                                                                      in1=gkvr_s[:, l, :], op0=ALU.mult, op1=ALU.mult),
                [pb, tb_, b_prm], [tb_])
            add("act", lambda e, pt=pt, tt=tt: e.activation(out=tt[:, 384:416], in_=pt[:, 128:160], func=AF.Copy), [pb, tb_], [tb_])
            add("sp", lambda e, s=s, tl=tl, tt=tt: e.dma_start(out=o_ckv[s, l, tl:tl + 128, :], in_=tt[:, 256:384]), [tb_], (), dma=True)
            add("sp", lambda e, s=s, tl=tl, tt=tt: e.dma_start(out=o_kr[s, l, tl:tl + 128, :], in_=tt[:, 384:416]), [tb_], (), dma=True)
        for h in range(4):
            pt, pb = gp()
            add("pe", lambda e, pt=pt, h=h: e.matmul(pt[0:64, :], lhsT=wkv_s[:, h * 128:h * 128 + 64], rhs=ckvTc[:, 0:T], start=True, stop=True),
                [b_wsm, b_ckvTc], [pb])
            evac(KTc2[1][0:64, h, :], pt[0:64, :], [pb], [b_KTc[1]])
        for tb in range(4):
            pt, pb = gp()
            add("pe", lambda e, pt=pt, tb=tb: e.matmul(pt[:, 0:256], lhsT=ckvTc[:, tb * 128:(tb + 1) * 128],
                                                      rhs=wkv_s[:, :].rearrange("p (h c) -> p h c", h=4)[:, :, 64:128], start=True, stop=True),
                [b_wsm, b_ckvTc], [pb])
            evac(VAc[1][:, tb, :, 0:64], pt[:, 0:256].rearrange("p (h d) -> p h d", h=4), [pb], [b_VAc[1]])
        wD, bD = wtile(w_in[l, :, 1440:1952], 512)
        for h in range(4):
            pt, pb = fm(wD, bD, h * 64, 64, 0)
            evac(KTc2[0][64:128, h, :], pt[0:64, :], [pb], [b_KTc[2]])
            if do_lat:
                pt, pb = fm(wD, bD, h * 64, 64, 1)
                tk, tkb = sqt()
                evac(tk[0:64, :], pt[0:64, :], [pb], [tkb])
                add("sp", lambda e, tk=tk, h=h: e.dma_start(out=payA_in[l].ap()[R_NAK + h * 64:R_NAK + (h + 1) * 64, :], in_=tk[0:64, :]),
                    [tkb], [b_payA[l]], dma=True)
        for tb in range(4):
            pt, pb = tm(wD, bD, 0, 512, 0, tb)
            s, tl = tb // 2, (tb % 2) * 128
            ostage, b_ostage = tmp()
            add("act", lambda e, pt=pt, ostage=ostage: e.activation(out=ostage[:, :], in_=pt[:, :], func=AF.Copy), [pb], [b_ostage])
            add("sp", lambda e, s=s, tl=tl, ostage=ostage: e.dma_start(out=o_nak[s, l, :, tl:tl + 128, :].rearrange("h t d -> t h d"),
                                                       in_=ostage[:, 0:256].rearrange("p (h d) -> p h d", h=4)), [b_ostage], (), dma=True)
            add("sp", lambda e, s=s, tl=tl, ostage=ostage: e.dma_start(out=o_nav[s, l, :, tl:tl + 128, :].rearrange("h t d -> t h d"),
                                                       in_=ostage[:, 256:512].rearrange("p (h d) -> p h d", h=4)), [b_ostage], (), dma=True)
            add("dve", lambda e, pt=pt, tb=tb: e.tensor_copy(out=VAc[2][:, tb, :, 0:64], in_=pt[:, 256:512].rearrange("p (h d) -> p h d", h=4)),
                [pb], [b_VAc[2]])
            if do_lat:
                pt, pb = tm(wD, bD, 256, 256, 1, tb)
                tv, tvb = sqt()
                evac(tv[:, 0:256], pt[:, 0:256], [pb], [tvb])
                add("sp", lambda e, tv=tv, tb=tb: e.dma_start(out=payB_in[l].ap()[R_V + tb * 128:R_V + (tb + 1) * 128, 256:512], in_=tv[:, 0:256]),
                    [tvb], [b_payB[l]], dma=True)
        wA2, bA2 = wtile(w_in[l, :, 256:512], 256)
        for tb in range(4):
            pt, pb = tm(wA2, bA2, 0, 256, 0, tb)
            s, tl = tb // 2, (tb % 2) * 128
            ostage, b_ostage = tmp()
            add("act", lambda e, pt=pt, ostage=ostage: e.activation(out=ostage[:, 0:256], in_=pt[:, 0:256], func=AF.Copy), [pb], [b_ostage])
            add("sp", lambda e, s=s, tl=tl, ostage=ostage: e.dma_start(out=o_dak[s, l, :, tl:tl + 128, :].rearrange("h t d -> t h d"),
                                                       in_=ostage[:, 0:256].rearrange("p (h d) -> p h d", h=4)), [b_ostage], (), dma=True)
        wE, bE = wtile(w_in[l, :, 1952:2464], 512)
        for g in groups:
            for c in range(2):
                pa, pba = fm(wE, bE, c * 128, 128, g)
                pbm, pbb = fm(wE, bE, 256 + c * 128, 128, g)
                tt, tb_ = tmp()
                add("act", lambda e, tt=tt, pbm=pbm: e.activation(out=tt[:], in_=pbm[:, :], func=AF.Sigmoid), [pbb], [tb_])
                add("dve", lambda e, tt=tt, pa=pa, c=c, g=g: e.tensor_tensor(out=gpad[g][:, c, :, 15:15 + 256],
                                                                           in0=pa[:, :].rearrange("p (s t) -> p s t", s=2),
                                                                           in1=tt[:].rearrange("p (s t) -> p s t", s=2), op=ALU.mult),
                    [pba, tb_], [b_gpad[g]])
        if do_lat:
            add("dve", lambda e: e.tensor_copy(out=halo[:, :, 0, 0:15], in_=gpad[1][:, :, 0, 15:30]), [b_gpad[1]], [b_halo])
            add("dve", lambda e: e.tensor_copy(out=halo[:, :, 0, 15:30], in_=gpad[1][:, :, 1, 256:271]), [b_gpad[1]], [b_halo])
            hdst = dram_ap(payB_in[l], R_HALO * T, [[30, 128], [128 * 30, 2], [1, 30]])
            add("sp", lambda e: e.dma_start(out=hdst, in_=halo[:, :, 0, :]), [b_halo], [b_payB[l]], dma=True)
            add("pool", lambda e: e.collective_compute("AllGather", ALU.bypass, replica_groups=[[0, 1, 2, 3], [4, 5, 6, 7]],
                                                       ins=[payA_in[l].ap().opt()], outs=[payA_out[l].ap().opt()]),
                [b_payA[l]], [b_payoA[l]], dma="cc")
            add("pool", lambda e: e.collective_compute("AllGather", ALU.bypass, replica_groups=[[0, 1, 2, 3], [4, 5, 6, 7]],
                                                       ins=[payB_in[l].ap().opt()], outs=[payB_out[l].ap().opt()]),
                [b_payB[l]], [b_payoB[l]], dma="cc")

    def da_post(l, g, h, at1, ab1, at2, ab2, nq=T, q0=0, repl=False):
        o1, bo1 = dao, b_dao
        o2, bo2 = rsum_s[0], b_rsum_s[0]
        if repl:
            normalize_rep(at1, ab1, nq, o1[0:64, 0:nq], [bo1])
            normalize_rep(at2, ab2, nq, o2[0:64, 0:nq], [bo2])
        else:
            normalize(at1, ab1, nq, o1[0:64, 0:nq], [bo1], 0)
            normalize(at2, ab2, nq, o2[0:64, 0:nq], [bo2], 1 if nq <= 256 else 0)
        add("dve", lambda e: e.scalar_tensor_tensor(out=o1[0:64, 0:nq], in0=o2[0:64, 0:nq], scalar=neglam[0:64, l:l + 1], in1=o1[0:64, 0:nq],
                                                    op0=ALU.mult, op1=ALU.add), [bo1, bo2, b_prm], [bo1])
        stat_rstd([o1[0:64, 0:nq]], 64, rstd[0:64, 0:nq], b_rstd, [bo1], K=64, n=nq)
        p0 = (h % 2) * 64
        add("dve", lambda e: e.scalar_tensor_tensor(out=mixT[g][p0:p0 + 64, h // 2, q0:q0 + nq], in0=o1[0:64, 0:nq], scalar=gsub2[0:64, l:l + 1],
                                                    in1=rstd[0:64, 0:nq], op0=ALU.mult, op1=ALU.mult),
            [bo1, b_rstd, b_prm], [b_mix[g][h // 2]])

    def plain_post(g, m, h, at, ab, nq=T, q0=0, repl=False):
        if repl:
            p0 = (h % 2) * 64
            ch = 2 * m + h // 2
            normalize_rep(at, ab, nq, mixT[g][p0:p0 + 64, ch, q0:q0 + nq], [b_mix[g][ch]])
            return
        k = (set_i[0] % NSET) if nq <= 256 else 0
        set_i[0] += 1
        o1, bo1 = tmp()
        normalize(at, ab, nq, o1[0:64, 0:nq], [bo1], k)
        p0 = (h % 2) * 64
        ch = 2 * m + h // 2
        add("act", lambda e: e.activation(out=mixT[g][p0:p0 + 64, ch, q0:q0 + nq], in_=o1[0:64, 0:nq], func=AF.Copy), [bo1], [b_mix[g][ch]])

    def ctx_attention(l, side_gen=None):
        scales = (32 ** -0.5, 96 ** -0.5, 64 ** -0.5)
        rows = {0: None, 1: (0, 96), 2: (0, 64)}
        jobs = []
        for s in range(2):
            for m in range(3):
                for h in range(4):
                    for mp in ((0, 1) if m == 0 else (0,)):
                        jobs.append(dict(s=s, m=m, h=h, mp=mp))
        n = len(jobs)

        def stage_a(j):
            s_, m, h, mp = j["s"], j["m"], j["h"], j["mp"]
            q0 = s_ * 256
            lo, hi = (mp * 32, mp * 32 + 32) if m == 0 else rows[m]
            pt, pb = gp()
            rd = [b_KTc[m], b_QT[0][m]]
            for kb in range(2):
                add("pe", lambda e, pt=pt, kb=kb, m=m, h=h, lo=lo, hi=hi, q0=q0: e.matmul(
                    pt[:, kb * 256:(kb + 1) * 256], lhsT=kc_ap(m, h, lo, hi, q0 + kb * 128, 128), rhs=q_ap(0, m, h, lo, hi, q0, 256),
                    start=True, stop=True), rd, [pb])
            i = e_i[0] % 3
            e_i[0] += 1
            add("act", lambda e, pt=pt, i=i, m=m: e.activation(out=Eb[i][:, :], in_=pt[:, :], func=AF.Exp, scale=scales[m]), [pb], [b_E[i]])
            j["e"] = i

        def stage_b(j):
            s_, m, h = j["s"], j["m"], j["h"]
            at, ab = acb()
            i = j["e"]
            for kb in range(2):
                add("pe", lambda e, at=at, kb=kb, i=i, m=m, h=h, s_=s_: e.matmul(at[0:65, 0:256], lhsT=VAc[m][:, s_ * 2 + kb, h, 0:65],
                                                                               rhs=Eb[i][:, kb * 256:(kb + 1) * 256], start=(kb == 0), stop=(kb == 1)),
                    [b_VAc[m], b_E[i]], [ab])
            for kb in range(2):
                add("pe", lambda e, at=at, kb=kb, i=i: e.matmul(at[0:64, 256:512], lhsT=ones_b[:, 0:64], rhs=Eb[i][:, kb * 256:(kb + 1) * 256],
                                                               start=(kb == 0), stop=(kb == 1)), [b_ones_b, b_E[i]], [ab])
            j["acc"] = (at, ab)

        def stage_c(idx):
            j = jobs[idx]
            s_, m, h, mp = j["s"], j["m"], j["h"], j["mp"]
            q0 = s_ * 256
            if m == 0:
                if mp == 1:
                    a1 = jobs[idx - 1]["acc"]
                    da_post(l, 0, h, a1[0], a1[1], j["acc"][0], j["acc"][1], nq=256, q0=q0, repl=True)
            else:
                plain_post(0, m, h, j["acc"][0], j["acc"][1], nq=256, q0=q0, repl=True)

        for i in range(n + 2):
            if i < n:
                stage_a(jobs[i])
            warm(WARM_CTX)
            if 0 <= i - 1 < n:
                stage_b(jobs[i - 1])
            if 0 <= i - 2 < n:
                stage_c(i - 2)
            if side_gen is not None and i % 2 == 1:
                next(side_gen, None)

    def cache_T(src_ap, ncol, dst_fn, wlist):
        stg, bstg = tmp()
        add("sp", lambda e: e.dma_start(out=stg[:, 0:2 * ncol].rearrange("p (t c) -> p t c", t=2),
                                        in_=src_ap.rearrange("(t p) c -> p t c", p=128)), (), [bstg], dma=True)
        for tb in range(2):
            pt, pb = gp()
            add("pe", lambda e, pt=pt, tb=tb: e.transpose(out=pt[0:ncol, 0:128], in_=stg[:, tb * ncol:(tb + 1) * ncol], identity=ident[:]),
                [bstg, b_ident], [pb])
            evac(dst_fn(tb), pt[0:ncol, 0:128], [pb], wlist)

    def load_V(l, c0, cache_ap):
        for rk in range(4):
            for tb in range(4):
                src = dram_ap(payB_out[l], (rk * PB + R_V + tb * 128) * T + c0, [[T, 128], [64, 4], [1, 64]])
                add("sp", lambda e, rk=rk, tb=tb, src=src: e.dma_start(out=VAb[:, rk * 4 + tb, :, 0:64], in_=src), [b_payoB[l]], [b_VAk[rk * 4 + tb]], dma=True)
        for tb in range(2):
            add("pool", lambda e, tb=tb: e.dma_start(out=VAb[:, 16 + tb, :, 0:64], in_=cache_ap[:, tb * 128:(tb + 1) * 128, :].rearrange("h t d -> t h d")),
                (), [b_VAk[16 + tb]], dma=True)

    def load_KT(l, row0, nrows, p0, heads=True):
        for h in range(4):
            r = row0 + (h * nrows if heads else 0)
            src = dram_ap(payA_out[l], r * T, [[T, nrows], [PA * T, 4], [1, T]])
            add("sp", lambda e, h=h, src=src: e.dma_start(out=KTb[p0:p0 + nrows, h, 0:DEC_S].rearrange("p (r t) -> p r t", r=4), in_=src),
                [b_payoA[l]], [b_KTh[h]], dma=True)

    def lat_attention(l, side_gen=None, mod_gen=None):
        rdv = list(b_VAk)

        def side_step(n=1):
            if side_gen is not None:
                for _ in range(n):
                    next(side_gen, None)
            if mod_gen is not None:
                next(mod_gen, None)

        load_KT(l, R_DAK, 64, 0)
        for h in range(4):
            cache_T(c_dak[l, h], 64, lambda tb, h=h: KTb[0:64, h, DEC_S + tb * 128:DEC_S + (tb + 1) * 128], [b_KTh[h]])
        load_V(l, 0, c_dav[l])
        jobs = []
        for h in range(4):
            for mp in range(2):
                r0 = mp * 32
                J = dict(kt_fn=(lambda kb, h=h, r0=r0: KTb[r0:r0 + 32, h, kb * 128:(kb + 1) * 128]), q_ap=q_ap(1, 0, h, r0, r0 + 32),
                         va_fn=(lambda kb, h=h: VAb[:, kb, h, 0:65]), scale=32 ** -0.5, reads=rdv + [b_KTh[h], b_QT[1][0]])
                jobs.append(J)
                if mp == 1:
                    J1, J2 = jobs[-2], jobs[-1]
                    J["post"] = (lambda h=h, J1=J1, J2=J2: da_post(l, 1, h, J1["acc"][0], J1["acc"][1], J2["acc"][0], J2["acc"][1]))
        stream_attention(jobs, side_cb=lambda: side_step(1), burst=BURST)
        src = dram_ap(payA_out[l], R_CKV * T, [[T, 128], [PA * T, 4], [1, T]])
        add("sp", lambda e, src=src: e.dma_start(out=ckvT[:, 0:DEC_S].rearrange("p (r t) -> p r t", r=4), in_=src), [b_payoA[l]], [b_ckvT], dma=True)
        cache_T(c_ckv[l], 128, lambda tb: ckvT[:, DEC_S + tb * 128:DEC_S + (tb + 1) * 128], [b_ckvT])
        load_KT(l, R_KR, 32, 64, heads=False)
        for h in range(4):
            cache_T(c_kr[l], 32, lambda tb, h=h: KTb[64:96, h, DEC_S + tb * 128:DEC_S + (tb + 1) * 128], [b_KTh[h]])
        for h in range(4):
            for cb in range(5):
                n0 = cb * 512
                n = min(512, DEC_S + PAST - n0)
                pt, pb = gp()
                add("pe", lambda e, pt=pt, h=h, n0=n0, n=n: e.matmul(pt[0:64, 0:n], lhsT=wkv_s[:, h * 128:h * 128 + 64], rhs=ckvT[:, n0:n0 + n],
                                                                    start=True, stop=True), [b_wsm, b_ckvT], [pb])
                evac(KTb[0:64, h, n0:n0 + n], pt[0:64, 0:n], [pb], [b_KTh[h]])
        for kb in range(18):
            pt, pb = gp()
            add("pe", lambda e, pt=pt, kb=kb: e.matmul(pt[:, 0:256], lhsT=ckvT[:, kb * 128:(kb + 1) * 128],
                                                      rhs=wkv_s[:, :].rearrange("p (h c) -> p h c", h=4)[:, :, 64:128], start=True, stop=True),
                [b_wsm, b_ckvT], [pb])
            evac(VAb[:, kb, :, 0:64], pt[:, 0:256].rearrange("p (h d) -> p h d", h=4), [pb], [b_VAk[kb]])
        jobs = []
        for h in range(4):
            J = dict(kt_fn=(lambda kb, h=h: KTb[0:96, h, kb * 128:(kb + 1) * 128]), q_ap=q_ap(1, 1, h, 0, 96),
                     va_fn=(lambda kb, h=h: VAb[:, kb, h, 0:65]), scale=96 ** -0.5, reads=rdv + [b_KTh[h], b_QT[1][1]])
            J["post"] = (lambda h=h, J=J: plain_post(1, 1, h, J["acc"][0], J["acc"][1]))
            jobs.append(J)
        stream_attention(jobs, side_cb=lambda: side_step(2))
        load_KT(l, R_NAK, 64, 0)
        for h in range(4):
            cache_T(c_nak[l, h], 64, lambda tb, h=h: KTb[0:64, h, DEC_S + tb * 128:DEC_S + (tb + 1) * 128], [b_KTh[h]])
            add("pool", lambda e, h=h: e.dma_start(out=KTb[64:96, h, 0:DEC_S], in_=rowind_d), (), [b_KTh[h]], dma=True)
            add("pool", lambda e, h=h: e.memset(KTb[64:96, h, DEC_S:DEC_S + PAST], 0.0), (), [b_KTh[h]])
            evac(QT2[1][0][0:64, h, :], QT2[1][0][64:128, h, :], [b_QT[1][2], b_QT[1][0]], [b_QT[1][0], b_QT[1][2]])
            add("pool", lambda e, h=h: e.dma_start(out=QT2[1][0][64:96, h, :], in_=rowsel_d), (), [b_QT[1][0], b_QT[1][2]], dma=True)
        load_V(l, 256, c_nav[l])
        jobs = []
        for h in range(4):
            ti = h % 2

            def pre(h=h, ti=ti):
                for j in range(2):
                    src = bass.AP(tz_all, l * TZ_SZ + (h * (A2 + 1) + (1 - j)) * 64 * TZL + 63, [[TZL - 1, 64], [64 * TZL, A2], [1, 64]])
                    add("pool", lambda e, j=j, src=src, ti=ti: e.dma_start(out=tbt[ti][j * 64:(j + 1) * 64, :, :], in_=src), (), [b_tbt[ti]], dma=True)
                add("dve", lambda e, ti=ti: e.scalar_tensor_tensor(out=tbt[ti][:], in0=tbt[ti][:], scalar=8.0,
                                                                   in1=colm_s[:, :].unsqueeze(1).to_broadcast([128, A2, 64]),
                                                                   op0=ALU.mult, op1=ALU.add), [b_tbt[ti], b_prm], [b_tbt[ti]])

            def extra(kb, ti=ti):
                if kb >= 16:
                    return []
                a0 = 37 - 2 * kb
                return [(identb[:, :], tbt[ti][:, a0:a0 + 8, :], [b_identb, b_tbt[ti]])]

            J = dict(kt_fn=(lambda kb, h=h: KTb[0:96, h, kb * 128:(kb + 1) * 128]), q_ap=QT2[1][0][0:96, h, :],
                     va_fn=(lambda kb, h=h: VAb[:, kb, h, 0:65]), scale=64 ** -0.5, reads=rdv + [b_KTh[h], b_QT[1][2]],
                     extra_fn=extra, pre=pre)
            J["post"] = (lambda h=h, J=J: plain_post(1, 2, h, J["acc"][0], J["acc"][1]))
            jobs.append(J)
        stream_attention(jobs, side_cb=lambda: side_step(2))

    def conv_module(l, g):
        if g == 1:
            for c in range(2):
                src = dram_ap(payB_out[l], R_HALO * T + c * 128 * 30, [[30, 128], [PB * T, 4], [1, 30]])
                add("sp", lambda e, src=src, c=c: e.dma_start(out=halo[:, c, :, :], in_=src), [b_payoB[l]], [b_halo], dma=True)
            for side in range(2):
                dst = gpad[1][:, :, 0, 0:15] if side == 0 else gpad[1][:, :, 1, 271:286]
                for rk in range(4):
                    srcv = halo[:, :, rk, 15:30] if side == 0 else halo[:, :, rk, 0:15]
                    sc = halsel_s[:, side * 4 + rk:side * 4 + rk + 1]
                    if rk == 0:
                        add("dve", lambda e, dst=dst, srcv=srcv, sc=sc: e.tensor_scalar(out=dst, in0=srcv, scalar1=sc, scalar2=None, op0=ALU.mult),
                            [b_halo, b_prm, b_gpad[1]], [b_gpad[1]])
                    else:
                        add("dve", lambda e, dst=dst, srcv=srcv, sc=sc: e.scalar_tensor_tensor(out=dst, in0=srcv, scalar=sc, in1=dst,
                                                                                              op0=ALU.mult, op1=ALU.add),
                            [b_halo, b_prm, b_gpad[1]], [b_gpad[1]])
        for c in range(2):
            for j in range(31):
                if g == 0:
                    src = gpad[0][:, c, :, j:j + 256]
                    dst = cacc[:, c, :].rearrange("p (s t) -> p s t", s=2)
                    ops = [(src, dst)]
                else:
                    ops = []
                    n_a = max(0, min(256, 271 - j))
                    if n_a > 0:
                        ops.append((gpad[1][:, c, 0, j:j + n_a], cacc[:, c, 0:n_a]))
                    if n_a < 256:
                        ops.append((gpad[1][:, c, 1, 15 + (n_a + j - 271):15 + (256 + j - 271)], cacc[:, c, n_a:256]))
                    n_b = max(0, min(256, 271 - (256 + j)))
                    if n_b > 0:
                        ops.append((gpad[1][:, c, 0, 256 + j:256 + j + n_b], cacc[:, c, 256:256 + n_b]))
                    ops.append((gpad[1][:, c, 1, 15 + (256 + n_b + j - 271):15 + (512 + j - 271)], cacc[:, c, 256 + n_b:512]))
                if j % 4 == 3:
                    yield
                for (src, dst) in ops:
                    if j == 0:
                        add("dve", lambda e, src=src, dst=dst, c=c: e.tensor_scalar(out=dst, in0=src, scalar1=dw_s[:, l, c, 0:1], scalar2=cb_s[:, l, c:c + 1],
                                                                                    op0=ALU.mult, op1=ALU.add), [b_gpad[g], b_prm, b_cacc], [b_cacc])
                    else:
                        add("dve", lambda e, src=src, dst=dst, c=c, j=j: e.scalar_tensor_tensor(out=dst, in0=src, scalar=dw_s[:, l, c, j:j + 1], in1=dst,
                                                                                                op0=ALU.mult, op1=ALU.add), [b_gpad[g], b_prm, b_cacc], [b_cacc])
        p1, pb1 = gp()
        p2, pb2 = gp()
        for c in range(2):
            tt, tb_ = tmp()
            add("act", lambda e, tt=tt, c=c: e.activation(out=tt[:], in_=cacc[:, c, :], func=AF.Square), [b_cacc], [tb_])
            add("pe", lambda e, c=c: e.matmul(p1[:, :], lhsT=ones_f[:, :], rhs=cacc[:, c, :], start=(c == 0), stop=(c == 1)), [b_cacc, b_ones_f], [pb1])
            add("pe", lambda e, tt=tt, c=c: e.matmul(p2[:, :], lhsT=ones_f[:, :], rhs=tt[:], start=(c == 0), stop=(c == 1)), [tb_, b_ones_f], [pb2])
        mean, bmean = tmp()
        var, bvar = tmp()
        add("act", lambda e: e.activation(out=mean[:], in_=p1[:, :], func=AF.Identity, scale=1.0 / 256), [pb1], [bmean])
        add("dve", lambda e: e.tensor_tensor(out=var[:], in0=mean[:], in1=mean[:], op=ALU.mult), [bmean], [bvar])
        add("dve", lambda e: e.scalar_tensor_tensor(out=var[:], in0=p2[:, :], scalar=1.0 / 256, in1=var[:], op0=ALU.mult, op1=ALU.subtract),
            [pb2, bvar], [bvar])
        add("act", lambda e: e.activation(out=var[:], in_=var[:], func=AF.Ln, bias=epsc[:, 0:1], scale=1.0), [bvar, b_epsc], [bvar])
        add("act", lambda e: e.activation(out=var[:], in_=var[:], func=AF.Exp, scale=-0.5), [bvar], [bvar])
        for c in range(2):
            add("dve", lambda e, c=c: e.tensor_tensor(out=cacc[:, c, :], in0=cacc[:, c, :], in1=mean[:], op=ALU.subtract), [b_cacc, bmean], [b_cacc])
            add("dve", lambda e, c=c: e.tensor_tensor(out=cacc[:, c, :], in0=cacc[:, c, :], in1=var[:], op=ALU.mult), [b_cacc, bvar], [b_cacc])
            add("act", lambda e, c=c: e.activation(out=cacc[:, c, :], in_=cacc[:, c, :], func=AF.Identity, bias=lnb_s[:, l, c:c + 1],
                                                   scale=lng_s[:, l, c:c + 1]), [b_cacc, b_prm], [b_cacc])
            tt, tb_ = tmp()
            add("act", lambda e, tt=tt, c=c: e.activation(out=tt[:], in_=cacc[:, c, :], func=AF.Sigmoid), [b_cacc], [tb_])
            add("dve", lambda e, tt=tt, c=c: e.tensor_tensor(out=mixT[g][:, 6 + c, :], in0=cacc[:, c, :], in1=tt[:], op=ALU.mult),
                [b_cacc, tb_], [b_mix[g][6 + c]])

    def out_proj(l, groups):
        for ti in range(2):
            wt, wb = wtile(w_out[l, :, ti * 512:(ti + 1) * 512], 512)
            for cc in range(4):
                oc = ti * 4 + cc
                for g in groups:
                    pt, pb = gp()
                    for k in range(8):
                        add("pe", lambda e, pt=pt, k=k, cc=cc, wt=wt, g=g: e.matmul(pt[:, :], lhsT=wt[:, k, cc * 128:(cc + 1) * 128], rhs=mixT[g][:, k, :],
                                                                                   start=(k == 0), stop=(k == 7)), [wb, b_mix[g][k]], [pb])
                    add("dve", lambda e, pt=pt, g=g, oc=oc: e.scalar_tensor_tensor(out=xT[g][:, oc, :], in0=pt[:, :], scalar=mcol(l, "g1", oc, g),
                                                                                 in1=xT[g][:, oc, :], op0=ALU.mult, op1=ALU.add),
                        [pb, b_mod[l], b_xT[g][oc]], [b_xT[g][oc]])

    def ffn(l, groups):
        for blk in range(4):
            for ti in range(2):
                wt, wb = wtile(w_ff1[l, :, blk * 1024 + ti * 512: blk * 1024 + (ti + 1) * 512], 512)
                for cc in range(4):
                    fc = ti * 4 + cc
                    for g in groups:
                        pt, pb = gp()
                        for k in range(8):
                            add("pe", lambda e, pt=pt, k=k, cc=cc, wt=wt, g=g: e.matmul(pt[:, :], lhsT=wt[:, k, cc * 128:(cc + 1) * 128], rhs=hT[g][:, k, :],
                                                                                       start=(k == 0), stop=(k == 7)), [wb, b_hT[g]], [pb])
                        sq, bq = sqt()
                        add("act", lambda e, pt=pt, sq=sq: e.activation(out=sq[:], in_=pt[:, :], func=AF.Relu), [pb], [bq])
                        add("dve", lambda e, sq=sq, g=g, fc=fc: e.tensor_tensor(out=mixT[g][:, fc, :], in0=sq[:], in1=sq[:], op=ALU.mult),
                            [bq], [b_mix[g][fc]])
            for ti in range(2):
                wt, wb = wtile(w_ff2[l, blk * 1024:(blk + 1) * 1024, ti * 512:(ti + 1) * 512], 512)
                for cc in range(4):
                    oc = ti * 4 + cc
                    for g in groups:
                        pt, pb = gp()
                        for k in range(8):
                            add("pe", lambda e, pt=pt, k=k, cc=cc, wt=wt, g=g: e.matmul(pt[:, :], lhsT=wt[:, k, cc * 128:(cc + 1) * 128], rhs=mixT[g][:, k, :],
                                                                                       start=(k == 0), stop=(k == 7)), [wb, b_mix[g][k]], [pb])
                        add("dve", lambda e, pt=pt, g=g, oc=oc: e.scalar_tensor_tensor(out=xT[g][:, oc, :], in0=pt[:, :], scalar=mcol(l, "g2", oc, g),
                                                                                     in1=xT[g][:, oc, :], op0=ALU.mult, op1=ALU.add),
                            [pb, b_mod[l], b_xT[g][oc]], [b_xT[g][oc]])

    def dump8(name, t, bufs):
        if name not in dbg:
            return
        o = dout("dbg_" + name, [128, 8, T])
        for j in range(8):
            tt, tb_ = tmp()
            add("dve", lambda e, tt=tt, j=j: e.tensor_copy(out=tt[:], in_=t[:, j, :]), [bufs[j]], [tb_])
            add("sp", lambda e, tt=tt, j=j: e.dma_start(out=o[:, j, :], in_=tt[:]), [tb_], (), dma=True)
        dbg_out[name] = [128, 8, T]

    groups = [0, 1] if do_lat else [0]
    import os as _os
    _l1 = _os.environ.get("L1STAGES")
    _stages0 = stages
    for l in range(nlayers):
        stages = _stages0 if (l == 0 or _l1 is None) else set(_l1.split(","))
        if "mod" in stages and l == 0:
            for _ in modulation(0):
                pass
        if "norm1" in stages:
            for g in groups:
                rmsnorm_mod(l, g, 0)
        if "proj" in stages:
            input_proj(l)
        side = modulation(l + 1) if ("mod" in stages and l + 1 < nlayers) else None
        if "ctxattn" in stages:
            ctx_attention(l, None if (do_lat and "latattn" in stages) else side)
        if "conv" in stages:
            for _ in conv_module(l, 0):
                pass
        dump8("mix0%d" % l, mixT[0], b_mix[0])
        if do_lat:
            side2 = conv_module(l, 1) if "conv" in stages else None
            if "latattn" in stages:
                lat_attention(l, side2, side)
            if side2 is not None:
                for _ in side2:
                    pass
        if side is not None:
            for _ in side:
                pass
            dump8("mix1%d" % l, mixT[1], b_mix[1])
        if "outproj" in stages:
            out_proj(l, groups)
        for g in groups:
            dump8("xattn%d%d" % (g, l), xT[g], b_xT[g])
        if "norm2" in stages:
            for g in groups:
                rmsnorm_mod(l, g, 1)
        if "ffn" in stages:
            ffn(l, groups)
        for g in groups:
            dump8("xffn%d%d" % (g, l), xT[g], b_xT[g])
    for g in groups:
        if "final" not in stages:
            continue
        pt, pb = gp()
        for j in range(8):
            sq, bq = sqt()
            add("act", lambda e, sq=sq, j=j, g=g: e.activation(out=sq[:], in_=xT[g][:, j, :], func=AF.Square), [b_xT[g][j]], [bq])
            add("pe", lambda e, sq=sq, j=j, pt=pt: e.matmul(pt[:, :], lhsT=ones_b[:, :], rhs=sq[:], start=(j == 0), stop=(j == 7)), [bq, b_ones_b], [pb])
        add("act", lambda e, pt=pt: e.activation(out=rstd[:], in_=pt[:, :], func=AF.Ln, bias=epsc[:, 0:1], scale=1.0 / D), [pb, b_epsc], [b_rstd])
        add("act", lambda e: e.activation(out=rstd[:], in_=rstd[:], func=AF.Exp, scale=-0.5), [b_rstd], [b_rstd])
        for j in range(8):
            add("dve", lambda e, j=j, g=g: e.scalar_tensor_tensor(out=xT[g][:, j, :], in0=xT[g][:, j, :], scalar=gfin_s[:, j:j + 1], in1=rstd[:],
                                                                  op0=ALU.mult, op1=ALU.mult), [b_xT[g][j], b_rstd, b_prm], [b_xT[g][j]])
    for g in groups:
        for tb in range(4):
            for half in range(2):
                pt, pb = gp()
                for jj in range(4):
                    j = half * 4 + jj
                    add("pe", lambda e, pt=pt, j=j, jj=jj, tb=tb, g=g: e.transpose(out=pt[:, jj * 128:(jj + 1) * 128],
                                                                                   in_=xT[g][:, j, tb * 128:(tb + 1) * 128], identity=ident[:]),
                        [b_xT[g][j], b_ident], [pb])
                stg, bstg = tmp()
                evac(stg[:, :], pt[:, :], [pb], [bstg])
                add("sp", lambda e, g=g, tb=tb, half=half, stg=stg: e.dma_start(
                    out=y_out[g * T + tb * 128:g * T + (tb + 1) * 128, half * 512:(half + 1) * 512], in_=stg[:, :]), [bstg], (), dma=True)
    S.emit(st)
    st.close()
    return nc, dbg_out


def _colT(v, nchunk):
    return np.ascontiguousarray(v.reshape(nchunk, 128).T)


def prepare_inputs(inp):
    f = lambda a: np.ascontiguousarray(np.asarray(a, dtype=np.float32))
    x_prompt, x_sample, c = f(inp["x_prompt"]), f(inp["x_sample"]), f(inp["c"])
    w_in = f(inp["w_in"])
    shared = {}
    shared["bmodT"] = np.ascontiguousarray(np.stack([_colT(f(inp["b_mod"])[l], 48) for l in range(L)], 1))
    shared["gmixT"] = np.ascontiguousarray(np.stack([_colT(f(inp["g_norm_mix"])[l], 8) for l in range(L)], 1))
    shared["gffT"] = np.ascontiguousarray(np.stack([_colT(f(inp["g_norm_ff"])[l], 8) for l in range(L)], 1))
    shared["gfinT"] = _colT(f(inp["g_final"]), 8)
    shared["w_mod"] = f(inp["w_mod"])
    shared["w_in"] = w_in
    idx = []
    for base in (0, 256):
        for h in range(4):
            for m in range(2):
                idx += [base + h * 64 + m * 32 + P32[j] for j in range(32)]
    idx += list(range(1088, 1152))
    idx += [1152 + P32[j] for j in range(32)]
    shared["w_inp"] = np.ascontiguousarray(w_in[:, :, idx])
    shared["w_out"] = f(inp["w_out"])
    shared["w_ff1"] = f(inp["w_ff1"])
    shared["w_ff2"] = f(inp["w_ff2"])
    wq = f(inp["w_mla_qup"])
    shared["wqup"] = wq
    wqp = np.zeros_like(wq)
    for h in range(4):
        for j in range(32):
            wqp[:, :, h * 96 + 64 + j] = wq[:, :, h * 96 + 64 + P32[j]]
    shared["wqupp"] = wqp
    shared["wkvup"] = f(inp["w_mla_kvup"])
    shared["gqT"] = np.ascontiguousarray(np.stack([_colT(f(inp["g_mla_q"])[l], 2) for l in range(L)], 1))
    shared["gkvc"] = np.ascontiguousarray(f(inp["g_mla_kv"]).T)
    shared["gkvr"] = np.ascontiguousarray(np.broadcast_to(f(inp["g_mla_kv"])[None], (128, L, 128)))
    gs = f(inp["g_da_subln"])
    shared["gsubc"] = np.ascontiguousarray(np.concatenate([gs.T, gs.T], 0))
    lamv = np.stack([f(inp["da_lambda_q1"]), f(inp["da_lambda_k1"]), f(inp["da_lambda_q2"]), f(inp["da_lambda_k2"])], 1)
    shared["lamv"] = np.ascontiguousarray(np.broadcast_to(lamv[None], (128, L, 4, 32)))
    dw = f(inp["conv_dw"])
    shared["dwT"] = np.ascontiguousarray(dw.reshape(L, 31, 2, 128).transpose(3, 0, 2, 1))
    for nm, key in (("cbT", "conv_b"), ("lngT", "conv_ln_g"), ("lnbT", "conv_ln_b")):
        shared[nm] = np.ascontiguousarray(f(inp[key]).reshape(L, 2, 128).transpose(2, 0, 1))
    shared["ident"] = np.eye(128, dtype=np.float32)
    w = np.arange(GRID_W)
    cs = np.clip(w - 8, 0, GRID_W - 16)
    col_ok = (w[None, :] >= cs[:, None]) & (w[None, :] < cs[:, None] + 16)
    cm = np.where(col_ok.T, 0.0, -BIG * 8).astype(np.float32)
    shared["colmask"] = np.ascontiguousarray(np.concatenate([cm, cm], 0))
    ri = np.zeros((32, DEC_S), np.float32)
    ri[np.arange(DEC_S) // 64, np.arange(DEC_S)] = 1.0
    shared["rowind"] = ri
    rpb = f(inp["na_rpb"])
    half = 8
    freqs = (10000.0 ** (-np.arange(half, dtype=np.float32) * 2.0 / 16)).astype(np.float32)
    maps = []
    for core in range(NCORE):
        b, r = core // 4, core % 4
        m = dict(shared)
        m["xin"] = np.ascontiguousarray(np.concatenate([x_prompt[2 * core], x_prompt[2 * core + 1], x_sample[b, r * T:(r + 1) * T]], 0))
        cv = np.stack([f(inp["c_ctx"]), c[b]], 1)
        m["cvT"] = np.ascontiguousarray(cv.reshape(8, 128, 2).transpose(1, 0, 2))
        m["c_dak"] = f(inp["cache_da_k"])[b]
        m["c_dav"] = f(inp["cache_da_v"])[b]
        m["c_ckv"] = f(inp["cache_mla_ckv"])[b]
        m["c_kr"] = f(inp["cache_mla_krope"])[b]
        m["c_nak"] = f(inp["cache_na_k"])[b]
        m["c_nav"] = f(inp["cache_na_v"])[b]
        t = r * T + np.arange(T)
        rows = (t // GRID_W).astype(np.float32)
        cols = (t % GRID_W).astype(np.float32)
        ang = np.concatenate([rows[None, :] * freqs[:, None], rows[None, :] * freqs[:, None],
                              cols[None, :] * freqs[:, None], cols[None, :] * freqs[:, None]], 0)
        cos32 = np.cos(ang).astype(np.float32)
        sin32 = np.sin(ang).astype(np.float32)
        sgn = np.concatenate([-np.ones(8), np.ones(8), -np.ones(8), np.ones(8)]).astype(np.float32)[:, None]
        m["cosT"] = np.ascontiguousarray(np.tile(cos32, (4, 1)))
        m["sinT"] = np.ascontiguousarray(np.tile(sin32 * sgn, (4, 1)))
        r0 = r * 8
        qrow = r0 + np.arange(T) // 64
        start = np.clip(qrow - 4, 0, NROWS - 8)
        rk = np.arange(32)
        ok = (rk[:, None] >= start[None, :]) & (rk[:, None] < start[None, :] + 8)
        m["rowsel"] = np.where(ok, 0.0, -BIG * 8).astype(np.float32)
        P = np.zeros((L, 4, A2 + 1, TZL), np.float32)
        for a2 in range(A2):
            a = 45 - a2 - r0
            if 0 <= a <= 14:
                P[:, :, a2, 48:79] = rpb[:, :, a, ::-1]
        m["tz_rep"] = np.ascontiguousarray(np.broadcast_to(P.reshape(L * 4 * (A2 + 1), 1, TZL), (L * 4 * (A2 + 1), 64, TZL)))
        hs = np.zeros((128, 8), np.float32)
        if r > 0:
            hs[:, r - 1] = 1.0
        if r < 3:
            hs[:, 4 + r + 1] = 1.0
        m["halsel"] = hs
        maps.append(m)
    return maps


_NC_CACHE = {}


def kernel(**inputs):
    maps = prepare_inputs(inputs)
    if "nc" not in _NC_CACHE:
        _NC_CACHE["nc"] = build()[0]
    nc = _NC_CACHE["nc"]
    res = run_bass_kernel_spmd(nc, maps, core_ids=list(range(NCORE)))
    R = res.results
    y_prompt = np.zeros((NB, SEQ, D), np.float32)
    y_sample = np.zeros((DEC_B, DEC_S, D), np.float32)
    outs = {k: [] for k in ("o_dak", "o_dav", "o_ckv", "o_kr", "o_nak", "o_nav")}
    for core in range(NCORE):
        b, r = core // 4, core % 4
        y = R[core]["y_out"]
        y_prompt[2 * core] = y[0:256]
        y_prompt[2 * core + 1] = y[256:512]
        y_sample[b, r * T:(r + 1) * T] = y[512:1024]
        for k in outs:
            outs[k].append(R[core][k])
    cat = lambda k: np.ascontiguousarray(np.concatenate(outs[k], 0).astype(np.float32))
    return (y_prompt, y_sample, cat("o_dak"), cat("o_dav"), cat("o_ckv"), cat("o_kr"), cat("o_nak"), cat("o_nav"))
```

```python
import contextlib
import math
import numpy as np
import concourse.bass as bass
import concourse.mybir as mybir
from concourse.bass_utils import run_bass_kernel_spmd

F32 = mybir.dt.float32
BF16 = mybir.dt.bfloat16
ALU = mybir.AluOpType
AF = mybir.ActivationFunctionType

ENGINES = ("pe", "act", "dve", "pool", "sp")

D = 1024
L = 2
NB = 16
SEQ = 256
DEC_B = 2
DEC_S = 2048
PAST = 256
GRID_W = 64
NROWS = DEC_S // GRID_W
EPS = 1e-6
T = 512
NCORE = 8
BIG = 30000.0
A2 = 46
TZL = 127
PA = 672
PB = 527
R_DAK, R_CKV, R_KR, R_NAK = 0, 256, 384, 416
R_V, R_HALO = 0, 512
P32 = [8, 9, 10, 11, 12, 13, 14, 15, 0, 1, 2, 3, 4, 5, 6, 7,
       24, 25, 26, 27, 28, 29, 30, 31, 16, 17, 18, 19, 20, 21, 22, 23]


class Buf:
    __slots__ = ("name", "last_w", "readers", "excl")

    def __init__(self, name, excl=False):
        self.name = name
        self.last_w = None
        self.readers = []
        self.excl = excl


class Op:
    __slots__ = ("eng", "fn", "deps", "dma", "sig", "sem", "val")

    def __init__(self, eng, fn, dma):
        self.eng = eng
        self.fn = fn
        self.dma = dma
        self.deps = []
        self.sig = False
        self.sem = None
        self.val = 0


class Sched:
    def __init__(self, nc, n_dma_slots=16):
        self.nc = nc
        self.ops = []
        self.n_dma_slots = n_dma_slots

    def add(self, eng, fn, reads=(), writes=(), dma=False):
        op = Op(eng, fn, dma)
        ex = [b for b in reads if b.excl]
        if ex:
            reads = [b for b in reads if not b.excl]
            writes = list(writes) + ex
        deps = {}
        for b in reads:
            if b.last_w is not None:
                deps[id(b.last_w)] = b.last_w
        for b in writes:
            if b.last_w is not None:
                deps[id(b.last_w)] = b.last_w
            for r in b.readers:
                deps[id(r)] = r
        for b in reads:
            b.readers.append(op)
        for b in writes:
            b.last_w = op
            b.readers = []
        for d in deps.values():
            if d.eng == "pe" and eng == "pe" and not d.dma and not dma:
                continue
            op.deps.append(d)
            d.sig = True
        self.ops.append(op)
        return op

    def emit(self, stack):
        nc = self.nc
        eng_sem = {e: stack.enter_context(nc.semaphore("s_" + e)) for e in ENGINES}
        dma_engs = sorted({op.eng for op in self.ops if op.dma is True})
        dma_slots = {e: [stack.enter_context(nc.semaphore("d_%s_%d" % (e, i)))
                         for i in range(self.n_dma_slots)] for e in dma_engs}
        cnt = {e: 0 for e in ENGINES}
        dcnt = {e: 0 for e in dma_engs}
        slot_uses = {e: [0] * self.n_dma_slots for e in dma_engs}
        slot_prev = {}
        ncc = 0
        for op in self.ops:
            if op.dma == "cc":
                op.sem = stack.enter_context(nc.semaphore("cc_%d" % ncc))
                ncc += 1
                op.val = 1
            elif op.dma:
                k = dcnt[op.eng] % self.n_dma_slots
                dcnt[op.eng] += 1
                slot_uses[op.eng][k] += 1
                op.sem = dma_slots[op.eng][k]
                op.val = 16 * slot_uses[op.eng][k]
                slot_prev[id(op)] = (op.sem, op.val - 16)
            elif op.sig:
                cnt[op.eng] += 1
                op.sem = eng_sem[op.eng]
                op.val = cnt[op.eng]
        block = stack.enter_context(nc.Block())
        handles = {"pe": block.tensor, "act": block.scalar, "dve": block.vector,
                   "pool": block.gpsimd, "sp": block.sync}
        all_async = [op for op in self.ops if op.dma]

        def run_engine(ename):
            def body(eng):
                waited = {}
                for op in self.ops:
                    if op.eng != ename:
                        continue
                    need = {}
                    for d in op.deps:
                        key = id(d.sem)
                        if key not in need or need[key][1] < d.val:
                            need[key] = (d.sem, d.val)
                    if op.dma is True:
                        s, v = slot_prev[id(op)]
                        if v > 0:
                            key = id(s)
                            if key not in need or need[key][1] < v:
                                need[key] = (s, v)
                    for key, (s, v) in need.items():
                        if waited.get(key, 0) >= v:
                            continue
                        eng.wait_ge(s, v)
                        waited[key] = v
                    ins = op.fn(eng)
                    if op.dma == "cc":
                        ins.then_inc(op.sem)
                    elif op.dma:
                        ins.then_inc(op.sem, 16)
                    elif op.sig:
                        ins.then_inc(op.sem, 1)
                if ename == "sp":
                    last = {}
                    for op in all_async:
                        last[id(op.sem)] = (op.sem, op.val)
                    for key, (s, v) in last.items():
                        if waited.get(key, 0) < v:
                            eng.wait_ge(s, v)
                    for e in ENGINES:
                        if cnt[e] > 0:
                            eng.wait_ge(eng_sem[e], cnt[e])
            handles[ename](body)

        for e in ENGINES:
            run_engine(e)


def build(dbg=(), nlayers=L, do_lat=True, stages=None):
    if stages is None:
        stages = {"mod", "norm1", "proj", "ctxattn", "conv", "latattn", "outproj", "norm2", "ffn", "final"}
    nc = bass.Bass("TRN2", target_bir_lowering=False)
    st = contextlib.ExitStack()
    S = Sched(nc)
    dbg_out = {}

    def din(name, shape, dt=F32):
        return nc.dram_tensor(name, list(shape), dt, kind="ExternalInput").ap()

    def dout(name, shape, dt=F32):
        return nc.dram_tensor(name, list(shape), dt, kind="ExternalOutput").ap()

    xin = din("xin", [2 * T, D])
    cvT = din("cvT", [128, 8, 2])
    bmodT = din("bmodT", [128, L, 48])
    gmixT = din("gmixT", [128, L, 8])
    gffT = din("gffT", [128, L, 8])
    gfinT = din("gfinT", [128, 8])
    w_mod = din("w_mod", [L, D, 6 * D])
    w_in = din("w_in", [L, D, 2464])
    w_inp = din("w_inp", [L, D, 608])
    w_out = din("w_out", [L, D, D])
    w_ff1 = din("w_ff1", [L, D, 4 * D])
    w_ff2 = din("w_ff2", [L, 4 * D, D])
    wqup = din("wqup", [L, 256, 384])
    wqupp = din("wqupp", [L, 256, 384])
    wkvup = din("wkvup", [L, 128, 512])
    gqT = din("gqT", [128, L, 2])
    gkvc = din("gkvc", [128, L])
    gkvr = din("gkvr", [128, L, 128])
    gsubc = din("gsubc", [128, L])
    lamv = din("lamv", [128, L, 4, 32])
    tz_all = din("tz_rep", [L * 4 * (A2 + 1), 64, TZL]).tensor
    dwT = din("dwT", [128, L, 2, 31])
    cbT = din("cbT", [128, L, 2])
    lngT = din("lngT", [128, L, 2])
    lnbT = din("lnbT", [128, L, 2])
    c_dak = din("c_dak", [L, 4, PAST, 64])
    c_dav = din("c_dav", [L, 4, PAST, 64])
    c_ckv = din("c_ckv", [L, PAST, 128])
    c_kr = din("c_kr", [L, PAST, 32])
    c_nak = din("c_nak", [L, 4, PAST, 64])
    c_nav = din("c_nav", [L, 4, PAST, 64])
    cosT_d = din("cosT", [128, T])
    sinT_d = din("sinT", [128, T])
    colmask_d = din("colmask", [128, 64])
    rowsel_d = din("rowsel", [32, T])
    rowind_d = din("rowind", [32, DEC_S])
    halsel_d = din("halsel", [128, 8])
    ident_d = din("ident", [128, 128])
    y_out = dout("y_out", [2 * T, D])
    o_dak = dout("o_dak", [2, L, 4, SEQ, 64])
    o_dav = dout("o_dav", [2, L, 4, SEQ, 64])
    o_ckv = dout("o_ckv", [2, L, SEQ, 128])
    o_kr = dout("o_kr", [2, L, SEQ, 32])
    o_nak = dout("o_nak", [2, L, 4, SEQ, 64])
    o_nav = dout("o_nav", [2, L, 4, SEQ, 64])
    LROWS = 6 * PA + 6 * PB
    pay_all = nc.dram_tensor("pay_all", [L * LROWS, T], BF16)
    O_AIN, O_AOUT, O_BIN, O_BOUT = 0, PA, 5 * PA, 5 * PA + PB

    class _Sub:
        def __init__(self, row0, nrows):
            self.row0, self.nrows = row0, nrows

        def ap(self):
            return pay_all.ap()[self.row0:self.row0 + self.nrows, :]

    payA_in = [_Sub(l * LROWS + O_AIN, PA) for l in range(L)]
    payA_out = [_Sub(l * LROWS + O_AOUT, 4 * PA) for l in range(L)]
    payB_in = [_Sub(l * LROWS + O_BIN, PB) for l in range(L)]
    payB_out = [_Sub(l * LROWS + O_BOUT, 4 * PB) for l in range(L)]
    TZ_SZ = 4 * (A2 + 1) * 64 * TZL

    def dram_ap(base, off, dims):
        if isinstance(base, _Sub):
            return bass.AP(pay_all, base.row0 * T + off, dims)
        return bass.AP(base, off, dims)

    def sb(name, shape, dt):
        return st.enter_context(nc.sbuf_tensor("sb_" + name, list(shape), dt))

    add = S.add

    banks = []
    for i in range(8):
        banks.append((st.enter_context(nc.psum_tensor("ps%d" % i, [128, 512], F32)), Buf("ps%d" % i, excl=True)))
    gp_i = [0]
    ac_i = [0]

    def gp():
        b = banks[(0, 1, 2, 7)[gp_i[0] % 4]]
        gp_i[0] += 1
        return b

    def acb():
        b = banks[4 + ac_i[0] % 3]
        ac_i[0] += 1
        return b

    ev_i = [0]

    def evac(dst, src, r, w, scale=None):
        ev_i[0] += 1
        if ev_i[0] % 2 == 0:
            if scale is None:
                add("act", lambda e: e.activation(out=dst, in_=src, func=AF.Copy), r, w)
            else:
                add("act", lambda e: e.activation(out=dst, in_=src, func=AF.Identity, scale=scale), r, w)
        else:
            if scale is None:
                add("dve", lambda e: e.tensor_copy(out=dst, in_=src), r, w)
            else:
                add("dve", lambda e: e.tensor_scalar(out=dst, in0=src, scalar1=scale, scalar2=None, op0=ALU.mult), r, w)

    ident = sb("ident", [128, 128], F32); b_ident = Buf("ident")
    identb = sb("identb", [128, 128], BF16); b_identb = Buf("identb")
    ones_f = sb("ones_f", [128, 128], F32); b_ones_f = Buf("ones_f")
    ones_b = sb("ones_b", [128, 128], BF16); b_ones_b = Buf("ones_b")
    epsc = sb("epsc", [128, 1], F32); b_epsc = Buf("epsc")
    add("sp", lambda e: e.dma_start(out=ident[:], in_=ident_d), (), [b_ident], dma=True)
    add("pool", lambda e: e.memset(ones_f[:], 1.0), (), [b_ones_f])
    add("pool", lambda e: e.memset(ones_b[:], 1.0), (), [b_ones_b])
    add("pool", lambda e: e.memset(epsc[:], EPS), (), [b_epsc])
    add("dve", lambda e: e.tensor_copy(out=identb[:], in_=ident[:]), [b_ident], [b_identb])

    prm = {}
    b_prm = Buf("prm")
    b_small = []

    def load_small(name, src, shape):
        t = sb(name, shape, F32)
        bt = Buf("ld_" + name)
        b_small.append(bt)
        add("sp", lambda e: e.dma_start(out=t[:], in_=src), (), [bt], dma=True)
        prm[name] = t
        return t

    cv_s = load_small("cv_s", cvT, [128, 8, 2])
    bmod_s = load_small("bmod_s", bmodT, [128, L, 48])
    gmix_s = load_small("gmix_s", gmixT, [128, L, 8])
    gff_s = load_small("gff_s", gffT, [128, L, 8])
    gfin_s = load_small("gfin_s", gfinT, [128, 8])
    gq_s = load_small("gq_s", gqT, [128, L, 2])
    gkvc_s = load_small("gkvc_s", gkvc, [128, L])
    gkvr_s = sb("gkvr_s", [128, L, 128], BF16)
    b_small.append(Buf("ld_gkvr"))
    add("pool", lambda e: e.dma_start(out=gkvr_s[:], in_=gkvr), (), [b_small[-1]], dma=True)
    gsub_s = load_small("gsub_s", gsubc, [128, L])
    lam_s = load_small("lam_s", lamv, [128, L, 4, 32])
    dw_s = load_small("dw_s", dwT, [128, L, 2, 31])
    cb_s = load_small("cb_s", cbT, [128, L, 2])
    lng_s = load_small("lng_s", lngT, [128, L, 2])
    lnb_s = load_small("lnb_s", lnbT, [128, L, 2])
    cos_s = load_small("cos_s", cosT_d, [128, T])
    sin_s = load_small("sin_s", sinT_d, [128, T])
    colm_s = load_small("colm_s", colmask_d, [128, 64])
    halsel_s = load_small("halsel_s", halsel_d, [128, 8])
    joinc = sb("joinc", [128, 1], F32)
    add("dve", lambda e: e.memset(joinc[:], 0.0), list(b_small), [b_prm])

    lams = sb("lams", [128, L, 2], F32)
    neglam = sb("neglam", [128, L], F32)
    gsub2 = sb("gsub2", [128, L], F32)
    for l in range(L):
        lam_init = 0.8 - 0.6 * math.exp(-0.3 * l)
        for m in range(2):
            add("dve", lambda e, l=l, m=m: e.tensor_tensor(out=lam_s[:, l, 2 * m, :], in0=lam_s[:, l, 2 * m, :],
                                                          in1=lam_s[:, l, 2 * m + 1, :], op=ALU.mult), [b_prm], [b_prm])
            add("dve", lambda e, l=l, m=m: e.reduce_sum(out=lams[:, l, m:m + 1], in_=lam_s[:, l, 2 * m, :],
                                                       axis=mybir.AxisListType.X), [b_prm], [b_prm])
        add("act", lambda e, l=l: e.activation(out=lams[:, l, :], in_=lams[:, l, :], func=AF.Exp), [b_prm], [b_prm])
        add("dve", lambda e, l=l: e.tensor_tensor(out=neglam[:, l:l + 1], in0=lams[:, l, 1:2], in1=lams[:, l, 0:1],
                                                 op=ALU.subtract), [b_prm], [b_prm])
        add("dve", lambda e, l=l, li=lam_init: e.tensor_scalar(out=neglam[:, l:l + 1], in0=neglam[:, l:l + 1],
                                                              scalar1=-li, scalar2=None, op0=ALU.add), [b_prm], [b_prm])
        add("dve", lambda e, l=l, li=lam_init: e.tensor_scalar(out=gsub2[:, l:l + 1], in0=gsub_s[:, l:l + 1],
                                                              scalar1=1.0 - li, scalar2=None, op0=ALU.mult), [b_prm], [b_prm])

    xT = [sb("xT%d" % g, [128, 8, T], F32) for g in range(2)]
    b_xT = [[Buf("xT%d_%d" % (g, j)) for j in range(8)] for g in range(2)]
    hT = [sb("hT%d" % g, [128, 8, T], BF16) for g in range(2)]
    b_hT = [Buf("hT%d" % g) for g in range(2)]
    mixT = [sb("mixT%d" % g, [128, 8, T], BF16) for g in range(2)]
    b_mix = [[Buf("mix%d_%d" % (g, j)) for j in range(8)] for g in range(2)]
    NW = 3
    wpool = [sb("wp%d" % i, [128, 8, 512], BF16) for i in range(NW)]
    b_wp = [Buf("wp%d" % i) for i in range(NW)]
    wp_i = [0]

    def wtile(src_ap, ncols, nk=8):
        i = wp_i[0] % NW
        wp_i[0] += 1
        t, b = wpool[i], b_wp[i]
        add("pool", lambda e: e.dma_start(out=t[:, 0:nk, 0:ncols], in_=src_ap.rearrange("(k p) c -> p k c", p=128)),
            (), [b], dma=True)
        return t, b

    qd = sb("qd", [128, 2, T], F32); b_qd = Buf("qd_cacc_stage")
    stage1 = qd[:, :, :].rearrange("p a b -> p (a b)")

    class _Stage:
        def __getitem__(self, key):
            p, sl, c = key
            return stage1[p, c]
    stage = _Stage()
    b_stage = [b_qd, b_qd]
    rstd = sb("rstd", [128, T], F32); b_rstd = Buf("rstd")
    tmpf = [sb("tmpf%d" % i, [128, T], F32) for i in range(4)]
    b_tmpf = [Buf("tmpf%d" % i) for i in range(4)]
    tf_i = [0]

    def tmp():
        i = tf_i[0] % 4
        tf_i[0] += 1
        return tmpf[i], b_tmpf[i]

    sqb = [sb("sqb%d" % i, [128, T], BF16) for i in range(2)]
    b_sqb = [Buf("sqb%d" % i) for i in range(2)]
    sq_i = [0]

    def sqt():
        i = sq_i[0] % 2
        sq_i[0] += 1
        return sqb[i], b_sqb[i]

    for g in range(2):
        for tb in range(4):
            for half in range(2):
                stg, bstg = tmp()
                add("sp", lambda e, g=g, tb=tb, half=half, stg=stg: e.dma_start(
                    out=stg[:, :], in_=xin[g * T + tb * 128: g * T + (tb + 1) * 128, half * 512:(half + 1) * 512]), (), [bstg], dma=True)
                pt, pb = gp()
                for jj in range(4):
                    add("pe", lambda e, pt=pt, stg=stg, jj=jj: e.transpose(out=pt[:, jj * 128:(jj + 1) * 128],
                                                                           in_=stg[:, jj * 128:(jj + 1) * 128], identity=ident[:]),
                        [bstg, b_ident], [pb])
                evac(xT[g][:, half * 4:half * 4 + 4, tb * 128:(tb + 1) * 128],
                     pt[:, :].rearrange("p (j t) -> p j t", j=4), [pb], [b_xT[g][half * 4 + jj] for jj in range(4)])

    sil = sb("sil", [128, 8, 2], BF16); b_sil = Buf("sil")
    add("act", lambda e: e.activation(out=sil[:], in_=cv_s[:], func=AF.Silu), [b_prm], [b_sil])
    modv = sb("modv", [128, L, 48, 2], F32)
    gsc = sb("gsc", [128, L, 2, 8, 2], F32)
    b_mod = [Buf("mod%d" % l) for l in range(L)]

    def modulation(l):
        pt, pb = banks[3]
        for ti in range(12):
            wt, wb = wtile(w_mod[l, :, ti * 512:(ti + 1) * 512], 512)
            for cc in range(4):
                n = ti * 4 + cc
                for k in range(8):
                    add("pe", lambda e, wt=wt, cc=cc, k=k, n=n, pt=pt: e.matmul(pt[:, n * 2:n * 2 + 2], lhsT=wt[:, k, cc * 128:(cc + 1) * 128],
                                                                             rhs=sil[:, k, :], start=(k == 0), stop=(k == 7)),
                        [wb, b_sil], [pb])
            yield
        add("dve", lambda e, pt=pt: e.tensor_tensor(out=modv[:, l, :, :], in0=pt[:, 0:96].rearrange("p (n v) -> p n v", v=2),
                                                   in1=bmod_s[:, l, :].unsqueeze(2).to_broadcast([128, 48, 2]), op=ALU.add),
            [pb, b_prm], [b_mod[l]])
        for ni, (gsrc, off) in enumerate(((gmix_s, 8), (gff_s, 32))):
            add("dve", lambda e, ni=ni, gsrc=gsrc, off=off: e.scalar_tensor_tensor(
                out=gsc[:, l, ni, :, :], in0=modv[:, l, off:off + 8, :], scalar=1.0,
                in1=gsrc[:, l, :].unsqueeze(2).to_broadcast([128, 8, 2]), op0=ALU.add, op1=ALU.mult),
                [b_mod[l], b_prm], [b_mod[l]])

    def mcol(l, which, j, v):
        off = {"sh1": 0, "sc1": 8, "g1": 16, "sh2": 24, "sc2": 32, "g2": 40}[which]
        return modv[:, l, off + j, v:v + 1]

    def rmsnorm_mod(l, g, ni):
        pt, pb = gp()
        for j in range(8):
            sq, bq = sqt()
            add("act", lambda e, sq=sq, j=j: e.activation(out=sq[:], in_=xT[g][:, j, :], func=AF.Square), [b_xT[g][j]], [bq])
            add("pe", lambda e, sq=sq, j=j, pt=pt: e.matmul(pt[:, :], lhsT=ones_b[:, :], rhs=sq[:], start=(j == 0), stop=(j == 7)),
                [bq, b_ones_b], [pb])
        add("act", lambda e, pt=pt: e.activation(out=rstd[:], in_=pt[:, :], func=AF.Ln, bias=epsc[:, 0:1], scale=1.0 / D),
            [pb, b_epsc], [b_rstd])
        add("act", lambda e: e.activation(out=rstd[:], in_=rstd[:], func=AF.Exp, scale=-0.5), [b_rstd], [b_rstd])
        for j in range(8):
            tt, tb_ = tmp()
            add("dve", lambda e, tt=tt, j=j: e.scalar_tensor_tensor(out=tt[:], in0=xT[g][:, j, :], scalar=gsc[:, l, ni, j, g:g + 1],
                                                                    in1=rstd[:], op0=ALU.mult, op1=ALU.mult),
                [b_xT[g][j], b_rstd, b_mod[l]], [tb_])
            add("act", lambda e, tt=tt, j=j: e.activation(out=hT[g][:, j, :], in_=tt[:], func=AF.Identity,
                                                          bias=mcol(l, "sh1" if ni == 0 else "sh2", j, g), scale=1.0),
                [tb_, b_mod[l]], [b_hT[g]])

    def stat_rstd(src_list, nfeat, dst, b_dst, reads, K=128, n=T):
        pt, pb = gp()
        for i, src in enumerate(src_list):
            tt, tb_ = tmp()
            add("act", lambda e, tt=tt, src=src: e.activation(out=tt[0:K, 0:n], in_=src, func=AF.Square), reads, [tb_])
            add("pe", lambda e, tt=tt, i=i, pt=pt: e.matmul(pt[0:K, 0:n], lhsT=ones_f[0:K, 0:K], rhs=tt[0:K, 0:n],
                                                          start=(i == 0), stop=(i == len(src_list) - 1)),
                [tb_, b_ones_f], [pb])
        add("act", lambda e, pt=pt: e.activation(out=dst, in_=pt[0:K, 0:n], func=AF.Ln, bias=epsc[0:K, 0:1], scale=1.0 / nfeat),
            [pb, b_epsc], [b_dst])
        add("act", lambda e: e.activation(out=dst, in_=dst, func=AF.Exp, scale=-0.5), [b_dst], [b_dst])

    KTb = sb("KTb", [128, 4, DEC_S + PAST], BF16); b_KTh = [Buf("KT%d" % h) for h in range(4)]
    VAb = sb("VAb", [128, 18, 4, 72], BF16); b_VAk = [Buf("VA%d" % k) for k in range(18)]
    KTc2 = [sb("KTc%d" % m, [128, 4, T], BF16) for m in range(2)]
    b_KTc = [Buf("KTc%d" % m) for m in range(3)]
    VAc = [sb("VAc%d" % m, [128, 4, 4, 72], BF16) for m in range(3)]; b_VAc = [Buf("VAc%d" % m) for m in range(3)]
    QT2 = [[sb("QT%d_%d" % (g, m), [128, 4, T], BF16) for m in range(2)] for g in range(2)]
    b_QT = [[Buf("QT%d_%d" % (g, m)) for m in range(3)] for g in range(2)]
    add("pool", lambda e: e.memset(VAb[:, :, :, 64:72], 1.0), (), b_VAk)
    add("pool", lambda e: e.memset(KTb[64:128, :, :], 0.0), (), b_KTh)
    for m in range(3):
        add("pool", lambda e, m=m: e.memset(VAc[m][:, :, :, 64:72], 1.0), (), [b_VAc[m]])

    def q_ap(g, m, h, p_lo, p_hi, c0=0, n=T):
        if m == 0:
            return QT2[g][0][p_lo:p_hi, h, c0:c0 + n]
        if m == 1:
            return QT2[g][1][p_lo:p_hi, h, c0:c0 + n]
        return QT2[g][0][64 + p_lo:64 + p_hi, h, c0:c0 + n]

    def kc_ap(m, h, p_lo, p_hi, c0, n):
        if m == 0:
            return KTc2[0][p_lo:p_hi, h, c0:c0 + n]
        if m == 1:
            return KTc2[1][p_lo:p_hi, h, c0:c0 + n]
        return KTc2[0][64 + p_lo:64 + p_hi, h, c0:c0 + n]
    Eb = [sb("E%d" % i, [128, T], BF16) for i in range(3)]
    b_E = [Buf("E%d" % i) for i in range(3)]
    e_i = [0]
    NSET = 2
    rsum_s = [sb("rsum0", [128, T], F32)] * 2; b_rsum_s = [Buf("rsum0")] * 2
    rsb0 = sb("rsb0", [128, T], BF16)
    rsb_row = [64, 64]
    b_rsb_s = [Buf("rsb0")] * 2
    dao = sb("dao", [128, T], F32); b_dao = Buf("dao")
    set_i = [0]
    ckvT = sb("ckvT", [128, DEC_S + PAST], BF16); b_ckvT = Buf("ckvT")
    ckvTc = ckvT; b_ckvTc = b_ckvT
    qdn = sb("qdn", [128, 2, T], BF16); b_qdn = Buf("qdn")
    wq_s = sb("wq_s", [128, 2, 384], BF16); wqp_s = sb("wqp_s", [128, 2, 384], BF16); wkv_s = sb("wkv_s", [128, 512], BF16)
    b_wsm = Buf("wsmall")
    gpad = [sb("gpad%d" % g, [128, 2, 2, 15 + 256 + 15], BF16) for g in range(2)]
    b_gpad = [Buf("gpad%d" % g) for g in range(2)]
    for g in range(2):
        add("pool", lambda e, g=g: e.memset(gpad[g][:], 0.0), (), [b_gpad[g]])
    cacc = qd; b_cacc = b_qd
    halo = sb("halo", [128, 2, 4, 30], BF16); b_halo = Buf("halo")
    NTBT = 2
    tbt = [sb("tbt%d" % i, [128, A2, 64], BF16) for i in range(NTBT)] * (2 // NTBT)
    b_tbt = [Buf("tbt%d" % i) for i in range(NTBT)] * (2 // NTBT)
    b_payA = [Buf("payA%d" % l) for l in range(L)]
    b_payB = [Buf("payB%d" % l) for l in range(L)]
    b_payoA = [Buf("payoA%d" % l) for l in range(L)]
    b_payoB = [Buf("payoB%d" % l) for l in range(L)]
    b_tz = [Buf("tz%d" % l) for l in range(L)]

    pending_post = [None]
    import os as _os2
    WARM_LAT = int(_os2.environ.get("WARM_LAT", "0"))
    WARM_CTX = int(_os2.environ.get("WARM_CTX", "0"))
    WARM_N = int(_os2.environ.get("WARM_N", "256"))
    BURST = int(_os2.environ.get("BURST", "20"))

    def warm(n):
        for _ in range(n):
            add("pe", lambda e: e.matmul(banks[7][0][:, 512 - WARM_N:512], lhsT=identb[:, 0:128], rhs=identb[:, 0:128], start=True, stop=True), (), ())

    def flush_post():
        if pending_post[0] is not None:
            f = pending_post[0]
            pending_post[0] = None
            f()

    def attention(kt_fn, q_ap, va_fn, nkb, nq, scale, reads, extra_fn=None):
        at, ab = acb()
        pend = []

        def score(kb):
            pt, pb = gp()
            ex = extra_fn(kb) if extra_fn is not None else []
            add("pe", lambda e, pt=pt, kb=kb: e.matmul(pt[:, 0:nq], lhsT=kt_fn(kb), rhs=q_ap, start=True, stop=(len(ex) == 0)),
                reads, [pb])
            for i, (lh, rh, rd) in enumerate(ex):
                add("pe", lambda e, pt=pt, lh=lh, rh=rh, i=i: e.matmul(pt[:, 0:nq], lhsT=lh, rhs=rh, start=False, stop=(i == len(ex) - 1)),
                    rd, [pb])
            i = e_i[0] % 3
            e_i[0] += 1
            add("act", lambda e, pt=pt, i=i: e.activation(out=Eb[i][:, 0:nq], in_=pt[:, 0:nq], func=AF.Exp, scale=scale), [pb], [b_E[i]])
            return i

        for kb in range(min(2, nkb)):
            pend.append(score(kb))
        flush_post()
        for kb in range(nkb):
            if kb + 2 < nkb:
                pend.append(score(kb + 2))
            warm(WARM_LAT)
            i = pend[kb]
            add("pe", lambda e, at=at, kb=kb, i=i: e.matmul(at[0:65, 0:nq], lhsT=va_fn(kb), rhs=Eb[i][:, 0:nq], start=(kb == 0), stop=(kb == nkb - 1)),
                reads + [b_E[i]], [ab])
        return at, ab

    def normalize_rep(at, ab, nq, dst, b_dst_list):
        r, br = tmp()
        add("act", lambda e: e.activation(out=r[0:64, 0:nq], in_=at[0:64, 256:256 + nq], func=AF.Ln), [ab], [br])
        add("act", lambda e: e.activation(out=r[0:64, 0:nq], in_=r[0:64, 0:nq], func=AF.Exp, scale=-1.0), [br], [br])
        add("dve", lambda e: e.tensor_tensor(out=dst, in0=at[0:64, 0:nq], in1=r[0:64, 0:nq], op=ALU.mult), [ab, br], b_dst_list)

    def normalize(at, ab, nq, dst, b_dst_list, k=0, c0=0):
        rsum, b_rsum, b_rsb, rr = rsum_s[k], b_rsum_s[k], b_rsb_s[k], rsb_row[k]
        add("act", lambda e: e.activation(out=rsum[64:65, 0:nq], in_=at[64:65, c0:c0 + nq], func=AF.Ln), [ab], [b_rsum])
        add("act", lambda e: e.activation(out=rsb0[rr:rr + 1, 0:nq], in_=rsum[64:65, 0:nq], func=AF.Exp, scale=-1.0), [b_rsum], [b_rsb])
        pt, pb = gp()
        add("pe", lambda e: e.matmul(pt[0:64, 0:nq], lhsT=ones_b[rr:rr + 1, 0:64], rhs=rsb0[rr:rr + 1, 0:nq], start=True, stop=True),
            [b_rsb, b_ones_b], [pb])
        bc, bbc = tmp()
        add("act", lambda e: e.activation(out=bc[0:64, 0:nq], in_=pt[0:64, 0:nq], func=AF.Copy), [pb], [bbc])
        add("dve", lambda e: e.tensor_tensor(out=dst, in0=at[0:64, c0:c0 + nq], in1=bc[0:64, 0:nq], op=ALU.mult), [ab, bbc], b_dst_list)

    def stream_attention(jobs, nkb=18, nq=T, side_cb=None, burst=0):
        blocks = [(ji, kb) for ji in range(len(jobs)) for kb in range(nkb)]
        N = len(blocks)
        pend = {}
        posts = {}

        def do_score(idx):
            ji, kb = blocks[idx]
            J = jobs[ji]
            if kb == 0 and J.get("pre") is not None:
                J["pre"]()
            pt, pb = gp()
            ex = J["extra_fn"](kb) if J.get("extra_fn") is not None else []
            add("pe", lambda e, pt=pt, kb=kb, J=J: e.matmul(pt[:, 0:nq], lhsT=J["kt_fn"](kb), rhs=J["q_ap"], start=True, stop=(len(ex) == 0)),
                J["reads"], [pb])
            for i2, (lh, rh, rd) in enumerate(ex):
                add("pe", lambda e, pt=pt, lh=lh, rh=rh, i2=i2: e.matmul(pt[:, 0:nq], lhsT=lh, rhs=rh, start=False, stop=(i2 == len(ex) - 1)),
                    rd, [pb])
            i = e_i[0] % 3
            e_i[0] += 1
            add("act", lambda e, pt=pt, i=i, J=J: e.activation(out=Eb[i][:, 0:nq], in_=pt[:, 0:nq], func=AF.Exp, scale=J["scale"]), [pb], [b_E[i]])
            pend[idx] = i

        for idx in range(min(2, N)):
            do_score(idx)
        if burst:
            wpt, wpb = gp()
            for _ in range(burst):
                add("pe", lambda e, wpt=wpt: e.matmul(wpt[:, :], lhsT=identb[:, :], rhs=hT[1][:, 0, :], start=True, stop=True),
                    [b_identb, b_hT[1]], [wpb])
        for idx in range(N):
            if idx + 2 < N:
                do_score(idx + 2)
            ji, kb = blocks[idx]
            J = jobs[ji]
            if kb == 0:
                J["acc"] = acb()
            at, ab = J["acc"]
            i = pend.pop(idx)
            add("pe", lambda e, at=at, kb=kb, i=i, J=J: e.matmul(at[0:65, 0:nq], lhsT=J["va_fn"](kb), rhs=Eb[i][:, 0:nq],
                                                              start=(kb == 0), stop=(kb == nkb - 1)), J["reads"] + [b_E[i]], [ab])
            if kb == nkb - 1:
                if J.get("post") is not None:
                    posts.setdefault(min(idx + 2, N - 1), []).append(J["post"])
                if side_cb is not None:
                    side_cb()
            for f in posts.pop(idx, []):
                f()

    def input_proj(l):
        res = {}
        add("pool", lambda e: e.dma_start(out=wq_s[:], in_=wqup[l].rearrange("(k p) c -> p k c", p=128)), (), [b_wsm], dma=True)
        add("pool", lambda e: e.dma_start(out=wqp_s[:], in_=wqupp[l].rearrange("(k p) c -> p k c", p=128)), (), [b_wsm], dma=True)
        add("pool", lambda e: e.dma_start(out=wkv_s[:], in_=wkvup[l]), (), [b_wsm], dma=True)

        def fm(wt, wb, c0, M, g, n0=0, n=T):
            pt, pb = gp()
            for k in range(8):
                add("pe", lambda e, pt=pt, k=k: e.matmul(pt[0:M, 0:n], lhsT=wt[:, k, c0:c0 + M], rhs=hT[g][:, k, n0:n0 + n],
                                                        start=(k == 0), stop=(k == 7)), [wb, b_hT[g]], [pb])
            return pt, pb

        def tm(wt, wb, c0, N, g, tb):
            pt, pb = gp()
            for k in range(8):
                add("pe", lambda e, pt=pt, k=k: e.matmul(pt[:, 0:N], lhsT=hT[g][:, k, tb * 128:(tb + 1) * 128], rhs=wt[:, k, c0:c0 + N],
                                                        start=(k == 0), stop=(k == 7)), [wb, b_hT[g]], [pb])
            return pt, pb

        def rope_evac(ptA, pbA, ptB, pbB, p0, p1, dst, wlist):
            t1, tb1 = tmp()
            t2, tb2 = tmp()
            add("dve", lambda e: e.tensor_tensor(out=t1[p0:p1, :], in0=ptA[p0:p1, :], in1=cos_s[p0:p1, :], op=ALU.mult), [pbA, b_prm], [tb1])
            add("dve", lambda e: e.tensor_tensor(out=t2[p0:p1, :], in0=ptB[p0:p1, :], in1=sin_s[p0:p1, :], op=ALU.mult), [pbB, b_prm], [tb2])
            add("dve", lambda e: e.tensor_tensor(out=dst, in0=t1[p0:p1, :], in1=t2[p0:p1, :], op=ALU.add), [tb1, tb2], wlist)

        groups = [0, 1] if do_lat else [0]
        wA, bA = wtile(w_in[l, :, 0:512], 512)
        if do_lat:
            wF1, bF1 = wtile(w_inp[l, :, 0:512], 512)
        for h in range(4):
            pt, pb = fm(wA, bA, h * 64, 64, 0)
            evac(QT2[0][0][0:64, h, :], pt[0:64, :], [pb], [b_QT[0][0]])
            pt, pb = fm(wA, bA, 256 + h * 64, 64, 0)
            evac(KTc2[0][0:64, h, :], pt[0:64, :], [pb], [b_KTc[0]])
        if do_lat:
            for h in range(4):
                ptA, pbA = fm(wA, bA, h * 64, 64, 1)
                ptB, pbB = fm(wF1, bF1, h * 64, 64, 1)
                rope_evac(ptA, pbA, ptB, pbB, 0, 64, QT2[1][0][0:64, h, :], [b_QT[1][0]])
                ptA, pbA = fm(wA, bA, 256 + h * 64, 64, 1)
                ptB, pbB = fm(wF1, bF1, 256 + h * 64, 64, 1)
                tk, tkb = sqt()
                rope_evac(ptA, pbA, ptB, pbB, 0, 64, tk[0:64, :], [tkb])
                add("sp", lambda e, tk=tk, h=h: e.dma_start(out=payA_in[l].ap()[R_DAK + h * 64:R_DAK + (h + 1) * 64, :], in_=tk[0:64, :]),
                    [tkb], [b_payA[l]], dma=True)
        wB, bB = wtile(w_in[l, :, 512:1024], 512)
        for g in groups:
            for c in range(2):
                pt, pb = fm(wB, bB, 256 + c * 128, 128, g)
                evac(qd[:, c, :], pt[:, :], [pb], [b_qd])
            stat_rstd([qd[:, 0, :], qd[:, 1, :]], 256, rstd[:], b_rstd, [b_qd])
            for c in range(2):
                add("dve", lambda e, c=c: e.scalar_tensor_tensor(out=qdn[:, c, :], in0=qd[:, c, :], scalar=gq_s[:, l, c:c + 1], in1=rstd[:],
                                                               op0=ALU.mult, op1=ALU.mult), [b_qd, b_rstd, b_prm], [b_qdn])
            for h in range(4):
                pt, pb = gp()
                for c in range(2):
                    add("pe", lambda e, pt=pt, c=c, h=h: e.matmul(pt[0:96, :], lhsT=wq_s[:, c, h * 96:(h + 1) * 96], rhs=qdn[:, c, :],
                                                                 start=(c == 0), stop=(c == 1)), [b_wsm, b_qdn], [pb])
                if g == 0:
                    evac(QT2[0][1][0:96, h, :], pt[0:96, :], [pb], [b_QT[0][1]])
                else:
                    pt2, pb2 = gp()
                    for c in range(2):
                        add("pe", lambda e, pt2=pt2, c=c, h=h: e.matmul(pt2[0:96, :], lhsT=wqp_s[:, c, h * 96:(h + 1) * 96], rhs=qdn[:, c, :],
                                                                       start=(c == 0), stop=(c == 1)), [b_wsm, b_qdn], [pb2])
                    evac(QT2[1][1][0:64, h, :], pt[0:64, :], [pb], [b_QT[1][1]])
                    rope_evac(pt, pb, pt2, pb2, 64, 96, QT2[1][1][64:96, h, :], [b_QT[1][1]])
        for g in groups:
            for tb in range(4):
                pt, pb = tm(wB, bB, 0, 256, g, tb)
                if g == 0:
                    ostage, b_ostage = tmp()
                    add("act", lambda e, pt=pt, ostage=ostage: e.activation(out=ostage[:, 0:256], in_=pt[:, 0:256], func=AF.Copy), [pb], [b_ostage])
                    s, tl = tb // 2, (tb % 2) * 128
                    add("sp", lambda e, s=s, tl=tl, ostage=ostage: e.dma_start(out=o_dav[s, l, :, tl:tl + 128, :].rearrange("h t d -> t h d"),
                                                               in_=ostage[:, 0:256].rearrange("p (h d) -> p h d", h=4)), [b_ostage], (), dma=True)
                    add("dve", lambda e, pt=pt, tb=tb: e.tensor_copy(out=VAc[0][:, tb, :, 0:64], in_=pt[:, 0:256].rearrange("p (h d) -> p h d", h=4)),
                        [pb], [b_VAc[0]])
                else:
                    tv, tvb = sqt()
                    evac(tv[:, 0:256], pt[:, 0:256], [pb], [tvb])
                    add("sp", lambda e, tv=tv, tb=tb: e.dma_start(out=payB_in[l].ap()[R_V + tb * 128:R_V + (tb + 1) * 128, 0:256], in_=tv[:, 0:256]),
                        [tvb], [b_payB[l]], dma=True)
        wC, bC = wtile(w_in[l, :, 1024:1440], 416)
        if do_lat:
            wF2, bF2 = wtile(w_inp[l, :, 512:608], 96)
        for g in groups:
            pt, pb = fm(wC, bC, 0, 128, g)
            kvd, b_kvd = tmp()
            evac(kvd[:, :], pt[:, :], [pb], [b_kvd])
            stat_rstd([kvd[:, :]], 128, rstd[:], b_rstd, [b_kvd])
            dstc = ckvTc if g == 0 else sqb[0]
            if g == 0:
                add("dve", lambda e, kvd=kvd: e.scalar_tensor_tensor(out=ckvTc[:, 0:T], in0=kvd[:, :], scalar=gkvc_s[:, l:l + 1], in1=rstd[:],
                                                            op0=ALU.mult, op1=ALU.mult), [b_kvd, b_rstd, b_prm], [b_ckvTc])
            else:
                tk, tkb = sqt()
                add("dve", lambda e, tk=tk, kvd=kvd: e.scalar_tensor_tensor(out=tk[:, :], in0=kvd[:, :], scalar=gkvc_s[:, l:l + 1], in1=rstd[:],
                                                                   op0=ALU.mult, op1=ALU.mult), [b_kvd, b_rstd, b_prm], [tkb])
                add("sp", lambda e, tk=tk: e.dma_start(out=payA_in[l].ap()[R_CKV:R_CKV + 128, :], in_=tk[:, :]), [tkb], [b_payA[l]], dma=True)
            pt, pb = fm(wC, bC, 64, 96, g)
            if g == 0:
                for h in range(4):
                    evac(KTc2[1][64:96, h, :], pt[64:96, :], [pb], [b_KTc[1]])
            else:
                ptB, pbB = fm(wF2, bF2, 0, 96, 1)
                tk, tkb = sqt()
                rope_evac(pt, pb, ptB, pbB, 64, 96, tk[64:96, :], [tkb])
                add("sp", lambda e, tk=tk: e.dma_start(out=payA_in[l].ap()[R_KR:R_KR + 32, :], in_=tk[64:96, :]), [tkb], [b_payA[l]], dma=True)
            for h in range(4):
                pt, pb = fm(wC, bC, 160 + h * 64, 64, g)
                evac(QT2[g][0][64:128, h, :], pt[0:64, :], [pb], [b_QT[g][2]])
        for tb in range(4):
            pt, pb = tm(wC, bC, 0, 160, 0, tb)
            s, tl = tb // 2, (tb % 2) * 128
            tt, tb_ = tmp()
            add("act", lambda e, pt=pt, tt=tt: e.activation(out=tt[:, 0:128], in_=pt[:, 0:128], func=AF.Square), [pb], [tb_])
            add("dve", lambda e, tt=tt: e.reduce_sum(out=tt[:, 200:201], in_=tt[:, 0:128], axis=mybir.AxisListType.X), [tb_], [tb_])
            add("act", lambda e, tt=tt: e.activation(out=tt[:, 201:202], in_=tt[:, 200:201], func=AF.Ln, bias=epsc[:, 0:1], scale=1.0 / 128),
                [tb_, b_epsc], [tb_])
            add("act", lambda e, tt=tt: e.activation(out=tt[:, 202:203], in_=tt[:, 201:202], func=AF.Exp, scale=-0.5), [tb_], [tb_])
            add("dve", lambda e, pt=pt, tt=tt: e.scalar_tensor_tensor(out=tt[:, 256:384], in0=pt[:, 0:128], scalar=tt[:, 202:203],
                                                                      in1=gkvr_s[:, l, :], op0=ALU.mult, op1=ALU.mult),
                [pb, tb_, b_prm], [tb_])
            add("act", lambda e, pt=pt, tt=tt: e.activation(out=tt[:, 384:416], in_=pt[:, 128:160], func=AF.Copy), [pb, tb_], [tb_])
            add("sp", lambda e, s=s, tl=tl, tt=tt: e.dma_start(out=o_ckv[s, l, tl:tl + 128, :], in_=tt[:, 256:384]), [tb_], (), dma=True)
            add("sp", lambda e, s=s, tl=tl, tt=tt: e.dma_start(out=o_kr[s, l, tl:tl + 128, :], in_=tt[:, 384:416]), [tb_], (), dma=True)
        for h in range(4):
            pt, pb = gp()
            add("pe", lambda e, pt=pt, h=h: e.matmul(pt[0:64, :], lhsT=wkv_s[:, h * 128:h * 128 + 64], rhs=ckvTc[:, 0:T], start=True, stop=True),
                [b_wsm, b_ckvTc], [pb])
            evac(KTc2[1][0:64, h, :], pt[0:64, :], [pb], [b_KTc[1]])
        for tb in range(4):
            pt, pb = gp()
            add("pe", lambda e, pt=pt, tb=tb: e.matmul(pt[:, 0:256], lhsT=ckvTc[:, tb * 128:(tb + 1) * 128],
                                                      rhs=wkv_s[:, :].rearrange("p (h c) -> p h c", h=4)[:, :, 64:128], start=True, stop=True),
                [b_wsm, b_ckvTc], [pb])
            evac(VAc[1][:, tb, :, 0:64], pt[:, 0:256].rearrange("p (h d) -> p h d", h=4), [pb], [b_VAc[1]])
        wD, bD = wtile(w_in[l, :, 1440:1952], 512)
        for h in range(4):
            pt, pb = fm(wD, bD, h * 64, 64, 0)
            evac(KTc2[0][64:128, h, :], pt[0:64, :], [pb], [b_KTc[2]])
            if do_lat:
                pt, pb = fm(wD, bD, h * 64, 64, 1)
                tk, tkb = sqt()
                evac(tk[0:64, :], pt[0:64, :], [pb], [tkb])
                add("sp", lambda e, tk=tk, h=h: e.dma_start(out=payA_in[l].ap()[R_NAK + h * 64:R_NAK + (h + 1) * 64, :], in_=tk[0:64, :]),
                    [tkb], [b_payA[l]], dma=True)
        for tb in range(4):
            pt, pb = tm(wD, bD, 0, 512, 0, tb)
            s, tl = tb // 2, (tb % 2) * 128
            ostage, b_ostage = tmp()
            add("act", lambda e, pt=pt, ostage=ostage: e.activation(out=ostage[:, :], in_=pt[:, :], func=AF.Copy), [pb], [b_ostage])
            add("sp", lambda e, s=s, tl=tl, ostage=ostage: e.dma_start(out=o_nak[s, l, :, tl:tl + 128, :].rearrange("h t d -> t h d"),
                                                       in_=ostage[:, 0:256].rearrange("p (h d) -> p h d", h=4)), [b_ostage], (), dma=True)
            add("sp", lambda e, s=s, tl=tl, ostage=ostage: e.dma_start(out=o_nav[s, l, :, tl:tl + 128, :].rearrange("h t d -> t h d"),
                                                       in_=ostage[:, 256:512].rearrange("p (h d) -> p h d", h=4)), [b_ostage], (), dma=True)
            add("dve", lambda e, pt=pt, tb=tb: e.tensor_copy(out=VAc[2][:, tb, :, 0:64], in_=pt[:, 256:512].rearrange("p (h d) -> p h d", h=4)),
                [pb], [b_VAc[2]])
            if do_lat:
                pt, pb = tm(wD, bD, 256, 256, 1, tb)
                tv, tvb = sqt()
                evac(tv[:, 0:256], pt[:, 0:256], [pb], [tvb])
                add("sp", lambda e, tv=tv, tb=tb: e.dma_start(out=payB_in[l].ap()[R_V + tb * 128:R_V + (tb + 1) * 128, 256:512], in_=tv[:, 0:256]),
                    [tvb], [b_payB[l]], dma=True)
        wA2, bA2 = wtile(w_in[l, :, 256:512], 256)
        for tb in range(4):
            pt, pb = tm(wA2, bA2, 0, 256, 0, tb)
            s, tl = tb // 2, (tb % 2) * 128
            ostage, b_ostage = tmp()
            add("act", lambda e, pt=pt, ostage=ostage: e.activation(out=ostage[:, 0:256], in_=pt[:, 0:256], func=AF.Copy), [pb], [b_ostage])
            add("sp", lambda e, s=s, tl=tl, ostage=ostage: e.dma_start(out=o_dak[s, l, :, tl:tl + 128, :].rearrange("h t d -> t h d"),
                                                       in_=ostage[:, 0:256].rearrange("p (h d) -> p h d", h=4)), [b_ostage], (), dma=True)
        wE, bE = wtile(w_in[l, :, 1952:2464], 512)
        for g in groups:
            for c in range(2):
                pa, pba = fm(wE, bE, c * 128, 128, g)
                pbm, pbb = fm(wE, bE, 256 + c * 128, 128, g)
                tt, tb_ = tmp()
                add("act", lambda e, tt=tt, pbm=pbm: e.activation(out=tt[:], in_=pbm[:, :], func=AF.Sigmoid), [pbb], [tb_])
                add("dve", lambda e, tt=tt, pa=pa, c=c, g=g: e.tensor_tensor(out=gpad[g][:, c, :, 15:15 + 256],
                                                                           in0=pa[:, :].rearrange("p (s t) -> p s t", s=2),
                                                                           in1=tt[:].rearrange("p (s t) -> p s t", s=2), op=ALU.mult),
                    [pba, tb_], [b_gpad[g]])
        if do_lat:
            add("dve", lambda e: e.tensor_copy(out=halo[:, :, 0, 0:15], in_=gpad[1][:, :, 0, 15:30]), [b_gpad[1]], [b_halo])
            add("dve", lambda e: e.tensor_copy(out=halo[:, :, 0, 15:30], in_=gpad[1][:, :, 1, 256:271]), [b_gpad[1]], [b_halo])
            hdst = dram_ap(payB_in[l], R_HALO * T, [[30, 128], [128 * 30, 2], [1, 30]])
            add("sp", lambda e: e.dma_start(out=hdst, in_=halo[:, :, 0, :]), [b_halo], [b_payB[l]], dma=True)
            add("pool", lambda e: e.collective_compute("AllGather", ALU.bypass, replica_groups=[[0, 1, 2, 3], [4, 5, 6, 7]],
                                                       ins=[payA_in[l].ap().opt()], outs=[payA_out[l].ap().opt()]),
                [b_payA[l]], [b_payoA[l]], dma="cc")
            add("pool", lambda e: e.collective_compute("AllGather", ALU.bypass, replica_groups=[[0, 1, 2, 3], [4, 5, 6, 7]],
                                                       ins=[payB_in[l].ap().opt()], outs=[payB_out[l].ap().opt()]),
                [b_payB[l]], [b_payoB[l]], dma="cc")

    def da_post(l, g, h, at1, ab1, at2, ab2, nq=T, q0=0, repl=False, c1=0, c2=0):
        o1, bo1 = dao, b_dao
        o2, bo2 = rsum_s[0], b_rsum_s[0]
        if repl:
            normalize_rep(at1, ab1, nq, o1[0:64, 0:nq], [bo1])
            normalize_rep(at2, ab2, nq, o2[0:64, 0:nq], [bo2])
        else:
            normalize(at1, ab1, nq, o1[0:64, 0:nq], [bo1], 0, c1)
            normalize(at2, ab2, nq, o2[0:64, 0:nq], [bo2], 1 if nq <= 256 else 0, c2)
        add("dve", lambda e: e.scalar_tensor_tensor(out=o1[0:64, 0:nq], in0=o2[0:64, 0:nq], scalar=neglam[0:64, l:l + 1], in1=o1[0:64, 0:nq],
                                                    op0=ALU.mult, op1=ALU.add), [bo1, bo2, b_prm], [bo1])
        stat_rstd([o1[0:64, 0:nq]], 64, rstd[0:64, 0:nq], b_rstd, [bo1], K=64, n=nq)
        p0 = (h % 2) * 64
        add("dve", lambda e: e.scalar_tensor_tensor(out=mixT[g][p0:p0 + 64, h // 2, q0:q0 + nq], in0=o1[0:64, 0:nq], scalar=gsub2[0:64, l:l + 1],
                                                    in1=rstd[0:64, 0:nq], op0=ALU.mult, op1=ALU.mult),
            [bo1, b_rstd, b_prm], [b_mix[g][h // 2]])

    def plain_post(g, m, h, at, ab, nq=T, q0=0, repl=False):
        if repl:
            p0 = (h % 2) * 64
            ch = 2 * m + h // 2
            normalize_rep(at, ab, nq, mixT[g][p0:p0 + 64, ch, q0:q0 + nq], [b_mix[g][ch]])
            return
        k = (set_i[0] % NSET) if nq <= 256 else 0
        set_i[0] += 1
        o1, bo1 = tmp()
        normalize(at, ab, nq, o1[0:64, 0:nq], [bo1], k)
        p0 = (h % 2) * 64
        ch = 2 * m + h // 2
        add("act", lambda e: e.activation(out=mixT[g][p0:p0 + 64, ch, q0:q0 + nq], in_=o1[0:64, 0:nq], func=AF.Copy), [bo1], [b_mix[g][ch]])

    def ctx_attention(l, side_gen=None):
        scales = (32 ** -0.5, 96 ** -0.5, 64 ** -0.5)
        rows = {0: None, 1: (0, 96), 2: (0, 64)}
        jobs = []
        for s in range(2):
            for m in range(3):
                for h in range(4):
                    for mp in ((0, 1) if m == 0 else (0,)):
                        jobs.append(dict(s=s, m=m, h=h, mp=mp))
        n = len(jobs)

        def stage_a(j):
            s_, m, h, mp = j["s"], j["m"], j["h"], j["mp"]
            q0 = s_ * 256
            lo, hi = (mp * 32, mp * 32 + 32) if m == 0 else rows[m]
            pt, pb = gp()
            rd = [b_KTc[m], b_QT[0][m]]
            for kb in range(2):
                add("pe", lambda e, pt=pt, kb=kb, m=m, h=h, lo=lo, hi=hi, q0=q0: e.matmul(
                    pt[:, kb * 256:(kb + 1) * 256], lhsT=kc_ap(m, h, lo, hi, q0 + kb * 128, 128), rhs=q_ap(0, m, h, lo, hi, q0, 256),
                    start=True, stop=True), rd, [pb])
            i = e_i[0] % 3
            e_i[0] += 1
            add("act", lambda e, pt=pt, i=i, m=m: e.activation(out=Eb[i][:, :], in_=pt[:, :], func=AF.Exp, scale=scales[m]), [pb], [b_E[i]])
            j["e"] = i

        def stage_b(j):
            s_, m, h = j["s"], j["m"], j["h"]
            at, ab = acb()
            i = j["e"]
            for kb in range(2):
                add("pe", lambda e, at=at, kb=kb, i=i, m=m, h=h, s_=s_: e.matmul(at[0:65, 0:256], lhsT=VAc[m][:, s_ * 2 + kb, h, 0:65],
                                                                               rhs=Eb[i][:, kb * 256:(kb + 1) * 256], start=(kb == 0), stop=(kb == 1)),
                    [b_VAc[m], b_E[i]], [ab])
            for kb in range(2):
                add("pe", lambda e, at=at, kb=kb, i=i: e.matmul(at[0:64, 256:512], lhsT=ones_b[:, 0:64], rhs=Eb[i][:, kb * 256:(kb + 1) * 256],
                                                               start=(kb == 0), stop=(kb == 1)), [b_ones_b, b_E[i]], [ab])
            j["acc"] = (at, ab)

        def stage_c(idx):
            j = jobs[idx]
            s_, m, h, mp = j["s"], j["m"], j["h"], j["mp"]
            q0 = s_ * 256
            if m == 0:
                if mp == 1:
                    a1 = jobs[idx - 1]["acc"]
                    da_post(l, 0, h, a1[0], a1[1], j["acc"][0], j["acc"][1], nq=256, q0=q0, repl=True)
            else:
                plain_post(0, m, h, j["acc"][0], j["acc"][1], nq=256, q0=q0, repl=True)

        for i in range(n + 2):
            if i < n:
                stage_a(jobs[i])
            warm(WARM_CTX)
            if 0 <= i - 1 < n:
                stage_b(jobs[i - 1])
            if 0 <= i - 2 < n:
                stage_c(i - 2)
            if side_gen is not None and i % 2 == 1:
                next(side_gen, None)

    def cache_T(src_ap, ncol, dst_fn, wlist):
        stg, bstg = tmp()
        add("sp", lambda e: e.dma_start(out=stg[:, 0:2 * ncol].rearrange("p (t c) -> p t c", t=2),
                                        in_=src_ap.rearrange("(t p) c -> p t c", p=128)), (), [bstg], dma=True)
        for tb in range(2):
            pt, pb = gp()
            add("pe", lambda e, pt=pt, tb=tb: e.transpose(out=pt[0:ncol, 0:128], in_=stg[:, tb * ncol:(tb + 1) * ncol], identity=ident[:]),
                [bstg, b_ident], [pb])
            evac(dst_fn(tb), pt[0:ncol, 0:128], [pb], wlist)

    def load_V(l, c0, cache_ap):
        for rk in range(4):
            for tb in range(4):
                src = dram_ap(payB_out[l], (rk * PB + R_V + tb * 128) * T + c0, [[T, 128], [64, 4], [1, 64]])
                add("sp", lambda e, rk=rk, tb=tb, src=src: e.dma_start(out=VAb[:, rk * 4 + tb, :, 0:64], in_=src), [b_payoB[l]], [b_VAk[rk * 4 + tb]], dma=True)
        for tb in range(2):
            add("pool", lambda e, tb=tb: e.dma_start(out=VAb[:, 16 + tb, :, 0:64], in_=cache_ap[:, tb * 128:(tb + 1) * 128, :].rearrange("h t d -> t h d")),
                (), [b_VAk[16 + tb]], dma=True)

    def load_KT(l, row0, nrows, p0, heads=True):
        for h in range(4):
            r = row0 + (h * nrows if heads else 0)
            src = dram_ap(payA_out[l], r * T, [[T, nrows], [PA * T, 4], [1, T]])
            add("sp", lambda e, h=h, src=src: e.dma_start(out=KTb[p0:p0 + nrows, h, 0:DEC_S].rearrange("p (r t) -> p r t", r=4), in_=src),
                [b_payoA[l]], [b_KTh[h]], dma=True)

    def lat_attention(l, side_gen=None, mod_gen=None):
        rdv = list(b_VAk)

        def side_step(n=1):
            if side_gen is not None:
                for _ in range(n):
                    next(side_gen, None)
            if mod_gen is not None:
                next(mod_gen, None)

        load_KT(l, R_DAK, 64, 0)
        for h in range(4):
            cache_T(c_dak[l, h], 64, lambda tb, h=h: KTb[0:64, h, DEC_S + tb * 128:DEC_S + (tb + 1) * 128], [b_KTh[h]])
        load_V(l, 0, c_dav[l])
        jobs = []
        for h in range(4):
            for qh in range(2):
                idx = h * 2 + qh
                tq, btq = tbt[idx // 5], b_tbt[idx // 5]
                qb = tq[:, :, :].rearrange("p a w -> p (a w)")[:, (idx % 5) * 512:(idx % 5 + 1) * 512]
                add("pool", lambda e, qb=qb: e.memset(qb, 0.0), (), [btq])
                add("pool", lambda e, qb=qb, h=h, qh=qh: e.tensor_copy(out=qb[0:32, 0:256], in_=QT2[1][0][0:32, h, qh * 256:(qh + 1) * 256]),
                    [b_QT[1][0]], [btq])
                add("pool", lambda e, qb=qb, h=h, qh=qh: e.tensor_copy(out=qb[32:64, 256:512], in_=QT2[1][0][32:64, h, qh * 256:(qh + 1) * 256]),
                    [b_QT[1][0]], [btq])
                J = dict(kt_fn=(lambda kb, h=h: KTb[0:128, h, kb * 128:(kb + 1) * 128]), q_ap=qb,
                         va_fn=(lambda kb, h=h: VAb[:, kb, h, 0:65]), scale=32 ** -0.5, reads=rdv + [b_KTh[h], btq])
                J["post"] = (lambda h=h, qh=qh, J=J: da_post(l, 1, h, J["acc"][0], J["acc"][1], J["acc"][0], J["acc"][1],
                                                             nq=256, q0=qh * 256, c1=0, c2=256))
                jobs.append(J)
        stream_attention(jobs, side_cb=lambda: side_step(1), burst=BURST)
        src = dram_ap(payA_out[l], R_CKV * T, [[T, 128], [PA * T, 4], [1, T]])
        add("sp", lambda e, src=src: e.dma_start(out=ckvT[:, 0:DEC_S].rearrange("p (r t) -> p r t", r=4), in_=src), [b_payoA[l]], [b_ckvT], dma=True)
        cache_T(c_ckv[l], 128, lambda tb: ckvT[:, DEC_S + tb * 128:DEC_S + (tb + 1) * 128], [b_ckvT])
        load_KT(l, R_KR, 32, 64, heads=False)
        for h in range(4):
            cache_T(c_kr[l], 32, lambda tb, h=h: KTb[64:96, h, DEC_S + tb * 128:DEC_S + (tb + 1) * 128], [b_KTh[h]])
        for h in range(4):
            for cb in range(5):
                n0 = cb * 512
                n = min(512, DEC_S + PAST - n0)
                pt, pb = gp()
                add("pe", lambda e, pt=pt, h=h, n0=n0, n=n: e.matmul(pt[0:64, 0:n], lhsT=wkv_s[:, h * 128:h * 128 + 64], rhs=ckvT[:, n0:n0 + n],
                                                                    start=True, stop=True), [b_wsm, b_ckvT], [pb])
                evac(KTb[0:64, h, n0:n0 + n], pt[0:64, 0:n], [pb], [b_KTh[h]])
        for kb in range(18):
            pt, pb = gp()
            add("pe", lambda e, pt=pt, kb=kb: e.matmul(pt[:, 0:256], lhsT=ckvT[:, kb * 128:(kb + 1) * 128],
                                                      rhs=wkv_s[:, :].rearrange("p (h c) -> p h c", h=4)[:, :, 64:128], start=True, stop=True),
                [b_wsm, b_ckvT], [pb])
            evac(VAb[:, kb, :, 0:64], pt[:, 0:256].rearrange("p (h d) -> p h d", h=4), [pb], [b_VAk[kb]])
        jobs = []
        for h in range(4):
            J = dict(kt_fn=(lambda kb, h=h: KTb[0:96, h, kb * 128:(kb + 1) * 128]), q_ap=q_ap(1, 1, h, 0, 96),
                     va_fn=(lambda kb, h=h: VAb[:, kb, h, 0:65]), scale=96 ** -0.5, reads=rdv + [b_KTh[h], b_QT[1][1]])
            J["post"] = (lambda h=h, J=J: plain_post(1, 1, h, J["acc"][0], J["acc"][1]))
            jobs.append(J)
        stream_attention(jobs, side_cb=lambda: side_step(2))
        load_KT(l, R_NAK, 64, 0)
        for h in range(4):
            cache_T(c_nak[l, h], 64, lambda tb, h=h: KTb[0:64, h, DEC_S + tb * 128:DEC_S + (tb + 1) * 128], [b_KTh[h]])
            add("pool", lambda e, h=h: e.dma_start(out=KTb[64:96, h, 0:DEC_S], in_=rowind_d), (), [b_KTh[h]], dma=True)
            add("pool", lambda e, h=h: e.memset(KTb[64:96, h, DEC_S:DEC_S + PAST], 0.0), (), [b_KTh[h]])
            evac(QT2[1][0][0:64, h, :], QT2[1][0][64:128, h, :], [b_QT[1][2], b_QT[1][0]], [b_QT[1][0], b_QT[1][2]])
            add("pool", lambda e, h=h: e.dma_start(out=QT2[1][0][64:96, h, :], in_=rowsel_d), (), [b_QT[1][0], b_QT[1][2]], dma=True)
        load_V(l, 256, c_nav[l])
        jobs = []
        for h in range(4):
            ti = h % 2

            def pre(h=h, ti=ti):
                for j in range(2):
                    src = bass.AP(tz_all, l * TZ_SZ + (h * (A2 + 1) + (1 - j)) * 64 * TZL + 63, [[TZL - 1, 64], [64 * TZL, A2], [1, 64]])
                    add("pool", lambda e, j=j, src=src, ti=ti: e.dma_start(out=tbt[ti][j * 64:(j + 1) * 64, :, :], in_=src), (), [b_tbt[ti]], dma=True)
                add("dve", lambda e, ti=ti: e.scalar_tensor_tensor(out=tbt[ti][:], in0=tbt[ti][:], scalar=8.0,
                                                                   in1=colm_s[:, :].unsqueeze(1).to_broadcast([128, A2, 64]),
                                                                   op0=ALU.mult, op1=ALU.add), [b_tbt[ti], b_prm], [b_tbt[ti]])

            def extra(kb, ti=ti):
                if kb >= 16:
                    return []
                a0 = 37 - 2 * kb
                return [(identb[:, :], tbt[ti][:, a0:a0 + 8, :], [b_identb, b_tbt[ti]])]

            J = dict(kt_fn=(lambda kb, h=h: KTb[0:96, h, kb * 128:(kb + 1) * 128]), q_ap=QT2[1][0][0:96, h, :],
                     va_fn=(lambda kb, h=h: VAb[:, kb, h, 0:65]), scale=64 ** -0.5, reads=rdv + [b_KTh[h], b_QT[1][2]],
                     extra_fn=extra, pre=pre)
            J["post"] = (lambda h=h, J=J: plain_post(1, 2, h, J["acc"][0], J["acc"][1]))
            jobs.append(J)
        stream_attention(jobs, side_cb=lambda: side_step(2))

    def conv_module(l, g):
        if g == 1:
            for c in range(2):
                src = dram_ap(payB_out[l], R_HALO * T + c * 128 * 30, [[30, 128], [PB * T, 4], [1, 30]])
                add("sp", lambda e, src=src, c=c: e.dma_start(out=halo[:, c, :, :], in_=src), [b_payoB[l]], [b_halo], dma=True)
            for side in range(2):
                dst = gpad[1][:, :, 0, 0:15] if side == 0 else gpad[1][:, :, 1, 271:286]
                for rk in range(4):
                    srcv = halo[:, :, rk, 15:30] if side == 0 else halo[:, :, rk, 0:15]
                    sc = halsel_s[:, side * 4 + rk:side * 4 + rk + 1]
                    if rk == 0:
                        add("dve", lambda e, dst=dst, srcv=srcv, sc=sc: e.tensor_scalar(out=dst, in0=srcv, scalar1=sc, scalar2=None, op0=ALU.mult),
                            [b_halo, b_prm, b_gpad[1]], [b_gpad[1]])
                    else:
                        add("dve", lambda e, dst=dst, srcv=srcv, sc=sc: e.scalar_tensor_tensor(out=dst, in0=srcv, scalar=sc, in1=dst,
                                                                                              op0=ALU.mult, op1=ALU.add),
                            [b_halo, b_prm, b_gpad[1]], [b_gpad[1]])
        for c in range(2):
            for j in range(31):
                if g == 0:
                    src = gpad[0][:, c, :, j:j + 256]
                    dst = cacc[:, c, :].rearrange("p (s t) -> p s t", s=2)
                    ops = [(src, dst)]
                else:
                    ops = []
                    n_a = max(0, min(256, 271 - j))
                    if n_a > 0:
                        ops.append((gpad[1][:, c, 0, j:j + n_a], cacc[:, c, 0:n_a]))
                    if n_a < 256:
                        ops.append((gpad[1][:, c, 1, 15 + (n_a + j - 271):15 + (256 + j - 271)], cacc[:, c, n_a:256]))
                    n_b = max(0, min(256, 271 - (256 + j)))
                    if n_b > 0:
                        ops.append((gpad[1][:, c, 0, 256 + j:256 + j + n_b], cacc[:, c, 256:256 + n_b]))
                    ops.append((gpad[1][:, c, 1, 15 + (256 + n_b + j - 271):15 + (512 + j - 271)], cacc[:, c, 256 + n_b:512]))
                if j % 4 == 3:
                    yield
                for (src, dst) in ops:
                    if j == 0:
                        add("dve", lambda e, src=src, dst=dst, c=c: e.tensor_scalar(out=dst, in0=src, scalar1=dw_s[:, l, c, 0:1], scalar2=cb_s[:, l, c:c + 1],
                                                                                    op0=ALU.mult, op1=ALU.add), [b_gpad[g], b_prm, b_cacc], [b_cacc])
                    else:
                        add("dve", lambda e, src=src, dst=dst, c=c, j=j: e.scalar_tensor_tensor(out=dst, in0=src, scalar=dw_s[:, l, c, j:j + 1], in1=dst,
                                                                                                op0=ALU.mult, op1=ALU.add), [b_gpad[g], b_prm, b_cacc], [b_cacc])
        p1, pb1 = gp()
        p2, pb2 = gp()
        for c in range(2):
            tt, tb_ = tmp()
            add("act", lambda e, tt=tt, c=c: e.activation(out=tt[:], in_=cacc[:, c, :], func=AF.Square), [b_cacc], [tb_])
            add("pe", lambda e, c=c: e.matmul(p1[:, :], lhsT=ones_f[:, :], rhs=cacc[:, c, :], start=(c == 0), stop=(c == 1)), [b_cacc, b_ones_f], [pb1])
            add("pe", lambda e, tt=tt, c=c: e.matmul(p2[:, :], lhsT=ones_f[:, :], rhs=tt[:], start=(c == 0), stop=(c == 1)), [tb_, b_ones_f], [pb2])
        mean, bmean = tmp()
        var, bvar = tmp()
        add("act", lambda e: e.activation(out=mean[:], in_=p1[:, :], func=AF.Identity, scale=1.0 / 256), [pb1], [bmean])
        add("dve", lambda e: e.tensor_tensor(out=var[:], in0=mean[:], in1=mean[:], op=ALU.mult), [bmean], [bvar])
        add("dve", lambda e: e.scalar_tensor_tensor(out=var[:], in0=p2[:, :], scalar=1.0 / 256, in1=var[:], op0=ALU.mult, op1=ALU.subtract),
            [pb2, bvar], [bvar])
        add("act", lambda e: e.activation(out=var[:], in_=var[:], func=AF.Ln, bias=epsc[:, 0:1], scale=1.0), [bvar, b_epsc], [bvar])
        add("act", lambda e: e.activation(out=var[:], in_=var[:], func=AF.Exp, scale=-0.5), [bvar], [bvar])
        for c in range(2):
            add("dve", lambda e, c=c: e.tensor_tensor(out=cacc[:, c, :], in0=cacc[:, c, :], in1=mean[:], op=ALU.subtract), [b_cacc, bmean], [b_cacc])
            add("dve", lambda e, c=c: e.tensor_tensor(out=cacc[:, c, :], in0=cacc[:, c, :], in1=var[:], op=ALU.mult), [b_cacc, bvar], [b_cacc])
            add("act", lambda e, c=c: e.activation(out=cacc[:, c, :], in_=cacc[:, c, :], func=AF.Identity, bias=lnb_s[:, l, c:c + 1],
                                                   scale=lng_s[:, l, c:c + 1]), [b_cacc, b_prm], [b_cacc])
            tt, tb_ = tmp()
            add("act", lambda e, tt=tt, c=c: e.activation(out=tt[:], in_=cacc[:, c, :], func=AF.Sigmoid), [b_cacc], [tb_])
            add("dve", lambda e, tt=tt, c=c: e.tensor_tensor(out=mixT[g][:, 6 + c, :], in0=cacc[:, c, :], in1=tt[:], op=ALU.mult),
                [b_cacc, tb_], [b_mix[g][6 + c]])

    def out_proj(l, groups):
        for ti in range(2):
            wt, wb = wtile(w_out[l, :, ti * 512:(ti + 1) * 512], 512)
            for cc in range(4):
                oc = ti * 4 + cc
                for g in groups:
                    pt, pb = gp()
                    for k in range(8):
                        add("pe", lambda e, pt=pt, k=k, cc=cc, wt=wt, g=g: e.matmul(pt[:, :], lhsT=wt[:, k, cc * 128:(cc + 1) * 128], rhs=mixT[g][:, k, :],
                                                                                   start=(k == 0), stop=(k == 7)), [wb, b_mix[g][k]], [pb])
                    add("dve", lambda e, pt=pt, g=g, oc=oc: e.scalar_tensor_tensor(out=xT[g][:, oc, :], in0=pt[:, :], scalar=mcol(l, "g1", oc, g),
                                                                                 in1=xT[g][:, oc, :], op0=ALU.mult, op1=ALU.add),
                        [pb, b_mod[l], b_xT[g][oc]], [b_xT[g][oc]])

    def ffn(l, groups):
        for blk in range(4):
            for ti in range(2):
                wt, wb = wtile(w_ff1[l, :, blk * 1024 + ti * 512: blk * 1024 + (ti + 1) * 512], 512)
                for cc in range(4):
                    fc = ti * 4 + cc
                    for g in groups:
                        pt, pb = gp()
                        for k in range(8):
                            add("pe", lambda e, pt=pt, k=k, cc=cc, wt=wt, g=g: e.matmul(pt[:, :], lhsT=wt[:, k, cc * 128:(cc + 1) * 128], rhs=hT[g][:, k, :],
                                                                                       start=(k == 0), stop=(k == 7)), [wb, b_hT[g]], [pb])
                        sq, bq = sqt()
                        add("act", lambda e, pt=pt, sq=sq: e.activation(out=sq[:], in_=pt[:, :], func=AF.Relu), [pb], [bq])
                        add("dve", lambda e, sq=sq, g=g, fc=fc: e.tensor_tensor(out=mixT[g][:, fc, :], in0=sq[:], in1=sq[:], op=ALU.mult),
                            [bq], [b_mix[g][fc]])
            for ti in range(2):
                wt, wb = wtile(w_ff2[l, blk * 1024:(blk + 1) * 1024, ti * 512:(ti + 1) * 512], 512)
                for cc in range(4):
                    oc = ti * 4 + cc
                    for g in groups:
                        pt, pb = gp()
                        for k in range(8):
                            add("pe", lambda e, pt=pt, k=k, cc=cc, wt=wt, g=g: e.matmul(pt[:, :], lhsT=wt[:, k, cc * 128:(cc + 1) * 128], rhs=mixT[g][:, k, :],
                                                                                       start=(k == 0), stop=(k == 7)), [wb, b_mix[g][k]], [pb])
                        add("dve", lambda e, pt=pt, g=g, oc=oc: e.scalar_tensor_tensor(out=xT[g][:, oc, :], in0=pt[:, :], scalar=mcol(l, "g2", oc, g),
                                                                                     in1=xT[g][:, oc, :], op0=ALU.mult, op1=ALU.add),
                            [pb, b_mod[l], b_xT[g][oc]], [b_xT[g][oc]])

    def dump8(name, t, bufs):
        if name not in dbg:
            return
        o = dout("dbg_" + name, [128, 8, T])
        for j in range(8):
            tt, tb_ = tmp()
            add("dve", lambda e, tt=tt, j=j: e.tensor_copy(out=tt[:], in_=t[:, j, :]), [bufs[j]], [tb_])
            add("sp", lambda e, tt=tt, j=j: e.dma_start(out=o[:, j, :], in_=tt[:]), [tb_], (), dma=True)
        dbg_out[name] = [128, 8, T]

    groups = [0, 1] if do_lat else [0]
    import os as _os
    _l1 = _os.environ.get("L1STAGES")
    _stages0 = stages
    for l in range(nlayers):
        stages = _stages0 if (l == 0 or _l1 is None) else set(_l1.split(","))
        if "mod" in stages and l == 0:
            for _ in modulation(0):
                pass
        if "norm1" in stages:
            for g in groups:
                rmsnorm_mod(l, g, 0)
        if "proj" in stages:
            input_proj(l)
        side = modulation(l + 1) if ("mod" in stages and l + 1 < nlayers) else None
        if "ctxattn" in stages:
            ctx_attention(l, None if (do_lat and "latattn" in stages) else side)
        if "conv" in stages:
            for _ in conv_module(l, 0):
                pass
        dump8("mix0%d" % l, mixT[0], b_mix[0])
        if do_lat:
            side2 = conv_module(l, 1) if "conv" in stages else None
            if "latattn" in stages:
                lat_attention(l, side2, side)
            if side2 is not None:
                for _ in side2:
                    pass
        if side is not None:
            for _ in side:
                pass
            dump8("mix1%d" % l, mixT[1], b_mix[1])
        if "outproj" in stages:
            out_proj(l, groups)
        for g in groups:
            dump8("xattn%d%d" % (g, l), xT[g], b_xT[g])
        if "norm2" in stages:
            for g in groups:
                rmsnorm_mod(l, g, 1)
        if "ffn" in stages:
            ffn(l, groups)
        for g in groups:
            dump8("xffn%d%d" % (g, l), xT[g], b_xT[g])
    for g in groups:
        if "final" not in stages:
            continue
        pt, pb = gp()
        for j in range(8):
            sq, bq = sqt()
            add("act", lambda e, sq=sq, j=j, g=g: e.activation(out=sq[:], in_=xT[g][:, j, :], func=AF.Square), [b_xT[g][j]], [bq])
            add("pe", lambda e, sq=sq, j=j, pt=pt: e.matmul(pt[:, :], lhsT=ones_b[:, :], rhs=sq[:], start=(j == 0), stop=(j == 7)), [bq, b_ones_b], [pb])
        add("act", lambda e, pt=pt: e.activation(out=rstd[:], in_=pt[:, :], func=AF.Ln, bias=epsc[:, 0:1], scale=1.0 / D), [pb, b_epsc], [b_rstd])
        add("act", lambda e: e.activation(out=rstd[:], in_=rstd[:], func=AF.Exp, scale=-0.5), [b_rstd], [b_rstd])
        for j in range(8):
            add("dve", lambda e, j=j, g=g: e.scalar_tensor_tensor(out=xT[g][:, j, :], in0=xT[g][:, j, :], scalar=gfin_s[:, j:j + 1], in1=rstd[:],
                                                                  op0=ALU.mult, op1=ALU.mult), [b_xT[g][j], b_rstd, b_prm], [b_xT[g][j]])
    for g in groups:
        for tb in range(4):
            for half in range(2):
                pt, pb = gp()
                for jj in range(4):
                    j = half * 4 + jj
                    add("pe", lambda e, pt=pt, j=j, jj=jj, tb=tb, g=g: e.transpose(out=pt[:, jj * 128:(jj + 1) * 128],
                                                                                   in_=xT[g][:, j, tb * 128:(tb + 1) * 128], identity=ident[:]),
                        [b_xT[g][j], b_ident], [pb])
                stg, bstg = tmp()
                evac(stg[:, :], pt[:, :], [pb], [bstg])
                add("sp", lambda e, g=g, tb=tb, half=half, stg=stg: e.dma_start(
                    out=y_out[g * T + tb * 128:g * T + (tb + 1) * 128, half * 512:(half + 1) * 512], in_=stg[:, :]), [bstg], (), dma=True)
    S.emit(st)
    st.close()
    return nc, dbg_out


def _colT(v, nchunk):
    return np.ascontiguousarray(v.reshape(nchunk, 128).T)


def prepare_inputs(inp):
    f = lambda a: np.ascontiguousarray(np.asarray(a, dtype=np.float32))
    x_prompt, x_sample, c = f(inp["x_prompt"]), f(inp["x_sample"]), f(inp["c"])
    w_in = f(inp["w_in"])
    shared = {}
    shared["bmodT"] = np.ascontiguousarray(np.stack([_colT(f(inp["b_mod"])[l], 48) for l in range(L)], 1))
    shared["gmixT"] = np.ascontiguousarray(np.stack([_colT(f(inp["g_norm_mix"])[l], 8) for l in range(L)], 1))
    shared["gffT"] = np.ascontiguousarray(np.stack([_colT(f(inp["g_norm_ff"])[l], 8) for l in range(L)], 1))
    shared["gfinT"] = _colT(f(inp["g_final"]), 8)
    shared["w_mod"] = f(inp["w_mod"])
    shared["w_in"] = w_in
    idx = []
    for base in (0, 256):
        for h in range(4):
            for m in range(2):
                idx += [base + h * 64 + m * 32 + P32[j] for j in range(32)]
    idx += list(range(1088, 1152))
    idx += [1152 + P32[j] for j in range(32)]
    shared["w_inp"] = np.ascontiguousarray(w_in[:, :, idx])
    shared["w_out"] = f(inp["w_out"])
    shared["w_ff1"] = f(inp["w_ff1"])
    shared["w_ff2"] = f(inp["w_ff2"])
    wq = f(inp["w_mla_qup"])
    shared["wqup"] = wq
    wqp = np.zeros_like(wq)
    for h in range(4):
        for j in range(32):
            wqp[:, :, h * 96 + 64 + j] = wq[:, :, h * 96 + 64 + P32[j]]
    shared["wqupp"] = wqp
    shared["wkvup"] = f(inp["w_mla_kvup"])
    shared["gqT"] = np.ascontiguousarray(np.stack([_colT(f(inp["g_mla_q"])[l], 2) for l in range(L)], 1))
    shared["gkvc"] = np.ascontiguousarray(f(inp["g_mla_kv"]).T)
    shared["gkvr"] = np.ascontiguousarray(np.broadcast_to(f(inp["g_mla_kv"])[None], (128, L, 128)))
    gs = f(inp["g_da_subln"])
    shared["gsubc"] = np.ascontiguousarray(np.concatenate([gs.T, gs.T], 0))
    lamv = np.stack([f(inp["da_lambda_q1"]), f(inp["da_lambda_k1"]), f(inp["da_lambda_q2"]), f(inp["da_lambda_k2"])], 1)
    shared["lamv"] = np.ascontiguousarray(np.broadcast_to(lamv[None], (128, L, 4, 32)))
    dw = f(inp["conv_dw"])
    shared["dwT"] = np.ascontiguousarray(dw.reshape(L, 31, 2, 128).transpose(3, 0, 2, 1))
    for nm, key in (("cbT", "conv_b"), ("lngT", "conv_ln_g"), ("lnbT", "conv_ln_b")):
        shared[nm] = np.ascontiguousarray(f(inp[key]).reshape(L, 2, 128).transpose(2, 0, 1))
    shared["ident"] = np.eye(128, dtype=np.float32)
    w = np.arange(GRID_W)
    cs = np.clip(w - 8, 0, GRID_W - 16)
    col_ok = (w[None, :] >= cs[:, None]) & (w[None, :] < cs[:, None] + 16)
    cm = np.where(col_ok.T, 0.0, -BIG * 8).astype(np.float32)
    shared["colmask"] = np.ascontiguousarray(np.concatenate([cm, cm], 0))
    ri = np.zeros((32, DEC_S), np.float32)
    ri[np.arange(DEC_S) // 64, np.arange(DEC_S)] = 1.0
    shared["rowind"] = ri
    rpb = f(inp["na_rpb"])
    half = 8
    freqs = (10000.0 ** (-np.arange(half, dtype=np.float32) * 2.0 / 16)).astype(np.float32)
    maps = []
    for core in range(NCORE):
        b, r = core // 4, core % 4
        m = dict(shared)
        m["xin"] = np.ascontiguousarray(np.concatenate([x_prompt[2 * core], x_prompt[2 * core + 1], x_sample[b, r * T:(r + 1) * T]], 0))
        cv = np.stack([f(inp["c_ctx"]), c[b]], 1)
        m["cvT"] = np.ascontiguousarray(cv.reshape(8, 128, 2).transpose(1, 0, 2))
        m["c_dak"] = f(inp["cache_da_k"])[b]
        m["c_dav"] = f(inp["cache_da_v"])[b]
        m["c_ckv"] = f(inp["cache_mla_ckv"])[b]
        m["c_kr"] = f(inp["cache_mla_krope"])[b]
        m["c_nak"] = f(inp["cache_na_k"])[b]
        m["c_nav"] = f(inp["cache_na_v"])[b]
        t = r * T + np.arange(T)
        rows = (t // GRID_W).astype(np.float32)
        cols = (t % GRID_W).astype(np.float32)
        ang = np.concatenate([rows[None, :] * freqs[:, None], rows[None, :] * freqs[:, None],
                              cols[None, :] * freqs[:, None], cols[None, :] * freqs[:, None]], 0)
        cos32 = np.cos(ang).astype(np.float32)
        sin32 = np.sin(ang).astype(np.float32)
        sgn = np.concatenate([-np.ones(8), np.ones(8), -np.ones(8), np.ones(8)]).astype(np.float32)[:, None]
        m["cosT"] = np.ascontiguousarray(np.tile(cos32, (4, 1)))
        m["sinT"] = np.ascontiguousarray(np.tile(sin32 * sgn, (4, 1)))
        r0 = r * 8
        qrow = r0 + np.arange(T) // 64
        start = np.clip(qrow - 4, 0, NROWS - 8)
        rk = np.arange(32)
        ok = (rk[:, None] >= start[None, :]) & (rk[:, None] < start[None, :] + 8)
        m["rowsel"] = np.where(ok, 0.0, -BIG * 8).astype(np.float32)
        P = np.zeros((L, 4, A2 + 1, TZL), np.float32)
        for a2 in range(A2):
            a = 45 - a2 - r0
            if 0 <= a <= 14:
                P[:, :, a2, 48:79] = rpb[:, :, a, ::-1]
        m["tz_rep"] = np.ascontiguousarray(np.broadcast_to(P.reshape(L * 4 * (A2 + 1), 1, TZL), (L * 4 * (A2 + 1), 64, TZL)))
        hs = np.zeros((128, 8), np.float32)
        if r > 0:
            hs[:, r - 1] = 1.0
        if r < 3:
            hs[:, 4 + r + 1] = 1.0
        m["halsel"] = hs
        maps.append(m)
    return maps


_NC_CACHE = {}


def kernel(**inputs):
    maps = prepare_inputs(inputs)
    if "nc" not in _NC_CACHE:
        _NC_CACHE["nc"] = build()[0]
    nc = _NC_CACHE["nc"]
    res = run_bass_kernel_spmd(nc, maps, core_ids=list(range(NCORE)))
    R = res.results
    y_prompt = np.zeros((NB, SEQ, D), np.float32)
    y_sample = np.zeros((DEC_B, DEC_S, D), np.float32)
    outs = {k: [] for k in ("o_dak", "o_dav", "o_ckv", "o_kr", "o_nak", "o_nav")}
    for core in range(NCORE):
        b, r = core // 4, core % 4
        y = R[core]["y_out"]
        y_prompt[2 * core] = y[0:256]
        y_prompt[2 * core + 1] = y[256:512]
        y_sample[b, r * T:(r + 1) * T] = y[512:1024]
        for k in outs:
            outs[k].append(R[core][k])
    cat = lambda k: np.ascontiguousarray(np.concatenate(outs[k], 0).astype(np.float32))
    return (y_prompt, y_sample, cat("o_dak"), cat("o_dav"), cat("o_ckv"), cat("o_kr"), cat("o_nak"), cat("o_nav"))
```

```python
import contextlib
import math
import numpy as np
import concourse.bass as bass
import concourse.mybir as mybir
from concourse.bass_utils import run_bass_kernel_spmd

F32 = mybir.dt.float32
BF16 = mybir.dt.bfloat16
ALU = mybir.AluOpType
AF = mybir.ActivationFunctionType

ENGINES = ("pe", "act", "dve", "pool", "sp")

D = 1024
L = 2
NB = 16
SEQ = 256
DEC_B = 2
DEC_S = 2048
PAST = 256
GRID_W = 64
NROWS = DEC_S // GRID_W
EPS = 1e-6
T = 512
NCORE = 8
BIG = 30000.0
A2 = 46
TZL = 127
PA = 672
PB = 527
R_DAK, R_CKV, R_KR, R_NAK = 0, 256, 384, 416
R_V, R_HALO = 0, 512
P32 = [8, 9, 10, 11, 12, 13, 14, 15, 0, 1, 2, 3, 4, 5, 6, 7,
       24, 25, 26, 27, 28, 29, 30, 31, 16, 17, 18, 19, 20, 21, 22, 23]


class Buf:
    __slots__ = ("name", "last_w", "readers", "excl")

    def __init__(self, name, excl=False):
        self.name = name
        self.last_w = None
        self.readers = []
        self.excl = excl


class Op:
    __slots__ = ("eng", "fn", "deps", "dma", "sig", "sem", "val")

    def __init__(self, eng, fn, dma):
        self.eng = eng
        self.fn = fn
        self.dma = dma
        self.deps = []
        self.sig = False
        self.sem = None
        self.val = 0


class Sched:
    def __init__(self, nc, n_dma_slots=16):
        self.nc = nc
        self.ops = []
        self.n_dma_slots = n_dma_slots

    def add(self, eng, fn, reads=(), writes=(), dma=False):
        op = Op(eng, fn, dma)
        ex = [b for b in reads if b.excl]
        if ex:
            reads = [b for b in reads if not b.excl]
            writes = list(writes) + ex
        deps = {}
        for b in reads:
            if b.last_w is not None:
                deps[id(b.last_w)] = b.last_w
        for b in writes:
            if b.last_w is not None:
                deps[id(b.last_w)] = b.last_w
            for r in b.readers:
                deps[id(r)] = r
        for b in reads:
            b.readers.append(op)
        for b in writes:
            b.last_w = op
            b.readers = []
        for d in deps.values():
            if d.eng == "pe" and eng == "pe" and not d.dma and not dma:
                continue
            op.deps.append(d)
            d.sig = True
        self.ops.append(op)
        return op

    def emit(self, stack):
        nc = self.nc
        eng_sem = {e: stack.enter_context(nc.semaphore("s_" + e)) for e in ENGINES}
        dma_engs = sorted({op.eng for op in self.ops if op.dma is True})
        dma_slots = {e: [stack.enter_context(nc.semaphore("d_%s_%d" % (e, i)))
                         for i in range(self.n_dma_slots)] for e in dma_engs}
        cnt = {e: 0 for e in ENGINES}
        dcnt = {e: 0 for e in dma_engs}
        slot_uses = {e: [0] * self.n_dma_slots for e in dma_engs}
        slot_prev = {}
        ncc = 0
        for op in self.ops:
            if op.dma == "cc":
                op.sem = stack.enter_context(nc.semaphore("cc_%d" % ncc))
                ncc += 1
                op.val = 1
            elif op.dma:
                k = dcnt[op.eng] % self.n_dma_slots
                dcnt[op.eng] += 1
                slot_uses[op.eng][k] += 1
                op.sem = dma_slots[op.eng][k]
                op.val = 16 * slot_uses[op.eng][k]
                slot_prev[id(op)] = (op.sem, op.val - 16)
            elif op.sig:
                cnt[op.eng] += 1
                op.sem = eng_sem[op.eng]
                op.val = cnt[op.eng]
        block = stack.enter_context(nc.Block())
        handles = {"pe": block.tensor, "act": block.scalar, "dve": block.vector,
                   "pool": block.gpsimd, "sp": block.sync}
        all_async = [op for op in self.ops if op.dma]

        def run_engine(ename):
            def body(eng):
                waited = {}
                for op in self.ops:
                    if op.eng != ename:
                        continue
                    need = {}
                    for d in op.deps:
                        key = id(d.sem)
                        if key not in need or need[key][1] < d.val:
                            need[key] = (d.sem, d.val)
                    if op.dma is True:
                        s, v = slot_prev[id(op)]
                        if v > 0:
                            key = id(s)
                            if key not in need or need[key][1] < v:
                                need[key] = (s, v)
                    for key, (s, v) in need.items():
                        if waited.get(key, 0) >= v:
                            continue
                        eng.wait_ge(s, v)
                        waited[key] = v
                    ins = op.fn(eng)
                    if op.dma == "cc":
                        ins.then_inc(op.sem)
                    elif op.dma:
                        ins.then_inc(op.sem, 16)
                    elif op.sig:
                        ins.then_inc(op.sem, 1)
                if ename == "sp":
                    last = {}
                    for op in all_async:
                        last[id(op.sem)] = (op.sem, op.val)
                    for key, (s, v) in last.items():
                        if waited.get(key, 0) < v:
                            eng.wait_ge(s, v)
                    for e in ENGINES:
                        if cnt[e] > 0:
                            eng.wait_ge(eng_sem[e], cnt[e])
            handles[ename](body)

        for e in ENGINES:
            run_engine(e)


def build(dbg=(), nlayers=L, do_lat=True, stages=None):
    if stages is None:
        stages = {"mod", "norm1", "proj", "ctxattn", "conv", "latattn", "outproj", "norm2", "ffn", "final"}
    nc = bass.Bass("TRN2", target_bir_lowering=False)
    st = contextlib.ExitStack()
    S = Sched(nc)
    dbg_out = {}

    def din(name, shape, dt=F32):
        return nc.dram_tensor(name, list(shape), dt, kind="ExternalInput").ap()

    def dout(name, shape, dt=F32):
        return nc.dram_tensor(name, list(shape), dt, kind="ExternalOutput").ap()

    xin = din("xin", [2 * T, D])
    cvT = din("cvT", [128, 8, 2])
    bmodT = din("bmodT", [128, L, 48])
    gmixT = din("gmixT", [128, L, 8])
    gffT = din("gffT", [128, L, 8])
    gfinT = din("gfinT", [128, 8])
    w_mod = din("w_mod", [L, D, 6 * D])
    w_in = din("w_in", [L, D, 2464])
    w_inp = din("w_inp", [L, D, 608])
    w_out = din("w_out", [L, D, D])
    w_ff1 = din("w_ff1", [L, D, 4 * D])
    w_ff2 = din("w_ff2", [L, 4 * D, D])
    wqup = din("wqup", [L, 256, 384])
    wqupp = din("wqupp", [L, 256, 384])
    wkvup = din("wkvup", [L, 128, 512])
    gqT = din("gqT", [128, L, 2])
    gkvc = din("gkvc", [128, L])
    gkvr = din("gkvr", [128, L, 128])
    gsubc = din("gsubc", [128, L])
    lamv = din("lamv", [128, L, 4, 32])
    tz_all = din("tz_rep", [L * 4 * (A2 + 1), 64, TZL]).tensor
    dwT = din("dwT", [128, L, 2, 31])
    cbT = din("cbT", [128, L, 2])
    lngT = din("lngT", [128, L, 2])
    lnbT = din("lnbT", [128, L, 2])
    c_dak = din("c_dak", [L, 4, PAST, 64])
    c_dav = din("c_dav", [L, 4, PAST, 64])
    c_ckv = din("c_ckv", [L, PAST, 128])
    c_kr = din("c_kr", [L, PAST, 32])
    c_nak = din("c_nak", [L, 4, PAST, 64])
    c_nav = din("c_nav", [L, 4, PAST, 64])
    cosT_d = din("cosT", [128, T])
    sinT_d = din("sinT", [128, T])
    colmask_d = din("colmask", [128, 64])
    rowsel_d = din("rowsel", [32, T])
    rowind_d = din("rowind", [32, DEC_S])
    halsel_d = din("halsel", [128, 8])
    ident_d = din("ident", [128, 128])
    y_out = dout("y_out", [2 * T, D])
    o_dak = dout("o_dak", [2, L, 4, SEQ, 64])
    o_dav = dout("o_dav", [2, L, 4, SEQ, 64])
    o_ckv = dout("o_ckv", [2, L, SEQ, 128])
    o_kr = dout("o_kr", [2, L, SEQ, 32])
    o_nak = dout("o_nak", [2, L, 4, SEQ, 64])
    o_nav = dout("o_nav", [2, L, 4, SEQ, 64])
    LROWS = 6 * PA + 6 * PB
    pay_all = nc.dram_tensor("pay_all", [L * LROWS, T], BF16)
    O_AIN, O_AOUT, O_BIN, O_BOUT = 0, PA, 5 * PA, 5 * PA + PB

    class _Sub:
        def __init__(self, row0, nrows):
            self.row0, self.nrows = row0, nrows

        def ap(self):
            return pay_all.ap()[self.row0:self.row0 + self.nrows, :]

    payA_in = [_Sub(l * LROWS + O_AIN, PA) for l in range(L)]
    payA_out = [_Sub(l * LROWS + O_AOUT, 4 * PA) for l in range(L)]
    payB_in = [_Sub(l * LROWS + O_BIN, PB) for l in range(L)]
    payB_out = [_Sub(l * LROWS + O_BOUT, 4 * PB) for l in range(L)]
    TZ_SZ = 4 * (A2 + 1) * 64 * TZL

    def dram_ap(base, off, dims):
        if isinstance(base, _Sub):
            return bass.AP(pay_all, base.row0 * T + off, dims)
        return bass.AP(base, off, dims)

    def sb(name, shape, dt):
        return st.enter_context(nc.sbuf_tensor("sb_" + name, list(shape), dt))

    add = S.add

    banks = []
    for i in range(8):
        banks.append((st.enter_context(nc.psum_tensor("ps%d" % i, [128, 512], F32)), Buf("ps%d" % i, excl=True)))
    gp_i = [0]
    ac_i = [0]

    def gp():
        b = banks[(0, 1, 2, 7)[gp_i[0] % 4]]
        gp_i[0] += 1
        return b

    def acb():
        b = banks[4 + ac_i[0] % 3]
        ac_i[0] += 1
        return b

    ev_i = [0]

    def evac(dst, src, r, w, scale=None):
        ev_i[0] += 1
        if ev_i[0] % 2 == 0:
            if scale is None:
                add("act", lambda e: e.activation(out=dst, in_=src, func=AF.Copy), r, w)
            else:
                add("act", lambda e: e.activation(out=dst, in_=src, func=AF.Identity, scale=scale), r, w)
        else:
            if scale is None:
                add("dve", lambda e: e.tensor_copy(out=dst, in_=src), r, w)
            else:
                add("dve", lambda e: e.tensor_scalar(out=dst, in0=src, scalar1=scale, scalar2=None, op0=ALU.mult), r, w)

    ident = sb("ident", [128, 128], F32); b_ident = Buf("ident")
    identb = sb("identb", [128, 128], BF16); b_identb = Buf("identb")
    ones_f = sb("ones_f", [128, 128], F32); b_ones_f = Buf("ones_f")
    ones_b = sb("ones_b", [128, 128], BF16); b_ones_b = Buf("ones_b")
    epsc = sb("epsc", [128, 1], F32); b_epsc = Buf("epsc")
    add("sp", lambda e: e.dma_start(out=ident[:], in_=ident_d), (), [b_ident], dma=True)
    add("pool", lambda e: e.memset(ones_f[:], 1.0), (), [b_ones_f])
    add("pool", lambda e: e.memset(ones_b[:], 1.0), (), [b_ones_b])
    add("pool", lambda e: e.memset(epsc[:], EPS), (), [b_epsc])
    add("dve", lambda e: e.tensor_copy(out=identb[:], in_=ident[:]), [b_ident], [b_identb])

    prm = {}
    b_prm = Buf("prm")
    b_small = []

    def load_small(name, src, shape):
        t = sb(name, shape, F32)
        bt = Buf("ld_" + name)
        b_small.append(bt)
        add("sp", lambda e: e.dma_start(out=t[:], in_=src), (), [bt], dma=True)
        prm[name] = t
        return t

    cv_s = load_small("cv_s", cvT, [128, 8, 2])
    bmod_s = load_small("bmod_s", bmodT, [128, L, 48])
    gmix_s = load_small("gmix_s", gmixT, [128, L, 8])
    gff_s = load_small("gff_s", gffT, [128, L, 8])
    gfin_s = load_small("gfin_s", gfinT, [128, 8])
    gq_s = load_small("gq_s", gqT, [128, L, 2])
    gkvc_s = load_small("gkvc_s", gkvc, [128, L])
    gkvr_s = sb("gkvr_s", [128, L, 128], BF16)
    b_small.append(Buf("ld_gkvr"))
    add("pool", lambda e: e.dma_start(out=gkvr_s[:], in_=gkvr), (), [b_small[-1]], dma=True)
    gsub_s = load_small("gsub_s", gsubc, [128, L])
    lam_s = load_small("lam_s", lamv, [128, L, 4, 32])
    dw_s = load_small("dw_s", dwT, [128, L, 2, 31])
    cb_s = load_small("cb_s", cbT, [128, L, 2])
    lng_s = load_small("lng_s", lngT, [128, L, 2])
    lnb_s = load_small("lnb_s", lnbT, [128, L, 2])
    cos_s = load_small("cos_s", cosT_d, [128, T])
    sin_s = load_small("sin_s", sinT_d, [128, T])
    colm_s = load_small("colm_s", colmask_d, [128, 64])
    halsel_s = load_small("halsel_s", halsel_d, [128, 8])
    joinc = sb("joinc", [128, 1], F32)
    add("dve", lambda e: e.memset(joinc[:], 0.0), list(b_small), [b_prm])

    lams = sb("lams", [128, L, 2], F32)
    neglam = sb("neglam", [128, L], F32)
    gsub2 = sb("gsub2", [128, L], F32)
    for l in range(L):
        lam_init = 0.8 - 0.6 * math.exp(-0.3 * l)
        for m in range(2):
            add("dve", lambda e, l=l, m=m: e.tensor_tensor(out=lam_s[:, l, 2 * m, :], in0=lam_s[:, l, 2 * m, :],
                                                          in1=lam_s[:, l, 2 * m + 1, :], op=ALU.mult), [b_prm], [b_prm])
            add("dve", lambda e, l=l, m=m: e.reduce_sum(out=lams[:, l, m:m + 1], in_=lam_s[:, l, 2 * m, :],
                                                       axis=mybir.AxisListType.X), [b_prm], [b_prm])
        add("act", lambda e, l=l: e.activation(out=lams[:, l, :], in_=lams[:, l, :], func=AF.Exp), [b_prm], [b_prm])
        add("dve", lambda e, l=l: e.tensor_tensor(out=neglam[:, l:l + 1], in0=lams[:, l, 1:2], in1=lams[:, l, 0:1],
                                                 op=ALU.subtract), [b_prm], [b_prm])
        add("dve", lambda e, l=l, li=lam_init: e.tensor_scalar(out=neglam[:, l:l + 1], in0=neglam[:, l:l + 1],
                                                              scalar1=-li, scalar2=None, op0=ALU.add), [b_prm], [b_prm])
        add("dve", lambda e, l=l, li=lam_init: e.tensor_scalar(out=gsub2[:, l:l + 1], in0=gsub_s[:, l:l + 1],
                                                              scalar1=1.0 - li, scalar2=None, op0=ALU.mult), [b_prm], [b_prm])

    xT = [sb("xT%d" % g, [128, 8, T], F32) for g in range(2)]
    b_xT = [[Buf("xT%d_%d" % (g, j)) for j in range(8)] for g in range(2)]
    hT = [sb("hT%d" % g, [128, 8, T], BF16) for g in range(2)]
    b_hT = [Buf("hT%d" % g) for g in range(2)]
    mixT = [sb("mixT%d" % g, [128, 8, T], BF16) for g in range(2)]
    b_mix = [[Buf("mix%d_%d" % (g, j)) for j in range(8)] for g in range(2)]
    NW = 3
    wpool = [sb("wp%d" % i, [128, 8, 512], BF16) for i in range(NW)]
    b_wp = [Buf("wp%d" % i) for i in range(NW)]
    wp_i = [0]

    def wtile(src_ap, ncols, nk=8):
        i = wp_i[0] % NW
        wp_i[0] += 1
        t, b = wpool[i], b_wp[i]
        add("pool", lambda e: e.dma_start(out=t[:, 0:nk, 0:ncols], in_=src_ap.rearrange("(k p) c -> p k c", p=128)),
            (), [b], dma=True)
        return t, b

    qd = sb("qd", [128, 2, T], F32); b_qd = Buf("qd_cacc_stage")
    stage1 = qd[:, :, :].rearrange("p a b -> p (a b)")

    class _Stage:
        def __getitem__(self, key):
            p, sl, c = key
            return stage1[p, c]
    stage = _Stage()
    b_stage = [b_qd, b_qd]
    rstd = sb("rstd", [128, T], F32); b_rstd = Buf("rstd")
    tmpf = [sb("tmpf%d" % i, [128, T], F32) for i in range(4)]
    b_tmpf = [Buf("tmpf%d" % i) for i in range(4)]
    tf_i = [0]

    def tmp():
        i = tf_i[0] % 4
        tf_i[0] += 1
        return tmpf[i], b_tmpf[i]

    sqb = [sb("sqb%d" % i, [128, T], BF16) for i in range(2)]
    b_sqb = [Buf("sqb%d" % i) for i in range(2)]
    sq_i = [0]

    def sqt():
        i = sq_i[0] % 2
        sq_i[0] += 1
        return sqb[i], b_sqb[i]

    for g in range(2):
        for tb in range(4):
            for half in range(2):
                stg, bstg = tmp()
                add("sp", lambda e, g=g, tb=tb, half=half, stg=stg: e.dma_start(
                    out=stg[:, :], in_=xin[g * T + tb * 128: g * T + (tb + 1) * 128, half * 512:(half + 1) * 512]), (), [bstg], dma=True)
                pt, pb = gp()
                for jj in range(4):
                    add("pe", lambda e, pt=pt, stg=stg, jj=jj: e.transpose(out=pt[:, jj * 128:(jj + 1) * 128],
                                                                           in_=stg[:, jj * 128:(jj + 1) * 128], identity=ident[:]),
                        [bstg, b_ident], [pb])
                evac(xT[g][:, half * 4:half * 4 + 4, tb * 128:(tb + 1) * 128],
                     pt[:, :].rearrange("p (j t) -> p j t", j=4), [pb], [b_xT[g][half * 4 + jj] for jj in range(4)])

    sil = sb("sil", [128, 8, 2], BF16); b_sil = Buf("sil")
    add("act", lambda e: e.activation(out=sil[:], in_=cv_s[:], func=AF.Silu), [b_prm], [b_sil])
    modv = sb("modv", [128, L, 48, 2], F32)
    gsc = sb("gsc", [128, L, 2, 8, 2], F32)
    b_mod = [Buf("mod%d" % l) for l in range(L)]

    def modulation(l):
        pt, pb = banks[3]
        for ti in range(12):
            wt, wb = wtile(w_mod[l, :, ti * 512:(ti + 1) * 512], 512)
            for cc in range(4):
                n = ti * 4 + cc
                for k in range(8):
                    add("pe", lambda e, wt=wt, cc=cc, k=k, n=n, pt=pt: e.matmul(pt[:, n * 2:n * 2 + 2], lhsT=wt[:, k, cc * 128:(cc + 1) * 128],
                                                                             rhs=sil[:, k, :], start=(k == 0), stop=(k == 7)),
                        [wb, b_sil], [pb])
            yield
        add("dve", lambda e, pt=pt: e.tensor_tensor(out=modv[:, l, :, :], in0=pt[:, 0:96].rearrange("p (n v) -> p n v", v=2),
                                                   in1=bmod_s[:, l, :].unsqueeze(2).to_broadcast([128, 48, 2]), op=ALU.add),
            [pb, b_prm], [b_mod[l]])
        for ni, (gsrc, off) in enumerate(((gmix_s, 8), (gff_s, 32))):
            add("dve", lambda e, ni=ni, gsrc=gsrc, off=off: e.scalar_tensor_tensor(
                out=gsc[:, l, ni, :, :], in0=modv[:, l, off:off + 8, :], scalar=1.0,
                in1=gsrc[:, l, :].unsqueeze(2).to_broadcast([128, 8, 2]), op0=ALU.add, op1=ALU.mult),
                [b_mod[l], b_prm], [b_mod[l]])

    def mcol(l, which, j, v):
        off = {"sh1": 0, "sc1": 8, "g1": 16, "sh2": 24, "sc2": 32, "g2": 40}[which]
        return modv[:, l, off + j, v:v + 1]

    def rmsnorm_mod(l, g, ni):
        pt, pb = gp()
        for j in range(8):
            sq, bq = sqt()
            add("act", lambda e, sq=sq, j=j: e.activation(out=sq[:], in_=xT[g][:, j, :], func=AF.Square), [b_xT[g][j]], [bq])
            add("pe", lambda e, sq=sq, j=j, pt=pt: e.matmul(pt[:, :], lhsT=ones_b[:, :], rhs=sq[:], start=(j == 0), stop=(j == 7)),
                [bq, b_ones_b], [pb])
        add("act", lambda e, pt=pt: e.activation(out=rstd[:], in_=pt[:, :], func=AF.Ln, bias=epsc[:, 0:1], scale=1.0 / D),
            [pb, b_epsc], [b_rstd])
        add("act", lambda e: e.activation(out=rstd[:], in_=rstd[:], func=AF.Exp, scale=-0.5), [b_rstd], [b_rstd])
        for j in range(8):
            tt, tb_ = tmp()
            add("dve", lambda e, tt=tt, j=j: e.scalar_tensor_tensor(out=tt[:], in0=xT[g][:, j, :], scalar=gsc[:, l, ni, j, g:g + 1],
                                                                    in1=rstd[:], op0=ALU.mult, op1=ALU.mult),
                [b_xT[g][j], b_rstd, b_mod[l]], [tb_])
            add("act", lambda e, tt=tt, j=j: e.activation(out=hT[g][:, j, :], in_=tt[:], func=AF.Identity,
                                                          bias=mcol(l, "sh1" if ni == 0 else "sh2", j, g), scale=1.0),
                [tb_, b_mod[l]], [b_hT[g]])

    def stat_rstd(src_list, nfeat, dst, b_dst, reads, K=128, n=T):
        pt, pb = gp()
        for i, src in enumerate(src_list):
            tt, tb_ = tmp()
            add("act", lambda e, tt=tt, src=src: e.activation(out=tt[0:K, 0:n], in_=src, func=AF.Square), reads, [tb_])
            add("pe", lambda e, tt=tt, i=i, pt=pt: e.matmul(pt[0:K, 0:n], lhsT=ones_f[0:K, 0:K], rhs=tt[0:K, 0:n],
                                                          start=(i == 0), stop=(i == len(src_list) - 1)),
                [tb_, b_ones_f], [pb])
        add("act", lambda e, pt=pt: e.activation(out=dst, in_=pt[0:K, 0:n], func=AF.Ln, bias=epsc[0:K, 0:1], scale=1.0 / nfeat),
            [pb, b_epsc], [b_dst])
        add("act", lambda e: e.activation(out=dst, in_=dst, func=AF.Exp, scale=-0.5), [b_dst], [b_dst])

    KTb = sb("KTb", [128, 4, DEC_S + PAST], BF16); b_KTh = [Buf("KT%d" % h) for h in range(4)]
    VAb = sb("VAb", [128, 18, 4, 72], BF16); b_VAk = [Buf("VA%d" % k) for k in range(18)]
    KTc2 = [sb("KTc%d" % m, [128, 4, T], BF16) for m in range(2)]
    b_KTc = [Buf("KTc%d" % m) for m in range(3)]
    VAc = [sb("VAc%d" % m, [128, 4, 4, 72], BF16) for m in range(3)]; b_VAc = [Buf("VAc%d" % m) for m in range(3)]
    QT2 = [[sb("QT%d_%d" % (g, m), [128, 4, T], BF16) for m in range(2)] for g in range(2)]
    b_QT = [[Buf("QT%d_%d" % (g, m)) for m in range(3)] for g in range(2)]
    add("pool", lambda e: e.memset(VAb[:, :, :, 64:72], 1.0), (), b_VAk)
    add("pool", lambda e: e.memset(KTb[64:128, :, :], 0.0), (), b_KTh)
    add("pool", lambda e: e.memset(QT2[1][1][96:128, :, :], 0.0), (), [b_QT[1][1]])
    for m in range(3):
        add("pool", lambda e, m=m: e.memset(VAc[m][:, :, :, 64:72], 1.0), (), [b_VAc[m]])

    def q_ap(g, m, h, p_lo, p_hi, c0=0, n=T):
        if m == 0:
            return QT2[g][0][p_lo:p_hi, h, c0:c0 + n]
        if m == 1:
            return QT2[g][1][p_lo:p_hi, h, c0:c0 + n]
        return QT2[g][0][64 + p_lo:64 + p_hi, h, c0:c0 + n]

    def kc_ap(m, h, p_lo, p_hi, c0, n):
        if m == 0:
            return KTc2[0][p_lo:p_hi, h, c0:c0 + n]
        if m == 1:
            return KTc2[1][p_lo:p_hi, h, c0:c0 + n]
        return KTc2[0][64 + p_lo:64 + p_hi, h, c0:c0 + n]
    Eb = [sb("E%d" % i, [128, T], BF16) for i in range(3)]
    b_E = [Buf("E%d" % i) for i in range(3)]
    e_i = [0]
    NSET = 2
    rsum_s = [sb("rsum0", [128, T], F32)] * 2; b_rsum_s = [Buf("rsum0")] * 2
    rsb0 = sb("rsb0", [128, T], BF16)
    rsb_row = [64, 64]
    b_rsb_s = [Buf("rsb0")] * 2
    dao = sb("dao", [128, T], F32); b_dao = Buf("dao")
    set_i = [0]
    ckvT = sb("ckvT", [128, DEC_S + PAST], BF16); b_ckvT = Buf("ckvT")
    ckvTc = ckvT; b_ckvTc = b_ckvT
    qdn = sb("qdn", [128, 2, T], BF16); b_qdn = Buf("qdn")
    wq_s = sb("wq_s", [128, 2, 384], BF16); wqp_s = sb("wqp_s", [128, 2, 384], BF16); wkv_s = sb("wkv_s", [128, 512], BF16)
    b_wsm = Buf("wsmall")
    gpad = [sb("gpad%d" % g, [128, 2, 2, 15 + 256 + 15], BF16) for g in range(2)]
    b_gpad = [Buf("gpad%d" % g) for g in range(2)]
    for g in range(2):
        add("pool", lambda e, g=g: e.memset(gpad[g][:], 0.0), (), [b_gpad[g]])
    cacc = qd; b_cacc = b_qd
    halo = sb("halo", [128, 2, 4, 30], BF16); b_halo = Buf("halo")
    NTBT = 2
    tbt = [sb("tbt%d" % i, [128, A2, 64], BF16) for i in range(NTBT)] * (2 // NTBT)
    b_tbt = [Buf("tbt%d" % i) for i in range(NTBT)] * (2 // NTBT)
    b_payA = [Buf("payA%d" % l) for l in range(L)]
    b_payB = [Buf("payB%d" % l) for l in range(L)]
    b_payoA = [Buf("payoA%d" % l) for l in range(L)]
    b_payoB = [Buf("payoB%d" % l) for l in range(L)]
    b_tz = [Buf("tz%d" % l) for l in range(L)]

    pending_post = [None]
    import os as _os2
    WARM_LAT = int(_os2.environ.get("WARM_LAT", "0"))
    WARM_CTX = int(_os2.environ.get("WARM_CTX", "0"))
    WARM_N = int(_os2.environ.get("WARM_N", "256"))
    BURST = int(_os2.environ.get("BURST", "20"))

    def warm(n):
        for _ in range(n):
            add("pe", lambda e: e.matmul(banks[7][0][:, 512 - WARM_N:512], lhsT=identb[:, 0:128], rhs=identb[:, 0:128], start=True, stop=True), (), ())

    def flush_post():
        if pending_post[0] is not None:
            f = pending_post[0]
            pending_post[0] = None
            f()

    def attention(kt_fn, q_ap, va_fn, nkb, nq, scale, reads, extra_fn=None):
        at, ab = acb()
        pend = []

        def score(kb):
            pt, pb = gp()
            ex = extra_fn(kb) if extra_fn is not None else []
            add("pe", lambda e, pt=pt, kb=kb: e.matmul(pt[:, 0:nq], lhsT=kt_fn(kb), rhs=q_ap, start=True, stop=(len(ex) == 0)),
                reads, [pb])
            for i, (lh, rh, rd) in enumerate(ex):
                add("pe", lambda e, pt=pt, lh=lh, rh=rh, i=i: e.matmul(pt[:, 0:nq], lhsT=lh, rhs=rh, start=False, stop=(i == len(ex) - 1)),
                    rd, [pb])
            i = e_i[0] % 3
            e_i[0] += 1
            add("act", lambda e, pt=pt, i=i: e.activation(out=Eb[i][:, 0:nq], in_=pt[:, 0:nq], func=AF.Exp, scale=scale), [pb], [b_E[i]])
            return i

        for kb in range(min(2, nkb)):
            pend.append(score(kb))
        flush_post()
        for kb in range(nkb):
            if kb + 2 < nkb:
                pend.append(score(kb + 2))
            warm(WARM_LAT)
            i = pend[kb]
            add("pe", lambda e, at=at, kb=kb, i=i: e.matmul(at[0:65, 0:nq], lhsT=va_fn(kb), rhs=Eb[i][:, 0:nq], start=(kb == 0), stop=(kb == nkb - 1)),
                reads + [b_E[i]], [ab])
        return at, ab

    def normalize_rep(at, ab, nq, dst, b_dst_list):
        r, br = tmp()
        add("act", lambda e: e.activation(out=r[0:64, 0:nq], in_=at[0:64, 256:256 + nq], func=AF.Ln), [ab], [br])
        add("act", lambda e: e.activation(out=r[0:64, 0:nq], in_=r[0:64, 0:nq], func=AF.Exp, scale=-1.0), [br], [br])
        add("dve", lambda e: e.tensor_tensor(out=dst, in0=at[0:64, 0:nq], in1=r[0:64, 0:nq], op=ALU.mult), [ab, br], b_dst_list)

    def normalize(at, ab, nq, dst, b_dst_list, k=0, c0=0):
        rsum, b_rsum, b_rsb, rr = rsum_s[k], b_rsum_s[k], b_rsb_s[k], rsb_row[k]
        add("act", lambda e: e.activation(out=rsum[64:65, 0:nq], in_=at[64:65, c0:c0 + nq], func=AF.Ln), [ab], [b_rsum])
        add("act", lambda e: e.activation(out=rsb0[rr:rr + 1, 0:nq], in_=rsum[64:65, 0:nq], func=AF.Exp, scale=-1.0), [b_rsum], [b_rsb])
        pt, pb = gp()
        add("pe", lambda e: e.matmul(pt[0:64, 0:nq], lhsT=ones_b[rr:rr + 1, 0:64], rhs=rsb0[rr:rr + 1, 0:nq], start=True, stop=True),
            [b_rsb, b_ones_b], [pb])
        bc, bbc = tmp()
        add("act", lambda e: e.activation(out=bc[0:64, 0:nq], in_=pt[0:64, 0:nq], func=AF.Copy), [pb], [bbc])
        add("dve", lambda e: e.tensor_tensor(out=dst, in0=at[0:64, c0:c0 + nq], in1=bc[0:64, 0:nq], op=ALU.mult), [ab, bbc], b_dst_list)

    def stream_attention(jobs, nkb=18, nq=T, side_cb=None, burst=0):
        blocks = [(ji, kb) for ji in range(len(jobs)) for kb in range(nkb)]
        N = len(blocks)
        pend = {}
        posts = {}

        def do_score(idx):
            ji, kb = blocks[idx]
            J = jobs[ji]
            if kb == 0 and J.get("pre") is not None:
                J["pre"]()
            pt, pb = gp()
            ex = J["extra_fn"](kb) if J.get("extra_fn") is not None else []
            add("pe", lambda e, pt=pt, kb=kb, J=J: e.matmul(pt[:, 0:nq], lhsT=J["kt_fn"](kb), rhs=J["q_ap"], start=True, stop=(len(ex) == 0)),
                J["reads"], [pb])
            for i2, (lh, rh, rd) in enumerate(ex):
                add("pe", lambda e, pt=pt, lh=lh, rh=rh, i2=i2: e.matmul(pt[:, 0:nq], lhsT=lh, rhs=rh, start=False, stop=(i2 == len(ex) - 1)),
                    rd, [pb])
            i = e_i[0] % 3
            e_i[0] += 1
            add("act", lambda e, pt=pt, i=i, J=J: e.activation(out=Eb[i][:, 0:nq], in_=pt[:, 0:nq], func=AF.Exp, scale=J["scale"]), [pb], [b_E[i]])
            pend[idx] = i

        for idx in range(min(2, N)):
            do_score(idx)
        if burst:
            wpt, wpb = gp()
            for _ in range(burst):
                add("pe", lambda e, wpt=wpt: e.matmul(wpt[:, :], lhsT=identb[:, :], rhs=hT[1][:, 0, :], start=True, stop=True),
                    [b_identb, b_hT[1]], [wpb])
        for idx in range(N):
            if idx + 2 < N:
                do_score(idx + 2)
            ji, kb = blocks[idx]
            J = jobs[ji]
            if kb == 0:
                J["acc"] = acb()
            at, ab = J["acc"]
            i = pend.pop(idx)
            add("pe", lambda e, at=at, kb=kb, i=i, J=J: e.matmul(at[0:65, 0:nq], lhsT=J["va_fn"](kb), rhs=Eb[i][:, 0:nq],
                                                              start=(kb == 0), stop=(kb == nkb - 1)), J["reads"] + [b_E[i]], [ab])
            if kb == nkb - 1:
                if J.get("post") is not None:
                    posts.setdefault(min(idx + 2, N - 1), []).append(J["post"])
                if side_cb is not None:
                    side_cb()
            for f in posts.pop(idx, []):
                f()

    def input_proj(l):
        res = {}
        add("pool", lambda e: e.dma_start(out=wq_s[:], in_=wqup[l].rearrange("(k p) c -> p k c", p=128)), (), [b_wsm], dma=True)
        add("pool", lambda e: e.dma_start(out=wqp_s[:], in_=wqupp[l].rearrange("(k p) c -> p k c", p=128)), (), [b_wsm], dma=True)
        add("pool", lambda e: e.dma_start(out=wkv_s[:], in_=wkvup[l]), (), [b_wsm], dma=True)

        def fm(wt, wb, c0, M, g, n0=0, n=T):
            pt, pb = gp()
            for k in range(8):
                add("pe", lambda e, pt=pt, k=k: e.matmul(pt[0:M, 0:n], lhsT=wt[:, k, c0:c0 + M], rhs=hT[g][:, k, n0:n0 + n],
                                                        start=(k == 0), stop=(k == 7)), [wb, b_hT[g]], [pb])
            return pt, pb

        def tm(wt, wb, c0, N, g, tb):
            pt, pb = gp()
            for k in range(8):
                add("pe", lambda e, pt=pt, k=k: e.matmul(pt[:, 0:N], lhsT=hT[g][:, k, tb * 128:(tb + 1) * 128], rhs=wt[:, k, c0:c0 + N],
                                                        start=(k == 0), stop=(k == 7)), [wb, b_hT[g]], [pb])
            return pt, pb

        def rope_evac(ptA, pbA, ptB, pbB, p0, p1, dst, wlist):
            t1, tb1 = tmp()
            t2, tb2 = tmp()
            add("dve", lambda e: e.tensor_tensor(out=t1[p0:p1, :], in0=ptA[p0:p1, :], in1=cos_s[p0:p1, :], op=ALU.mult), [pbA, b_prm], [tb1])
            add("dve", lambda e: e.tensor_tensor(out=t2[p0:p1, :], in0=ptB[p0:p1, :], in1=sin_s[p0:p1, :], op=ALU.mult), [pbB, b_prm], [tb2])
            add("dve", lambda e: e.tensor_tensor(out=dst, in0=t1[p0:p1, :], in1=t2[p0:p1, :], op=ALU.add), [tb1, tb2], wlist)

        groups = [0, 1] if do_lat else [0]
        wA, bA = wtile(w_in[l, :, 0:512], 512)
        if do_lat:
            wF1, bF1 = wtile(w_inp[l, :, 0:512], 512)
        for h in range(4):
            pt, pb = fm(wA, bA, h * 64, 64, 0)
            evac(QT2[0][0][0:64, h, :], pt[0:64, :], [pb], [b_QT[0][0]])
            pt, pb = fm(wA, bA, 256 + h * 64, 64, 0)
            evac(KTc2[0][0:64, h, :], pt[0:64, :], [pb], [b_KTc[0]])
        if do_lat:
            for h in range(4):
                ptA, pbA = fm(wA, bA, h * 64, 64, 1)
                ptB, pbB = fm(wF1, bF1, h * 64, 64, 1)
                rope_evac(ptA, pbA, ptB, pbB, 0, 64, QT2[1][0][0:64, h, :], [b_QT[1][0]])
                ptA, pbA = fm(wA, bA, 256 + h * 64, 64, 1)
                ptB, pbB = fm(wF1, bF1, 256 + h * 64, 64, 1)
                tk, tkb = sqt()
                rope_evac(ptA, pbA, ptB, pbB, 0, 64, tk[0:64, :], [tkb])
                add("sp", lambda e, tk=tk, h=h: e.dma_start(out=payA_in[l].ap()[R_DAK + h * 64:R_DAK + (h + 1) * 64, :], in_=tk[0:64, :]),
                    [tkb], [b_payA[l]], dma=True)
        wB, bB = wtile(w_in[l, :, 512:1024], 512)
        for g in groups:
            for c in range(2):
                pt, pb = fm(wB, bB, 256 + c * 128, 128, g)
                evac(qd[:, c, :], pt[:, :], [pb], [b_qd])
            stat_rstd([qd[:, 0, :], qd[:, 1, :]], 256, rstd[:], b_rstd, [b_qd])
            for c in range(2):
                add("dve", lambda e, c=c: e.scalar_tensor_tensor(out=qdn[:, c, :], in0=qd[:, c, :], scalar=gq_s[:, l, c:c + 1], in1=rstd[:],
                                                               op0=ALU.mult, op1=ALU.mult), [b_qd, b_rstd, b_prm], [b_qdn])
            for h in range(4):
                pt, pb = gp()
                for c in range(2):
                    add("pe", lambda e, pt=pt, c=c, h=h: e.matmul(pt[0:96, :], lhsT=wq_s[:, c, h * 96:(h + 1) * 96], rhs=qdn[:, c, :],
                                                                 start=(c == 0), stop=(c == 1)), [b_wsm, b_qdn], [pb])
                if g == 0:
                    evac(QT2[0][1][0:96, h, :], pt[0:96, :], [pb], [b_QT[0][1]])
                else:
                    pt2, pb2 = gp()
                    for c in range(2):
                        add("pe", lambda e, pt2=pt2, c=c, h=h: e.matmul(pt2[0:96, :], lhsT=wqp_s[:, c, h * 96:(h + 1) * 96], rhs=qdn[:, c, :],
                                                                       start=(c == 0), stop=(c == 1)), [b_wsm, b_qdn], [pb2])
                    evac(QT2[1][1][0:64, h, :], pt[0:64, :], [pb], [b_QT[1][1]])
                    rope_evac(pt, pb, pt2, pb2, 64, 96, QT2[1][1][64:96, h, :], [b_QT[1][1]])
        for g in groups:
            for tb in range(4):
                pt, pb = tm(wB, bB, 0, 256, g, tb)
                if g == 0:
                    ostage, b_ostage = tmp()
                    add("act", lambda e, pt=pt, ostage=ostage: e.activation(out=ostage[:, 0:256], in_=pt[:, 0:256], func=AF.Copy), [pb], [b_ostage])
                    s, tl = tb // 2, (tb % 2) * 128
                    add("sp", lambda e, s=s, tl=tl, ostage=ostage: e.dma_start(out=o_dav[s, l, :, tl:tl + 128, :].rearrange("h t d -> t h d"),
                                                               in_=ostage[:, 0:256].rearrange("p (h d) -> p h d", h=4)), [b_ostage], (), dma=True)
                    add("dve", lambda e, pt=pt, tb=tb: e.tensor_copy(out=VAc[0][:, tb, :, 0:64], in_=pt[:, 0:256].rearrange("p (h d) -> p h d", h=4)),
                        [pb], [b_VAc[0]])
                else:
                    tv, tvb = sqt()
                    evac(tv[:, 0:256], pt[:, 0:256], [pb], [tvb])
                    add("sp", lambda e, tv=tv, tb=tb: e.dma_start(out=payB_in[l].ap()[R_V + tb * 128:R_V + (tb + 1) * 128, 0:256], in_=tv[:, 0:256]),
                        [tvb], [b_payB[l]], dma=True)
        wC, bC = wtile(w_in[l, :, 1024:1440], 416)
        if do_lat:
            wF2, bF2 = wtile(w_inp[l, :, 512:608], 96)
        for g in groups:
            pt, pb = fm(wC, bC, 0, 128, g)
            kvd, b_kvd = tmp()
            evac(kvd[:, :], pt[:, :], [pb], [b_kvd])
            stat_rstd([kvd[:, :]], 128, rstd[:], b_rstd, [b_kvd])
            dstc = ckvTc if g == 0 else sqb[0]
            if g == 0:
                add("dve", lambda e, kvd=kvd: e.scalar_tensor_tensor(out=ckvTc[:, 0:T], in0=kvd[:, :], scalar=gkvc_s[:, l:l + 1], in1=rstd[:],
                                                            op0=ALU.mult, op1=ALU.mult), [b_kvd, b_rstd, b_prm], [b_ckvTc])
            else:
                tk, tkb = sqt()
                add("dve", lambda e, tk=tk, kvd=kvd: e.scalar_tensor_tensor(out=tk[:, :], in0=kvd[:, :], scalar=gkvc_s[:, l:l + 1], in1=rstd[:],
                                                                   op0=ALU.mult, op1=ALU.mult), [b_kvd, b_rstd, b_prm], [tkb])
                add("sp", lambda e, tk=tk: e.dma_start(out=payA_in[l].ap()[R_CKV:R_CKV + 128, :], in_=tk[:, :]), [tkb], [b_payA[l]], dma=True)
            pt, pb = fm(wC, bC, 64, 96, g)
            if g == 0:
                for h in range(4):
                    evac(KTc2[1][64:96, h, :], pt[64:96, :], [pb], [b_KTc[1]])
            else:
                ptB, pbB = fm(wF2, bF2, 0, 96, 1)
                tk, tkb = sqt()
                rope_evac(pt, pb, ptB, pbB, 64, 96, tk[64:96, :], [tkb])
                add("sp", lambda e, tk=tk: e.dma_start(out=payA_in[l].ap()[R_KR:R_KR + 32, :], in_=tk[64:96, :]), [tkb], [b_payA[l]], dma=True)
            for h in range(4):
                pt, pb = fm(wC, bC, 160 + h * 64, 64, g)
                evac(QT2[g][0][64:128, h, :], pt[0:64, :], [pb], [b_QT[g][2]])
        for tb in range(4):
            pt, pb = tm(wC, bC, 0, 160, 0, tb)
            s, tl = tb // 2, (tb % 2) * 128
            tt, tb_ = tmp()
            add("act", lambda e, pt=pt, tt=tt: e.activation(out=tt[:, 0:128], in_=pt[:, 0:128], func=AF.Square), [pb], [tb_])
            add("dve", lambda e, tt=tt: e.reduce_sum(out=tt[:, 200:201], in_=tt[:, 0:128], axis=mybir.AxisListType.X), [tb_], [tb_])
            add("act", lambda e, tt=tt: e.activation(out=tt[:, 201:202], in_=tt[:, 200:201], func=AF.Ln, bias=epsc[:, 0:1], scale=1.0 / 128),
                [tb_, b_epsc], [tb_])
            add("act", lambda e, tt=tt: e.activation(out=tt[:, 202:203], in_=tt[:, 201:202], func=AF.Exp, scale=-0.5), [tb_], [tb_])
            add("dve", lambda e, pt=pt, tt=tt: e.scalar_tensor_tensor(out=tt[:, 256:384], in0=pt[:, 0:128], scalar=tt[:, 202:203],
                                                                      in1=gkvr_s[:, l, :], op0=ALU.mult, op1=ALU.mult),
                [pb, tb_, b_prm], [tb_])
            add("act", lambda e, pt=pt, tt=tt: e.activation(out=tt[:, 384:416], in_=pt[:, 128:160], func=AF.Copy), [pb, tb_], [tb_])
            add("sp", lambda e, s=s, tl=tl, tt=tt: e.dma_start(out=o_ckv[s, l, tl:tl + 128, :], in_=tt[:, 256:384]), [tb_], (), dma=True)
            add("sp", lambda e, s=s, tl=tl, tt=tt: e.dma_start(out=o_kr[s, l, tl:tl + 128, :], in_=tt[:, 384:416]), [tb_], (), dma=True)
        for h in range(4):
            pt, pb = gp()
            add("pe", lambda e, pt=pt, h=h: e.matmul(pt[0:64, :], lhsT=wkv_s[:, h * 128:h * 128 + 64], rhs=ckvTc[:, 0:T], start=True, stop=True),
                [b_wsm, b_ckvTc], [pb])
            evac(KTc2[1][0:64, h, :], pt[0:64, :], [pb], [b_KTc[1]])
        for tb in range(4):
            pt, pb = gp()
            add("pe", lambda e, pt=pt, tb=tb: e.matmul(pt[:, 0:256], lhsT=ckvTc[:, tb * 128:(tb + 1) * 128],
                                                      rhs=wkv_s[:, :].rearrange("p (h c) -> p h c", h=4)[:, :, 64:128], start=True, stop=True),
                [b_wsm, b_ckvTc], [pb])
            evac(VAc[1][:, tb, :, 0:64], pt[:, 0:256].rearrange("p (h d) -> p h d", h=4), [pb], [b_VAc[1]])
        wD, bD = wtile(w_in[l, :, 1440:1952], 512)
        for h in range(4):
            pt, pb = fm(wD, bD, h * 64, 64, 0)
            evac(KTc2[0][64:128, h, :], pt[0:64, :], [pb], [b_KTc[2]])
            if do_lat:
                pt, pb = fm(wD, bD, h * 64, 64, 1)
                tk, tkb = sqt()
                evac(tk[0:64, :], pt[0:64, :], [pb], [tkb])
                add("sp", lambda e, tk=tk, h=h: e.dma_start(out=payA_in[l].ap()[R_NAK + h * 64:R_NAK + (h + 1) * 64, :], in_=tk[0:64, :]),
                    [tkb], [b_payA[l]], dma=True)
        for tb in range(4):
            pt, pb = tm(wD, bD, 0, 512, 0, tb)
            s, tl = tb // 2, (tb % 2) * 128
            ostage, b_ostage = tmp()
            add("act", lambda e, pt=pt, ostage=ostage: e.activation(out=ostage[:, :], in_=pt[:, :], func=AF.Copy), [pb], [b_ostage])
            add("sp", lambda e, s=s, tl=tl, ostage=ostage: e.dma_start(out=o_nak[s, l, :, tl:tl + 128, :].rearrange("h t d -> t h d"),
                                                       in_=ostage[:, 0:256].rearrange("p (h d) -> p h d", h=4)), [b_ostage], (), dma=True)
            add("sp", lambda e, s=s, tl=tl, ostage=ostage: e.dma_start(out=o_nav[s, l, :, tl:tl + 128, :].rearrange("h t d -> t h d"),
                                                       in_=ostage[:, 256:512].rearrange("p (h d) -> p h d", h=4)), [b_ostage], (), dma=True)
            add("dve", lambda e, pt=pt, tb=tb: e.tensor_copy(out=VAc[2][:, tb, :, 0:64], in_=pt[:, 256:512].rearrange("p (h d) -> p h d", h=4)),
                [pb], [b_VAc[2]])
            if do_lat:
                pt, pb = tm(wD, bD, 256, 256, 1, tb)
                tv, tvb = sqt()
                evac(tv[:, 0:256], pt[:, 0:256], [pb], [tvb])
                add("sp", lambda e, tv=tv, tb=tb: e.dma_start(out=payB_in[l].ap()[R_V + tb * 128:R_V + (tb + 1) * 128, 256:512], in_=tv[:, 0:256]),
                    [tvb], [b_payB[l]], dma=True)
        wA2, bA2 = wtile(w_in[l, :, 256:512], 256)
        for tb in range(4):
            pt, pb = tm(wA2, bA2, 0, 256, 0, tb)
            s, tl = tb // 2, (tb % 2) * 128
            ostage, b_ostage = tmp()
            add("act", lambda e, pt=pt, ostage=ostage: e.activation(out=ostage[:, 0:256], in_=pt[:, 0:256], func=AF.Copy), [pb], [b_ostage])
            add("sp", lambda e, s=s, tl=tl, ostage=ostage: e.dma_start(out=o_dak[s, l, :, tl:tl + 128, :].rearrange("h t d -> t h d"),
                                                       in_=ostage[:, 0:256].rearrange("p (h d) -> p h d", h=4)), [b_ostage], (), dma=True)
        wE, bE = wtile(w_in[l, :, 1952:2464], 512)
        for g in groups:
            for c in range(2):
                pa, pba = fm(wE, bE, c * 128, 128, g)
                pbm, pbb = fm(wE, bE, 256 + c * 128, 128, g)
                tt, tb_ = tmp()
                add("act", lambda e, tt=tt, pbm=pbm: e.activation(out=tt[:], in_=pbm[:, :], func=AF.Sigmoid), [pbb], [tb_])
                add("dve", lambda e, tt=tt, pa=pa, c=c, g=g: e.tensor_tensor(out=gpad[g][:, c, :, 15:15 + 256],
                                                                           in0=pa[:, :].rearrange("p (s t) -> p s t", s=2),
                                                                           in1=tt[:].rearrange("p (s t) -> p s t", s=2), op=ALU.mult),
                    [pba, tb_], [b_gpad[g]])
        if do_lat:
            add("dve", lambda e: e.tensor_copy(out=halo[:, :, 0, 0:15], in_=gpad[1][:, :, 0, 15:30]), [b_gpad[1]], [b_halo])
            add("dve", lambda e: e.tensor_copy(out=halo[:, :, 0, 15:30], in_=gpad[1][:, :, 1, 256:271]), [b_gpad[1]], [b_halo])
            hdst = dram_ap(payB_in[l], R_HALO * T, [[30, 128], [128 * 30, 2], [1, 30]])
            add("sp", lambda e: e.dma_start(out=hdst, in_=halo[:, :, 0, :]), [b_halo], [b_payB[l]], dma=True)
            add("pool", lambda e: e.collective_compute("AllGather", ALU.bypass, replica_groups=[[0, 1, 2, 3], [4, 5, 6, 7]],
                                                       ins=[payA_in[l].ap().opt()], outs=[payA_out[l].ap().opt()]),
                [b_payA[l]], [b_payoA[l]], dma="cc")
            add("pool", lambda e: e.collective_compute("AllGather", ALU.bypass, replica_groups=[[0, 1, 2, 3], [4, 5, 6, 7]],
                                                       ins=[payB_in[l].ap().opt()], outs=[payB_out[l].ap().opt()]),
                [b_payB[l]], [b_payoB[l]], dma="cc")

    def da_post(l, g, h, at1, ab1, at2, ab2, nq=T, q0=0, repl=False, c1=0, c2=0):
        o1, bo1 = dao, b_dao
        o2, bo2 = rsum_s[0], b_rsum_s[0]
        if repl:
            normalize_rep(at1, ab1, nq, o1[0:64, 0:nq], [bo1])
            normalize_rep(at2, ab2, nq, o2[0:64, 0:nq], [bo2])
        else:
            normalize(at1, ab1, nq, o1[0:64, 0:nq], [bo1], 0, c1)
            normalize(at2, ab2, nq, o2[0:64, 0:nq], [bo2], 1 if nq <= 256 else 0, c2)
        add("dve", lambda e: e.scalar_tensor_tensor(out=o1[0:64, 0:nq], in0=o2[0:64, 0:nq], scalar=neglam[0:64, l:l + 1], in1=o1[0:64, 0:nq],
                                                    op0=ALU.mult, op1=ALU.add), [bo1, bo2, b_prm], [bo1])
        stat_rstd([o1[0:64, 0:nq]], 64, rstd[0:64, 0:nq], b_rstd, [bo1], K=64, n=nq)
        p0 = (h % 2) * 64
        add("dve", lambda e: e.scalar_tensor_tensor(out=mixT[g][p0:p0 + 64, h // 2, q0:q0 + nq], in0=o1[0:64, 0:nq], scalar=gsub2[0:64, l:l + 1],
                                                    in1=rstd[0:64, 0:nq], op0=ALU.mult, op1=ALU.mult),
            [bo1, b_rstd, b_prm], [b_mix[g][h // 2]])

    def plain_post(g, m, h, at, ab, nq=T, q0=0, repl=False):
        if repl:
            p0 = (h % 2) * 64
            ch = 2 * m + h // 2
            normalize_rep(at, ab, nq, mixT[g][p0:p0 + 64, ch, q0:q0 + nq], [b_mix[g][ch]])
            return
        k = (set_i[0] % NSET) if nq <= 256 else 0
        set_i[0] += 1
        o1, bo1 = tmp()
        normalize(at, ab, nq, o1[0:64, 0:nq], [bo1], k)
        p0 = (h % 2) * 64
        ch = 2 * m + h // 2
        add("act", lambda e: e.activation(out=mixT[g][p0:p0 + 64, ch, q0:q0 + nq], in_=o1[0:64, 0:nq], func=AF.Copy), [bo1], [b_mix[g][ch]])

    def ctx_attention(l, side_gen=None):
        scales = (32 ** -0.5, 96 ** -0.5, 64 ** -0.5)
        rows = {0: None, 1: (0, 96), 2: (0, 64)}
        jobs = []
        for s in range(2):
            for m in range(3):
                for h in range(4):
                    for mp in ((0, 1) if m == 0 else (0,)):
                        jobs.append(dict(s=s, m=m, h=h, mp=mp))
        n = len(jobs)

        def stage_a(j):
            s_, m, h, mp = j["s"], j["m"], j["h"], j["mp"]
            q0 = s_ * 256
            lo, hi = (mp * 32, mp * 32 + 32) if m == 0 else rows[m]
            pt, pb = gp()
            rd = [b_KTc[m], b_QT[0][m]]
            for kb in range(2):
                add("pe", lambda e, pt=pt, kb=kb, m=m, h=h, lo=lo, hi=hi, q0=q0: e.matmul(
                    pt[:, kb * 256:(kb + 1) * 256], lhsT=kc_ap(m, h, lo, hi, q0 + kb * 128, 128), rhs=q_ap(0, m, h, lo, hi, q0, 256),
                    start=True, stop=True), rd, [pb])
            i = e_i[0] % 3
            e_i[0] += 1
            add("act", lambda e, pt=pt, i=i, m=m: e.activation(out=Eb[i][:, :], in_=pt[:, :], func=AF.Exp, scale=scales[m]), [pb], [b_E[i]])
            j["e"] = i

        def stage_b(j):
            s_, m, h = j["s"], j["m"], j["h"]
            at, ab = acb()
            i = j["e"]
            for kb in range(2):
                add("pe", lambda e, at=at, kb=kb, i=i, m=m, h=h, s_=s_: e.matmul(at[0:65, 0:256], lhsT=VAc[m][:, s_ * 2 + kb, h, 0:65],
                                                                               rhs=Eb[i][:, kb * 256:(kb + 1) * 256], start=(kb == 0), stop=(kb == 1)),
                    [b_VAc[m], b_E[i]], [ab])
            for kb in range(2):
                add("pe", lambda e, at=at, kb=kb, i=i: e.matmul(at[0:64, 256:512], lhsT=ones_b[:, 0:64], rhs=Eb[i][:, kb * 256:(kb + 1) * 256],
                                                               start=(kb == 0), stop=(kb == 1)), [b_ones_b, b_E[i]], [ab])
            j["acc"] = (at, ab)

        def stage_c(idx):
            j = jobs[idx]
            s_, m, h, mp = j["s"], j["m"], j["h"], j["mp"]
            q0 = s_ * 256
            if m == 0:
                if mp == 1:
                    a1 = jobs[idx - 1]["acc"]
                    da_post(l, 0, h, a1[0], a1[1], j["acc"][0], j["acc"][1], nq=256, q0=q0, repl=True)
            else:
                plain_post(0, m, h, j["acc"][0], j["acc"][1], nq=256, q0=q0, repl=True)

        for i in range(n + 2):
            if i < n:
                stage_a(jobs[i])
            warm(WARM_CTX)
            if 0 <= i - 1 < n:
                stage_b(jobs[i - 1])
            if 0 <= i - 2 < n:
                stage_c(i - 2)
            if side_gen is not None and i % 2 == 1:
                next(side_gen, None)

    def cache_T(src_ap, ncol, dst_fn, wlist):
        stg, bstg = tmp()
        add("sp", lambda e: e.dma_start(out=stg[:, 0:2 * ncol].rearrange("p (t c) -> p t c", t=2),
                                        in_=src_ap.rearrange("(t p) c -> p t c", p=128)), (), [bstg], dma=True)
        for tb in range(2):
            pt, pb = gp()
            add("pe", lambda e, pt=pt, tb=tb: e.transpose(out=pt[0:ncol, 0:128], in_=stg[:, tb * ncol:(tb + 1) * ncol], identity=ident[:]),
                [bstg, b_ident], [pb])
            evac(dst_fn(tb), pt[0:ncol, 0:128], [pb], wlist)

    def load_V(l, c0, cache_ap):
        for rk in range(4):
            for tb in range(4):
                src = dram_ap(payB_out[l], (rk * PB + R_V + tb * 128) * T + c0, [[T, 128], [64, 4], [1, 64]])
                add("sp", lambda e, rk=rk, tb=tb, src=src: e.dma_start(out=VAb[:, rk * 4 + tb, :, 0:64], in_=src), [b_payoB[l]], [b_VAk[rk * 4 + tb]], dma=True)
        for tb in range(2):
            add("pool", lambda e, tb=tb: e.dma_start(out=VAb[:, 16 + tb, :, 0:64], in_=cache_ap[:, tb * 128:(tb + 1) * 128, :].rearrange("h t d -> t h d")),
                (), [b_VAk[16 + tb]], dma=True)

    def load_KT(l, row0, nrows, p0, heads=True):
        for h in range(4):
            r = row0 + (h * nrows if heads else 0)
            src = dram_ap(payA_out[l], r * T, [[T, nrows], [PA * T, 4], [1, T]])
            add("sp", lambda e, h=h, src=src: e.dma_start(out=KTb[p0:p0 + nrows, h, 0:DEC_S].rearrange("p (r t) -> p r t", r=4), in_=src),
                [b_payoA[l]], [b_KTh[h]], dma=True)

    def lat_attention(l, side_gen=None, mod_gen=None):
        rdv = list(b_VAk)

        def side_step(n=1):
            if side_gen is not None:
                for _ in range(n):
                    next(side_gen, None)
            if mod_gen is not None:
                next(mod_gen, None)

        load_KT(l, R_DAK, 64, 0)
        for h in range(4):
            cache_T(c_dak[l, h], 64, lambda tb, h=h: KTb[0:64, h, DEC_S + tb * 128:DEC_S + (tb + 1) * 128], [b_KTh[h]])
        load_V(l, 0, c_dav[l])
        jobs = []
        for h in range(4):
            for qh in range(2):
                idx = h * 2 + qh
                tq, btq = tbt[idx // 5], b_tbt[idx // 5]
                qb = tq[:, :, :].rearrange("p a w -> p (a w)")[:, (idx % 5) * 512:(idx % 5 + 1) * 512]
                add("pool", lambda e, qb=qb: e.memset(qb, 0.0), (), [btq])
                add("pool", lambda e, qb=qb, h=h, qh=qh: e.tensor_copy(out=qb[0:32, 0:256], in_=QT2[1][0][0:32, h, qh * 256:(qh + 1) * 256]),
                    [b_QT[1][0]], [btq])
                add("pool", lambda e, qb=qb, h=h, qh=qh: e.tensor_copy(out=qb[32:64, 256:512], in_=QT2[1][0][32:64, h, qh * 256:(qh + 1) * 256]),
                    [b_QT[1][0]], [btq])
                J = dict(kt_fn=(lambda kb, h=h: KTb[0:128, h, kb * 128:(kb + 1) * 128]), q_ap=qb,
                         va_fn=(lambda kb, h=h: VAb[:, kb, h, 0:65]), scale=32 ** -0.5, reads=rdv + [b_KTh[h], btq])
                J["post"] = (lambda h=h, qh=qh, J=J: da_post(l, 1, h, J["acc"][0], J["acc"][1], J["acc"][0], J["acc"][1],
                                                             nq=256, q0=qh * 256, c1=0, c2=256))
                jobs.append(J)
        stream_attention(jobs, side_cb=lambda: side_step(1), burst=BURST)
        src = dram_ap(payA_out[l], R_CKV * T, [[T, 128], [PA * T, 4], [1, T]])
        add("sp", lambda e, src=src: e.dma_start(out=ckvT[:, 0:DEC_S].rearrange("p (r t) -> p r t", r=4), in_=src), [b_payoA[l]], [b_ckvT], dma=True)
        cache_T(c_ckv[l], 128, lambda tb: ckvT[:, DEC_S + tb * 128:DEC_S + (tb + 1) * 128], [b_ckvT])
        load_KT(l, R_KR, 32, 64, heads=False)
        for h in range(4):
            cache_T(c_kr[l], 32, lambda tb, h=h: KTb[64:96, h, DEC_S + tb * 128:DEC_S + (tb + 1) * 128], [b_KTh[h]])
        for h in range(4):
            for cb in range(5):
                n0 = cb * 512
                n = min(512, DEC_S + PAST - n0)
                pt, pb = gp()
                add("pe", lambda e, pt=pt, h=h, n0=n0, n=n: e.matmul(pt[0:64, 0:n], lhsT=wkv_s[:, h * 128:h * 128 + 64], rhs=ckvT[:, n0:n0 + n],
                                                                    start=True, stop=True), [b_wsm, b_ckvT], [pb])
                evac(KTb[0:64, h, n0:n0 + n], pt[0:64, 0:n], [pb], [b_KTh[h]])
        for kb in range(18):
            pt, pb = gp()
            add("pe", lambda e, pt=pt, kb=kb: e.matmul(pt[:, 0:256], lhsT=ckvT[:, kb * 128:(kb + 1) * 128],
                                                      rhs=wkv_s[:, :].rearrange("p (h c) -> p h c", h=4)[:, :, 64:128], start=True, stop=True),
                [b_wsm, b_ckvT], [pb])
            evac(VAb[:, kb, :, 0:64], pt[:, 0:256].rearrange("p (h d) -> p h d", h=4), [pb], [b_VAk[kb]])
        jobs = []
        for h in range(4):
            J = dict(kt_fn=(lambda kb, h=h: KTb[0:128, h, kb * 128:(kb + 1) * 128]), q_ap=q_ap(1, 1, h, 0, 128),
                     va_fn=(lambda kb, h=h: VAb[:, kb, h, 0:65]), scale=96 ** -0.5, reads=rdv + [b_KTh[h], b_QT[1][1]])
            J["post"] = (lambda h=h, J=J: plain_post(1, 1, h, J["acc"][0], J["acc"][1]))
            jobs.append(J)
        stream_attention(jobs, side_cb=lambda: side_step(2))
        load_KT(l, R_NAK, 64, 0)
        for h in range(4):
            cache_T(c_nak[l, h], 64, lambda tb, h=h: KTb[0:64, h, DEC_S + tb * 128:DEC_S + (tb + 1) * 128], [b_KTh[h]])
            add("pool", lambda e, h=h: e.dma_start(out=KTb[64:96, h, 0:DEC_S], in_=rowind_d), (), [b_KTh[h]], dma=True)
            add("pool", lambda e, h=h: e.memset(KTb[64:96, h, DEC_S:DEC_S + PAST], 0.0), (), [b_KTh[h]])
            evac(QT2[1][0][0:64, h, :], QT2[1][0][64:128, h, :], [b_QT[1][2], b_QT[1][0]], [b_QT[1][0], b_QT[1][2]])
            add("pool", lambda e, h=h: e.dma_start(out=QT2[1][0][64:96, h, :], in_=rowsel_d), (), [b_QT[1][0], b_QT[1][2]], dma=True)
        load_V(l, 256, c_nav[l])
        jobs = []
        for h in range(4):
            ti = h % 2

            def pre(h=h, ti=ti):
                for j in range(2):
                    src = bass.AP(tz_all, l * TZ_SZ + (h * (A2 + 1) + (1 - j)) * 64 * TZL + 63, [[TZL - 1, 64], [64 * TZL, A2], [1, 64]])
                    add("pool", lambda e, j=j, src=src, ti=ti: e.dma_start(out=tbt[ti][j * 64:(j + 1) * 64, :, :], in_=src), (), [b_tbt[ti]], dma=True)
                add("dve", lambda e, ti=ti: e.scalar_tensor_tensor(out=tbt[ti][:], in0=tbt[ti][:], scalar=8.0,
                                                                   in1=colm_s[:, :].unsqueeze(1).to_broadcast([128, A2, 64]),
                                                                   op0=ALU.mult, op1=ALU.add), [b_tbt[ti], b_prm], [b_tbt[ti]])

            def extra(kb, ti=ti):
                if kb >= 16:
                    return []
                a0 = 37 - 2 * kb
                return [(identb[:, :], tbt[ti][:, a0:a0 + 8, :], [b_identb, b_tbt[ti]])]

            J = dict(kt_fn=(lambda kb, h=h: KTb[0:128, h, kb * 128:(kb + 1) * 128]), q_ap=QT2[1][0][0:128, h, :],
                     va_fn=(lambda kb, h=h: VAb[:, kb, h, 0:65]), scale=64 ** -0.5, reads=rdv + [b_KTh[h], b_QT[1][2]],
                     extra_fn=extra, pre=pre)
            J["post"] = (lambda h=h, J=J: plain_post(1, 2, h, J["acc"][0], J["acc"][1]))
            jobs.append(J)
        stream_attention(jobs, side_cb=lambda: side_step(2))

    def conv_module(l, g):
        if g == 1:
            for c in range(2):
                src = dram_ap(payB_out[l], R_HALO * T + c * 128 * 30, [[30, 128], [PB * T, 4], [1, 30]])
                add("sp", lambda e, src=src, c=c: e.dma_start(out=halo[:, c, :, :], in_=src), [b_payoB[l]], [b_halo], dma=True)
            for side in range(2):
                dst = gpad[1][:, :, 0, 0:15] if side == 0 else gpad[1][:, :, 1, 271:286]
                for rk in range(4):
                    srcv = halo[:, :, rk, 15:30] if side == 0 else halo[:, :, rk, 0:15]
                    sc = halsel_s[:, side * 4 + rk:side * 4 + rk + 1]
                    if rk == 0:
                        add("dve", lambda e, dst=dst, srcv=srcv, sc=sc: e.tensor_scalar(out=dst, in0=srcv, scalar1=sc, scalar2=None, op0=ALU.mult),
                            [b_halo, b_prm, b_gpad[1]], [b_gpad[1]])
                    else:
                        add("dve", lambda e, dst=dst, srcv=srcv, sc=sc: e.scalar_tensor_tensor(out=dst, in0=srcv, scalar=sc, in1=dst,
                                                                                              op0=ALU.mult, op1=ALU.add),
                            [b_halo, b_prm, b_gpad[1]], [b_gpad[1]])
        for c in range(2):
            for j in range(31):
                if g == 0:
                    src = gpad[0][:, c, :, j:j + 256]
                    dst = cacc[:, c, :].rearrange("p (s t) -> p s t", s=2)
                    ops = [(src, dst)]
                else:
                    ops = []
                    n_a = max(0, min(256, 271 - j))
                    if n_a > 0:
                        ops.append((gpad[1][:, c, 0, j:j + n_a], cacc[:, c, 0:n_a]))
                    if n_a < 256:
                        ops.append((gpad[1][:, c, 1, 15 + (n_a + j - 271):15 + (256 + j - 271)], cacc[:, c, n_a:256]))
                    n_b = max(0, min(256, 271 - (256 + j)))
                    if n_b > 0:
                        ops.append((gpad[1][:, c, 0, 256 + j:256 + j + n_b], cacc[:, c, 256:256 + n_b]))
                    ops.append((gpad[1][:, c, 1, 15 + (256 + n_b + j - 271):15 + (512 + j - 271)], cacc[:, c, 256 + n_b:512]))
                if j % 4 == 3:
                    yield
                for (src, dst) in ops:
                    if j == 0:
                        add("dve", lambda e, src=src, dst=dst, c=c: e.tensor_scalar(out=dst, in0=src, scalar1=dw_s[:, l, c, 0:1], scalar2=cb_s[:, l, c:c + 1],
                                                                                    op0=ALU.mult, op1=ALU.add), [b_gpad[g], b_prm, b_cacc], [b_cacc])
                    else:
                        add("dve", lambda e, src=src, dst=dst, c=c, j=j: e.scalar_tensor_tensor(out=dst, in0=src, scalar=dw_s[:, l, c, j:j + 1], in1=dst,
                                                                                                op0=ALU.mult, op1=ALU.add), [b_gpad[g], b_prm, b_cacc], [b_cacc])
        p1, pb1 = gp()
        p2, pb2 = gp()
        for c in range(2):
            tt, tb_ = tmp()
            add("act", lambda e, tt=tt, c=c: e.activation(out=tt[:], in_=cacc[:, c, :], func=AF.Square), [b_cacc], [tb_])
            add("pe", lambda e, c=c: e.matmul(p1[:, :], lhsT=ones_f[:, :], rhs=cacc[:, c, :], start=(c == 0), stop=(c == 1)), [b_cacc, b_ones_f], [pb1])
            add("pe", lambda e, tt=tt, c=c: e.matmul(p2[:, :], lhsT=ones_f[:, :], rhs=tt[:], start=(c == 0), stop=(c == 1)), [tb_, b_ones_f], [pb2])
        mean, bmean = tmp()
        var, bvar = tmp()
        add("act", lambda e: e.activation(out=mean[:], in_=p1[:, :], func=AF.Identity, scale=1.0 / 256), [pb1], [bmean])
        add("dve", lambda e: e.tensor_tensor(out=var[:], in0=mean[:], in1=mean[:], op=ALU.mult), [bmean], [bvar])
        add("dve", lambda e: e.scalar_tensor_tensor(out=var[:], in0=p2[:, :], scalar=1.0 / 256, in1=var[:], op0=ALU.mult, op1=ALU.subtract),
            [pb2, bvar], [bvar])
        add("act", lambda e: e.activation(out=var[:], in_=var[:], func=AF.Ln, bias=epsc[:, 0:1], scale=1.0), [bvar, b_epsc], [bvar])
        add("act", lambda e: e.activation(out=var[:], in_=var[:], func=AF.Exp, scale=-0.5), [bvar], [bvar])
        for c in range(2):
            add("dve", lambda e, c=c: e.tensor_tensor(out=cacc[:, c, :], in0=cacc[:, c, :], in1=mean[:], op=ALU.subtract), [b_cacc, bmean], [b_cacc])
            add("dve", lambda e, c=c: e.tensor_tensor(out=cacc[:, c, :], in0=cacc[:, c, :], in1=var[:], op=ALU.mult), [b_cacc, bvar], [b_cacc])
            add("act", lambda e, c=c: e.activation(out=cacc[:, c, :], in_=cacc[:, c, :], func=AF.Identity, bias=lnb_s[:, l, c:c + 1],
                                                   scale=lng_s[:, l, c:c + 1]), [b_cacc, b_prm], [b_cacc])
            tt, tb_ = tmp()
            add("act", lambda e, tt=tt, c=c: e.activation(out=tt[:], in_=cacc[:, c, :], func=AF.Sigmoid), [b_cacc], [tb_])
            add("dve", lambda e, tt=tt, c=c: e.tensor_tensor(out=mixT[g][:, 6 + c, :], in0=cacc[:, c, :], in1=tt[:], op=ALU.mult),
                [b_cacc, tb_], [b_mix[g][6 + c]])

    def out_proj(l, groups):
        for ti in range(2):
            wt, wb = wtile(w_out[l, :, ti * 512:(ti + 1) * 512], 512)
            for cc in range(4):
                oc = ti * 4 + cc
                for g in groups:
                    pt, pb = gp()
                    for k in range(8):
                        add("pe", lambda e, pt=pt, k=k, cc=cc, wt=wt, g=g: e.matmul(pt[:, :], lhsT=wt[:, k, cc * 128:(cc + 1) * 128], rhs=mixT[g][:, k, :],
                                                                                   start=(k == 0), stop=(k == 7)), [wb, b_mix[g][k]], [pb])
                    add("dve", lambda e, pt=pt, g=g, oc=oc: e.scalar_tensor_tensor(out=xT[g][:, oc, :], in0=pt[:, :], scalar=mcol(l, "g1", oc, g),
                                                                                 in1=xT[g][:, oc, :], op0=ALU.mult, op1=ALU.add),
                        [pb, b_mod[l], b_xT[g][oc]], [b_xT[g][oc]])

    def ffn(l, groups):
        for blk in range(4):
            for ti in range(2):
                wt, wb = wtile(w_ff1[l, :, blk * 1024 + ti * 512: blk * 1024 + (ti + 1) * 512], 512)
                for cc in range(4):
                    fc = ti * 4 + cc
                    for g in groups:
                        pt, pb = gp()
                        for k in range(8):
                            add("pe", lambda e, pt=pt, k=k, cc=cc, wt=wt, g=g: e.matmul(pt[:, :], lhsT=wt[:, k, cc * 128:(cc + 1) * 128], rhs=hT[g][:, k, :],
                                                                                       start=(k == 0), stop=(k == 7)), [wb, b_hT[g]], [pb])
                        sq, bq = sqt()
                        add("act", lambda e, pt=pt, sq=sq: e.activation(out=sq[:], in_=pt[:, :], func=AF.Relu), [pb], [bq])
                        add("dve", lambda e, sq=sq, g=g, fc=fc: e.tensor_tensor(out=mixT[g][:, fc, :], in0=sq[:], in1=sq[:], op=ALU.mult),
                            [bq], [b_mix[g][fc]])
            for ti in range(2):
                wt, wb = wtile(w_ff2[l, blk * 1024:(blk + 1) * 1024, ti * 512:(ti + 1) * 512], 512)
                for cc in range(4):
                    oc = ti * 4 + cc
                    for g in groups:
                        pt, pb = gp()
                        for k in range(8):
                            add("pe", lambda e, pt=pt, k=k, cc=cc, wt=wt, g=g: e.matmul(pt[:, :], lhsT=wt[:, k, cc * 128:(cc + 1) * 128], rhs=mixT[g][:, k, :],
                                                                                       start=(k == 0), stop=(k == 7)), [wb, b_mix[g][k]], [pb])
                        add("dve", lambda e, pt=pt, g=g, oc=oc: e.scalar_tensor_tensor(out=xT[g][:, oc, :], in0=pt[:, :], scalar=mcol(l, "g2", oc, g),
                                                                                     in1=xT[g][:, oc, :], op0=ALU.mult, op1=ALU.add),
                            [pb, b_mod[l], b_xT[g][oc]], [b_xT[g][oc]])

    def dump8(name, t, bufs):
        if name not in dbg:
            return
        o = dout("dbg_" + name, [128, 8, T])
        for j in range(8):
            tt, tb_ = tmp()
            add("dve", lambda e, tt=tt, j=j: e.tensor_copy(out=tt[:], in_=t[:, j, :]), [bufs[j]], [tb_])
            add("sp", lambda e, tt=tt, j=j: e.dma_start(out=o[:, j, :], in_=tt[:]), [tb_], (), dma=True)
        dbg_out[name] = [128, 8, T]

    groups = [0, 1] if do_lat else [0]
    import os as _os
    _l1 = _os.environ.get("L1STAGES")
    _stages0 = stages
    for l in range(nlayers):
        stages = _stages0 if (l == 0 or _l1 is None) else set(_l1.split(","))
        if "mod" in stages and l == 0:
            for _ in modulation(0):
                pass
        if "norm1" in stages:
            for g in groups:
                rmsnorm_mod(l, g, 0)
        if "proj" in stages:
            input_proj(l)
        side = modulation(l + 1) if ("mod" in stages and l + 1 < nlayers) else None
        if "ctxattn" in stages:
            ctx_attention(l, None if (do_lat and "latattn" in stages) else side)
        if "conv" in stages:
            for _ in conv_module(l, 0):
                pass
        dump8("mix0%d" % l, mixT[0], b_mix[0])
        if do_lat:
            side2 = conv_module(l, 1) if "conv" in stages else None
            if "latattn" in stages:
                lat_attention(l, side2, side)
            if side2 is not None:
                for _ in side2:
                    pass
        if side is not None:
            for _ in side:
                pass
            dump8("mix1%d" % l, mixT[1], b_mix[1])
        if "outproj" in stages:
            out_proj(l, groups)
        for g in groups:
            dump8("xattn%d%d" % (g, l), xT[g], b_xT[g])
        if "norm2" in stages:
            for g in groups:
                rmsnorm_mod(l, g, 1)
        if "ffn" in stages:
            ffn(l, groups)
        for g in groups:
            dump8("xffn%d%d" % (g, l), xT[g], b_xT[g])
    for g in groups:
        if "final" not in stages:
            continue
        pt, pb = gp()
        for j in range(8):
            sq, bq = sqt()
            add("act", lambda e, sq=sq, j=j, g=g: e.activation(out=sq[:], in_=xT[g][:, j, :], func=AF.Square), [b_xT[g][j]], [bq])
            add("pe", lambda e, sq=sq, j=j, pt=pt: e.matmul(pt[:, :], lhsT=ones_b[:, :], rhs=sq[:], start=(j == 0), stop=(j == 7)), [bq, b_ones_b], [pb])
        add("act", lambda e, pt=pt: e.activation(out=rstd[:], in_=pt[:, :], func=AF.Ln, bias=epsc[:, 0:1], scale=1.0 / D), [pb, b_epsc], [b_rstd])
        add("act", lambda e: e.activation(out=rstd[:], in_=rstd[:], func=AF.Exp, scale=-0.5), [b_rstd], [b_rstd])
        for j in range(8):
            add("dve", lambda e, j=j, g=g: e.scalar_tensor_tensor(out=xT[g][:, j, :], in0=xT[g][:, j, :], scalar=gfin_s[:, j:j + 1], in1=rstd[:],
                                                                  op0=ALU.mult, op1=ALU.mult), [b_xT[g][j], b_rstd, b_prm], [b_xT[g][j]])
    for g in groups:
        for tb in range(4):
            for half in range(2):
                pt, pb = gp()
                for jj in range(4):
                    j = half * 4 + jj
                    add("pe", lambda e, pt=pt, j=j, jj=jj, tb=tb, g=g: e.transpose(out=pt[:, jj * 128:(jj + 1) * 128],
                                                                                   in_=xT[g][:, j, tb * 128:(tb + 1) * 128], identity=ident[:]),
                        [b_xT[g][j], b_ident], [pb])
                stg, bstg = tmp()
                evac(stg[:, :], pt[:, :], [pb], [bstg])
                add("sp", lambda e, g=g, tb=tb, half=half, stg=stg: e.dma_start(
                    out=y_out[g * T + tb * 128:g * T + (tb + 1) * 128, half * 512:(half + 1) * 512], in_=stg[:, :]), [bstg], (), dma=True)
    S.emit(st)
    st.close()
    return nc, dbg_out


def _colT(v, nchunk):
    return np.ascontiguousarray(v.reshape(nchunk, 128).T)


def prepare_inputs(inp):
    f = lambda a: np.ascontiguousarray(np.asarray(a, dtype=np.float32))
    x_prompt, x_sample, c = f(inp["x_prompt"]), f(inp["x_sample"]), f(inp["c"])
    w_in = f(inp["w_in"])
    shared = {}
    shared["bmodT"] = np.ascontiguousarray(np.stack([_colT(f(inp["b_mod"])[l], 48) for l in range(L)], 1))
    shared["gmixT"] = np.ascontiguousarray(np.stack([_colT(f(inp["g_norm_mix"])[l], 8) for l in range(L)], 1))
    shared["gffT"] = np.ascontiguousarray(np.stack([_colT(f(inp["g_norm_ff"])[l], 8) for l in range(L)], 1))
    shared["gfinT"] = _colT(f(inp["g_final"]), 8)
    shared["w_mod"] = f(inp["w_mod"])
    shared["w_in"] = w_in
    idx = []
    for base in (0, 256):
        for h in range(4):
            for m in range(2):
                idx += [base + h * 64 + m * 32 + P32[j] for j in range(32)]
    idx += list(range(1088, 1152))
    idx += [1152 + P32[j] for j in range(32)]
    shared["w_inp"] = np.ascontiguousarray(w_in[:, :, idx])
    shared["w_out"] = f(inp["w_out"])
    shared["w_ff1"] = f(inp["w_ff1"])
    shared["w_ff2"] = f(inp["w_ff2"])
    wq = f(inp["w_mla_qup"])
    shared["wqup"] = wq
    wqp = np.zeros_like(wq)
    for h in range(4):
        for j in range(32):
            wqp[:, :, h * 96 + 64 + j] = wq[:, :, h * 96 + 64 + P32[j]]
    shared["wqupp"] = wqp
    shared["wkvup"] = f(inp["w_mla_kvup"])
    shared["gqT"] = np.ascontiguousarray(np.stack([_colT(f(inp["g_mla_q"])[l], 2) for l in range(L)], 1))
    shared["gkvc"] = np.ascontiguousarray(f(inp["g_mla_kv"]).T)
    shared["gkvr"] = np.ascontiguousarray(np.broadcast_to(f(inp["g_mla_kv"])[None], (128, L, 128)))
    gs = f(inp["g_da_subln"])
    shared["gsubc"] = np.ascontiguousarray(np.concatenate([gs.T, gs.T], 0))
    lamv = np.stack([f(inp["da_lambda_q1"]), f(inp["da_lambda_k1"]), f(inp["da_lambda_q2"]), f(inp["da_lambda_k2"])], 1)
    shared["lamv"] = np.ascontiguousarray(np.broadcast_to(lamv[None], (128, L, 4, 32)))
    dw = f(inp["conv_dw"])
    shared["dwT"] = np.ascontiguousarray(dw.reshape(L, 31, 2, 128).transpose(3, 0, 2, 1))
    for nm, key in (("cbT", "conv_b"), ("lngT", "conv_ln_g"), ("lnbT", "conv_ln_b")):
        shared[nm] = np.ascontiguousarray(f(inp[key]).reshape(L, 2, 128).transpose(2, 0, 1))
    shared["ident"] = np.eye(128, dtype=np.float32)
    w = np.arange(GRID_W)
    cs = np.clip(w - 8, 0, GRID_W - 16)
    col_ok = (w[None, :] >= cs[:, None]) & (w[None, :] < cs[:, None] + 16)
    cm = np.where(col_ok.T, 0.0, -BIG * 8).astype(np.float32)
    shared["colmask"] = np.ascontiguousarray(np.concatenate([cm, cm], 0))
    ri = np.zeros((32, DEC_S), np.float32)
    ri[np.arange(DEC_S) // 64, np.arange(DEC_S)] = 1.0
    shared["rowind"] = ri
    rpb = f(inp["na_rpb"])
    half = 8
    freqs = (10000.0 ** (-np.arange(half, dtype=np.float32) * 2.0 / 16)).astype(np.float32)
    maps = []
    for core in range(NCORE):
        b, r = core // 4, core % 4
        m = dict(shared)
        m["xin"] = np.ascontiguousarray(np.concatenate([x_prompt[2 * core], x_prompt[2 * core + 1], x_sample[b, r * T:(r + 1) * T]], 0))
        cv = np.stack([f(inp["c_ctx"]), c[b]], 1)
        m["cvT"] = np.ascontiguousarray(cv.reshape(8, 128, 2).transpose(1, 0, 2))
        m["c_dak"] = f(inp["cache_da_k"])[b]
        m["c_dav"] = f(inp["cache_da_v"])[b]
        m["c_ckv"] = f(inp["cache_mla_ckv"])[b]
        m["c_kr"] = f(inp["cache_mla_krope"])[b]
        m["c_nak"] = f(inp["cache_na_k"])[b]
        m["c_nav"] = f(inp["cache_na_v"])[b]
        t = r * T + np.arange(T)
        rows = (t // GRID_W).astype(np.float32)
        cols = (t % GRID_W).astype(np.float32)
        ang = np.concatenate([rows[None, :] * freqs[:, None], rows[None, :] * freqs[:, None],
                              cols[None, :] * freqs[:, None], cols[None, :] * freqs[:, None]], 0)
        cos32 = np.cos(ang).astype(np.float32)
        sin32 = np.sin(ang).astype(np.float32)
        sgn = np.concatenate([-np.ones(8), np.ones(8), -np.ones(8), np.ones(8)]).astype(np.float32)[:, None]
        m["cosT"] = np.ascontiguousarray(np.tile(cos32, (4, 1)))
        m["sinT"] = np.ascontiguousarray(np.tile(sin32 * sgn, (4, 1)))
        r0 = r * 8
        qrow = r0 + np.arange(T) // 64
        start = np.clip(qrow - 4, 0, NROWS - 8)
        rk = np.arange(32)
        ok = (rk[:, None] >= start[None, :]) & (rk[:, None] < start[None, :] + 8)
        m["rowsel"] = np.where(ok, 0.0, -BIG * 8).astype(np.float32)
        P = np.zeros((L, 4, A2 + 1, TZL), np.float32)
        for a2 in range(A2):
            a = 45 - a2 - r0
            if 0 <= a <= 14:
                P[:, :, a2, 48:79] = rpb[:, :, a, ::-1]
        m["tz_rep"] = np.ascontiguousarray(np.broadcast_to(P.reshape(L * 4 * (A2 + 1), 1, TZL), (L * 4 * (A2 + 1), 64, TZL)))
        hs = np.zeros((128, 8), np.float32)
        if r > 0:
            hs[:, r - 1] = 1.0
        if r < 3:
            hs[:, 4 + r + 1] = 1.0
        m["halsel"] = hs
        maps.append(m)
    return maps


_NC_CACHE = {}


def kernel(**inputs):
    maps = prepare_inputs(inputs)
    if "nc" not in _NC_CACHE:
        _NC_CACHE["nc"] = build()[0]
    nc = _NC_CACHE["nc"]
    res = run_bass_kernel_spmd(nc, maps, core_ids=list(range(NCORE)))
    R = res.results
    y_prompt = np.zeros((NB, SEQ, D), np.float32)
    y_sample = np.zeros((DEC_B, DEC_S, D), np.float32)
    outs = {k: [] for k in ("o_dak", "o_dav", "o_ckv", "o_kr", "o_nak", "o_nav")}
    for core in range(NCORE):
        b, r = core // 4, core % 4
        y = R[core]["y_out"]
        y_prompt[2 * core] = y[0:256]
        y_prompt[2 * core + 1] = y[256:512]
        y_sample[b, r * T:(r + 1) * T] = y[512:1024]
        for k in outs:
            outs[k].append(R[core][k])
    cat = lambda k: np.ascontiguousarray(np.concatenate(outs[k], 0).astype(np.float32))
    return (y_prompt, y_sample, cat("o_dak"), cat("o_dav"), cat("o_ckv"), cat("o_kr"), cat("o_nak"), cat("o_nav"))
```

```python
import contextlib
import math
import numpy as np
import concourse.bass as bass
import concourse.mybir as mybir
from concourse.bass_utils import run_bass_kernel_spmd

F32 = mybir.dt.float32
BF16 = mybir.dt.bfloat16
ALU = mybir.AluOpType
AF = mybir.ActivationFunctionType

ENGINES = ("pe", "act", "dve", "pool", "sp")

D = 1024
L = 2
NB = 16
SEQ = 256
DEC_B = 2
DEC_S = 2048
PAST = 256
GRID_W = 64
NROWS = DEC_S // GRID_W
EPS = 1e-6
T = 512
NCORE = 8
BIG = 30000.0
A2 = 46
TZL = 127
PA = 672
PB = 527
R_DAK, R_CKV, R_KR, R_NAK = 0, 256, 384, 416
R_V, R_HALO = 0, 512
P32 = [8, 9, 10, 11, 12, 13, 14, 15, 0, 1, 2, 3, 4, 5, 6, 7,
       24, 25, 26, 27, 28, 29, 30, 31, 16, 17, 18, 19, 20, 21, 22, 23]


class Buf:
    __slots__ = ("name", "last_w", "readers", "excl")

    def __init__(self, name, excl=False):
        self.name = name
        self.last_w = None
        self.readers = []
        self.excl = excl


class Op:
    __slots__ = ("eng", "fn", "deps", "dma", "sig", "sem", "val")

    def __init__(self, eng, fn, dma):
        self.eng = eng
        self.fn = fn
        self.dma = dma
        self.deps = []
        self.sig = False
        self.sem = None
        self.val = 0


class Sched:
    def __init__(self, nc, n_dma_slots=16):
        self.nc = nc
        self.ops = []
        self.n_dma_slots = n_dma_slots

    def add(self, eng, fn, reads=(), writes=(), dma=False):
        op = Op(eng, fn, dma)
        ex = [b for b in reads if b.excl]
        if ex:
            reads = [b for b in reads if not b.excl]
            writes = list(writes) + ex
        deps = {}
        for b in reads:
            if b.last_w is not None:
                deps[id(b.last_w)] = b.last_w
        for b in writes:
            if b.last_w is not None:
                deps[id(b.last_w)] = b.last_w
            for r in b.readers:
                deps[id(r)] = r
        for b in reads:
            b.readers.append(op)
        for b in writes:
            b.last_w = op
            b.readers = []
        for d in deps.values():
            if d.eng == "pe" and eng == "pe" and not d.dma and not dma:
                continue
            op.deps.append(d)
            d.sig = True
        self.ops.append(op)
        return op

    def emit(self, stack):
        nc = self.nc
        eng_sem = {e: stack.enter_context(nc.semaphore("s_" + e)) for e in ENGINES}
        dma_engs = sorted({op.eng for op in self.ops if op.dma is True})
        dma_slots = {e: [stack.enter_context(nc.semaphore("d_%s_%d" % (e, i)))
                         for i in range(self.n_dma_slots)] for e in dma_engs}
        cnt = {e: 0 for e in ENGINES}
        dcnt = {e: 0 for e in dma_engs}
        slot_uses = {e: [0] * self.n_dma_slots for e in dma_engs}
        slot_prev = {}
        ncc = 0
        for op in self.ops:
            if op.dma == "cc":
                op.sem = stack.enter_context(nc.semaphore("cc_%d" % ncc))
                ncc += 1
                op.val = 1
            elif op.dma:
                k = dcnt[op.eng] % self.n_dma_slots
                dcnt[op.eng] += 1
                slot_uses[op.eng][k] += 1
                op.sem = dma_slots[op.eng][k]
                op.val = 16 * slot_uses[op.eng][k]
                slot_prev[id(op)] = (op.sem, op.val - 16)
            elif op.sig:
                cnt[op.eng] += 1
                op.sem = eng_sem[op.eng]
                op.val = cnt[op.eng]
        block = stack.enter_context(nc.Block())
        handles = {"pe": block.tensor, "act": block.scalar, "dve": block.vector,
                   "pool": block.gpsimd, "sp": block.sync}
        all_async = [op for op in self.ops if op.dma]

        def run_engine(ename):
            def body(eng):
                waited = {}
                for op in self.ops:
                    if op.eng != ename:
                        continue
                    need = {}
                    for d in op.deps:
                        key = id(d.sem)
                        if key not in need or need[key][1] < d.val:
                            need[key] = (d.sem, d.val)
                    if op.dma is True:
                        s, v = slot_prev[id(op)]
                        if v > 0:
                            key = id(s)
                            if key not in need or need[key][1] < v:
                                need[key] = (s, v)
                    for key, (s, v) in need.items():
                        if waited.get(key, 0) >= v:
                            continue
                        eng.wait_ge(s, v)
                        waited[key] = v
                    ins = op.fn(eng)
                    if op.dma == "cc":
                        ins.then_inc(op.sem)
                    elif op.dma:
                        ins.then_inc(op.sem, 16)
                    elif op.sig:
                        ins.then_inc(op.sem, 1)
                if ename == "sp":
                    last = {}
                    for op in all_async:
                        last[id(op.sem)] = (op.sem, op.val)
                    for key, (s, v) in last.items():
                        if waited.get(key, 0) < v:
                            eng.wait_ge(s, v)
                    for e in ENGINES:
                        if cnt[e] > 0:
                            eng.wait_ge(eng_sem[e], cnt[e])
            handles[ename](body)

        for e in ENGINES:
            run_engine(e)


def build(dbg=(), nlayers=L, do_lat=True, stages=None):
    if stages is None:
        stages = {"mod", "norm1", "proj", "ctxattn", "conv", "latattn", "outproj", "norm2", "ffn", "final"}
    nc = bass.Bass("TRN2", target_bir_lowering=False)
    st = contextlib.ExitStack()
    S = Sched(nc)
    dbg_out = {}

    def din(name, shape, dt=F32):
        return nc.dram_tensor(name, list(shape), dt, kind="ExternalInput").ap()

    def dout(name, shape, dt=F32):
        return nc.dram_tensor(name, list(shape), dt, kind="ExternalOutput").ap()

    xin = din("xin", [2 * T, D])
    cvT = din("cvT", [128, 8, 2])
    bmodT = din("bmodT", [128, L, 48])
    gmixT = din("gmixT", [128, L, 8])
    gffT = din("gffT", [128, L, 8])
    gfinT = din("gfinT", [128, 8])
    w_mod = din("w_mod", [L, D, 6 * D])
    w_in = din("w_in", [L, D, 2464])
    w_inp = din("w_inp", [L, D, 608])
    w_out = din("w_out", [L, D, D])
    w_ff1 = din("w_ff1", [L, D, 4 * D])
    w_ff2 = din("w_ff2", [L, 4 * D, D])
    wqup = din("wqup", [L, 256, 384])
    wqupp = din("wqupp", [L, 256, 384])
    wkvup = din("wkvup", [L, 128, 512])
    gqT = din("gqT", [128, L, 2])
    gkvc = din("gkvc", [128, L])
    gkvr = din("gkvr", [128, L, 128])
    gsubc = din("gsubc", [128, L])
    lamv = din("lamv", [128, L, 4, 32])
    tz_all = din("tz_rep", [L * 4 * (A2 + 1), 64, TZL]).tensor
    dwT = din("dwT", [128, L, 2, 31])
    cbT = din("cbT", [128, L, 2])
    lngT = din("lngT", [128, L, 2])
    lnbT = din("lnbT", [128, L, 2])
    c_dak = din("c_dak", [L, 4, PAST, 64])
    c_dav = din("c_dav", [L, 4, PAST, 64])
    c_ckv = din("c_ckv", [L, PAST, 128])
    c_kr = din("c_kr", [L, PAST, 32])
    c_nak = din("c_nak", [L, 4, PAST, 64])
    c_nav = din("c_nav", [L, 4, PAST, 64])
    cosT_d = din("cosT", [128, T])
    sinT_d = din("sinT", [128, T])
    colmask_d = din("colmask", [128, 64])
    rowsel_d = din("rowsel", [32, T])
    rowind_d = din("rowind", [32, DEC_S])
    halsel_d = din("halsel", [128, 8])
    ident_d = din("ident", [128, 128])
    y_out = dout("y_out", [2 * T, D])
    o_dak = dout("o_dak", [2, L, 4, SEQ, 64])
    o_dav = dout("o_dav", [2, L, 4, SEQ, 64])
    o_ckv = dout("o_ckv", [2, L, SEQ, 128])
    o_kr = dout("o_kr", [2, L, SEQ, 32])
    o_nak = dout("o_nak", [2, L, 4, SEQ, 64])
    o_nav = dout("o_nav", [2, L, 4, SEQ, 64])
    LROWS = 6 * PA + 6 * PB
    pay_all = nc.dram_tensor("pay_all", [L * LROWS, T], BF16)
    O_AIN, O_AOUT, O_BIN, O_BOUT = 0, PA, 5 * PA, 5 * PA + PB

    class _Sub:
        def __init__(self, row0, nrows):
            self.row0, self.nrows = row0, nrows

        def ap(self):
            return pay_all.ap()[self.row0:self.row0 + self.nrows, :]

    payA_in = [_Sub(l * LROWS + O_AIN, PA) for l in range(L)]
    payA_out = [_Sub(l * LROWS + O_AOUT, 4 * PA) for l in range(L)]
    payB_in = [_Sub(l * LROWS + O_BIN, PB) for l in range(L)]
    payB_out = [_Sub(l * LROWS + O_BOUT, 4 * PB) for l in range(L)]
    TZ_SZ = 4 * (A2 + 1) * 64 * TZL

    def dram_ap(base, off, dims):
        if isinstance(base, _Sub):
            return bass.AP(pay_all, base.row0 * T + off, dims)
        return bass.AP(base, off, dims)

    def sb(name, shape, dt):
        return st.enter_context(nc.sbuf_tensor("sb_" + name, list(shape), dt))

    add = S.add

    banks = []
    for i in range(8):
        banks.append((st.enter_context(nc.psum_tensor("ps%d" % i, [128, 512], F32)), Buf("ps%d" % i, excl=True)))
    gp_i = [0]
    ac_i = [0]

    def gp():
        b = banks[(0, 1, 2, 7)[gp_i[0] % 4]]
        gp_i[0] += 1
        return b

    def acb():
        b = banks[4 + ac_i[0] % 3]
        ac_i[0] += 1
        return b

    ev_i = [0]

    def evac(dst, src, r, w, scale=None):
        ev_i[0] += 1
        if ev_i[0] % 2 == 0:
            if scale is None:
                add("act", lambda e: e.activation(out=dst, in_=src, func=AF.Copy), r, w)
            else:
                add("act", lambda e: e.activation(out=dst, in_=src, func=AF.Identity, scale=scale), r, w)
        else:
            if scale is None:
                add("dve", lambda e: e.tensor_copy(out=dst, in_=src), r, w)
            else:
                add("dve", lambda e: e.tensor_scalar(out=dst, in0=src, scalar1=scale, scalar2=None, op0=ALU.mult), r, w)

    ident = sb("ident", [128, 128], F32); b_ident = Buf("ident")
    identb = sb("identb", [128, 128], BF16); b_identb = Buf("identb")
    ones_f = sb("ones_f", [128, 128], F32); b_ones_f = Buf("ones_f")
    ones_b = sb("ones_b", [128, 128], BF16); b_ones_b = Buf("ones_b")
    epsc = sb("epsc", [128, 1], F32); b_epsc = Buf("epsc")
    add("sp", lambda e: e.dma_start(out=ident[:], in_=ident_d), (), [b_ident], dma=True)
    add("pool", lambda e: e.memset(ones_f[:], 1.0), (), [b_ones_f])
    add("pool", lambda e: e.memset(ones_b[:], 1.0), (), [b_ones_b])
    add("pool", lambda e: e.memset(epsc[:], EPS), (), [b_epsc])
    add("dve", lambda e: e.tensor_copy(out=identb[:], in_=ident[:]), [b_ident], [b_identb])

    prm = {}
    b_prm = Buf("prm")
    b_small = []

    def load_small(name, src, shape):
        t = sb(name, shape, F32)
        bt = Buf("ld_" + name)
        b_small.append(bt)
        add("sp", lambda e: e.dma_start(out=t[:], in_=src), (), [bt], dma=True)
        prm[name] = t
        return t

    cv_s = load_small("cv_s", cvT, [128, 8, 2])
    bmod_s = load_small("bmod_s", bmodT, [128, L, 48])
    gmix_s = load_small("gmix_s", gmixT, [128, L, 8])
    gff_s = load_small("gff_s", gffT, [128, L, 8])
    gfin_s = load_small("gfin_s", gfinT, [128, 8])
    gq_s = load_small("gq_s", gqT, [128, L, 2])
    gkvc_s = load_small("gkvc_s", gkvc, [128, L])
    gkvr_s = sb("gkvr_s", [128, L, 128], BF16)
    b_small.append(Buf("ld_gkvr"))
    add("pool", lambda e: e.dma_start(out=gkvr_s[:], in_=gkvr), (), [b_small[-1]], dma=True)
    gsub_s = load_small("gsub_s", gsubc, [128, L])
    lam_s = load_small("lam_s", lamv, [128, L, 4, 32])
    dw_s = load_small("dw_s", dwT, [128, L, 2, 31])
    cb_s = load_small("cb_s", cbT, [128, L, 2])
    lng_s = load_small("lng_s", lngT, [128, L, 2])
    lnb_s = load_small("lnb_s", lnbT, [128, L, 2])
    cos_s = load_small("cos_s", cosT_d, [128, T])
    sin_s = load_small("sin_s", sinT_d, [128, T])
    colm_s = load_small("colm_s", colmask_d, [128, 64])
    halsel_s = load_small("halsel_s", halsel_d, [128, 8])
    joinc = sb("joinc", [128, 1], F32)
    add("dve", lambda e: e.memset(joinc[:], 0.0), list(b_small), [b_prm])

    lams = sb("lams", [128, L, 2], F32)
    neglam = sb("neglam", [128, L], F32)
    gsub2 = sb("gsub2", [128, L], F32)
    for l in range(L):
        lam_init = 0.8 - 0.6 * math.exp(-0.3 * l)
        for m in range(2):
            add("dve", lambda e, l=l, m=m: e.tensor_tensor(out=lam_s[:, l, 2 * m, :], in0=lam_s[:, l, 2 * m, :],
                                                          in1=lam_s[:, l, 2 * m + 1, :], op=ALU.mult), [b_prm], [b_prm])
            add("dve", lambda e, l=l, m=m: e.reduce_sum(out=lams[:, l, m:m + 1], in_=lam_s[:, l, 2 * m, :],
                                                       axis=mybir.AxisListType.X), [b_prm], [b_prm])
        add("act", lambda e, l=l: e.activation(out=lams[:, l, :], in_=lams[:, l, :], func=AF.Exp), [b_prm], [b_prm])
        add("dve", lambda e, l=l: e.tensor_tensor(out=neglam[:, l:l + 1], in0=lams[:, l, 1:2], in1=lams[:, l, 0:1],
                                                 op=ALU.subtract), [b_prm], [b_prm])
        add("dve", lambda e, l=l, li=lam_init: e.tensor_scalar(out=neglam[:, l:l + 1], in0=neglam[:, l:l + 1],
                                                              scalar1=-li, scalar2=None, op0=ALU.add), [b_prm], [b_prm])
        add("dve", lambda e, l=l, li=lam_init: e.tensor_scalar(out=gsub2[:, l:l + 1], in0=gsub_s[:, l:l + 1],
                                                              scalar1=1.0 - li, scalar2=None, op0=ALU.mult), [b_prm], [b_prm])

    xT = [sb("xT%d" % g, [128, 8, T], F32) for g in range(2)]
    b_xT = [[Buf("xT%d_%d" % (g, j)) for j in range(8)] for g in range(2)]
    hT = [sb("hT%d" % g, [128, 8, T], BF16) for g in range(2)]
    b_hT = [Buf("hT%d" % g) for g in range(2)]
    mixT = [sb("mixT%d" % g, [128, 8, T], BF16) for g in range(2)]
    b_mix = [[Buf("mix%d_%d" % (g, j)) for j in range(8)] for g in range(2)]
    NW = 3
    wpool = [sb("wp%d" % i, [128, 8, 512], BF16) for i in range(NW)]
    b_wp = [Buf("wp%d" % i) for i in range(NW)]
    wp_i = [0]

    def wtile(src_ap, ncols, nk=8):
        i = wp_i[0] % NW
        wp_i[0] += 1
        t, b = wpool[i], b_wp[i]
        add("pool", lambda e: e.dma_start(out=t[:, 0:nk, 0:ncols], in_=src_ap.rearrange("(k p) c -> p k c", p=128)),
            (), [b], dma=True)
        return t, b

    qd = sb("qd", [128, 2, T], F32); b_qd = Buf("qd_cacc_stage")
    stage1 = qd[:, :, :].rearrange("p a b -> p (a b)")

    class _Stage:
        def __getitem__(self, key):
            p, sl, c = key
            return stage1[p, c]
    stage = _Stage()
    b_stage = [b_qd, b_qd]
    rstd = sb("rstd", [128, T], F32); b_rstd = Buf("rstd")
    tmpf = [sb("tmpf%d" % i, [128, T], F32) for i in range(4)]
    b_tmpf = [Buf("tmpf%d" % i) for i in range(4)]
    tf_i = [0]

    def tmp():
        i = tf_i[0] % 4
        tf_i[0] += 1
        return tmpf[i], b_tmpf[i]

    sqb = [sb("sqb%d" % i, [128, T], BF16) for i in range(2)]
    b_sqb = [Buf("sqb%d" % i) for i in range(2)]
    sq_i = [0]

    def sqt():
        i = sq_i[0] % 2
        sq_i[0] += 1
        return sqb[i], b_sqb[i]

    for g in range(2):
        for tb in range(4):
            for half in range(2):
                stg, bstg = tmp()
                add("sp", lambda e, g=g, tb=tb, half=half, stg=stg: e.dma_start(
                    out=stg[:, :], in_=xin[g * T + tb * 128: g * T + (tb + 1) * 128, half * 512:(half + 1) * 512]), (), [bstg], dma=True)
                pt, pb = gp()
                for jj in range(4):
                    add("pe", lambda e, pt=pt, stg=stg, jj=jj: e.transpose(out=pt[:, jj * 128:(jj + 1) * 128],
                                                                           in_=stg[:, jj * 128:(jj + 1) * 128], identity=ident[:]),
                        [bstg, b_ident], [pb])
                evac(xT[g][:, half * 4:half * 4 + 4, tb * 128:(tb + 1) * 128],
                     pt[:, :].rearrange("p (j t) -> p j t", j=4), [pb], [b_xT[g][half * 4 + jj] for jj in range(4)])

    sil = sb("sil", [128, 8, 2], BF16); b_sil = Buf("sil")
    add("act", lambda e: e.activation(out=sil[:], in_=cv_s[:], func=AF.Silu), [b_prm], [b_sil])
    modv = sb("modv", [128, L, 48, 2], F32)
    gsc = sb("gsc", [128, L, 2, 8, 2], F32)
    b_mod = [Buf("mod%d" % l) for l in range(L)]

    def modulation(l):
        pt, pb = banks[3]
        for ti in range(12):
            wt, wb = wtile(w_mod[l, :, ti * 512:(ti + 1) * 512], 512)
            for cc in range(4):
                n = ti * 4 + cc
                for k in range(8):
                    add("pe", lambda e, wt=wt, cc=cc, k=k, n=n, pt=pt: e.matmul(pt[:, n * 2:n * 2 + 2], lhsT=wt[:, k, cc * 128:(cc + 1) * 128],
                                                                             rhs=sil[:, k, :], start=(k == 0), stop=(k == 7)),
                        [wb, b_sil], [pb])
            yield
        add("dve", lambda e, pt=pt: e.tensor_tensor(out=modv[:, l, :, :], in0=pt[:, 0:96].rearrange("p (n v) -> p n v", v=2),
                                                   in1=bmod_s[:, l, :].unsqueeze(2).to_broadcast([128, 48, 2]), op=ALU.add),
            [pb, b_prm], [b_mod[l]])
        for ni, (gsrc, off) in enumerate(((gmix_s, 8), (gff_s, 32))):
            add("dve", lambda e, ni=ni, gsrc=gsrc, off=off: e.scalar_tensor_tensor(
                out=gsc[:, l, ni, :, :], in0=modv[:, l, off:off + 8, :], scalar=1.0,
                in1=gsrc[:, l, :].unsqueeze(2).to_broadcast([128, 8, 2]), op0=ALU.add, op1=ALU.mult),
                [b_mod[l], b_prm], [b_mod[l]])

    def mcol(l, which, j, v):
        off = {"sh1": 0, "sc1": 8, "g1": 16, "sh2": 24, "sc2": 32, "g2": 40}[which]
        return modv[:, l, off + j, v:v + 1]

    def rmsnorm_mod(l, g, ni):
        pt, pb = gp()
        for j in range(8):
            sq, bq = sqt()
            add("act", lambda e, sq=sq, j=j: e.activation(out=sq[:], in_=xT[g][:, j, :], func=AF.Square), [b_xT[g][j]], [bq])
            add("pe", lambda e, sq=sq, j=j, pt=pt: e.matmul(pt[:, :], lhsT=ones_b[:, :], rhs=sq[:], start=(j == 0), stop=(j == 7)),
                [bq, b_ones_b], [pb])
        add("act", lambda e, pt=pt: e.activation(out=rstd[:], in_=pt[:, :], func=AF.Ln, bias=epsc[:, 0:1], scale=1.0 / D),
            [pb, b_epsc], [b_rstd])
        add("act", lambda e: e.activation(out=rstd[:], in_=rstd[:], func=AF.Exp, scale=-0.5), [b_rstd], [b_rstd])
        for j in range(8):
            tt, tb_ = tmp()
            add("dve", lambda e, tt=tt, j=j: e.scalar_tensor_tensor(out=tt[:], in0=xT[g][:, j, :], scalar=gsc[:, l, ni, j, g:g + 1],
                                                                    in1=rstd[:], op0=ALU.mult, op1=ALU.mult),
                [b_xT[g][j], b_rstd, b_mod[l]], [tb_])
            add("act", lambda e, tt=tt, j=j: e.activation(out=hT[g][:, j, :], in_=tt[:], func=AF.Identity,
                                                          bias=mcol(l, "sh1" if ni == 0 else "sh2", j, g), scale=1.0),
                [tb_, b_mod[l]], [b_hT[g]])

    def stat_rstd(src_list, nfeat, dst, b_dst, reads, K=128, n=T):
        pt, pb = gp()
        for i, src in enumerate(src_list):
            tt, tb_ = tmp()
            add("act", lambda e, tt=tt, src=src: e.activation(out=tt[0:K, 0:n], in_=src, func=AF.Square), reads, [tb_])
            add("pe", lambda e, tt=tt, i=i, pt=pt: e.matmul(pt[0:K, 0:n], lhsT=ones_f[0:K, 0:K], rhs=tt[0:K, 0:n],
                                                          start=(i == 0), stop=(i == len(src_list) - 1)),
                [tb_, b_ones_f], [pb])
        add("act", lambda e, pt=pt: e.activation(out=dst, in_=pt[0:K, 0:n], func=AF.Ln, bias=epsc[0:K, 0:1], scale=1.0 / nfeat),
            [pb, b_epsc], [b_dst])
        add("act", lambda e: e.activation(out=dst, in_=dst, func=AF.Exp, scale=-0.5), [b_dst], [b_dst])

    KTb = sb("KTb", [128, 4, DEC_S + PAST], BF16); b_KTh = [Buf("KT%d" % h) for h in range(4)]
    VAb = sb("VAb", [128, 18, 4, 72], BF16); b_VAk = [Buf("VA%d" % k) for k in range(18)]
    KTc2 = [sb("KTc%d" % m, [128, 4, T], BF16) for m in range(2)]
    b_KTc = [Buf("KTc%d" % m) for m in range(3)]
    VAc = [sb("VAc%d" % m, [128, 4, 4, 72], BF16) for m in range(3)]; b_VAc = [Buf("VAc%d" % m) for m in range(3)]
    QT2 = [[sb("QT%d_%d" % (g, m), [128, 4, T], BF16) for m in range(2)] for g in range(2)]
    b_QT = [[Buf("QT%d_%d" % (g, m)) for m in range(3)] for g in range(2)]
    add("pool", lambda e: e.memset(VAb[:, :, :, 64:72], 1.0), (), b_VAk)
    add("pool", lambda e: e.memset(KTb[64:128, :, :], 0.0), (), b_KTh)
    add("pool", lambda e: e.memset(QT2[1][1][96:128, :, :], 0.0), (), [b_QT[1][1]])
    for m in range(3):
        add("pool", lambda e, m=m: e.memset(VAc[m][:, :, :, 64:72], 1.0), (), [b_VAc[m]])

    def q_ap(g, m, h, p_lo, p_hi, c0=0, n=T):
        if m == 0:
            return QT2[g][0][p_lo:p_hi, h, c0:c0 + n]
        if m == 1:
            return QT2[g][1][p_lo:p_hi, h, c0:c0 + n]
        return QT2[g][0][64 + p_lo:64 + p_hi, h, c0:c0 + n]

    def kc_ap(m, h, p_lo, p_hi, c0, n):
        if m == 0:
            return KTc2[0][p_lo:p_hi, h, c0:c0 + n]
        if m == 1:
            return KTc2[1][p_lo:p_hi, h, c0:c0 + n]
        return KTc2[0][64 + p_lo:64 + p_hi, h, c0:c0 + n]
    Eb = [sb("E%d" % i, [128, T], BF16) for i in range(3)]
    b_E = [Buf("E%d" % i) for i in range(3)]
    e_i = [0]
    NSET = 2
    rsum_s = [sb("rsum0", [128, T], F32)] * 2; b_rsum_s = [Buf("rsum0")] * 2
    rsb0 = sb("rsb0", [128, T], BF16)
    rsb_row = [64, 64]
    b_rsb_s = [Buf("rsb0")] * 2
    dao = sb("dao", [128, T], F32); b_dao = Buf("dao")
    set_i = [0]
    ckvT = sb("ckvT", [128, DEC_S + PAST], BF16); b_ckvT = Buf("ckvT")
    ckvTc = ckvT; b_ckvTc = b_ckvT
    qdn = sb("qdn", [128, 2, T], BF16); b_qdn = Buf("qdn")
    wq_s = sb("wq_s", [128, 2, 384], BF16); wqp_s = sb("wqp_s", [128, 2, 384], BF16); wkv_s = sb("wkv_s", [128, 512], BF16)
    b_wsm = Buf("wsmall")
    gpad = [sb("gpad%d" % g, [128, 2, 2, 15 + 256 + 15], BF16) for g in range(2)]
    b_gpad = [Buf("gpad%d" % g) for g in range(2)]
    for g in range(2):
        add("pool", lambda e, g=g: e.memset(gpad[g][:], 0.0), (), [b_gpad[g]])
    cacc = qd; b_cacc = b_qd
    halo = sb("halo", [128, 2, 4, 30], BF16); b_halo = Buf("halo")
    NTBT = 2
    tbt = [sb("tbt%d" % i, [128, A2, 64], BF16) for i in range(NTBT)] * (2 // NTBT)
    b_tbt = [Buf("tbt%d" % i) for i in range(NTBT)] * (2 // NTBT)
    b_payA = [Buf("payA%d" % l) for l in range(L)]
    b_payB = [Buf("payB%d" % l) for l in range(L)]
    b_payoA = [Buf("payoA%d" % l) for l in range(L)]
    b_payoB = [Buf("payoB%d" % l) for l in range(L)]
    b_tz = [Buf("tz%d" % l) for l in range(L)]

    pending_post = [None]
    import os as _os2
    WARM_LAT = int(_os2.environ.get("WARM_LAT", "0"))
    WARM_CTX = int(_os2.environ.get("WARM_CTX", "0"))
    WARM_N = int(_os2.environ.get("WARM_N", "256"))
    BURST = int(_os2.environ.get("BURST", "20"))

    def warm(n):
        for _ in range(n):
            add("pe", lambda e: e.matmul(banks[7][0][:, 512 - WARM_N:512], lhsT=identb[:, 0:128], rhs=identb[:, 0:128], start=True, stop=True), (), ())

    def flush_post():
        if pending_post[0] is not None:
            f = pending_post[0]
            pending_post[0] = None
            f()

    def attention(kt_fn, q_ap, va_fn, nkb, nq, scale, reads, extra_fn=None):
        at, ab = acb()
        pend = []

        def score(kb):
            pt, pb = gp()
            ex = extra_fn(kb) if extra_fn is not None else []
            add("pe", lambda e, pt=pt, kb=kb: e.matmul(pt[:, 0:nq], lhsT=kt_fn(kb), rhs=q_ap, start=True, stop=(len(ex) == 0)),
                reads, [pb])
            for i, (lh, rh, rd) in enumerate(ex):
                add("pe", lambda e, pt=pt, lh=lh, rh=rh, i=i: e.matmul(pt[:, 0:nq], lhsT=lh, rhs=rh, start=False, stop=(i == len(ex) - 1)),
                    rd, [pb])
            i = e_i[0] % 3
            e_i[0] += 1
            add("act", lambda e, pt=pt, i=i: e.activation(out=Eb[i][:, 0:nq], in_=pt[:, 0:nq], func=AF.Exp, scale=scale), [pb], [b_E[i]])
            return i

        for kb in range(min(2, nkb)):
            pend.append(score(kb))
        flush_post()
        for kb in range(nkb):
            if kb + 2 < nkb:
                pend.append(score(kb + 2))
            warm(WARM_LAT)
            i = pend[kb]
            add("pe", lambda e, at=at, kb=kb, i=i: e.matmul(at[0:65, 0:nq], lhsT=va_fn(kb), rhs=Eb[i][:, 0:nq], start=(kb == 0), stop=(kb == nkb - 1)),
                reads + [b_E[i]], [ab])
        return at, ab

    def normalize_rep(at, ab, nq, dst, b_dst_list):
        r, br = tmp()
        add("act", lambda e: e.activation(out=r[0:64, 0:nq], in_=at[0:64, 256:256 + nq], func=AF.Ln), [ab], [br])
        add("act", lambda e: e.activation(out=r[0:64, 0:nq], in_=r[0:64, 0:nq], func=AF.Exp, scale=-1.0), [br], [br])
        add("dve", lambda e: e.tensor_tensor(out=dst, in0=at[0:64, 0:nq], in1=r[0:64, 0:nq], op=ALU.mult), [ab, br], b_dst_list)

    def normalize(at, ab, nq, dst, b_dst_list, k=0, c0=0):
        rsum, b_rsum, b_rsb, rr = rsum_s[k], b_rsum_s[k], b_rsb_s[k], rsb_row[k]
        add("act", lambda e: e.activation(out=rsum[64:65, 0:nq], in_=at[64:65, c0:c0 + nq], func=AF.Ln), [ab], [b_rsum])
        add("act", lambda e: e.activation(out=rsb0[rr:rr + 1, 0:nq], in_=rsum[64:65, 0:nq], func=AF.Exp, scale=-1.0), [b_rsum], [b_rsb])
        pt, pb = gp()
        add("pe", lambda e: e.matmul(pt[0:64, 0:nq], lhsT=ones_b[rr:rr + 1, 0:64], rhs=rsb0[rr:rr + 1, 0:nq], start=True, stop=True),
            [b_rsb, b_ones_b], [pb])
        bc, bbc = tmp()
        add("act", lambda e: e.activation(out=bc[0:64, 0:nq], in_=pt[0:64, 0:nq], func=AF.Copy), [pb], [bbc])
        add("dve", lambda e: e.tensor_tensor(out=dst, in0=at[0:64, c0:c0 + nq], in1=bc[0:64, 0:nq], op=ALU.mult), [ab, bbc], b_dst_list)

    def stream_attention(jobs, nkb=18, nq=T, side_cb=None, burst=0):
        blocks = [(ji, kb) for ji in range(len(jobs)) for kb in range(nkb)]
        N = len(blocks)
        pend = {}
        posts = {}

        def do_score(idx):
            ji, kb = blocks[idx]
            J = jobs[ji]
            if kb == 0 and J.get("pre") is not None:
                J["pre"]()
            pt, pb = gp()
            ex = J["extra_fn"](kb) if J.get("extra_fn") is not None else []
            add("pe", lambda e, pt=pt, kb=kb, J=J: e.matmul(pt[:, 0:nq], lhsT=J["kt_fn"](kb), rhs=J["q_ap"], start=True, stop=(len(ex) == 0)),
                J["reads"], [pb])
            for i2, (lh, rh, rd) in enumerate(ex):
                add("pe", lambda e, pt=pt, lh=lh, rh=rh, i2=i2: e.matmul(pt[:, 0:nq], lhsT=lh, rhs=rh, start=False, stop=(i2 == len(ex) - 1)),
                    rd, [pb])
            i = e_i[0] % 3
            e_i[0] += 1
            add("act", lambda e, pt=pt, i=i, J=J: e.activation(out=Eb[i][:, 0:nq], in_=pt[:, 0:nq], func=AF.Exp, scale=J["scale"]), [pb], [b_E[i]])
            pend[idx] = i

        for idx in range(min(2, N)):
            do_score(idx)
        if burst:
            wpt, wpb = gp()
            for _ in range(burst):
                add("pe", lambda e, wpt=wpt: e.matmul(wpt[:, :], lhsT=identb[:, :], rhs=hT[1][:, 0, :], start=True, stop=True),
                    [b_identb, b_hT[1]], [wpb])
        for idx in range(N):
            if idx + 2 < N:
                do_score(idx + 2)
            ji, kb = blocks[idx]
            J = jobs[ji]
            if kb == 0:
                J["acc"] = acb()
            at, ab = J["acc"]
            i = pend.pop(idx)
            add("pe", lambda e, at=at, kb=kb, i=i, J=J: e.matmul(at[0:65, 0:nq], lhsT=J["va_fn"](kb), rhs=Eb[i][:, 0:nq],
                                                              start=(kb == 0), stop=(kb == nkb - 1)), J["reads"] + [b_E[i]], [ab])
            if kb == nkb - 1:
                if J.get("post") is not None:
                    posts.setdefault(min(idx + 2, N - 1), []).append(J["post"])
                if side_cb is not None:
                    side_cb()
            for f in posts.pop(idx, []):
                f()

    def input_proj(l):
        res = {}
        add("pool", lambda e: e.dma_start(out=wq_s[:], in_=wqup[l].rearrange("(k p) c -> p k c", p=128)), (), [b_wsm], dma=True)
        add("pool", lambda e: e.dma_start(out=wqp_s[:], in_=wqupp[l].rearrange("(k p) c -> p k c", p=128)), (), [b_wsm], dma=True)
        add("pool", lambda e: e.dma_start(out=wkv_s[:], in_=wkvup[l]), (), [b_wsm], dma=True)

        def fm(wt, wb, c0, M, g, n0=0, n=T):
            pt, pb = gp()
            for k in range(8):
                add("pe", lambda e, pt=pt, k=k: e.matmul(pt[0:M, 0:n], lhsT=wt[:, k, c0:c0 + M], rhs=hT[g][:, k, n0:n0 + n],
                                                        start=(k == 0), stop=(k == 7)), [wb, b_hT[g]], [pb])
            return pt, pb

        def tm(wt, wb, c0, N, g, tb):
            pt, pb = gp()
            for k in range(8):
                add("pe", lambda e, pt=pt, k=k: e.matmul(pt[:, 0:N], lhsT=hT[g][:, k, tb * 128:(tb + 1) * 128], rhs=wt[:, k, c0:c0 + N],
                                                        start=(k == 0), stop=(k == 7)), [wb, b_hT[g]], [pb])
            return pt, pb

        def rope_evac(ptA, pbA, ptB, pbB, p0, p1, dst, wlist):
            t1, tb1 = tmp()
            t2, tb2 = tmp()
            add("dve", lambda e: e.tensor_tensor(out=t1[p0:p1, :], in0=ptA[p0:p1, :], in1=cos_s[p0:p1, :], op=ALU.mult), [pbA, b_prm], [tb1])
            add("dve", lambda e: e.tensor_tensor(out=t2[p0:p1, :], in0=ptB[p0:p1, :], in1=sin_s[p0:p1, :], op=ALU.mult), [pbB, b_prm], [tb2])
            add("dve", lambda e: e.tensor_tensor(out=dst, in0=t1[p0:p1, :], in1=t2[p0:p1, :], op=ALU.add), [tb1, tb2], wlist)

        groups = [0, 1] if do_lat else [0]
        wA, bA = wtile(w_in[l, :, 0:512], 512)
        if do_lat:
            wF1, bF1 = wtile(w_inp[l, :, 0:512], 512)
        for hp in range(2):
            pt, pb = fm(wA, bA, hp * 128, 128, 0)
            evac(QT2[0][0][0:64, 2 * hp, :], pt[0:64, :], [pb], [b_QT[0][0]])
            evac(QT2[0][0][0:64, 2 * hp + 1, :], pt[64:128, :], [pb], [b_QT[0][0]])
            pt, pb = fm(wA, bA, 256 + hp * 128, 128, 0)
            evac(KTc2[0][0:64, 2 * hp, :], pt[0:64, :], [pb], [b_KTc[0]])
            evac(KTc2[0][0:64, 2 * hp + 1, :], pt[64:128, :], [pb], [b_KTc[0]])
        if do_lat:
            def rope_pair(c0):
                ptA, pbA = fm(wA, bA, c0, 128, 1)
                ptB, pbB = fm(wF1, bF1, c0, 128, 1)
                t1, tb1 = tmp()
                t2, tb2 = tmp()
                add("dve", lambda e: e.tensor_tensor(out=t1[:, :], in0=ptA[:, :], in1=cos_s[:, :], op=ALU.mult), [pbA, b_prm], [tb1])
                add("dve", lambda e: e.tensor_tensor(out=t2[:, :], in0=ptB[:, :], in1=sin_s[:, :], op=ALU.mult), [pbB, b_prm], [tb2])
                return t1, tb1, t2, tb2
            for hp in range(2):
                t1, tb1, t2, tb2 = rope_pair(hp * 128)
                for i in range(2):
                    add("dve", lambda e, t1=t1, t2=t2, i=i, hp=hp: e.tensor_tensor(out=QT2[1][0][0:64, 2 * hp + i, :], in0=t1[i * 64:(i + 1) * 64, :],
                                                                                in1=t2[i * 64:(i + 1) * 64, :], op=ALU.add), [tb1, tb2], [b_QT[1][0]])
                t1, tb1, t2, tb2 = rope_pair(256 + hp * 128)
                tk, tkb = sqt()
                add("dve", lambda e, t1=t1, t2=t2, tk=tk: e.tensor_tensor(out=tk[:, :], in0=t1[:, :], in1=t2[:, :], op=ALU.add), [tb1, tb2], [tkb])
                add("sp", lambda e, tk=tk, hp=hp: e.dma_start(out=payA_in[l].ap()[R_DAK + hp * 128:R_DAK + (hp + 1) * 128, :], in_=tk[:, :]),
                    [tkb], [b_payA[l]], dma=True)
        wB, bB = wtile(w_in[l, :, 512:1024], 512)
        for g in groups:
            for c in range(2):
                pt, pb = fm(wB, bB, 256 + c * 128, 128, g)
                evac(qd[:, c, :], pt[:, :], [pb], [b_qd])
            stat_rstd([qd[:, 0, :], qd[:, 1, :]], 256, rstd[:], b_rstd, [b_qd])
            for c in range(2):
                add("dve", lambda e, c=c: e.scalar_tensor_tensor(out=qdn[:, c, :], in0=qd[:, c, :], scalar=gq_s[:, l, c:c + 1], in1=rstd[:],
                                                               op0=ALU.mult, op1=ALU.mult), [b_qd, b_rstd, b_prm], [b_qdn])
            for h in range(4):
                pt, pb = gp()
                for c in range(2):
                    add("pe", lambda e, pt=pt, c=c, h=h: e.matmul(pt[0:96, :], lhsT=wq_s[:, c, h * 96:(h + 1) * 96], rhs=qdn[:, c, :],
                                                                 start=(c == 0), stop=(c == 1)), [b_wsm, b_qdn], [pb])
                if g == 0:
                    evac(QT2[0][1][0:96, h, :], pt[0:96, :], [pb], [b_QT[0][1]])
                else:
                    pt2, pb2 = gp()
                    for c in range(2):
                        add("pe", lambda e, pt2=pt2, c=c, h=h: e.matmul(pt2[0:96, :], lhsT=wqp_s[:, c, h * 96:(h + 1) * 96], rhs=qdn[:, c, :],
                                                                       start=(c == 0), stop=(c == 1)), [b_wsm, b_qdn], [pb2])
                    evac(QT2[1][1][0:64, h, :], pt[0:64, :], [pb], [b_QT[1][1]])
                    rope_evac(pt, pb, pt2, pb2, 64, 96, QT2[1][1][64:96, h, :], [b_QT[1][1]])
        for g in groups:
            for tb in range(4):
                pt, pb = tm(wB, bB, 0, 256, g, tb)
                if g == 0:
                    ostage, b_ostage = tmp()
                    add("act", lambda e, pt=pt, ostage=ostage: e.activation(out=ostage[:, 0:256], in_=pt[:, 0:256], func=AF.Copy), [pb], [b_ostage])
                    s, tl = tb // 2, (tb % 2) * 128
                    add("sp", lambda e, s=s, tl=tl, ostage=ostage: e.dma_start(out=o_dav[s, l, :, tl:tl + 128, :].rearrange("h t d -> t h d"),
                                                               in_=ostage[:, 0:256].rearrange("p (h d) -> p h d", h=4)), [b_ostage], (), dma=True)
                    add("dve", lambda e, pt=pt, tb=tb: e.tensor_copy(out=VAc[0][:, tb, :, 0:64], in_=pt[:, 0:256].rearrange("p (h d) -> p h d", h=4)),
                        [pb], [b_VAc[0]])
                else:
                    tv, tvb = sqt()
                    evac(tv[:, 0:256], pt[:, 0:256], [pb], [tvb])
                    add("sp", lambda e, tv=tv, tb=tb: e.dma_start(out=payB_in[l].ap()[R_V + tb * 128:R_V + (tb + 1) * 128, 0:256], in_=tv[:, 0:256]),
                        [tvb], [b_payB[l]], dma=True)
        wC, bC = wtile(w_in[l, :, 1024:1440], 416)
        if do_lat:
            wF2, bF2 = wtile(w_inp[l, :, 512:608], 96)
        for g in groups:
            pt, pb = fm(wC, bC, 0, 128, g)
            kvd, b_kvd = tmp()
            evac(kvd[:, :], pt[:, :], [pb], [b_kvd])
            stat_rstd([kvd[:, :]], 128, rstd[:], b_rstd, [b_kvd])
            dstc = ckvTc if g == 0 else sqb[0]
            if g == 0:
                add("dve", lambda e, kvd=kvd: e.scalar_tensor_tensor(out=ckvTc[:, 0:T], in0=kvd[:, :], scalar=gkvc_s[:, l:l + 1], in1=rstd[:],
                                                            op0=ALU.mult, op1=ALU.mult), [b_kvd, b_rstd, b_prm], [b_ckvTc])
            else:
                tk, tkb = sqt()
                add("dve", lambda e, tk=tk, kvd=kvd: e.scalar_tensor_tensor(out=tk[:, :], in0=kvd[:, :], scalar=gkvc_s[:, l:l + 1], in1=rstd[:],
                                                                   op0=ALU.mult, op1=ALU.mult), [b_kvd, b_rstd, b_prm], [tkb])
                add("sp", lambda e, tk=tk: e.dma_start(out=payA_in[l].ap()[R_CKV:R_CKV + 128, :], in_=tk[:, :]), [tkb], [b_payA[l]], dma=True)
            pt, pb = fm(wC, bC, 64, 96, g)
            if g == 0:
                for h in range(4):
                    evac(KTc2[1][64:96, h, :], pt[64:96, :], [pb], [b_KTc[1]])
            else:
                ptB, pbB = fm(wF2, bF2, 0, 96, 1)
                tk, tkb = sqt()
                rope_evac(pt, pb, ptB, pbB, 64, 96, tk[64:96, :], [tkb])
                add("sp", lambda e, tk=tk: e.dma_start(out=payA_in[l].ap()[R_KR:R_KR + 32, :], in_=tk[64:96, :]), [tkb], [b_payA[l]], dma=True)
            for hp in range(2):
                pt, pb = fm(wC, bC, 160 + hp * 128, 128, g)
                evac(QT2[g][0][64:128, 2 * hp, :], pt[0:64, :], [pb], [b_QT[g][2]])
                evac(QT2[g][0][64:128, 2 * hp + 1, :], pt[64:128, :], [pb], [b_QT[g][2]])
        for tb in range(4):
            pt, pb = tm(wC, bC, 0, 160, 0, tb)
            s, tl = tb // 2, (tb % 2) * 128
            tt, tb_ = tmp()
            add("act", lambda e, pt=pt, tt=tt: e.activation(out=tt[:, 0:128], in_=pt[:, 0:128], func=AF.Square), [pb], [tb_])
            add("dve", lambda e, tt=tt: e.reduce_sum(out=tt[:, 200:201], in_=tt[:, 0:128], axis=mybir.AxisListType.X), [tb_], [tb_])
            add("act", lambda e, tt=tt: e.activation(out=tt[:, 201:202], in_=tt[:, 200:201], func=AF.Ln, bias=epsc[:, 0:1], scale=1.0 / 128),
                [tb_, b_epsc], [tb_])
            add("act", lambda e, tt=tt: e.activation(out=tt[:, 202:203], in_=tt[:, 201:202], func=AF.Exp, scale=-0.5), [tb_], [tb_])
            add("dve", lambda e, pt=pt, tt=tt: e.scalar_tensor_tensor(out=tt[:, 256:384], in0=pt[:, 0:128], scalar=tt[:, 202:203],
                                                                      in1=gkvr_s[:, l, :], op0=ALU.mult, op1=ALU.mult),
                [pb, tb_, b_prm], [tb_])
            add("act", lambda e, pt=pt, tt=tt: e.activation(out=tt[:, 384:416], in_=pt[:, 128:160], func=AF.Copy), [pb, tb_], [tb_])
            add("sp", lambda e, s=s, tl=tl, tt=tt: e.dma_start(out=o_ckv[s, l, tl:tl + 128, :], in_=tt[:, 256:384]), [tb_], (), dma=True)
            add("sp", lambda e, s=s, tl=tl, tt=tt: e.dma_start(out=o_kr[s, l, tl:tl + 128, :], in_=tt[:, 384:416]), [tb_], (), dma=True)
        for h in range(4):
            pt, pb = gp()
            add("pe", lambda e, pt=pt, h=h: e.matmul(pt[0:64, :], lhsT=wkv_s[:, h * 128:h * 128 + 64], rhs=ckvTc[:, 0:T], start=True, stop=True),
                [b_wsm, b_ckvTc], [pb])
            evac(KTc2[1][0:64, h, :], pt[0:64, :], [pb], [b_KTc[1]])
        for tb in range(4):
            pt, pb = gp()
            add("pe", lambda e, pt=pt, tb=tb: e.matmul(pt[:, 0:256], lhsT=ckvTc[:, tb * 128:(tb + 1) * 128],
                                                      rhs=wkv_s[:, :].rearrange("p (h c) -> p h c", h=4)[:, :, 64:128], start=True, stop=True),
                [b_wsm, b_ckvTc], [pb])
            evac(VAc[1][:, tb, :, 0:64], pt[:, 0:256].rearrange("p (h d) -> p h d", h=4), [pb], [b_VAc[1]])
        wD, bD = wtile(w_in[l, :, 1440:1952], 512)
        for hp in range(2):
            pt, pb = fm(wD, bD, hp * 128, 128, 0)
            evac(KTc2[0][64:128, 2 * hp, :], pt[0:64, :], [pb], [b_KTc[2]])
            evac(KTc2[0][64:128, 2 * hp + 1, :], pt[64:128, :], [pb], [b_KTc[2]])
            if do_lat:
                pt, pb = fm(wD, bD, hp * 128, 128, 1)
                tk, tkb = sqt()
                evac(tk[:, :], pt[:, :], [pb], [tkb])
                add("sp", lambda e, tk=tk, hp=hp: e.dma_start(out=payA_in[l].ap()[R_NAK + hp * 128:R_NAK + (hp + 1) * 128, :], in_=tk[:, :]),
                    [tkb], [b_payA[l]], dma=True)
        for tb in range(4):
            pt, pb = tm(wD, bD, 0, 512, 0, tb)
            s, tl = tb // 2, (tb % 2) * 128
            ostage, b_ostage = tmp()
            add("act", lambda e, pt=pt, ostage=ostage: e.activation(out=ostage[:, :], in_=pt[:, :], func=AF.Copy), [pb], [b_ostage])
            add("sp", lambda e, s=s, tl=tl, ostage=ostage: e.dma_start(out=o_nak[s, l, :, tl:tl + 128, :].rearrange("h t d -> t h d"),
                                                       in_=ostage[:, 0:256].rearrange("p (h d) -> p h d", h=4)), [b_ostage], (), dma=True)
            add("sp", lambda e, s=s, tl=tl, ostage=ostage: e.dma_start(out=o_nav[s, l, :, tl:tl + 128, :].rearrange("h t d -> t h d"),
                                                       in_=ostage[:, 256:512].rearrange("p (h d) -> p h d", h=4)), [b_ostage], (), dma=True)
            add("dve", lambda e, pt=pt, tb=tb: e.tensor_copy(out=VAc[2][:, tb, :, 0:64], in_=pt[:, 256:512].rearrange("p (h d) -> p h d", h=4)),
                [pb], [b_VAc[2]])
            if do_lat:
                pt, pb = tm(wD, bD, 256, 256, 1, tb)
                tv, tvb = sqt()
                evac(tv[:, 0:256], pt[:, 0:256], [pb], [tvb])
                add("sp", lambda e, tv=tv, tb=tb: e.dma_start(out=payB_in[l].ap()[R_V + tb * 128:R_V + (tb + 1) * 128, 256:512], in_=tv[:, 0:256]),
                    [tvb], [b_payB[l]], dma=True)
        wA2, bA2 = wtile(w_in[l, :, 256:512], 256)
        for tb in range(4):
            pt, pb = tm(wA2, bA2, 0, 256, 0, tb)
            s, tl = tb // 2, (tb % 2) * 128
            ostage, b_ostage = tmp()
            add("act", lambda e, pt=pt, ostage=ostage: e.activation(out=ostage[:, 0:256], in_=pt[:, 0:256], func=AF.Copy), [pb], [b_ostage])
            add("sp", lambda e, s=s, tl=tl, ostage=ostage: e.dma_start(out=o_dak[s, l, :, tl:tl + 128, :].rearrange("h t d -> t h d"),
                                                       in_=ostage[:, 0:256].rearrange("p (h d) -> p h d", h=4)), [b_ostage], (), dma=True)
        wE, bE = wtile(w_in[l, :, 1952:2464], 512)
        for g in groups:
            for c in range(2):
                pa, pba = fm(wE, bE, c * 128, 128, g)
                pbm, pbb = fm(wE, bE, 256 + c * 128, 128, g)
                tt, tb_ = tmp()
                add("act", lambda e, tt=tt, pbm=pbm: e.activation(out=tt[:], in_=pbm[:, :], func=AF.Sigmoid), [pbb], [tb_])
                add("dve", lambda e, tt=tt, pa=pa, c=c, g=g: e.tensor_tensor(out=gpad[g][:, c, :, 15:15 + 256],
                                                                           in0=pa[:, :].rearrange("p (s t) -> p s t", s=2),
                                                                           in1=tt[:].rearrange("p (s t) -> p s t", s=2), op=ALU.mult),
                    [pba, tb_], [b_gpad[g]])
        if do_lat:
            add("dve", lambda e: e.tensor_copy(out=halo[:, :, 0, 0:15], in_=gpad[1][:, :, 0, 15:30]), [b_gpad[1]], [b_halo])
            add("dve", lambda e: e.tensor_copy(out=halo[:, :, 0, 15:30], in_=gpad[1][:, :, 1, 256:271]), [b_gpad[1]], [b_halo])
            hdst = dram_ap(payB_in[l], R_HALO * T, [[30, 128], [128 * 30, 2], [1, 30]])
            add("sp", lambda e: e.dma_start(out=hdst, in_=halo[:, :, 0, :]), [b_halo], [b_payB[l]], dma=True)
            add("pool", lambda e: e.collective_compute("AllGather", ALU.bypass, replica_groups=[[0, 1, 2, 3], [4, 5, 6, 7]],
                                                       ins=[payA_in[l].ap().opt()], outs=[payA_out[l].ap().opt()]),
                [b_payA[l]], [b_payoA[l]], dma="cc")
            add("pool", lambda e: e.collective_compute("AllGather", ALU.bypass, replica_groups=[[0, 1, 2, 3], [4, 5, 6, 7]],
                                                       ins=[payB_in[l].ap().opt()], outs=[payB_out[l].ap().opt()]),
                [b_payB[l]], [b_payoB[l]], dma="cc")

    def da_post(l, g, h, at1, ab1, at2, ab2, nq=T, q0=0, repl=False, c1=0, c2=0):
        o1, bo1 = dao, b_dao
        o2, bo2 = rsum_s[0], b_rsum_s[0]
        if repl:
            normalize_rep(at1, ab1, nq, o1[0:64, 0:nq], [bo1])
            normalize_rep(at2, ab2, nq, o2[0:64, 0:nq], [bo2])
        else:
            normalize(at1, ab1, nq, o1[0:64, 0:nq], [bo1], 0, c1)
            normalize(at2, ab2, nq, o2[0:64, 0:nq], [bo2], 1 if nq <= 256 else 0, c2)
        add("dve", lambda e: e.scalar_tensor_tensor(out=o1[0:64, 0:nq], in0=o2[0:64, 0:nq], scalar=neglam[0:64, l:l + 1], in1=o1[0:64, 0:nq],
                                                    op0=ALU.mult, op1=ALU.add), [bo1, bo2, b_prm], [bo1])
        stat_rstd([o1[0:64, 0:nq]], 64, rstd[0:64, 0:nq], b_rstd, [bo1], K=64, n=nq)
        p0 = (h % 2) * 64
        add("dve", lambda e: e.scalar_tensor_tensor(out=mixT[g][p0:p0 + 64, h // 2, q0:q0 + nq], in0=o1[0:64, 0:nq], scalar=gsub2[0:64, l:l + 1],
                                                    in1=rstd[0:64, 0:nq], op0=ALU.mult, op1=ALU.mult),
            [bo1, b_rstd, b_prm], [b_mix[g][h // 2]])

    def plain_post(g, m, h, at, ab, nq=T, q0=0, repl=False):
        if repl:
            p0 = (h % 2) * 64
            ch = 2 * m + h // 2
            normalize_rep(at, ab, nq, mixT[g][p0:p0 + 64, ch, q0:q0 + nq], [b_mix[g][ch]])
            return
        k = (set_i[0] % NSET) if nq <= 256 else 0
        set_i[0] += 1
        o1, bo1 = tmp()
        normalize(at, ab, nq, o1[0:64, 0:nq], [bo1], k)
        p0 = (h % 2) * 64
        ch = 2 * m + h // 2
        add("act", lambda e: e.activation(out=mixT[g][p0:p0 + 64, ch, q0:q0 + nq], in_=o1[0:64, 0:nq], func=AF.Copy), [bo1], [b_mix[g][ch]])

    def ctx_attention(l, side_gen=None):
        scales = (32 ** -0.5, 96 ** -0.5, 64 ** -0.5)
        rows = {0: None, 1: (0, 96), 2: (0, 64)}
        jobs = []
        for s in range(2):
            for m in range(3):
                for h in range(4):
                    for mp in ((0, 1) if m == 0 else (0,)):
                        jobs.append(dict(s=s, m=m, h=h, mp=mp))
        n = len(jobs)

        def stage_a(j):
            s_, m, h, mp = j["s"], j["m"], j["h"], j["mp"]
            q0 = s_ * 256
            lo, hi = (mp * 32, mp * 32 + 32) if m == 0 else rows[m]
            pt, pb = gp()
            rd = [b_KTc[m], b_QT[0][m]]
            for kb in range(2):
                add("pe", lambda e, pt=pt, kb=kb, m=m, h=h, lo=lo, hi=hi, q0=q0: e.matmul(
                    pt[:, kb * 256:(kb + 1) * 256], lhsT=kc_ap(m, h, lo, hi, q0 + kb * 128, 128), rhs=q_ap(0, m, h, lo, hi, q0, 256),
                    start=True, stop=True), rd, [pb])
            i = e_i[0] % 3
            e_i[0] += 1
            add("act", lambda e, pt=pt, i=i, m=m: e.activation(out=Eb[i][:, :], in_=pt[:, :], func=AF.Exp, scale=scales[m]), [pb], [b_E[i]])
            j["e"] = i

        def stage_b(j):
            s_, m, h = j["s"], j["m"], j["h"]
            at, ab = acb()
            i = j["e"]
            for kb in range(2):
                add("pe", lambda e, at=at, kb=kb, i=i, m=m, h=h, s_=s_: e.matmul(at[0:65, 0:256], lhsT=VAc[m][:, s_ * 2 + kb, h, 0:65],
                                                                               rhs=Eb[i][:, kb * 256:(kb + 1) * 256], start=(kb == 0), stop=(kb == 1)),
                    [b_VAc[m], b_E[i]], [ab])
            for kb in range(2):
                add("pe", lambda e, at=at, kb=kb, i=i: e.matmul(at[0:64, 256:512], lhsT=ones_b[:, 0:64], rhs=Eb[i][:, kb * 256:(kb + 1) * 256],
                                                               start=(kb == 0), stop=(kb == 1)), [b_ones_b, b_E[i]], [ab])
            j["acc"] = (at, ab)

        def stage_c(idx):
            j = jobs[idx]
            s_, m, h, mp = j["s"], j["m"], j["h"], j["mp"]
            q0 = s_ * 256
            if m == 0:
                if mp == 1:
                    a1 = jobs[idx - 1]["acc"]
                    da_post(l, 0, h, a1[0], a1[1], j["acc"][0], j["acc"][1], nq=256, q0=q0, repl=True)
            else:
                plain_post(0, m, h, j["acc"][0], j["acc"][1], nq=256, q0=q0, repl=True)

        for i in range(n + 2):
            if i < n:
                stage_a(jobs[i])
            warm(WARM_CTX)
            if 0 <= i - 1 < n:
                stage_b(jobs[i - 1])
            if 0 <= i - 2 < n:
                stage_c(i - 2)
            if side_gen is not None and i % 2 == 1:
                next(side_gen, None)

    def cache_T(src_ap, ncol, dst_fn, wlist):
        stg, bstg = tmp()
        add("sp", lambda e: e.dma_start(out=stg[:, 0:2 * ncol].rearrange("p (t c) -> p t c", t=2),
                                        in_=src_ap.rearrange("(t p) c -> p t c", p=128)), (), [bstg], dma=True)
        for tb in range(2):
            pt, pb = gp()
            add("pe", lambda e, pt=pt, tb=tb: e.transpose(out=pt[0:ncol, 0:128], in_=stg[:, tb * ncol:(tb + 1) * ncol], identity=ident[:]),
                [bstg, b_ident], [pb])
            evac(dst_fn(tb), pt[0:ncol, 0:128], [pb], wlist)

    def load_V(l, c0, cache_ap):
        for rk in range(4):
            for tb in range(4):
                src = dram_ap(payB_out[l], (rk * PB + R_V + tb * 128) * T + c0, [[T, 128], [64, 4], [1, 64]])
                add("sp", lambda e, rk=rk, tb=tb, src=src: e.dma_start(out=VAb[:, rk * 4 + tb, :, 0:64], in_=src), [b_payoB[l]], [b_VAk[rk * 4 + tb]], dma=True)
        for tb in range(2):
            add("pool", lambda e, tb=tb: e.dma_start(out=VAb[:, 16 + tb, :, 0:64], in_=cache_ap[:, tb * 128:(tb + 1) * 128, :].rearrange("h t d -> t h d")),
                (), [b_VAk[16 + tb]], dma=True)

    def load_KT(l, row0, nrows, p0, heads=True):
        for h in range(4):
            r = row0 + (h * nrows if heads else 0)
            src = dram_ap(payA_out[l], r * T, [[T, nrows], [PA * T, 4], [1, T]])
            add("sp", lambda e, h=h, src=src: e.dma_start(out=KTb[p0:p0 + nrows, h, 0:DEC_S].rearrange("p (r t) -> p r t", r=4), in_=src),
                [b_payoA[l]], [b_KTh[h]], dma=True)

    def lat_attention(l, side_gen=None, mod_gen=None):
        rdv = list(b_VAk)

        def side_step(n=1):
            if side_gen is not None:
                for _ in range(n):
                    next(side_gen, None)
            if mod_gen is not None:
                next(mod_gen, None)

        load_KT(l, R_DAK, 64, 0)
        for h in range(4):
            cache_T(c_dak[l, h], 64, lambda tb, h=h: KTb[0:64, h, DEC_S + tb * 128:DEC_S + (tb + 1) * 128], [b_KTh[h]])
        load_V(l, 0, c_dav[l])
        jobs = []
        for h in range(4):
            for qh in range(2):
                idx = h * 2 + qh
                tq, btq = tbt[idx // 5], b_tbt[idx // 5]
                qb = tq[:, :, :].rearrange("p a w -> p (a w)")[:, (idx % 5) * 512:(idx % 5 + 1) * 512]
                add("pool", lambda e, qb=qb: e.memset(qb, 0.0), (), [btq])
                add("pool", lambda e, qb=qb, h=h, qh=qh: e.tensor_copy(out=qb[0:32, 0:256], in_=QT2[1][0][0:32, h, qh * 256:(qh + 1) * 256]),
                    [b_QT[1][0]], [btq])
                add("pool", lambda e, qb=qb, h=h, qh=qh: e.tensor_copy(out=qb[32:64, 256:512], in_=QT2[1][0][32:64, h, qh * 256:(qh + 1) * 256]),
                    [b_QT[1][0]], [btq])
                J = dict(kt_fn=(lambda kb, h=h: KTb[0:128, h, kb * 128:(kb + 1) * 128]), q_ap=qb,
                         va_fn=(lambda kb, h=h: VAb[:, kb, h, 0:65]), scale=32 ** -0.5, reads=rdv + [b_KTh[h], btq])
                J["post"] = (lambda h=h, qh=qh, J=J: da_post(l, 1, h, J["acc"][0], J["acc"][1], J["acc"][0], J["acc"][1],
                                                             nq=256, q0=qh * 256, c1=0, c2=256))
                jobs.append(J)
        stream_attention(jobs, side_cb=lambda: side_step(1), burst=BURST)
        src = dram_ap(payA_out[l], R_CKV * T, [[T, 128], [PA * T, 4], [1, T]])
        add("sp", lambda e, src=src: e.dma_start(out=ckvT[:, 0:DEC_S].rearrange("p (r t) -> p r t", r=4), in_=src), [b_payoA[l]], [b_ckvT], dma=True)
        cache_T(c_ckv[l], 128, lambda tb: ckvT[:, DEC_S + tb * 128:DEC_S + (tb + 1) * 128], [b_ckvT])
        load_KT(l, R_KR, 32, 64, heads=False)
        for h in range(4):
            cache_T(c_kr[l], 32, lambda tb, h=h: KTb[64:96, h, DEC_S + tb * 128:DEC_S + (tb + 1) * 128], [b_KTh[h]])
        for h in range(4):
            for cb in range(5):
                n0 = cb * 512
                n = min(512, DEC_S + PAST - n0)
                pt, pb = gp()
                add("pe", lambda e, pt=pt, h=h, n0=n0, n=n: e.matmul(pt[0:64, 0:n], lhsT=wkv_s[:, h * 128:h * 128 + 64], rhs=ckvT[:, n0:n0 + n],
                                                                    start=True, stop=True), [b_wsm, b_ckvT], [pb])
                evac(KTb[0:64, h, n0:n0 + n], pt[0:64, 0:n], [pb], [b_KTh[h]])
        for kb in range(18):
            pt, pb = gp()
            add("pe", lambda e, pt=pt, kb=kb: e.matmul(pt[:, 0:256], lhsT=ckvT[:, kb * 128:(kb + 1) * 128],
                                                      rhs=wkv_s[:, :].rearrange("p (h c) -> p h c", h=4)[:, :, 64:128], start=True, stop=True),
                [b_wsm, b_ckvT], [pb])
            evac(VAb[:, kb, :, 0:64], pt[:, 0:256].rearrange("p (h d) -> p h d", h=4), [pb], [b_VAk[kb]])
        jobs = []
        for h in range(4):
            J = dict(kt_fn=(lambda kb, h=h: KTb[0:128, h, kb * 128:(kb + 1) * 128]), q_ap=q_ap(1, 1, h, 0, 128),
                     va_fn=(lambda kb, h=h: VAb[:, kb, h, 0:65]), scale=96 ** -0.5, reads=rdv + [b_KTh[h], b_QT[1][1]])
            J["post"] = (lambda h=h, J=J: plain_post(1, 1, h, J["acc"][0], J["acc"][1]))
            jobs.append(J)
        stream_attention(jobs, side_cb=lambda: side_step(2))
        load_KT(l, R_NAK, 64, 0)
        for h in range(4):
            cache_T(c_nak[l, h], 64, lambda tb, h=h: KTb[0:64, h, DEC_S + tb * 128:DEC_S + (tb + 1) * 128], [b_KTh[h]])
            add("pool", lambda e, h=h: e.dma_start(out=KTb[64:96, h, 0:DEC_S], in_=rowind_d), (), [b_KTh[h]], dma=True)
            add("pool", lambda e, h=h: e.memset(KTb[64:96, h, DEC_S:DEC_S + PAST], 0.0), (), [b_KTh[h]])
            evac(QT2[1][0][0:64, h, :], QT2[1][0][64:128, h, :], [b_QT[1][2], b_QT[1][0]], [b_QT[1][0], b_QT[1][2]])
            add("pool", lambda e, h=h: e.dma_start(out=QT2[1][0][64:96, h, :], in_=rowsel_d), (), [b_QT[1][0], b_QT[1][2]], dma=True)
        load_V(l, 256, c_nav[l])
        jobs = []
        for h in range(4):
            ti = h % 2

            def pre(h=h, ti=ti):
                for j in range(2):
                    src = bass.AP(tz_all, l * TZ_SZ + (h * (A2 + 1) + (1 - j)) * 64 * TZL + 63, [[TZL - 1, 64], [64 * TZL, A2], [1, 64]])
                    add("pool", lambda e, j=j, src=src, ti=ti: e.dma_start(out=tbt[ti][j * 64:(j + 1) * 64, :, :], in_=src), (), [b_tbt[ti]], dma=True)
                add("dve", lambda e, ti=ti: e.scalar_tensor_tensor(out=tbt[ti][:], in0=tbt[ti][:], scalar=8.0,
                                                                   in1=colm_s[:, :].unsqueeze(1).to_broadcast([128, A2, 64]),
                                                                   op0=ALU.mult, op1=ALU.add), [b_tbt[ti], b_prm], [b_tbt[ti]])

            def extra(kb, ti=ti):
                if kb >= 16:
                    return []
                a0 = 37 - 2 * kb
                return [(identb[:, :], tbt[ti][:, a0:a0 + 8, :], [b_identb, b_tbt[ti]])]

            J = dict(kt_fn=(lambda kb, h=h: KTb[0:128, h, kb * 128:(kb + 1) * 128]), q_ap=QT2[1][0][0:128, h, :],
                     va_fn=(lambda kb, h=h: VAb[:, kb, h, 0:65]), scale=64 ** -0.5, reads=rdv + [b_KTh[h], b_QT[1][2]],
                     extra_fn=extra, pre=pre)
            J["post"] = (lambda h=h, J=J: plain_post(1, 2, h, J["acc"][0], J["acc"][1]))
            jobs.append(J)
        stream_attention(jobs, side_cb=lambda: side_step(2))

    def conv_module(l, g):
        if g == 1:
            for c in range(2):
                src = dram_ap(payB_out[l], R_HALO * T + c * 128 * 30, [[30, 128], [PB * T, 4], [1, 30]])
                add("sp", lambda e, src=src, c=c: e.dma_start(out=halo[:, c, :, :], in_=src), [b_payoB[l]], [b_halo], dma=True)
            for side in range(2):
                dst = gpad[1][:, :, 0, 0:15] if side == 0 else gpad[1][:, :, 1, 271:286]
                for rk in range(4):
                    srcv = halo[:, :, rk, 15:30] if side == 0 else halo[:, :, rk, 0:15]
                    sc = halsel_s[:, side * 4 + rk:side * 4 + rk + 1]
                    if rk == 0:
                        add("dve", lambda e, dst=dst, srcv=srcv, sc=sc: e.tensor_scalar(out=dst, in0=srcv, scalar1=sc, scalar2=None, op0=ALU.mult),
                            [b_halo, b_prm, b_gpad[1]], [b_gpad[1]])
                    else:
                        add("dve", lambda e, dst=dst, srcv=srcv, sc=sc: e.scalar_tensor_tensor(out=dst, in0=srcv, scalar=sc, in1=dst,
                                                                                              op0=ALU.mult, op1=ALU.add),
                            [b_halo, b_prm, b_gpad[1]], [b_gpad[1]])
        for c in range(2):
            for j in range(31):
                if g == 0:
                    src = gpad[0][:, c, :, j:j + 256]
                    dst = cacc[:, c, :].rearrange("p (s t) -> p s t", s=2)
                    ops = [(src, dst)]
                else:
                    ops = []
                    n_a = max(0, min(256, 271 - j))
                    if n_a > 0:
                        ops.append((gpad[1][:, c, 0, j:j + n_a], cacc[:, c, 0:n_a]))
                    if n_a < 256:
                        ops.append((gpad[1][:, c, 1, 15 + (n_a + j - 271):15 + (256 + j - 271)], cacc[:, c, n_a:256]))
                    n_b = max(0, min(256, 271 - (256 + j)))
                    if n_b > 0:
                        ops.append((gpad[1][:, c, 0, 256 + j:256 + j + n_b], cacc[:, c, 256:256 + n_b]))
                    ops.append((gpad[1][:, c, 1, 15 + (256 + n_b + j - 271):15 + (512 + j - 271)], cacc[:, c, 256 + n_b:512]))
                if j % 4 == 3:
                    yield
                for (src, dst) in ops:
                    if j == 0:
                        add("dve", lambda e, src=src, dst=dst, c=c: e.tensor_scalar(out=dst, in0=src, scalar1=dw_s[:, l, c, 0:1], scalar2=cb_s[:, l, c:c + 1],
                                                                                    op0=ALU.mult, op1=ALU.add), [b_gpad[g], b_prm, b_cacc], [b_cacc])
                    else:
                        add("dve", lambda e, src=src, dst=dst, c=c, j=j: e.scalar_tensor_tensor(out=dst, in0=src, scalar=dw_s[:, l, c, j:j + 1], in1=dst,
                                                                                                op0=ALU.mult, op1=ALU.add), [b_gpad[g], b_prm, b_cacc], [b_cacc])
        p1, pb1 = gp()
        p2, pb2 = gp()
        for c in range(2):
            tt, tb_ = tmp()
            add("act", lambda e, tt=tt, c=c: e.activation(out=tt[:], in_=cacc[:, c, :], func=AF.Square), [b_cacc], [tb_])
            add("pe", lambda e, c=c: e.matmul(p1[:, :], lhsT=ones_f[:, :], rhs=cacc[:, c, :], start=(c == 0), stop=(c == 1)), [b_cacc, b_ones_f], [pb1])
            add("pe", lambda e, tt=tt, c=c: e.matmul(p2[:, :], lhsT=ones_f[:, :], rhs=tt[:], start=(c == 0), stop=(c == 1)), [tb_, b_ones_f], [pb2])
        mean, bmean = tmp()
        var, bvar = tmp()
        add("act", lambda e: e.activation(out=mean[:], in_=p1[:, :], func=AF.Identity, scale=1.0 / 256), [pb1], [bmean])
        add("dve", lambda e: e.tensor_tensor(out=var[:], in0=mean[:], in1=mean[:], op=ALU.mult), [bmean], [bvar])
        add("dve", lambda e: e.scalar_tensor_tensor(out=var[:], in0=p2[:, :], scalar=1.0 / 256, in1=var[:], op0=ALU.mult, op1=ALU.subtract),
            [pb2, bvar], [bvar])
        add("act", lambda e: e.activation(out=var[:], in_=var[:], func=AF.Ln, bias=epsc[:, 0:1], scale=1.0), [bvar, b_epsc], [bvar])
        add("act", lambda e: e.activation(out=var[:], in_=var[:], func=AF.Exp, scale=-0.5), [bvar], [bvar])
        for c in range(2):
            add("dve", lambda e, c=c: e.tensor_tensor(out=cacc[:, c, :], in0=cacc[:, c, :], in1=mean[:], op=ALU.subtract), [b_cacc, bmean], [b_cacc])
            add("dve", lambda e, c=c: e.tensor_tensor(out=cacc[:, c, :], in0=cacc[:, c, :], in1=var[:], op=ALU.mult), [b_cacc, bvar], [b_cacc])
            add("act", lambda e, c=c: e.activation(out=cacc[:, c, :], in_=cacc[:, c, :], func=AF.Identity, bias=lnb_s[:, l, c:c + 1],
                                                   scale=lng_s[:, l, c:c + 1]), [b_cacc, b_prm], [b_cacc])
            tt, tb_ = tmp()
            add("act", lambda e, tt=tt, c=c: e.activation(out=tt[:], in_=cacc[:, c, :], func=AF.Sigmoid), [b_cacc], [tb_])
            add("dve", lambda e, tt=tt, c=c: e.tensor_tensor(out=mixT[g][:, 6 + c, :], in0=cacc[:, c, :], in1=tt[:], op=ALU.mult),
                [b_cacc, tb_], [b_mix[g][6 + c]])

    def out_proj(l, groups):
        for ti in range(2):
            wt, wb = wtile(w_out[l, :, ti * 512:(ti + 1) * 512], 512)
            for cc in range(4):
                oc = ti * 4 + cc
                for g in groups:
                    pt, pb = gp()
                    for k in range(8):
                        add("pe", lambda e, pt=pt, k=k, cc=cc, wt=wt, g=g: e.matmul(pt[:, :], lhsT=wt[:, k, cc * 128:(cc + 1) * 128], rhs=mixT[g][:, k, :],
                                                                                   start=(k == 0), stop=(k == 7)), [wb, b_mix[g][k]], [pb])
                    add("dve", lambda e, pt=pt, g=g, oc=oc: e.scalar_tensor_tensor(out=xT[g][:, oc, :], in0=pt[:, :], scalar=mcol(l, "g1", oc, g),
                                                                                 in1=xT[g][:, oc, :], op0=ALU.mult, op1=ALU.add),
                        [pb, b_mod[l], b_xT[g][oc]], [b_xT[g][oc]])

    def ffn(l, groups):
        for blk in range(4):
            for ti in range(2):
                wt, wb = wtile(w_ff1[l, :, blk * 1024 + ti * 512: blk * 1024 + (ti + 1) * 512], 512)
                for cc in range(4):
                    fc = ti * 4 + cc
                    for g in groups:
                        pt, pb = gp()
                        for k in range(8):
                            add("pe", lambda e, pt=pt, k=k, cc=cc, wt=wt, g=g: e.matmul(pt[:, :], lhsT=wt[:, k, cc * 128:(cc + 1) * 128], rhs=hT[g][:, k, :],
                                                                                       start=(k == 0), stop=(k == 7)), [wb, b_hT[g]], [pb])
                        sq, bq = sqt()
                        add("act", lambda e, pt=pt, sq=sq: e.activation(out=sq[:], in_=pt[:, :], func=AF.Relu), [pb], [bq])
                        add("dve", lambda e, sq=sq, g=g, fc=fc: e.tensor_tensor(out=mixT[g][:, fc, :], in0=sq[:], in1=sq[:], op=ALU.mult),
                            [bq], [b_mix[g][fc]])
            for ti in range(2):
                wt, wb = wtile(w_ff2[l, blk * 1024:(blk + 1) * 1024, ti * 512:(ti + 1) * 512], 512)
                for cc in range(4):
                    oc = ti * 4 + cc
                    for g in groups:
                        pt, pb = gp()
                        for k in range(8):
                            add("pe", lambda e, pt=pt, k=k, cc=cc, wt=wt, g=g: e.matmul(pt[:, :], lhsT=wt[:, k, cc * 128:(cc + 1) * 128], rhs=mixT[g][:, k, :],
                                                                                       start=(k == 0), stop=(k == 7)), [wb, b_mix[g][k]], [pb])
                        add("dve", lambda e, pt=pt, g=g, oc=oc: e.scalar_tensor_tensor(out=xT[g][:, oc, :], in0=pt[:, :], scalar=mcol(l, "g2", oc, g),
                                                                                     in1=xT[g][:, oc, :], op0=ALU.mult, op1=ALU.add),
                            [pb, b_mod[l], b_xT[g][oc]], [b_xT[g][oc]])

    def dump8(name, t, bufs):
        if name not in dbg:
            return
        o = dout("dbg_" + name, [128, 8, T])
        for j in range(8):
            tt, tb_ = tmp()
            add("dve", lambda e, tt=tt, j=j: e.tensor_copy(out=tt[:], in_=t[:, j, :]), [bufs[j]], [tb_])
            add("sp", lambda e, tt=tt, j=j: e.dma_start(out=o[:, j, :], in_=tt[:]), [tb_], (), dma=True)
        dbg_out[name] = [128, 8, T]

    groups = [0, 1] if do_lat else [0]
    import os as _os
    _l1 = _os.environ.get("L1STAGES")
    _stages0 = stages
    for l in range(nlayers):
        stages = _stages0 if (l == 0 or _l1 is None) else set(_l1.split(","))
        if "mod" in stages and l == 0:
            for _ in modulation(0):
                pass
        if "norm1" in stages:
            for g in groups:
                rmsnorm_mod(l, g, 0)
        if "proj" in stages:
            input_proj(l)
        side = modulation(l + 1) if ("mod" in stages and l + 1 < nlayers) else None
        if "ctxattn" in stages:
            ctx_attention(l, None if (do_lat and "latattn" in stages) else side)
        if "conv" in stages:
            for _ in conv_module(l, 0):
                pass
        dump8("mix0%d" % l, mixT[0], b_mix[0])
        if do_lat:
            side2 = conv_module(l, 1) if "conv" in stages else None
            if "latattn" in stages:
                lat_attention(l, side2, side)
            if side2 is not None:
                for _ in side2:
                    pass
        if side is not None:
            for _ in side:
                pass
            dump8("mix1%d" % l, mixT[1], b_mix[1])
        if "outproj" in stages:
            out_proj(l, groups)
        for g in groups:
            dump8("xattn%d%d" % (g, l), xT[g], b_xT[g])
        if "norm2" in stages:
            for g in groups:
                rmsnorm_mod(l, g, 1)
        if "ffn" in stages:
            ffn(l, groups)
        for g in groups:
            dump8("xffn%d%d" % (g, l), xT[g], b_xT[g])
    for g in groups:
        if "final" not in stages:
            continue
        pt, pb = gp()
        for j in range(8):
            sq, bq = sqt()
            add("act", lambda e, sq=sq, j=j, g=g: e.activation(out=sq[:], in_=xT[g][:, j, :], func=AF.Square), [b_xT[g][j]], [bq])
            add("pe", lambda e, sq=sq, j=j, pt=pt: e.matmul(pt[:, :], lhsT=ones_b[:, :], rhs=sq[:], start=(j == 0), stop=(j == 7)), [bq, b_ones_b], [pb])
        add("act", lambda e, pt=pt: e.activation(out=rstd[:], in_=pt[:, :], func=AF.Ln, bias=epsc[:, 0:1], scale=1.0 / D), [pb, b_epsc], [b_rstd])
        add("act", lambda e: e.activation(out=rstd[:], in_=rstd[:], func=AF.Exp, scale=-0.5), [b_rstd], [b_rstd])
        for j in range(8):
            add("dve", lambda e, j=j, g=g: e.scalar_tensor_tensor(out=xT[g][:, j, :], in0=xT[g][:, j, :], scalar=gfin_s[:, j:j + 1], in1=rstd[:],
                                                                  op0=ALU.mult, op1=ALU.mult), [b_xT[g][j], b_rstd, b_prm], [b_xT[g][j]])
    for g in groups:
        for tb in range(4):
            for half in range(2):
                pt, pb = gp()
                for jj in range(4):
                    j = half * 4 + jj
                    add("pe", lambda e, pt=pt, j=j, jj=jj, tb=tb, g=g: e.transpose(out=pt[:, jj * 128:(jj + 1) * 128],
                                                                                   in_=xT[g][:, j, tb * 128:(tb + 1) * 128], identity=ident[:]),
                        [b_xT[g][j], b_ident], [pb])
                stg, bstg = tmp()
                evac(stg[:, :], pt[:, :], [pb], [bstg])
                add("sp", lambda e, g=g, tb=tb, half=half, stg=stg: e.dma_start(
                    out=y_out[g * T + tb * 128:g * T + (tb + 1) * 128, half * 512:(half + 1) * 512], in_=stg[:, :]), [bstg], (), dma=True)
    S.emit(st)
    st.close()
    return nc, dbg_out


def _colT(v, nchunk):
    return np.ascontiguousarray(v.reshape(nchunk, 128).T)


def prepare_inputs(inp):
    f = lambda a: np.ascontiguousarray(np.asarray(a, dtype=np.float32))
    x_prompt, x_sample, c = f(inp["x_prompt"]), f(inp["x_sample"]), f(inp["c"])
    w_in = f(inp["w_in"])
    shared = {}
    shared["bmodT"] = np.ascontiguousarray(np.stack([_colT(f(inp["b_mod"])[l], 48) for l in range(L)], 1))
    shared["gmixT"] = np.ascontiguousarray(np.stack([_colT(f(inp["g_norm_mix"])[l], 8) for l in range(L)], 1))
    shared["gffT"] = np.ascontiguousarray(np.stack([_colT(f(inp["g_norm_ff"])[l], 8) for l in range(L)], 1))
    shared["gfinT"] = _colT(f(inp["g_final"]), 8)
    shared["w_mod"] = f(inp["w_mod"])
    shared["w_in"] = w_in
    idx = []
    for base in (0, 256):
        for h in range(4):
            for m in range(2):
                idx += [base + h * 64 + m * 32 + P32[j] for j in range(32)]
    idx += list(range(1088, 1152))
    idx += [1152 + P32[j] for j in range(32)]
    shared["w_inp"] = np.ascontiguousarray(w_in[:, :, idx])
    shared["w_out"] = f(inp["w_out"])
    shared["w_ff1"] = f(inp["w_ff1"])
    shared["w_ff2"] = f(inp["w_ff2"])
    wq = f(inp["w_mla_qup"])
    shared["wqup"] = wq
    wqp = np.zeros_like(wq)
    for h in range(4):
        for j in range(32):
            wqp[:, :, h * 96 + 64 + j] = wq[:, :, h * 96 + 64 + P32[j]]
    shared["wqupp"] = wqp
    shared["wkvup"] = f(inp["w_mla_kvup"])
    shared["gqT"] = np.ascontiguousarray(np.stack([_colT(f(inp["g_mla_q"])[l], 2) for l in range(L)], 1))
    shared["gkvc"] = np.ascontiguousarray(f(inp["g_mla_kv"]).T)
    shared["gkvr"] = np.ascontiguousarray(np.broadcast_to(f(inp["g_mla_kv"])[None], (128, L, 128)))
    gs = f(inp["g_da_subln"])
    shared["gsubc"] = np.ascontiguousarray(np.concatenate([gs.T, gs.T], 0))
    lamv = np.stack([f(inp["da_lambda_q1"]), f(inp["da_lambda_k1"]), f(inp["da_lambda_q2"]), f(inp["da_lambda_k2"])], 1)
    shared["lamv"] = np.ascontiguousarray(np.broadcast_to(lamv[None], (128, L, 4, 32)))
    dw = f(inp["conv_dw"])
    shared["dwT"] = np.ascontiguousarray(dw.reshape(L, 31, 2, 128).transpose(3, 0, 2, 1))
    for nm, key in (("cbT", "conv_b"), ("lngT", "conv_ln_g"), ("lnbT", "conv_ln_b")):
        shared[nm] = np.ascontiguousarray(f(inp[key]).reshape(L, 2, 128).transpose(2, 0, 1))
    shared["ident"] = np.eye(128, dtype=np.float32)
    w = np.arange(GRID_W)
    cs = np.clip(w - 8, 0, GRID_W - 16)
    col_ok = (w[None, :] >= cs[:, None]) & (w[None, :] < cs[:, None] + 16)
    cm = np.where(col_ok.T, 0.0, -BIG * 8).astype(np.float32)
    shared["colmask"] = np.ascontiguousarray(np.concatenate([cm, cm], 0))
    ri = np.zeros((32, DEC_S), np.float32)
    ri[np.arange(DEC_S) // 64, np.arange(DEC_S)] = 1.0
    shared["rowind"] = ri
    rpb = f(inp["na_rpb"])
    half = 8
    freqs = (10000.0 ** (-np.arange(half, dtype=np.float32) * 2.0 / 16)).astype(np.float32)
    maps = []
    for core in range(NCORE):
        b, r = core // 4, core % 4
        m = dict(shared)
        m["xin"] = np.ascontiguousarray(np.concatenate([x_prompt[2 * core], x_prompt[2 * core + 1], x_sample[b, r * T:(r + 1) * T]], 0))
        cv = np.stack([f(inp["c_ctx"]), c[b]], 1)
        m["cvT"] = np.ascontiguousarray(cv.reshape(8, 128, 2).transpose(1, 0, 2))
        m["c_dak"] = f(inp["cache_da_k"])[b]
        m["c_dav"] = f(inp["cache_da_v"])[b]
        m["c_ckv"] = f(inp["cache_mla_ckv"])[b]
        m["c_kr"] = f(inp["cache_mla_krope"])[b]
        m["c_nak"] = f(inp["cache_na_k"])[b]
        m["c_nav"] = f(inp["cache_na_v"])[b]
        t = r * T + np.arange(T)
        rows = (t // GRID_W).astype(np.float32)
        cols = (t % GRID_W).astype(np.float32)
        ang = np.concatenate([rows[None, :] * freqs[:, None], rows[None, :] * freqs[:, None],
                              cols[None, :] * freqs[:, None], cols[None, :] * freqs[:, None]], 0)
        cos32 = np.cos(ang).astype(np.float32)
        sin32 = np.sin(ang).astype(np.float32)
        sgn = np.concatenate([-np.ones(8), np.ones(8), -np.ones(8), np.ones(8)]).astype(np.float32)[:, None]
        m["cosT"] = np.ascontiguousarray(np.tile(cos32, (4, 1)))
        m["sinT"] = np.ascontiguousarray(np.tile(sin32 * sgn, (4, 1)))
        r0 = r * 8
        qrow = r0 + np.arange(T) // 64
        start = np.clip(qrow - 4, 0, NROWS - 8)
        rk = np.arange(32)
        ok = (rk[:, None] >= start[None, :]) & (rk[:, None] < start[None, :] + 8)
        m["rowsel"] = np.where(ok, 0.0, -BIG * 8).astype(np.float32)
        P = np.zeros((L, 4, A2 + 1, TZL), np.float32)
        for a2 in range(A2):
            a = 45 - a2 - r0
            if 0 <= a <= 14:
                P[:, :, a2, 48:79] = rpb[:, :, a, ::-1]
        m["tz_rep"] = np.ascontiguousarray(np.broadcast_to(P.reshape(L * 4 * (A2 + 1), 1, TZL), (L * 4 * (A2 + 1), 64, TZL)))
        hs = np.zeros((128, 8), np.float32)
        if r > 0:
            hs[:, r - 1] = 1.0
        if r < 3:
            hs[:, 4 + r + 1] = 1.0
        m["halsel"] = hs
        maps.append(m)
    return maps


_NC_CACHE = {}


def kernel(**inputs):
    maps = prepare_inputs(inputs)
    if "nc" not in _NC_CACHE:
        _NC_CACHE["nc"] = build()[0]
    nc = _NC_CACHE["nc"]
    res = run_bass_kernel_spmd(nc, maps, core_ids=list(range(NCORE)))
    R = res.results
    y_prompt = np.zeros((NB, SEQ, D), np.float32)
    y_sample = np.zeros((DEC_B, DEC_S, D), np.float32)
    outs = {k: [] for k in ("o_dak", "o_dav", "o_ckv", "o_kr", "o_nak", "o_nav")}
    for core in range(NCORE):
        b, r = core // 4, core % 4
        y = R[core]["y_out"]
        y_prompt[2 * core] = y[0:256]
        y_prompt[2 * core + 1] = y[256:512]
        y_sample[b, r * T:(r + 1) * T] = y[512:1024]
        for k in outs:
            outs[k].append(R[core][k])
    cat = lambda k: np.ascontiguousarray(np.concatenate(outs[k], 0).astype(np.float32))
    return (y_prompt, y_sample, cat("o_dak"), cat("o_dav"), cat("o_ckv"), cat("o_kr"), cat("o_nak"), cat("o_nav"))
```

```python
import contextlib
import math
import numpy as np
import concourse.bass as bass
import concourse.mybir as mybir
from concourse.bass_utils import run_bass_kernel_spmd

F32 = mybir.dt.float32
BF16 = mybir.dt.bfloat16
ALU = mybir.AluOpType
AF = mybir.ActivationFunctionType

ENGINES = ("pe", "act", "dve", "pool", "sp")

D = 1024
L = 2
NB = 16
SEQ = 256
DEC_B = 2
DEC_S = 2048
PAST = 256
GRID_W = 64
NROWS = DEC_S // GRID_W
EPS = 1e-6
T = 512
NCORE = 8
BIG = 30000.0
A2 = 46
TZL = 127
PA = 672
PB = 527
R_DAK, R_CKV, R_KR, R_NAK = 0, 256, 384, 416
R_V, R_HALO = 0, 512
P32 = [8, 9, 10, 11, 12, 13, 14, 15, 0, 1, 2, 3, 4, 5, 6, 7,
       24, 25, 26, 27, 28, 29, 30, 31, 16, 17, 18, 19, 20, 21, 22, 23]


class Buf:
    __slots__ = ("name", "last_w", "readers", "excl")

    def __init__(self, name, excl=False):
        self.name = name
        self.last_w = None
        self.readers = []
        self.excl = excl


class Op:
    __slots__ = ("eng", "fn", "deps", "dma", "sig", "sem", "val")

    def __init__(self, eng, fn, dma):
        self.eng = eng
        self.fn = fn
        self.dma = dma
        self.deps = []
        self.sig = False
        self.sem = None
        self.val = 0


class Sched:
    def __init__(self, nc, n_dma_slots=16):
        self.nc = nc
        self.ops = []
        self.n_dma_slots = n_dma_slots

    def add(self, eng, fn, reads=(), writes=(), dma=False):
        op = Op(eng, fn, dma)
        ex = [b for b in reads if b.excl]
        if ex:
            reads = [b for b in reads if not b.excl]
            writes = list(writes) + ex
        deps = {}
        for b in reads:
            if b.last_w is not None:
                deps[id(b.last_w)] = b.last_w
        for b in writes:
            if b.last_w is not None:
                deps[id(b.last_w)] = b.last_w
            for r in b.readers:
                deps[id(r)] = r
        for b in reads:
            b.readers.append(op)
        for b in writes:
            b.last_w = op
            b.readers = []
        for d in deps.values():
            if d.eng == "pe" and eng == "pe" and not d.dma and not dma:
                continue
            op.deps.append(d)
            d.sig = True
        self.ops.append(op)
        return op

    def emit(self, stack):
        nc = self.nc
        eng_sem = {e: stack.enter_context(nc.semaphore("s_" + e)) for e in ENGINES}
        dma_engs = sorted({op.eng for op in self.ops if op.dma is True})
        dma_slots = {e: [stack.enter_context(nc.semaphore("d_%s_%d" % (e, i)))
                         for i in range(self.n_dma_slots)] for e in dma_engs}
        cnt = {e: 0 for e in ENGINES}
        dcnt = {e: 0 for e in dma_engs}
        slot_uses = {e: [0] * self.n_dma_slots for e in dma_engs}
        slot_prev = {}
        ncc = 0
        for op in self.ops:
            if op.dma == "cc":
                op.sem = stack.enter_context(nc.semaphore("cc_%d" % ncc))
                ncc += 1
                op.val = 1
            elif op.dma:
                k = dcnt[op.eng] % self.n_dma_slots
                dcnt[op.eng] += 1
                slot_uses[op.eng][k] += 1
                op.sem = dma_slots[op.eng][k]
                op.val = 16 * slot_uses[op.eng][k]
                slot_prev[id(op)] = (op.sem, op.val - 16)
            elif op.sig:
                cnt[op.eng] += 1
                op.sem = eng_sem[op.eng]
                op.val = cnt[op.eng]
        block = stack.enter_context(nc.Block())
        handles = {"pe": block.tensor, "act": block.scalar, "dve": block.vector,
                   "pool": block.gpsimd, "sp": block.sync}
        all_async = [op for op in self.ops if op.dma]

        def run_engine(ename):
            def body(eng):
                waited = {}
                for op in self.ops:
                    if op.eng != ename:
                        continue
                    need = {}
                    for d in op.deps:
                        key = id(d.sem)
                        if key not in need or need[key][1] < d.val:
                            need[key] = (d.sem, d.val)
                    if op.dma is True:
                        s, v = slot_prev[id(op)]
                        if v > 0:
                            key = id(s)
                            if key not in need or need[key][1] < v:
                                need[key] = (s, v)
                    for key, (s, v) in need.items():
                        if waited.get(key, 0) >= v:
                            continue
                        eng.wait_ge(s, v)
                        waited[key] = v
                    ins = op.fn(eng)
                    if op.dma == "cc":
                        ins.then_inc(op.sem)
                    elif op.dma:
                        ins.then_inc(op.sem, 16)
                    elif op.sig:
                        ins.then_inc(op.sem, 1)
                if ename == "sp":
                    last = {}
                    for op in all_async:
                        last[id(op.sem)] = (op.sem, op.val)
                    for key, (s, v) in last.items():
                        if waited.get(key, 0) < v:
                            eng.wait_ge(s, v)
                    for e in ENGINES:
                        if cnt[e] > 0:
                            eng.wait_ge(eng_sem[e], cnt[e])
            handles[ename](body)

        for e in ENGINES:
            run_engine(e)


def build(dbg=(), nlayers=L, do_lat=True, stages=None):
    if stages is None:
        stages = {"mod", "norm1", "proj", "ctxattn", "conv", "latattn", "outproj", "norm2", "ffn", "final"}
    nc = bass.Bass("TRN2", target_bir_lowering=False)
    st = contextlib.ExitStack()
    S = Sched(nc)
    dbg_out = {}

    def din(name, shape, dt=F32):
        return nc.dram_tensor(name, list(shape), dt, kind="ExternalInput").ap()

    def dout(name, shape, dt=F32):
        return nc.dram_tensor(name, list(shape), dt, kind="ExternalOutput").ap()

    xin = din("xin", [2 * T, D])
    cvT = din("cvT", [128, 8, 2])
    bmodT = din("bmodT", [128, L, 48])
    gmixT = din("gmixT", [128, L, 8])
    gffT = din("gffT", [128, L, 8])
    gfinT = din("gfinT", [128, 8])
    w_mod = din("w_mod", [L, D, 6 * D])
    w_in = din("w_in", [L, D, 2464])
    w_inp = din("w_inp", [L, D, 608])
    w_out = din("w_out", [L, D, D])
    w_ff1 = din("w_ff1", [L, D, 4 * D])
    w_ff2 = din("w_ff2", [L, 4 * D, D])
    wqup = din("wqup", [L, 256, 384])
    wqupp = din("wqupp", [L, 256, 384])
    wkvup = din("wkvup", [L, 128, 512])
    gqT = din("gqT", [128, L, 2])
    gkvc = din("gkvc", [128, L])
    gkvr = din("gkvr", [128, L, 128])
    gsubc = din("gsubc", [128, L])
    lamv = din("lamv", [128, L, 4, 32])
    tz_all = din("tz_rep", [L * 4 * (A2 + 1), 64, TZL]).tensor
    dwT = din("dwT", [128, L, 2, 31])
    cbT = din("cbT", [128, L, 2])
    lngT = din("lngT", [128, L, 2])
    lnbT = din("lnbT", [128, L, 2])
    c_dak = din("c_dak", [L, 4, PAST, 64])
    c_dav = din("c_dav", [L, 4, PAST, 64])
    c_ckv = din("c_ckv", [L, PAST, 128])
    c_kr = din("c_kr", [L, PAST, 32])
    c_nak = din("c_nak", [L, 4, PAST, 64])
    c_nav = din("c_nav", [L, 4, PAST, 64])
    cosT_d = din("cosT", [128, T])
    sinT_d = din("sinT", [128, T])
    colmask_d = din("colmask", [128, 64])
    rowsel_d = din("rowsel", [32, T])
    rowind_d = din("rowind", [32, DEC_S])
    halsel_d = din("halsel", [128, 8])
    ident_d = din("ident", [128, 128])
    y_out = dout("y_out", [2 * T, D])
    o_dak = dout("o_dak", [2, L, 4, SEQ, 64])
    o_dav = dout("o_dav", [2, L, 4, SEQ, 64])
    o_ckv = dout("o_ckv", [2, L, SEQ, 128])
    o_kr = dout("o_kr", [2, L, SEQ, 32])
    o_nak = dout("o_nak", [2, L, 4, SEQ, 64])
    o_nav = dout("o_nav", [2, L, 4, SEQ, 64])
    LROWS = 6 * PA + 6 * PB
    pay_all = nc.dram_tensor("pay_all", [L * LROWS, T], BF16)
    O_AIN, O_AOUT, O_BIN, O_BOUT = 0, PA, 5 * PA, 5 * PA + PB

    class _Sub:
        def __init__(self, row0, nrows):
            self.row0, self.nrows = row0, nrows

        def ap(self):
            return pay_all.ap()[self.row0:self.row0 + self.nrows, :]

    payA_in = [_Sub(l * LROWS + O_AIN, PA) for l in range(L)]
    payA_out = [_Sub(l * LROWS + O_AOUT, 4 * PA) for l in range(L)]
    payB_in = [_Sub(l * LROWS + O_BIN, PB) for l in range(L)]
    payB_out = [_Sub(l * LROWS + O_BOUT, 4 * PB) for l in range(L)]
    TZ_SZ = 4 * (A2 + 1) * 64 * TZL

    def dram_ap(base, off, dims):
        if isinstance(base, _Sub):
            return bass.AP(pay_all, base.row0 * T + off, dims)
        return bass.AP(base, off, dims)

    def sb(name, shape, dt):
        return st.enter_context(nc.sbuf_tensor("sb_" + name, list(shape), dt))

    add = S.add

    banks = []
    for i in range(8):
        banks.append((st.enter_context(nc.psum_tensor("ps%d" % i, [128, 512], F32)), Buf("ps%d" % i, excl=True)))
    gp_i = [0]
    ac_i = [0]

    def gp():
        b = banks[(0, 1, 2, 7)[gp_i[0] % 4]]
        gp_i[0] += 1
        return b

    def acb():
        b = banks[4 + ac_i[0] % 3]
        ac_i[0] += 1
        return b

    ev_i = [0]

    def evac(dst, src, r, w, scale=None):
        ev_i[0] += 1
        if ev_i[0] % 2 == 0:
            if scale is None:
                add("act", lambda e: e.activation(out=dst, in_=src, func=AF.Copy), r, w)
            else:
                add("act", lambda e: e.activation(out=dst, in_=src, func=AF.Identity, scale=scale), r, w)
        else:
            if scale is None:
                add("dve", lambda e: e.tensor_copy(out=dst, in_=src), r, w)
            else:
                add("dve", lambda e: e.tensor_scalar(out=dst, in0=src, scalar1=scale, scalar2=None, op0=ALU.mult), r, w)

    ident = sb("ident", [128, 128], F32); b_ident = Buf("ident")
    identb = sb("identb", [128, 128], BF16); b_identb = Buf("identb")
    ones_f = sb("ones_f", [128, 128], F32); b_ones_f = Buf("ones_f")
    ones_b = sb("ones_b", [128, 128], BF16); b_ones_b = Buf("ones_b")
    epsc = sb("epsc", [128, 1], F32); b_epsc = Buf("epsc")
    add("sp", lambda e: e.dma_start(out=ident[:], in_=ident_d), (), [b_ident], dma=True)
    add("pool", lambda e: e.memset(ones_f[:], 1.0), (), [b_ones_f])
    add("pool", lambda e: e.memset(ones_b[:], 1.0), (), [b_ones_b])
    add("pool", lambda e: e.memset(epsc[:], EPS), (), [b_epsc])
    add("dve", lambda e: e.tensor_copy(out=identb[:], in_=ident[:]), [b_ident], [b_identb])

    prm = {}
    b_prm = Buf("prm")
    b_small = []

    def load_small(name, src, shape):
        t = sb(name, shape, F32)
        bt = Buf("ld_" + name)
        b_small.append(bt)
        add("sp", lambda e: e.dma_start(out=t[:], in_=src), (), [bt], dma=True)
        prm[name] = t
        return t

    cv_s = load_small("cv_s", cvT, [128, 8, 2])
    bmod_s = load_small("bmod_s", bmodT, [128, L, 48])
    gmix_s = load_small("gmix_s", gmixT, [128, L, 8])
    gff_s = load_small("gff_s", gffT, [128, L, 8])
    gfin_s = load_small("gfin_s", gfinT, [128, 8])
    gq_s = load_small("gq_s", gqT, [128, L, 2])
    gkvc_s = load_small("gkvc_s", gkvc, [128, L])
    gkvr_s = sb("gkvr_s", [128, L, 128], BF16)
    b_small.append(Buf("ld_gkvr"))
    add("pool", lambda e: e.dma_start(out=gkvr_s[:], in_=gkvr), (), [b_small[-1]], dma=True)
    gsub_s = load_small("gsub_s", gsubc, [128, L])
    lam_s = load_small("lam_s", lamv, [128, L, 4, 32])
    dw_s = load_small("dw_s", dwT, [128, L, 2, 31])
    cb_s = load_small("cb_s", cbT, [128, L, 2])
    lng_s = load_small("lng_s", lngT, [128, L, 2])
    lnb_s = load_small("lnb_s", lnbT, [128, L, 2])
    cos_s = load_small("cos_s", cosT_d, [128, T])
    sin_s = load_small("sin_s", sinT_d, [128, T])
    colm_s = load_small("colm_s", colmask_d, [128, 64])
    halsel_s = load_small("halsel_s", halsel_d, [128, 8])
    joinc = sb("joinc", [128, 1], F32)
    add("dve", lambda e: e.memset(joinc[:], 0.0), list(b_small), [b_prm])

    lams = sb("lams", [128, L, 2], F32)
    neglam = sb("neglam", [128, L], F32)
    gsub2 = sb("gsub2", [128, L], F32)
    for l in range(L):
        lam_init = 0.8 - 0.6 * math.exp(-0.3 * l)
        for m in range(2):
            add("dve", lambda e, l=l, m=m: e.tensor_tensor(out=lam_s[:, l, 2 * m, :], in0=lam_s[:, l, 2 * m, :],
                                                          in1=lam_s[:, l, 2 * m + 1, :], op=ALU.mult), [b_prm], [b_prm])
            add("dve", lambda e, l=l, m=m: e.reduce_sum(out=lams[:, l, m:m + 1], in_=lam_s[:, l, 2 * m, :],
                                                       axis=mybir.AxisListType.X), [b_prm], [b_prm])
        add("act", lambda e, l=l: e.activation(out=lams[:, l, :], in_=lams[:, l, :], func=AF.Exp), [b_prm], [b_prm])
        add("dve", lambda e, l=l: e.tensor_tensor(out=neglam[:, l:l + 1], in0=lams[:, l, 1:2], in1=lams[:, l, 0:1],
                                                 op=ALU.subtract), [b_prm], [b_prm])
        add("dve", lambda e, l=l, li=lam_init: e.tensor_scalar(out=neglam[:, l:l + 1], in0=neglam[:, l:l + 1],
                                                              scalar1=-li, scalar2=None, op0=ALU.add), [b_prm], [b_prm])
        add("dve", lambda e, l=l, li=lam_init: e.tensor_scalar(out=gsub2[:, l:l + 1], in0=gsub_s[:, l:l + 1],
                                                              scalar1=1.0 - li, scalar2=None, op0=ALU.mult), [b_prm], [b_prm])

    xT = [sb("xT%d" % g, [128, 8, T], F32) for g in range(2)]
    b_xT = [[Buf("xT%d_%d" % (g, j)) for j in range(8)] for g in range(2)]
    hT = [sb("hT%d" % g, [128, 8, T], BF16) for g in range(2)]
    b_hT = [Buf("hT%d" % g) for g in range(2)]
    mixT = [sb("mixT%d" % g, [128, 8, T], BF16) for g in range(2)]
    b_mix = [[Buf("mix%d_%d" % (g, j)) for j in range(8)] for g in range(2)]
    NW = 3
    wpool = [sb("wp%d" % i, [128, 8, 512], BF16) for i in range(NW)]
    b_wp = [Buf("wp%d" % i) for i in range(NW)]
    wp_i = [0]

    def wtile(src_ap, ncols, nk=8):
        i = wp_i[0] % NW
        wp_i[0] += 1
        t, b = wpool[i], b_wp[i]
        add("pool", lambda e: e.dma_start(out=t[:, 0:nk, 0:ncols], in_=src_ap.rearrange("(k p) c -> p k c", p=128)),
            (), [b], dma=True)
        return t, b

    qd = sb("qd", [128, 2, T], F32); b_qd = Buf("qd_cacc_stage")
    stage1 = qd[:, :, :].rearrange("p a b -> p (a b)")

    class _Stage:
        def __getitem__(self, key):
            p, sl, c = key
            return stage1[p, c]
    stage = _Stage()
    b_stage = [b_qd, b_qd]
    rstd = sb("rstd", [128, T], F32); b_rstd = Buf("rstd")
    tmpf = [sb("tmpf%d" % i, [128, T], F32) for i in range(4)]
    b_tmpf = [Buf("tmpf%d" % i) for i in range(4)]
    tf_i = [0]

    def tmp():
        i = tf_i[0] % 4
        tf_i[0] += 1
        return tmpf[i], b_tmpf[i]

    sqb = [sb("sqb%d" % i, [128, T], BF16) for i in range(2)]
    b_sqb = [Buf("sqb%d" % i) for i in range(2)]
    sq_i = [0]

    def sqt():
        i = sq_i[0] % 2
        sq_i[0] += 1
        return sqb[i], b_sqb[i]

    sil = sb("sil", [128, 8, 2], BF16); b_sil = Buf("sil")
    add("act", lambda e: e.activation(out=sil[:], in_=cv_s[:], func=AF.Silu), [b_prm], [b_sil])
    modv = sb("modv", [128, L, 48, 2], F32)
    gsc = sb("gsc", [128, L, 2, 8, 2], F32)
    b_mod = [Buf("mod%d" % l) for l in range(L)]

    def modulation(l):
        pt, pb = banks[3]
        for ti in range(12):
            wt, wb = wtile(w_mod[l, :, ti * 512:(ti + 1) * 512], 512)
            for cc in range(4):
                n = ti * 4 + cc
                for k in range(8):
                    add("pe", lambda e, wt=wt, cc=cc, k=k, n=n, pt=pt: e.matmul(pt[:, n * 2:n * 2 + 2], lhsT=wt[:, k, cc * 128:(cc + 1) * 128],
                                                                             rhs=sil[:, k, :], start=(k == 0), stop=(k == 7)),
                        [wb, b_sil], [pb])
            yield
        add("dve", lambda e, pt=pt: e.tensor_tensor(out=modv[:, l, :, :], in0=pt[:, 0:96].rearrange("p (n v) -> p n v", v=2),
                                                   in1=bmod_s[:, l, :].unsqueeze(2).to_broadcast([128, 48, 2]), op=ALU.add),
            [pb, b_prm], [b_mod[l]])
        for ni, (gsrc, off) in enumerate(((gmix_s, 8), (gff_s, 32))):
            add("dve", lambda e, ni=ni, gsrc=gsrc, off=off: e.scalar_tensor_tensor(
                out=gsc[:, l, ni, :, :], in0=modv[:, l, off:off + 8, :], scalar=1.0,
                in1=gsrc[:, l, :].unsqueeze(2).to_broadcast([128, 8, 2]), op0=ALU.add, op1=ALU.mult),
                [b_mod[l], b_prm], [b_mod[l]])

    def mcol(l, which, j, v):
        off = {"sh1": 0, "sc1": 8, "g1": 16, "sh2": 24, "sc2": 32, "g2": 40}[which]
        return modv[:, l, off + j, v:v + 1]

    def rmsnorm_mod(l, g, ni):
        pt, pb = gp()
        for j in range(8):
            sq, bq = sqt()
            add("act", lambda e, sq=sq, j=j: e.activation(out=sq[:], in_=xT[g][:, j, :], func=AF.Square), [b_xT[g][j]], [bq])
            add("pe", lambda e, sq=sq, j=j, pt=pt: e.matmul(pt[:, :], lhsT=ones_b[:, :], rhs=sq[:], start=(j == 0), stop=(j == 7)),
                [bq, b_ones_b], [pb])
        add("act", lambda e, pt=pt: e.activation(out=rstd[:], in_=pt[:, :], func=AF.Ln, bias=epsc[:, 0:1], scale=1.0 / D),
            [pb, b_epsc], [b_rstd])
        add("act", lambda e: e.activation(out=rstd[:], in_=rstd[:], func=AF.Exp, scale=-0.5), [b_rstd], [b_rstd])
        for j in range(8):
            tt, tb_ = tmp()
            add("dve", lambda e, tt=tt, j=j: e.scalar_tensor_tensor(out=tt[:], in0=xT[g][:, j, :], scalar=gsc[:, l, ni, j, g:g + 1],
                                                                    in1=rstd[:], op0=ALU.mult, op1=ALU.mult),
                [b_xT[g][j], b_rstd, b_mod[l]], [tb_])
            add("act", lambda e, tt=tt, j=j: e.activation(out=hT[g][:, j, :], in_=tt[:], func=AF.Identity,
                                                          bias=mcol(l, "sh1" if ni == 0 else "sh2", j, g), scale=1.0),
                [tb_, b_mod[l]], [b_hT[g]])

    def stat_rstd(src_list, nfeat, dst, b_dst, reads, K=128, n=T):
        pt, pb = gp()
        for i, src in enumerate(src_list):
            tt, tb_ = tmp()
            add("act", lambda e, tt=tt, src=src: e.activation(out=tt[0:K, 0:n], in_=src, func=AF.Square), reads, [tb_])
            add("pe", lambda e, tt=tt, i=i, pt=pt: e.matmul(pt[0:K, 0:n], lhsT=ones_f[0:K, 0:K], rhs=tt[0:K, 0:n],
                                                          start=(i == 0), stop=(i == len(src_list) - 1)),
                [tb_, b_ones_f], [pb])
        add("act", lambda e, pt=pt: e.activation(out=dst, in_=pt[0:K, 0:n], func=AF.Ln, bias=epsc[0:K, 0:1], scale=1.0 / nfeat),
            [pb, b_epsc], [b_dst])
        add("act", lambda e: e.activation(out=dst, in_=dst, func=AF.Exp, scale=-0.5), [b_dst], [b_dst])

    KTb = sb("KTb", [128, 4, DEC_S + PAST], BF16); b_KTh = [Buf("KT%d" % h) for h in range(4)]
    VAb = sb("VAb", [128, 18, 4, 72], BF16); b_VAk = [Buf("VA%d" % k) for k in range(18)]
    KTc2 = [sb("KTc%d" % m, [128, 4, T], BF16) for m in range(2)]
    b_KTc = [Buf("KTc%d" % m) for m in range(3)]
    VAc = [sb("VAc%d" % m, [128, 4, 4, 72], BF16) for m in range(3)]; b_VAc = [Buf("VAc%d" % m) for m in range(3)]
    QT2 = [[sb("QT%d_%d" % (g, m), [128, 4, T], BF16) for m in range(2)] for g in range(2)]
    b_QT = [[Buf("QT%d_%d" % (g, m)) for m in range(3)] for g in range(2)]
    add("pool", lambda e: e.memset(VAb[:, :, :, 64:72], 1.0), (), b_VAk)
    add("pool", lambda e: e.memset(KTb[64:128, :, :], 0.0), (), b_KTh)
    add("pool", lambda e: e.memset(QT2[1][1][96:128, :, :], 0.0), (), [b_QT[1][1]])
    for m in range(3):
        add("pool", lambda e, m=m: e.memset(VAc[m][:, :, :, 64:72], 1.0), (), [b_VAc[m]])

    def q_ap(g, m, h, p_lo, p_hi, c0=0, n=T):
        if m == 0:
            return QT2[g][0][p_lo:p_hi, h, c0:c0 + n]
        if m == 1:
            return QT2[g][1][p_lo:p_hi, h, c0:c0 + n]
        return QT2[g][0][64 + p_lo:64 + p_hi, h, c0:c0 + n]

    def kc_ap(m, h, p_lo, p_hi, c0, n):
        if m == 0:
            return KTc2[0][p_lo:p_hi, h, c0:c0 + n]
        if m == 1:
            return KTc2[1][p_lo:p_hi, h, c0:c0 + n]
        return KTc2[0][64 + p_lo:64 + p_hi, h, c0:c0 + n]
    NE = 4
    Eb = [sb("E%d" % i, [128, T], BF16) for i in range(NE)]
    b_E = [Buf("E%d" % i) for i in range(NE)]
    e_i = [0]
    NSET = 2
    rsum_s = [sb("rsum0", [128, T], F32)] * 2; b_rsum_s = [Buf("rsum0")] * 2
    rsb0 = sb("rsb0", [128, T], BF16)
    rsb_row = [64, 64]
    b_rsb_s = [Buf("rsb0")] * 2
    dao = sb("dao", [128, T], F32); b_dao = Buf("dao")
    set_i = [0]
    ckvT = sb("ckvT", [128, DEC_S + PAST], BF16); b_ckvT = Buf("ckvT")
    ckvTc = ckvT; b_ckvTc = b_ckvT
    qdn = sb("qdn", [128, 2, T], BF16); b_qdn = Buf("qdn")
    wq_s = sb("wq_s", [128, 2, 384], BF16); wqp_s = sb("wqp_s", [128, 2, 384], BF16); wkv_s = sb("wkv_s", [128, 512], BF16)
    b_wsm = Buf("wsmall")
    gpad = [sb("gpad%d" % g, [128, 2, 2, 15 + 256 + 15], BF16) for g in range(2)]
    b_gpad = [Buf("gpad%d" % g) for g in range(2)]
    for g in range(2):
        add("pool", lambda e, g=g: e.memset(gpad[g][:], 0.0), (), [b_gpad[g]])
    cacc = qd; b_cacc = b_qd
    halo = sb("halo", [128, 2, 4, 30], BF16); b_halo = Buf("halo")
    NTBT = 2
    A2U = 39
    tbt = [sb("tbt%d" % i, [128, A2U, 64], BF16) for i in range(NTBT)] * (2 // NTBT)
    b_tbt = [Buf("tbt%d" % i) for i in range(NTBT)] * (2 // NTBT)
    b_payA = [Buf("payA%d" % l) for l in range(L)]
    b_payB = [Buf("payB%d" % l) for l in range(L)]
    b_payoA = [Buf("payoA%d" % l) for l in range(L)]
    b_payoB = [Buf("payoB%d" % l) for l in range(L)]
    b_tz = [Buf("tz%d" % l) for l in range(L)]

    pending_post = [None]
    import os as _os2
    WARM_LAT = int(_os2.environ.get("WARM_LAT", "0"))
    WARM_CTX = int(_os2.environ.get("WARM_CTX", "0"))
    WARM_N = int(_os2.environ.get("WARM_N", "256"))
    BURST = int(_os2.environ.get("BURST", "20"))

    def warm(n):
        for _ in range(n):
            add("pe", lambda e: e.matmul(banks[7][0][:, 512 - WARM_N:512], lhsT=identb[:, 0:128], rhs=identb[:, 0:128], start=True, stop=True), (), ())

    def flush_post():
        if pending_post[0] is not None:
            f = pending_post[0]
            pending_post[0] = None
            f()

    def attention(kt_fn, q_ap, va_fn, nkb, nq, scale, reads, extra_fn=None):
        at, ab = acb()
        pend = []

        def score(kb):
            pt, pb = gp()
            ex = extra_fn(kb) if extra_fn is not None else []
            add("pe", lambda e, pt=pt, kb=kb: e.matmul(pt[:, 0:nq], lhsT=kt_fn(kb), rhs=q_ap, start=True, stop=(len(ex) == 0)),
                reads, [pb])
            for i, (lh, rh, rd) in enumerate(ex):
                add("pe", lambda e, pt=pt, lh=lh, rh=rh, i=i: e.matmul(pt[:, 0:nq], lhsT=lh, rhs=rh, start=False, stop=(i == len(ex) - 1)),
                    rd, [pb])
            i = e_i[0] % NE
            e_i[0] += 1
            add("act", lambda e, pt=pt, i=i: e.activation(out=Eb[i][:, 0:nq], in_=pt[:, 0:nq], func=AF.Exp, scale=scale), [pb], [b_E[i]])
            return i

        for kb in range(min(2, nkb)):
            pend.append(score(kb))
        flush_post()
        for kb in range(nkb):
            if kb + 2 < nkb:
                pend.append(score(kb + 2))
            warm(WARM_LAT)
            i = pend[kb]
            add("pe", lambda e, at=at, kb=kb, i=i: e.matmul(at[0:65, 0:nq], lhsT=va_fn(kb), rhs=Eb[i][:, 0:nq], start=(kb == 0), stop=(kb == nkb - 1)),
                reads + [b_E[i]], [ab])
        return at, ab

    def normalize_rep(at, ab, nq, dst, b_dst_list):
        r, br = tmp()
        add("act", lambda e: e.activation(out=r[0:64, 0:nq], in_=at[0:64, 256:256 + nq], func=AF.Ln), [ab], [br])
        add("act", lambda e: e.activation(out=r[0:64, 0:nq], in_=r[0:64, 0:nq], func=AF.Exp, scale=-1.0), [br], [br])
        add("dve", lambda e: e.tensor_tensor(out=dst, in0=at[0:64, 0:nq], in1=r[0:64, 0:nq], op=ALU.mult), [ab, br], b_dst_list)

    def normalize(at, ab, nq, dst, b_dst_list, k=0, c0=0):
        rsum, b_rsum, b_rsb, rr = rsum_s[k], b_rsum_s[k], b_rsb_s[k], rsb_row[k]
        add("act", lambda e: e.activation(out=rsum[64:65, 0:nq], in_=at[64:65, c0:c0 + nq], func=AF.Ln), [ab], [b_rsum])
        add("act", lambda e: e.activation(out=rsb0[rr:rr + 1, 0:nq], in_=rsum[64:65, 0:nq], func=AF.Exp, scale=-1.0), [b_rsum], [b_rsb])
        pt, pb = gp()
        add("pe", lambda e: e.matmul(pt[0:64, 0:nq], lhsT=ones_b[rr:rr + 1, 0:64], rhs=rsb0[rr:rr + 1, 0:nq], start=True, stop=True),
            [b_rsb, b_ones_b], [pb])
        bc, bbc = tmp()
        add("act", lambda e: e.activation(out=bc[0:64, 0:nq], in_=pt[0:64, 0:nq], func=AF.Copy), [pb], [bbc])
        add("dve", lambda e: e.tensor_tensor(out=dst, in0=at[0:64, c0:c0 + nq], in1=bc[0:64, 0:nq], op=ALU.mult), [ab, bbc], b_dst_list)

    def stream_attention(jobs, nkb=18, nq=T, side_cb=None, burst=0):
        blocks = [(ji, kb) for ji in range(len(jobs)) for kb in range(nkb)]
        N = len(blocks)
        pend = {}
        posts = {}

        def do_score(idx):
            ji, kb = blocks[idx]
            J = jobs[ji]
            if kb == 0 and J.get("pre") is not None:
                J["pre"]()
            pt, pb = gp()
            ex = J["extra_fn"](kb) if J.get("extra_fn") is not None else []
            add("pe", lambda e, pt=pt, kb=kb, J=J: e.matmul(pt[:, 0:nq], lhsT=J["kt_fn"](kb), rhs=J["q_ap"], start=True, stop=(len(ex) == 0)),
                J["reads"], [pb])
            for i2, (lh, rh, rd) in enumerate(ex):
                add("pe", lambda e, pt=pt, lh=lh, rh=rh, i2=i2: e.matmul(pt[:, 0:nq], lhsT=lh, rhs=rh, start=False, stop=(i2 == len(ex) - 1)),
                    rd, [pb])
            i = e_i[0] % NE
            e_i[0] += 1
            add("act", lambda e, pt=pt, i=i, J=J: e.activation(out=Eb[i][:, 0:nq], in_=pt[:, 0:nq], func=AF.Exp, scale=J["scale"]), [pb], [b_E[i]])
            pend[idx] = i

        LA = 3
        for idx in range(min(LA, N)):
            do_score(idx)
        if burst:
            wpt, wpb = gp()
            for _ in range(burst):
                add("pe", lambda e, wpt=wpt: e.matmul(wpt[:, :], lhsT=identb[:, :], rhs=hT[1][:, 0, :], start=True, stop=True),
                    [b_identb, b_hT[1]], [wpb])
        for idx in range(N):
            if idx + LA < N:
                do_score(idx + LA)
            ji, kb = blocks[idx]
            J = jobs[ji]
            if kb == 0:
                J["acc"] = acb()
            at, ab = J["acc"]
            i = pend.pop(idx)
            add("pe", lambda e, at=at, kb=kb, i=i, J=J: e.matmul(at[0:65, 0:nq], lhsT=J["va_fn"](kb), rhs=Eb[i][:, 0:nq],
                                                              start=(kb == 0), stop=(kb == nkb - 1)), J["reads"] + [b_E[i]], [ab])
            if kb == nkb - 1:
                if J.get("post") is not None:
                    posts.setdefault(min(idx + 2, N - 1), []).append(J["post"])
                if side_cb is not None:
                    side_cb()
            for f in posts.pop(idx, []):
                f()

    def input_proj(l):
        res = {}
        add("pool", lambda e: e.dma_start(out=wq_s[:], in_=wqup[l].rearrange("(k p) c -> p k c", p=128)), (), [b_wsm], dma=True)
        add("pool", lambda e: e.dma_start(out=wqp_s[:], in_=wqupp[l].rearrange("(k p) c -> p k c", p=128)), (), [b_wsm], dma=True)
        add("pool", lambda e: e.dma_start(out=wkv_s[:], in_=wkvup[l]), (), [b_wsm], dma=True)

        def fm(wt, wb, c0, M, g, n0=0, n=T):
            pt, pb = gp()
            for k in range(8):
                add("pe", lambda e, pt=pt, k=k: e.matmul(pt[0:M, 0:n], lhsT=wt[:, k, c0:c0 + M], rhs=hT[g][:, k, n0:n0 + n],
                                                        start=(k == 0), stop=(k == 7)), [wb, b_hT[g]], [pb])
            return pt, pb

        def tm(wt, wb, c0, N, g, tb):
            pt, pb = gp()
            for k in range(8):
                add("pe", lambda e, pt=pt, k=k: e.matmul(pt[:, 0:N], lhsT=hT[g][:, k, tb * 128:(tb + 1) * 128], rhs=wt[:, k, c0:c0 + N],
                                                        start=(k == 0), stop=(k == 7)), [wb, b_hT[g]], [pb])
            return pt, pb

        def rope_evac(ptA, pbA, ptB, pbB, p0, p1, dst, wlist):
            t1, tb1 = tmp()
            t2, tb2 = tmp()
            add("dve", lambda e: e.tensor_tensor(out=t1[p0:p1, :], in0=ptA[p0:p1, :], in1=cos_s[p0:p1, :], op=ALU.mult), [pbA, b_prm], [tb1])
            add("dve", lambda e: e.tensor_tensor(out=t2[p0:p1, :], in0=ptB[p0:p1, :], in1=sin_s[p0:p1, :], op=ALU.mult), [pbB, b_prm], [tb2])
            add("dve", lambda e: e.tensor_tensor(out=dst, in0=t1[p0:p1, :], in1=t2[p0:p1, :], op=ALU.add), [tb1, tb2], wlist)

        groups = [0, 1] if do_lat else [0]
        wA, bA = wtile(w_in[l, :, 0:512], 512)
        if do_lat:
            wF1, bF1 = wtile(w_inp[l, :, 0:512], 512)
        for hp in range(2):
            pt, pb = fm(wA, bA, hp * 128, 128, 0)
            evac(QT2[0][0][0:64, 2 * hp, :], pt[0:64, :], [pb], [b_QT[0][0]])
            evac(QT2[0][0][0:64, 2 * hp + 1, :], pt[64:128, :], [pb], [b_QT[0][0]])
            pt, pb = fm(wA, bA, 256 + hp * 128, 128, 0)
            evac(KTc2[0][0:64, 2 * hp, :], pt[0:64, :], [pb], [b_KTc[0]])
            evac(KTc2[0][0:64, 2 * hp + 1, :], pt[64:128, :], [pb], [b_KTc[0]])
        if do_lat:
            def rope_pair(c0):
                ptA, pbA = fm(wA, bA, c0, 128, 1)
                ptB, pbB = fm(wF1, bF1, c0, 128, 1)
                t1, tb1 = tmp()
                t2, tb2 = tmp()
                add("dve", lambda e: e.tensor_tensor(out=t1[:, :], in0=ptA[:, :], in1=cos_s[:, :], op=ALU.mult), [pbA, b_prm], [tb1])
                add("dve", lambda e: e.tensor_tensor(out=t2[:, :], in0=ptB[:, :], in1=sin_s[:, :], op=ALU.mult), [pbB, b_prm], [tb2])
                return t1, tb1, t2, tb2
            for hp in range(2):
                t1, tb1, t2, tb2 = rope_pair(hp * 128)
                for i in range(2):
                    add("dve", lambda e, t1=t1, t2=t2, i=i, hp=hp: e.tensor_tensor(out=QT2[1][0][0:64, 2 * hp + i, :], in0=t1[i * 64:(i + 1) * 64, :],
                                                                                in1=t2[i * 64:(i + 1) * 64, :], op=ALU.add), [tb1, tb2], [b_QT[1][0]])
                t1, tb1, t2, tb2 = rope_pair(256 + hp * 128)
                tk, tkb = sqt()
                add("dve", lambda e, t1=t1, t2=t2, tk=tk: e.tensor_tensor(out=tk[:, :], in0=t1[:, :], in1=t2[:, :], op=ALU.add), [tb1, tb2], [tkb])
                add("sp", lambda e, tk=tk, hp=hp: e.dma_start(out=payA_in[l].ap()[R_DAK + hp * 128:R_DAK + (hp + 1) * 128, :], in_=tk[:, :]),
                    [tkb], [b_payA[l]], dma=True)
        wB, bB = wtile(w_in[l, :, 512:1024], 512)
        for g in groups:
            for c in range(2):
                pt, pb = fm(wB, bB, 256 + c * 128, 128, g)
                evac(qd[:, c, :], pt[:, :], [pb], [b_qd])
            stat_rstd([qd[:, 0, :], qd[:, 1, :]], 256, rstd[:], b_rstd, [b_qd])
            for c in range(2):
                add("dve", lambda e, c=c: e.scalar_tensor_tensor(out=qdn[:, c, :], in0=qd[:, c, :], scalar=gq_s[:, l, c:c + 1], in1=rstd[:],
                                                               op0=ALU.mult, op1=ALU.mult), [b_qd, b_rstd, b_prm], [b_qdn])
            for h in range(4):
                pt, pb = gp()
                for c in range(2):
                    add("pe", lambda e, pt=pt, c=c, h=h: e.matmul(pt[0:96, :], lhsT=wq_s[:, c, h * 96:(h + 1) * 96], rhs=qdn[:, c, :],
                                                                 start=(c == 0), stop=(c == 1)), [b_wsm, b_qdn], [pb])
                if g == 0:
                    evac(QT2[0][1][0:96, h, :], pt[0:96, :], [pb], [b_QT[0][1]])
                else:
                    pt2, pb2 = gp()
                    for c in range(2):
                        add("pe", lambda e, pt2=pt2, c=c, h=h: e.matmul(pt2[0:96, :], lhsT=wqp_s[:, c, h * 96:(h + 1) * 96], rhs=qdn[:, c, :],
                                                                       start=(c == 0), stop=(c == 1)), [b_wsm, b_qdn], [pb2])
                    evac(QT2[1][1][0:64, h, :], pt[0:64, :], [pb], [b_QT[1][1]])
                    rope_evac(pt, pb, pt2, pb2, 64, 96, QT2[1][1][64:96, h, :], [b_QT[1][1]])
        for g in groups:
            for tb in range(4):
                pt, pb = tm(wB, bB, 0, 256, g, tb)
                if g == 0:
                    ostage, b_ostage = tmp()
                    add("act", lambda e, pt=pt, ostage=ostage: e.activation(out=ostage[:, 0:256], in_=pt[:, 0:256], func=AF.Copy), [pb], [b_ostage])
                    s, tl = tb // 2, (tb % 2) * 128
                    add("sp", lambda e, s=s, tl=tl, ostage=ostage: e.dma_start(out=o_dav[s, l, :, tl:tl + 128, :].rearrange("h t d -> t h d"),
                                                               in_=ostage[:, 0:256].rearrange("p (h d) -> p h d", h=4)), [b_ostage], (), dma=True)
                    add("dve", lambda e, pt=pt, tb=tb: e.tensor_copy(out=VAc[0][:, tb, :, 0:64], in_=pt[:, 0:256].rearrange("p (h d) -> p h d", h=4)),
                        [pb], [b_VAc[0]])
                else:
                    tv, tvb = sqt()
                    evac(tv[:, 0:256], pt[:, 0:256], [pb], [tvb])
                    add("sp", lambda e, tv=tv, tb=tb: e.dma_start(out=payB_in[l].ap()[R_V + tb * 128:R_V + (tb + 1) * 128, 0:256], in_=tv[:, 0:256]),
                        [tvb], [b_payB[l]], dma=True)
        wC, bC = wtile(w_in[l, :, 1024:1440], 416)
        if do_lat:
            wF2, bF2 = wtile(w_inp[l, :, 512:608], 96)
        for g in groups:
            pt, pb = fm(wC, bC, 0, 128, g)
            kvd, b_kvd = tmp()
            evac(kvd[:, :], pt[:, :], [pb], [b_kvd])
            stat_rstd([kvd[:, :]], 128, rstd[:], b_rstd, [b_kvd])
            dstc = ckvTc if g == 0 else sqb[0]
            if g == 0:
                add("dve", lambda e, kvd=kvd: e.scalar_tensor_tensor(out=ckvTc[:, 0:T], in0=kvd[:, :], scalar=gkvc_s[:, l:l + 1], in1=rstd[:],
                                                            op0=ALU.mult, op1=ALU.mult), [b_kvd, b_rstd, b_prm], [b_ckvTc])
            else:
                tk, tkb = sqt()
                add("dve", lambda e, tk=tk, kvd=kvd: e.scalar_tensor_tensor(out=tk[:, :], in0=kvd[:, :], scalar=gkvc_s[:, l:l + 1], in1=rstd[:],
                                                                   op0=ALU.mult, op1=ALU.mult), [b_kvd, b_rstd, b_prm], [tkb])
                add("sp", lambda e, tk=tk: e.dma_start(out=payA_in[l].ap()[R_CKV:R_CKV + 128, :], in_=tk[:, :]), [tkb], [b_payA[l]], dma=True)
            pt, pb = fm(wC, bC, 64, 96, g)
            if g == 0:
                for h in range(4):
                    evac(KTc2[1][64:96, h, :], pt[64:96, :], [pb], [b_KTc[1]])
            else:
                ptB, pbB = fm(wF2, bF2, 0, 96, 1)
                tk, tkb = sqt()
                rope_evac(pt, pb, ptB, pbB, 64, 96, tk[64:96, :], [tkb])
                add("sp", lambda e, tk=tk: e.dma_start(out=payA_in[l].ap()[R_KR:R_KR + 32, :], in_=tk[64:96, :]), [tkb], [b_payA[l]], dma=True)
            for hp in range(2):
                pt, pb = fm(wC, bC, 160 + hp * 128, 128, g)
                evac(QT2[g][0][64:128, 2 * hp, :], pt[0:64, :], [pb], [b_QT[g][2]])
                evac(QT2[g][0][64:128, 2 * hp + 1, :], pt[64:128, :], [pb], [b_QT[g][2]])
        for tb in range(4):
            pt, pb = tm(wC, bC, 0, 160, 0, tb)
            s, tl = tb // 2, (tb % 2) * 128
            tt, tb_ = tmp()
            add("act", lambda e, pt=pt, tt=tt: e.activation(out=tt[:, 0:128], in_=pt[:, 0:128], func=AF.Square), [pb], [tb_])
            add("dve", lambda e, tt=tt: e.reduce_sum(out=tt[:, 200:201], in_=tt[:, 0:128], axis=mybir.AxisListType.X), [tb_], [tb_])
            add("act", lambda e, tt=tt: e.activation(out=tt[:, 201:202], in_=tt[:, 200:201], func=AF.Ln, bias=epsc[:, 0:1], scale=1.0 / 128),
                [tb_, b_epsc], [tb_])
            add("act", lambda e, tt=tt: e.activation(out=tt[:, 202:203], in_=tt[:, 201:202], func=AF.Exp, scale=-0.5), [tb_], [tb_])
            add("dve", lambda e, pt=pt, tt=tt: e.scalar_tensor_tensor(out=tt[:, 256:384], in0=pt[:, 0:128], scalar=tt[:, 202:203],
                                                                      in1=gkvr_s[:, l, :], op0=ALU.mult, op1=ALU.mult),
                [pb, tb_, b_prm], [tb_])
            add("act", lambda e, pt=pt, tt=tt: e.activation(out=tt[:, 384:416], in_=pt[:, 128:160], func=AF.Copy), [pb, tb_], [tb_])
            add("sp", lambda e, s=s, tl=tl, tt=tt: e.dma_start(out=o_ckv[s, l, tl:tl + 128, :], in_=tt[:, 256:384]), [tb_], (), dma=True)
            add("sp", lambda e, s=s, tl=tl, tt=tt: e.dma_start(out=o_kr[s, l, tl:tl + 128, :], in_=tt[:, 384:416]), [tb_], (), dma=True)
        for h in range(4):
            pt, pb = gp()
            add("pe", lambda e, pt=pt, h=h: e.matmul(pt[0:64, :], lhsT=wkv_s[:, h * 128:h * 128 + 64], rhs=ckvTc[:, 0:T], start=True, stop=True),
                [b_wsm, b_ckvTc], [pb])
            evac(KTc2[1][0:64, h, :], pt[0:64, :], [pb], [b_KTc[1]])
        for tb in range(4):
            pt, pb = gp()
            add("pe", lambda e, pt=pt, tb=tb: e.matmul(pt[:, 0:256], lhsT=ckvTc[:, tb * 128:(tb + 1) * 128],
                                                      rhs=wkv_s[:, :].rearrange("p (h c) -> p h c", h=4)[:, :, 64:128], start=True, stop=True),
                [b_wsm, b_ckvTc], [pb])
            evac(VAc[1][:, tb, :, 0:64], pt[:, 0:256].rearrange("p (h d) -> p h d", h=4), [pb], [b_VAc[1]])
        wD, bD = wtile(w_in[l, :, 1440:1952], 512)
        for hp in range(2):
            pt, pb = fm(wD, bD, hp * 128, 128, 0)
            evac(KTc2[0][64:128, 2 * hp, :], pt[0:64, :], [pb], [b_KTc[2]])
            evac(KTc2[0][64:128, 2 * hp + 1, :], pt[64:128, :], [pb], [b_KTc[2]])
            if do_lat:
                pt, pb = fm(wD, bD, hp * 128, 128, 1)
                tk, tkb = sqt()
                evac(tk[:, :], pt[:, :], [pb], [tkb])
                add("sp", lambda e, tk=tk, hp=hp: e.dma_start(out=payA_in[l].ap()[R_NAK + hp * 128:R_NAK + (hp + 1) * 128, :], in_=tk[:, :]),
                    [tkb], [b_payA[l]], dma=True)
        for tb in range(4):
            pt, pb = tm(wD, bD, 0, 512, 0, tb)
            s, tl = tb // 2, (tb % 2) * 128
            ostage, b_ostage = tmp()
            add("act", lambda e, pt=pt, ostage=ostage: e.activation(out=ostage[:, :], in_=pt[:, :], func=AF.Copy), [pb], [b_ostage])
            add("sp", lambda e, s=s, tl=tl, ostage=ostage: e.dma_start(out=o_nak[s, l, :, tl:tl + 128, :].rearrange("h t d -> t h d"),
                                                       in_=ostage[:, 0:256].rearrange("p (h d) -> p h d", h=4)), [b_ostage], (), dma=True)
            add("sp", lambda e, s=s, tl=tl, ostage=ostage: e.dma_start(out=o_nav[s, l, :, tl:tl + 128, :].rearrange("h t d -> t h d"),
                                                       in_=ostage[:, 256:512].rearrange("p (h d) -> p h d", h=4)), [b_ostage], (), dma=True)
            add("dve", lambda e, pt=pt, tb=tb: e.tensor_copy(out=VAc[2][:, tb, :, 0:64], in_=pt[:, 256:512].rearrange("p (h d) -> p h d", h=4)),
                [pb], [b_VAc[2]])
            if do_lat:
                pt, pb = tm(wD, bD, 256, 256, 1, tb)
                tv, tvb = sqt()
                evac(tv[:, 0:256], pt[:, 0:256], [pb], [tvb])
                add("sp", lambda e, tv=tv, tb=tb: e.dma_start(out=payB_in[l].ap()[R_V + tb * 128:R_V + (tb + 1) * 128, 256:512], in_=tv[:, 0:256]),
                    [tvb], [b_payB[l]], dma=True)
        wA2, bA2 = wtile(w_in[l, :, 256:512], 256)
        for tb in range(4):
            pt, pb = tm(wA2, bA2, 0, 256, 0, tb)
            s, tl = tb // 2, (tb % 2) * 128
            ostage, b_ostage = tmp()
            add("act", lambda e, pt=pt, ostage=ostage: e.activation(out=ostage[:, 0:256], in_=pt[:, 0:256], func=AF.Copy), [pb], [b_ostage])
            add("sp", lambda e, s=s, tl=tl, ostage=ostage: e.dma_start(out=o_dak[s, l, :, tl:tl + 128, :].rearrange("h t d -> t h d"),
                                                       in_=ostage[:, 0:256].rearrange("p (h d) -> p h d", h=4)), [b_ostage], (), dma=True)
        wE, bE = wtile(w_in[l, :, 1952:2464], 512)
        for g in groups:
            for c in range(2):
                pa, pba = fm(wE, bE, c * 128, 128, g)
                pbm, pbb = fm(wE, bE, 256 + c * 128, 128, g)
                tt, tb_ = tmp()
                add("act", lambda e, tt=tt, pbm=pbm: e.activation(out=tt[:], in_=pbm[:, :], func=AF.Sigmoid), [pbb], [tb_])
                add("dve", lambda e, tt=tt, pa=pa, c=c, g=g: e.tensor_tensor(out=gpad[g][:, c, :, 15:15 + 256],
                                                                           in0=pa[:, :].rearrange("p (s t) -> p s t", s=2),
                                                                           in1=tt[:].rearrange("p (s t) -> p s t", s=2), op=ALU.mult),
                    [pba, tb_], [b_gpad[g]])
        if do_lat:
            add("dve", lambda e: e.tensor_copy(out=halo[:, :, 0, 0:15], in_=gpad[1][:, :, 0, 15:30]), [b_gpad[1]], [b_halo])
            add("dve", lambda e: e.tensor_copy(out=halo[:, :, 0, 15:30], in_=gpad[1][:, :, 1, 256:271]), [b_gpad[1]], [b_halo])
            hdst = dram_ap(payB_in[l], R_HALO * T, [[30, 128], [128 * 30, 2], [1, 30]])
            add("sp", lambda e: e.dma_start(out=hdst, in_=halo[:, :, 0, :]), [b_halo], [b_payB[l]], dma=True)
            add("pool", lambda e: e.collective_compute("AllGather", ALU.bypass, replica_groups=[[0, 1, 2, 3], [4, 5, 6, 7]],
                                                       ins=[payA_in[l].ap().opt()], outs=[payA_out[l].ap().opt()]),
                [b_payA[l]], [b_payoA[l]], dma="cc")
            add("pool", lambda e: e.collective_compute("AllGather", ALU.bypass, replica_groups=[[0, 1, 2, 3], [4, 5, 6, 7]],
                                                       ins=[payB_in[l].ap().opt()], outs=[payB_out[l].ap().opt()]),
                [b_payB[l]], [b_payoB[l]], dma="cc")

    def da_post(l, g, h, at1, ab1, at2, ab2, nq=T, q0=0, repl=False, c1=0, c2=0):
        o1, bo1 = dao, b_dao
        o2, bo2 = rsum_s[0], b_rsum_s[0]
        if repl:
            normalize_rep(at1, ab1, nq, o1[0:64, 0:nq], [bo1])
            normalize_rep(at2, ab2, nq, o2[0:64, 0:nq], [bo2])
        else:
            normalize(at1, ab1, nq, o1[0:64, 0:nq], [bo1], 0, c1)
            normalize(at2, ab2, nq, o2[0:64, 0:nq], [bo2], 1 if nq <= 256 else 0, c2)
        add("dve", lambda e: e.scalar_tensor_tensor(out=o1[0:64, 0:nq], in0=o2[0:64, 0:nq], scalar=neglam[0:64, l:l + 1], in1=o1[0:64, 0:nq],
                                                    op0=ALU.mult, op1=ALU.add), [bo1, bo2, b_prm], [bo1])
        stat_rstd([o1[0:64, 0:nq]], 64, rstd[0:64, 0:nq], b_rstd, [bo1], K=64, n=nq)
        p0 = (h % 2) * 64
        add("dve", lambda e: e.scalar_tensor_tensor(out=mixT[g][p0:p0 + 64, h // 2, q0:q0 + nq], in0=o1[0:64, 0:nq], scalar=gsub2[0:64, l:l + 1],
                                                    in1=rstd[0:64, 0:nq], op0=ALU.mult, op1=ALU.mult),
            [bo1, b_rstd, b_prm], [b_mix[g][h // 2]])

    def plain_post(g, m, h, at, ab, nq=T, q0=0, repl=False):
        if repl:
            p0 = (h % 2) * 64
            ch = 2 * m + h // 2
            normalize_rep(at, ab, nq, mixT[g][p0:p0 + 64, ch, q0:q0 + nq], [b_mix[g][ch]])
            return
        k = (set_i[0] % NSET) if nq <= 256 else 0
        set_i[0] += 1
        o1, bo1 = tmp()
        normalize(at, ab, nq, o1[0:64, 0:nq], [bo1], k)
        p0 = (h % 2) * 64
        ch = 2 * m + h // 2
        add("act", lambda e: e.activation(out=mixT[g][p0:p0 + 64, ch, q0:q0 + nq], in_=o1[0:64, 0:nq], func=AF.Copy), [bo1], [b_mix[g][ch]])

    def ctx_attention(l, side_gen=None):
        scales = (32 ** -0.5, 96 ** -0.5, 64 ** -0.5)
        rows = {0: None, 1: (0, 96), 2: (0, 64)}
        jobs = []
        for s in range(2):
            for m in range(3):
                for h in range(4):
                    for mp in ((0, 1) if m == 0 else (0,)):
                        jobs.append(dict(s=s, m=m, h=h, mp=mp))
        n = len(jobs)

        def stage_a(j):
            s_, m, h, mp = j["s"], j["m"], j["h"], j["mp"]
            q0 = s_ * 256
            lo, hi = (mp * 32, mp * 32 + 32) if m == 0 else rows[m]
            pt, pb = gp()
            rd = [b_KTc[m], b_QT[0][m]]
            for kb in range(2):
                add("pe", lambda e, pt=pt, kb=kb, m=m, h=h, lo=lo, hi=hi, q0=q0: e.matmul(
                    pt[:, kb * 256:(kb + 1) * 256], lhsT=kc_ap(m, h, lo, hi, q0 + kb * 128, 128), rhs=q_ap(0, m, h, lo, hi, q0, 256),
                    start=True, stop=True), rd, [pb])
            i = e_i[0] % NE
            e_i[0] += 1
            add("act", lambda e, pt=pt, i=i, m=m: e.activation(out=Eb[i][:, :], in_=pt[:, :], func=AF.Exp, scale=scales[m]), [pb], [b_E[i]])
            j["e"] = i

        def stage_b(j):
            s_, m, h = j["s"], j["m"], j["h"]
            at, ab = acb()
            i = j["e"]
            for kb in range(2):
                add("pe", lambda e, at=at, kb=kb, i=i, m=m, h=h, s_=s_: e.matmul(at[0:65, 0:256], lhsT=VAc[m][:, s_ * 2 + kb, h, 0:65],
                                                                               rhs=Eb[i][:, kb * 256:(kb + 1) * 256], start=(kb == 0), stop=(kb == 1)),
                    [b_VAc[m], b_E[i]], [ab])
            for kb in range(2):
                add("pe", lambda e, at=at, kb=kb, i=i: e.matmul(at[0:64, 256:512], lhsT=ones_b[:, 0:64], rhs=Eb[i][:, kb * 256:(kb + 1) * 256],
                                                               start=(kb == 0), stop=(kb == 1)), [b_ones_b, b_E[i]], [ab])
            j["acc"] = (at, ab)

        def stage_c(idx):
            j = jobs[idx]
            s_, m, h, mp = j["s"], j["m"], j["h"], j["mp"]
            q0 = s_ * 256
            if m == 0:
                if mp == 1:
                    a1 = jobs[idx - 1]["acc"]
                    da_post(l, 0, h, a1[0], a1[1], j["acc"][0], j["acc"][1], nq=256, q0=q0, repl=True)
            else:
                plain_post(0, m, h, j["acc"][0], j["acc"][1], nq=256, q0=q0, repl=True)

        for i in range(n + 2):
            if i < n:
                stage_a(jobs[i])
            warm(WARM_CTX)
            if 0 <= i - 1 < n:
                stage_b(jobs[i - 1])
            if 0 <= i - 2 < n:
                stage_c(i - 2)
            if side_gen is not None and i % 2 == 1:
                next(side_gen, None)

    def cache_T(src_ap, ncol, dst_fn, wlist):
        stg, bstg = tmp()
        add("sp", lambda e: e.dma_start(out=stg[:, 0:2 * ncol].rearrange("p (t c) -> p t c", t=2),
                                        in_=src_ap.rearrange("(t p) c -> p t c", p=128)), (), [bstg], dma=True)
        for tb in range(2):
            pt, pb = gp()
            add("pe", lambda e, pt=pt, tb=tb: e.transpose(out=pt[0:ncol, 0:128], in_=stg[:, tb * ncol:(tb + 1) * ncol], identity=ident[:]),
                [bstg, b_ident], [pb])
            evac(dst_fn(tb), pt[0:ncol, 0:128], [pb], wlist)

    def load_V(l, c0, cache_ap):
        for rk in range(4):
            for tb in range(4):
                src = dram_ap(payB_out[l], (rk * PB + R_V + tb * 128) * T + c0, [[T, 128], [64, 4], [1, 64]])
                add("sp", lambda e, rk=rk, tb=tb, src=src: e.dma_start(out=VAb[:, rk * 4 + tb, :, 0:64], in_=src), [b_payoB[l]], [b_VAk[rk * 4 + tb]], dma=True)
        for tb in range(2):
            add("pool", lambda e, tb=tb: e.dma_start(out=VAb[:, 16 + tb, :, 0:64], in_=cache_ap[:, tb * 128:(tb + 1) * 128, :].rearrange("h t d -> t h d")),
                (), [b_VAk[16 + tb]], dma=True)

    def load_KT(l, row0, nrows, p0, heads=True):
        for h in range(4):
            r = row0 + (h * nrows if heads else 0)
            src = dram_ap(payA_out[l], r * T, [[T, nrows], [PA * T, 4], [1, T]])
            add("sp", lambda e, h=h, src=src: e.dma_start(out=KTb[p0:p0 + nrows, h, 0:DEC_S].rearrange("p (r t) -> p r t", r=4), in_=src),
                [b_payoA[l]], [b_KTh[h]], dma=True)

    def lat_attention(l, side_gen=None, mod_gen=None):
        rdv = list(b_VAk)

        def side_step(n=1):
            if side_gen is not None:
                for _ in range(n):
                    next(side_gen, None)
            if mod_gen is not None:
                next(mod_gen, None)

        load_KT(l, R_DAK, 64, 0)
        for h in range(4):
            cache_T(c_dak[l, h], 64, lambda tb, h=h: KTb[0:64, h, DEC_S + tb * 128:DEC_S + (tb + 1) * 128], [b_KTh[h]])
        load_V(l, 0, c_dav[l])
        jobs = []
        for h in range(4):
            for qh in range(2):
                idx = h * 2 + qh
                tq, btq = tbt[idx // 4], b_tbt[idx // 4]
                qb = tq[:, :, :].rearrange("p a w -> p (a w)")[:, (idx % 4) * 512:(idx % 4 + 1) * 512]
                add("pool", lambda e, qb=qb: e.memset(qb, 0.0), (), [btq])
                add("pool", lambda e, qb=qb, h=h, qh=qh: e.tensor_copy(out=qb[0:32, 0:256], in_=QT2[1][0][0:32, h, qh * 256:(qh + 1) * 256]),
                    [b_QT[1][0]], [btq])
                add("pool", lambda e, qb=qb, h=h, qh=qh: e.tensor_copy(out=qb[32:64, 256:512], in_=QT2[1][0][32:64, h, qh * 256:(qh + 1) * 256]),
                    [b_QT[1][0]], [btq])
                J = dict(kt_fn=(lambda kb, h=h: KTb[0:128, h, kb * 128:(kb + 1) * 128]), q_ap=qb,
                         va_fn=(lambda kb, h=h: VAb[:, kb, h, 0:65]), scale=32 ** -0.5, reads=rdv + [b_KTh[h], btq])
                J["post"] = (lambda h=h, qh=qh, J=J: da_post(l, 1, h, J["acc"][0], J["acc"][1], J["acc"][0], J["acc"][1],
                                                             nq=256, q0=qh * 256, c1=0, c2=256))
                jobs.append(J)
        stream_attention(jobs, side_cb=lambda: side_step(1), burst=BURST)
        src = dram_ap(payA_out[l], R_CKV * T, [[T, 128], [PA * T, 4], [1, T]])
        add("sp", lambda e, src=src: e.dma_start(out=ckvT[:, 0:DEC_S].rearrange("p (r t) -> p r t", r=4), in_=src), [b_payoA[l]], [b_ckvT], dma=True)
        cache_T(c_ckv[l], 128, lambda tb: ckvT[:, DEC_S + tb * 128:DEC_S + (tb + 1) * 128], [b_ckvT])
        load_KT(l, R_KR, 32, 64, heads=False)
        for h in range(4):
            cache_T(c_kr[l], 32, lambda tb, h=h: KTb[64:96, h, DEC_S + tb * 128:DEC_S + (tb + 1) * 128], [b_KTh[h]])
        for h in range(4):
            for cb in range(5):
                n0 = cb * 512
                n = min(512, DEC_S + PAST - n0)
                pt, pb = gp()
                add("pe", lambda e, pt=pt, h=h, n0=n0, n=n: e.matmul(pt[0:64, 0:n], lhsT=wkv_s[:, h * 128:h * 128 + 64], rhs=ckvT[:, n0:n0 + n],
                                                                    start=True, stop=True), [b_wsm, b_ckvT], [pb])
                evac(KTb[0:64, h, n0:n0 + n], pt[0:64, 0:n], [pb], [b_KTh[h]])
        for kb in range(18):
            pt, pb = gp()
            add("pe", lambda e, pt=pt, kb=kb: e.matmul(pt[:, 0:256], lhsT=ckvT[:, kb * 128:(kb + 1) * 128],
                                                      rhs=wkv_s[:, :].rearrange("p (h c) -> p h c", h=4)[:, :, 64:128], start=True, stop=True),
                [b_wsm, b_ckvT], [pb])
            evac(VAb[:, kb, :, 0:64], pt[:, 0:256].rearrange("p (h d) -> p h d", h=4), [pb], [b_VAk[kb]])
        jobs = []
        for h in range(4):
            J = dict(kt_fn=(lambda kb, h=h: KTb[0:128, h, kb * 128:(kb + 1) * 128]), q_ap=q_ap(1, 1, h, 0, 128),
                     va_fn=(lambda kb, h=h: VAb[:, kb, h, 0:65]), scale=96 ** -0.5, reads=rdv + [b_KTh[h], b_QT[1][1]])
            J["post"] = (lambda h=h, J=J: plain_post(1, 1, h, J["acc"][0], J["acc"][1]))
            jobs.append(J)
        stream_attention(jobs, side_cb=lambda: side_step(2))
        load_KT(l, R_NAK, 64, 0)
        for h in range(4):
            cache_T(c_nak[l, h], 64, lambda tb, h=h: KTb[0:64, h, DEC_S + tb * 128:DEC_S + (tb + 1) * 128], [b_KTh[h]])
            add("pool", lambda e, h=h: e.dma_start(out=KTb[64:96, h, 0:DEC_S], in_=rowind_d), (), [b_KTh[h]], dma=True)
            add("pool", lambda e, h=h: e.memset(KTb[64:96, h, DEC_S:DEC_S + PAST], 0.0), (), [b_KTh[h]])
            evac(QT2[1][0][0:64, h, :], QT2[1][0][64:128, h, :], [b_QT[1][2], b_QT[1][0]], [b_QT[1][0], b_QT[1][2]])
            add("pool", lambda e, h=h: e.dma_start(out=QT2[1][0][64:96, h, :], in_=rowsel_d), (), [b_QT[1][0], b_QT[1][2]], dma=True)
        load_V(l, 256, c_nav[l])
        jobs = []
        for h in range(4):
            ti = h % 2

            def pre(h=h, ti=ti):
                for j in range(2):
                    src = bass.AP(tz_all, l * TZ_SZ + (h * (A2 + 1) + (1 - j) + 7) * 64 * TZL + 63, [[TZL - 1, 64], [64 * TZL, A2U], [1, 64]])
                    add("pool", lambda e, j=j, src=src, ti=ti: e.dma_start(out=tbt[ti][j * 64:(j + 1) * 64, :, :], in_=src), (), [b_tbt[ti]], dma=True)
                add("dve", lambda e, ti=ti: e.scalar_tensor_tensor(out=tbt[ti][:], in0=tbt[ti][:], scalar=8.0,
                                                                   in1=colm_s[:, :].unsqueeze(1).to_broadcast([128, A2U, 64]),
                                                                   op0=ALU.mult, op1=ALU.add), [b_tbt[ti], b_prm], [b_tbt[ti]])

            def extra(kb, ti=ti):
                if kb >= 16:
                    return []
                a0 = 30 - 2 * kb
                return [(identb[:, :], tbt[ti][:, a0:a0 + 8, :], [b_identb, b_tbt[ti]])]

            J = dict(kt_fn=(lambda kb, h=h: KTb[0:128, h, kb * 128:(kb + 1) * 128]), q_ap=QT2[1][0][0:128, h, :],
                     va_fn=(lambda kb, h=h: VAb[:, kb, h, 0:65]), scale=64 ** -0.5, reads=rdv + [b_KTh[h], b_QT[1][2]],
                     extra_fn=extra, pre=pre)
            J["post"] = (lambda h=h, J=J: plain_post(1, 2, h, J["acc"][0], J["acc"][1]))
            jobs.append(J)
        stream_attention(jobs, side_cb=lambda: side_step(2))

    def conv_module(l, g):
        if g == 1:
            for c in range(2):
                src = dram_ap(payB_out[l], R_HALO * T + c * 128 * 30, [[30, 128], [PB * T, 4], [1, 30]])
                add("sp", lambda e, src=src, c=c: e.dma_start(out=halo[:, c, :, :], in_=src), [b_payoB[l]], [b_halo], dma=True)
            for side in range(2):
                dst = gpad[1][:, :, 0, 0:15] if side == 0 else gpad[1][:, :, 1, 271:286]
                for rk in range(4):
                    srcv = halo[:, :, rk, 15:30] if side == 0 else halo[:, :, rk, 0:15]
                    sc = halsel_s[:, side * 4 + rk:side * 4 + rk + 1]
                    if rk == 0:
                        add("dve", lambda e, dst=dst, srcv=srcv, sc=sc: e.tensor_scalar(out=dst, in0=srcv, scalar1=sc, scalar2=None, op0=ALU.mult),
                            [b_halo, b_prm, b_gpad[1]], [b_gpad[1]])
                    else:
                        add("dve", lambda e, dst=dst, srcv=srcv, sc=sc: e.scalar_tensor_tensor(out=dst, in0=srcv, scalar=sc, in1=dst,
                                                                                              op0=ALU.mult, op1=ALU.add),
                            [b_halo, b_prm, b_gpad[1]], [b_gpad[1]])
        for c in range(2):
            for j in range(31):
                if g == 0:
                    src = gpad[0][:, c, :, j:j + 256]
                    dst = cacc[:, c, :].rearrange("p (s t) -> p s t", s=2)
                    ops = [(src, dst)]
                else:
                    ops = []
                    n_a = max(0, min(256, 271 - j))
                    if n_a > 0:
                        ops.append((gpad[1][:, c, 0, j:j + n_a], cacc[:, c, 0:n_a]))
                    if n_a < 256:
                        ops.append((gpad[1][:, c, 1, 15 + (n_a + j - 271):15 + (256 + j - 271)], cacc[:, c, n_a:256]))
                    n_b = max(0, min(256, 271 - (256 + j)))
                    if n_b > 0:
                        ops.append((gpad[1][:, c, 0, 256 + j:256 + j + n_b], cacc[:, c, 256:256 + n_b]))
                    ops.append((gpad[1][:, c, 1, 15 + (256 + n_b + j - 271):15 + (512 + j - 271)], cacc[:, c, 256 + n_b:512]))
                if j % 4 == 3:
                    yield
                for (src, dst) in ops:
                    if j == 0:
                        add("dve", lambda e, src=src, dst=dst, c=c: e.tensor_scalar(out=dst, in0=src, scalar1=dw_s[:, l, c, 0:1], scalar2=cb_s[:, l, c:c + 1],
                                                                                    op0=ALU.mult, op1=ALU.add), [b_gpad[g], b_prm, b_cacc], [b_cacc])
                    else:
                        add("dve", lambda e, src=src, dst=dst, c=c, j=j: e.scalar_tensor_tensor(out=dst, in0=src, scalar=dw_s[:, l, c, j:j + 1], in1=dst,
                                                                                                op0=ALU.mult, op1=ALU.add), [b_gpad[g], b_prm, b_cacc], [b_cacc])
        p1, pb1 = gp()
        p2, pb2 = gp()
        for c in range(2):
            tt, tb_ = tmp()
            add("act", lambda e, tt=tt, c=c: e.activation(out=tt[:], in_=cacc[:, c, :], func=AF.Square), [b_cacc], [tb_])
            add("pe", lambda e, c=c: e.matmul(p1[:, :], lhsT=ones_f[:, :], rhs=cacc[:, c, :], start=(c == 0), stop=(c == 1)), [b_cacc, b_ones_f], [pb1])
            add("pe", lambda e, tt=tt, c=c: e.matmul(p2[:, :], lhsT=ones_f[:, :], rhs=tt[:], start=(c == 0), stop=(c == 1)), [tb_, b_ones_f], [pb2])
        mean, bmean = tmp()
        var, bvar = tmp()
        add("act", lambda e: e.activation(out=mean[:], in_=p1[:, :], func=AF.Identity, scale=1.0 / 256), [pb1], [bmean])
        add("dve", lambda e: e.tensor_tensor(out=var[:], in0=mean[:], in1=mean[:], op=ALU.mult), [bmean], [bvar])
        add("dve", lambda e: e.scalar_tensor_tensor(out=var[:], in0=p2[:, :], scalar=1.0 / 256, in1=var[:], op0=ALU.mult, op1=ALU.subtract),
            [pb2, bvar], [bvar])
        add("act", lambda e: e.activation(out=var[:], in_=var[:], func=AF.Ln, bias=epsc[:, 0:1], scale=1.0), [bvar, b_epsc], [bvar])
        add("act", lambda e: e.activation(out=var[:], in_=var[:], func=AF.Exp, scale=-0.5), [bvar], [bvar])
        for c in range(2):
            add("dve", lambda e, c=c: e.tensor_tensor(out=cacc[:, c, :], in0=cacc[:, c, :], in1=mean[:], op=ALU.subtract), [b_cacc, bmean], [b_cacc])
            add("dve", lambda e, c=c: e.tensor_tensor(out=cacc[:, c, :], in0=cacc[:, c, :], in1=var[:], op=ALU.mult), [b_cacc, bvar], [b_cacc])
            add("act", lambda e, c=c: e.activation(out=cacc[:, c, :], in_=cacc[:, c, :], func=AF.Identity, bias=lnb_s[:, l, c:c + 1],
                                                   scale=lng_s[:, l, c:c + 1]), [b_cacc, b_prm], [b_cacc])
            tt, tb_ = tmp()
            add("act", lambda e, tt=tt, c=c: e.activation(out=tt[:], in_=cacc[:, c, :], func=AF.Sigmoid), [b_cacc], [tb_])
            add("dve", lambda e, tt=tt, c=c: e.tensor_tensor(out=mixT[g][:, 6 + c, :], in0=cacc[:, c, :], in1=tt[:], op=ALU.mult),
                [b_cacc, tb_], [b_mix[g][6 + c]])

    def out_proj(l, groups):
        for ti in range(2):
            wt, wb = wtile(w_out[l, :, ti * 512:(ti + 1) * 512], 512)
            for cc in range(4):
                oc = ti * 4 + cc
                for g in groups:
                    pt, pb = gp()
                    for k in range(8):
                        add("pe", lambda e, pt=pt, k=k, cc=cc, wt=wt, g=g: e.matmul(pt[:, :], lhsT=wt[:, k, cc * 128:(cc + 1) * 128], rhs=mixT[g][:, k, :],
                                                                                   start=(k == 0), stop=(k == 7)), [wb, b_mix[g][k]], [pb])
                    add("dve", lambda e, pt=pt, g=g, oc=oc: e.scalar_tensor_tensor(out=xT[g][:, oc, :], in0=pt[:, :], scalar=mcol(l, "g1", oc, g),
                                                                                 in1=xT[g][:, oc, :], op0=ALU.mult, op1=ALU.add),
                        [pb, b_mod[l], b_xT[g][oc]], [b_xT[g][oc]])

    def ffn(l, groups):
        for blk in range(4):
            for ti in range(2):
                wt, wb = wtile(w_ff1[l, :, blk * 1024 + ti * 512: blk * 1024 + (ti + 1) * 512], 512)
                for cc in range(4):
                    fc = ti * 4 + cc
                    for g in groups:
                        pt, pb = gp()
                        for k in range(8):
                            add("pe", lambda e, pt=pt, k=k, cc=cc, wt=wt, g=g: e.matmul(pt[:, :], lhsT=wt[:, k, cc * 128:(cc + 1) * 128], rhs=hT[g][:, k, :],
                                                                                       start=(k == 0), stop=(k == 7)), [wb, b_hT[g]], [pb])
                        sq, bq = sqt()
                        add("act", lambda e, pt=pt, sq=sq: e.activation(out=sq[:], in_=pt[:, :], func=AF.Relu), [pb], [bq])
                        add("dve", lambda e, sq=sq, g=g, fc=fc: e.tensor_tensor(out=mixT[g][:, fc, :], in0=sq[:], in1=sq[:], op=ALU.mult),
                            [bq], [b_mix[g][fc]])
            for ti in range(2):
                wt, wb = wtile(w_ff2[l, blk * 1024:(blk + 1) * 1024, ti * 512:(ti + 1) * 512], 512)
                for cc in range(4):
                    oc = ti * 4 + cc
                    for g in groups:
                        pt, pb = gp()
                        for k in range(8):
                            add("pe", lambda e, pt=pt, k=k, cc=cc, wt=wt, g=g: e.matmul(pt[:, :], lhsT=wt[:, k, cc * 128:(cc + 1) * 128], rhs=mixT[g][:, k, :],
                                                                                       start=(k == 0), stop=(k == 7)), [wb, b_mix[g][k]], [pb])
                        add("dve", lambda e, pt=pt, g=g, oc=oc: e.scalar_tensor_tensor(out=xT[g][:, oc, :], in0=pt[:, :], scalar=mcol(l, "g2", oc, g),
                                                                                     in1=xT[g][:, oc, :], op0=ALU.mult, op1=ALU.add),
                            [pb, b_mod[l], b_xT[g][oc]], [b_xT[g][oc]])

    def dump8(name, t, bufs):
        if name not in dbg:
            return
        o = dout("dbg_" + name, [128, 8, T])
        for j in range(8):
            tt, tb_ = tmp()
            add("dve", lambda e, tt=tt, j=j: e.tensor_copy(out=tt[:], in_=t[:, j, :]), [bufs[j]], [tb_])
            add("sp", lambda e, tt=tt, j=j: e.dma_start(out=o[:, j, :], in_=tt[:]), [tb_], (), dma=True)
        dbg_out[name] = [128, 8, T]

    mod0 = modulation(0) if "mod" in stages else None
    for g in range(2):
        for tb in range(4):
            for half in range(2):
                if mod0 is not None:
                    next(mod0, None)
                stg, bstg = tmp()
                add("sp", lambda e, g=g, tb=tb, half=half, stg=stg: e.dma_start(
                    out=stg[:, :], in_=xin[g * T + tb * 128: g * T + (tb + 1) * 128, half * 512:(half + 1) * 512]), (), [bstg], dma=True)
                pt, pb = gp()
                for jj in range(4):
                    add("pe", lambda e, pt=pt, stg=stg, jj=jj: e.transpose(out=pt[:, jj * 128:(jj + 1) * 128],
                                                                           in_=stg[:, jj * 128:(jj + 1) * 128], identity=ident[:]),
                        [bstg, b_ident], [pb])
                evac(xT[g][:, half * 4:half * 4 + 4, tb * 128:(tb + 1) * 128],
                     pt[:, :].rearrange("p (j t) -> p j t", j=4), [pb], [b_xT[g][half * 4 + jj] for jj in range(4)])

    if mod0 is not None:
        for _ in mod0:
            pass

    groups = [0, 1] if do_lat else [0]
    import os as _os
    _l1 = _os.environ.get("L1STAGES")
    _stages0 = stages
    for l in range(nlayers):
        stages = _stages0 if (l == 0 or _l1 is None) else set(_l1.split(","))
        if "norm1" in stages:
            for g in groups:
                rmsnorm_mod(l, g, 0)
        if "proj" in stages:
            input_proj(l)
        side = modulation(l + 1) if ("mod" in stages and l + 1 < nlayers) else None
        if "ctxattn" in stages:
            ctx_attention(l, None if (do_lat and "latattn" in stages) else side)
        if "conv" in stages:
            for _ in conv_module(l, 0):
                pass
        dump8("mix0%d" % l, mixT[0], b_mix[0])
        if do_lat:
            side2 = conv_module(l, 1) if "conv" in stages else None
            if "latattn" in stages:
                lat_attention(l, side2, side)
            if side2 is not None:
                for _ in side2:
                    pass
        if side is not None:
            for _ in side:
                pass
            dump8("mix1%d" % l, mixT[1], b_mix[1])
        if "outproj" in stages:
            out_proj(l, groups)
        for g in groups:
            dump8("xattn%d%d" % (g, l), xT[g], b_xT[g])
        if "norm2" in stages:
            for g in groups:
                rmsnorm_mod(l, g, 1)
        if "ffn" in stages:
            ffn(l, groups)
        for g in groups:
            dump8("xffn%d%d" % (g, l), xT[g], b_xT[g])
    for g in groups:
        if "final" not in stages:
            continue
        pt, pb = gp()
        for j in range(8):
            sq, bq = sqt()
            add("act", lambda e, sq=sq, j=j, g=g: e.activation(out=sq[:], in_=xT[g][:, j, :], func=AF.Square), [b_xT[g][j]], [bq])
            add("pe", lambda e, sq=sq, j=j, pt=pt: e.matmul(pt[:, :], lhsT=ones_b[:, :], rhs=sq[:], start=(j == 0), stop=(j == 7)), [bq, b_ones_b], [pb])
        add("act", lambda e, pt=pt: e.activation(out=rstd[:], in_=pt[:, :], func=AF.Ln, bias=epsc[:, 0:1], scale=1.0 / D), [pb, b_epsc], [b_rstd])
        add("act", lambda e: e.activation(out=rstd[:], in_=rstd[:], func=AF.Exp, scale=-0.5), [b_rstd], [b_rstd])
        for j in range(8):
            add("dve", lambda e, j=j, g=g: e.scalar_tensor_tensor(out=xT[g][:, j, :], in0=xT[g][:, j, :], scalar=gfin_s[:, j:j + 1], in1=rstd[:],
                                                                  op0=ALU.mult, op1=ALU.mult), [b_xT[g][j], b_rstd, b_prm], [b_xT[g][j]])
    for g in groups:
        for tb in range(4):
            for half in range(2):
                pt, pb = gp()
                for jj in range(4):
                    j = half * 4 + jj
                    add("pe", lambda e, pt=pt, j=j, jj=jj, tb=tb, g=g: e.transpose(out=pt[:, jj * 128:(jj + 1) * 128],
                                                                                   in_=xT[g][:, j, tb * 128:(tb + 1) * 128], identity=ident[:]),
                        [b_xT[g][j], b_ident], [pb])
                stg, bstg = tmp()
                evac(stg[:, :], pt[:, :], [pb], [bstg])
                add("sp", lambda e, g=g, tb=tb, half=half, stg=stg: e.dma_start(
                    out=y_out[g * T + tb * 128:g * T + (tb + 1) * 128, half * 512:(half + 1) * 512], in_=stg[:, :]), [bstg], (), dma=True)
    S.emit(st)
    st.close()
    return nc, dbg_out


def _colT(v, nchunk):
    return np.ascontiguousarray(v.reshape(nchunk, 128).T)


def prepare_inputs(inp):
    f = lambda a: np.ascontiguousarray(np.asarray(a, dtype=np.float32))
    x_prompt, x_sample, c = f(inp["x_prompt"]), f(inp["x_sample"]), f(inp["c"])
    w_in = f(inp["w_in"])
    shared = {}
    shared["bmodT"] = np.ascontiguousarray(np.stack([_colT(f(inp["b_mod"])[l], 48) for l in range(L)], 1))
    shared["gmixT"] = np.ascontiguousarray(np.stack([_colT(f(inp["g_norm_mix"])[l], 8) for l in range(L)], 1))
    shared["gffT"] = np.ascontiguousarray(np.stack([_colT(f(inp["g_norm_ff"])[l], 8) for l in range(L)], 1))
    shared["gfinT"] = _colT(f(inp["g_final"]), 8)
    shared["w_mod"] = f(inp["w_mod"])
    shared["w_in"] = w_in
    idx = []
    for base in (0, 256):
        for h in range(4):
            for m in range(2):
                idx += [base + h * 64 + m * 32 + P32[j] for j in range(32)]
    idx += list(range(1088, 1152))
    idx += [1152 + P32[j] for j in range(32)]
    shared["w_inp"] = np.ascontiguousarray(w_in[:, :, idx])
    shared["w_out"] = f(inp["w_out"])
    shared["w_ff1"] = f(inp["w_ff1"])
    shared["w_ff2"] = f(inp["w_ff2"])
    wq = f(inp["w_mla_qup"])
    shared["wqup"] = wq
    wqp = np.zeros_like(wq)
    for h in range(4):
        for j in range(32):
            wqp[:, :, h * 96 + 64 + j] = wq[:, :, h * 96 + 64 + P32[j]]
    shared["wqupp"] = wqp
    shared["wkvup"] = f(inp["w_mla_kvup"])
    shared["gqT"] = np.ascontiguousarray(np.stack([_colT(f(inp["g_mla_q"])[l], 2) for l in range(L)], 1))
    shared["gkvc"] = np.ascontiguousarray(f(inp["g_mla_kv"]).T)
    shared["gkvr"] = np.ascontiguousarray(np.broadcast_to(f(inp["g_mla_kv"])[None], (128, L, 128)))
    gs = f(inp["g_da_subln"])
    shared["gsubc"] = np.ascontiguousarray(np.concatenate([gs.T, gs.T], 0))
    lamv = np.stack([f(inp["da_lambda_q1"]), f(inp["da_lambda_k1"]), f(inp["da_lambda_q2"]), f(inp["da_lambda_k2"])], 1)
    shared["lamv"] = np.ascontiguousarray(np.broadcast_to(lamv[None], (128, L, 4, 32)))
    dw = f(inp["conv_dw"])
    shared["dwT"] = np.ascontiguousarray(dw.reshape(L, 31, 2, 128).transpose(3, 0, 2, 1))
    for nm, key in (("cbT", "conv_b"), ("lngT", "conv_ln_g"), ("lnbT", "conv_ln_b")):
        shared[nm] = np.ascontiguousarray(f(inp[key]).reshape(L, 2, 128).transpose(2, 0, 1))
    shared["ident"] = np.eye(128, dtype=np.float32)
    w = np.arange(GRID_W)
    cs = np.clip(w - 8, 0, GRID_W - 16)
    col_ok = (w[None, :] >= cs[:, None]) & (w[None, :] < cs[:, None] + 16)
    cm = np.where(col_ok.T, 0.0, -BIG * 8).astype(np.float32)
    shared["colmask"] = np.ascontiguousarray(np.concatenate([cm, cm], 0))
    ri = np.zeros((32, DEC_S), np.float32)
    ri[np.arange(DEC_S) // 64, np.arange(DEC_S)] = 1.0
    shared["rowind"] = ri
    rpb = f(inp["na_rpb"])
    half = 8
    freqs = (10000.0 ** (-np.arange(half, dtype=np.float32) * 2.0 / 16)).astype(np.float32)
    maps = []
    for core in range(NCORE):
        b, r = core // 4, core % 4
        m = dict(shared)
        m["xin"] = np.ascontiguousarray(np.concatenate([x_prompt[2 * core], x_prompt[2 * core + 1], x_sample[b, r * T:(r + 1) * T]], 0))
        cv = np.stack([f(inp["c_ctx"]), c[b]], 1)
        m["cvT"] = np.ascontiguousarray(cv.reshape(8, 128, 2).transpose(1, 0, 2))
        m["c_dak"] = f(inp["cache_da_k"])[b]
        m["c_dav"] = f(inp["cache_da_v"])[b]
        m["c_ckv"] = f(inp["cache_mla_ckv"])[b]
        m["c_kr"] = f(inp["cache_mla_krope"])[b]
        m["c_nak"] = f(inp["cache_na_k"])[b]
        m["c_nav"] = f(inp["cache_na_v"])[b]
        t = r * T + np.arange(T)
        rows = (t // GRID_W).astype(np.float32)
        cols = (t % GRID_W).astype(np.float32)
        ang = np.concatenate([rows[None, :] * freqs[:, None], rows[None, :] * freqs[:, None],
                              cols[None, :] * freqs[:, None], cols[None, :] * freqs[:, None]], 0)
        cos32 = np.cos(ang).astype(np.float32)
        sin32 = np.sin(ang).astype(np.float32)
        sgn = np.concatenate([-np.ones(8), np.ones(8), -np.ones(8), np.ones(8)]).astype(np.float32)[:, None]
        m["cosT"] = np.ascontiguousarray(np.tile(cos32, (4, 1)))
        m["sinT"] = np.ascontiguousarray(np.tile(sin32 * sgn, (4, 1)))
        r0 = r * 8
        qrow = r0 + np.arange(T) // 64
        start = np.clip(qrow - 4, 0, NROWS - 8)
        rk = np.arange(32)
        ok = (rk[:, None] >= start[None, :]) & (rk[:, None] < start[None, :] + 8)
        m["rowsel"] = np.where(ok, 0.0, -BIG * 8).astype(np.float32)
        P = np.zeros((L, 4, A2 + 1, TZL), np.float32)
        for a2 in range(A2):
            a = 45 - a2 - r0
            if 0 <= a <= 14:
                P[:, :, a2, 48:79] = rpb[:, :, a, ::-1]
        m["tz_rep"] = np.ascontiguousarray(np.broadcast_to(P.reshape(L * 4 * (A2 + 1), 1, TZL), (L * 4 * (A2 + 1), 64, TZL)))
        hs = np.zeros((128, 8), np.float32)
        if r > 0:
            hs[:, r - 1] = 1.0
        if r < 3:
            hs[:, 4 + r + 1] = 1.0
        m["halsel"] = hs
        maps.append(m)
    return maps


_NC_CACHE = {}


def kernel(**inputs):
    maps = prepare_inputs(inputs)
    if "nc" not in _NC_CACHE:
        _NC_CACHE["nc"] = build()[0]
    nc = _NC_CACHE["nc"]
    res = run_bass_kernel_spmd(nc, maps, core_ids=list(range(NCORE)))
    R = res.results
    y_prompt = np.zeros((NB, SEQ, D), np.float32)
    y_sample = np.zeros((DEC_B, DEC_S, D), np.float32)
    outs = {k: [] for k in ("o_dak", "o_dav", "o_ckv", "o_kr", "o_nak", "o_nav")}
    for core in range(NCORE):
        b, r = core // 4, core % 4
        y = R[core]["y_out"]
        y_prompt[2 * core] = y[0:256]
        y_prompt[2 * core + 1] = y[256:512]
        y_sample[b, r * T:(r + 1) * T] = y[512:1024]
        for k in outs:
            outs[k].append(R[core][k])
    cat = lambda k: np.ascontiguousarray(np.concatenate(outs[k], 0).astype(np.float32))
    return (y_prompt, y_sample, cat("o_dak"), cat("o_dav"), cat("o_ckv"), cat("o_kr"), cat("o_nak"), cat("o_nav"))
```

```python
import contextlib
import itertools
import math
import numpy as np
import concourse.bass as bass
import concourse.mybir as mybir
from concourse.bass_utils import run_bass_kernel_spmd

F32 = mybir.dt.float32
BF16 = mybir.dt.bfloat16
ALU = mybir.AluOpType
AF = mybir.ActivationFunctionType

ENGINES = ("pe", "act", "dve", "pool", "sp")

D = 1024
L = 2
NB = 16
SEQ = 256
DEC_B = 2
DEC_S = 2048
PAST = 256
GRID_W = 64
NROWS = DEC_S // GRID_W
EPS = 1e-6
T = 512
NCORE = 8
BIG = 30000.0
A2 = 46
TZL = 127
PA = 672
PB = 527
R_DAK, R_CKV, R_KR, R_NAK = 0, 256, 384, 416
R_V, R_HALO = 0, 512
P32 = [8, 9, 10, 11, 12, 13, 14, 15, 0, 1, 2, 3, 4, 5, 6, 7,
       24, 25, 26, 27, 28, 29, 30, 31, 16, 17, 18, 19, 20, 21, 22, 23]


class Buf:
    __slots__ = ("name", "last_w", "readers", "excl")

    def __init__(self, name, excl=False):
        self.name = name
        self.last_w = None
        self.readers = []
        self.excl = excl


class Op:
    __slots__ = ("eng", "fn", "deps", "dma", "sig", "sem", "val")

    def __init__(self, eng, fn, dma):
        self.eng = eng
        self.fn = fn
        self.dma = dma
        self.deps = []
        self.sig = False
        self.sem = None
        self.val = 0


class Sched:
    def __init__(self, nc, n_dma_slots=16):
        self.nc = nc
        self.ops = []
        self.n_dma_slots = n_dma_slots

    def add(self, eng, fn, reads=(), writes=(), dma=False):
        op = Op(eng, fn, dma)
        ex = [b for b in reads if b.excl]
        if ex:
            reads = [b for b in reads if not b.excl]
            writes = list(writes) + ex
        deps = {}
        for b in reads:
            if b.last_w is not None:
                deps[id(b.last_w)] = b.last_w
        for b in writes:
            if b.last_w is not None:
                deps[id(b.last_w)] = b.last_w
            for r in b.readers:
                deps[id(r)] = r
        for b in reads:
            b.readers.append(op)
        for b in writes:
            b.last_w = op
            b.readers = []
        for d in deps.values():
            if d.eng == "pe" and eng == "pe" and not d.dma and not dma:
                continue
            op.deps.append(d)
            d.sig = True
        self.ops.append(op)
        return op

    def emit(self, stack):
        nc = self.nc
        eng_sem = {e: stack.enter_context(nc.semaphore("s_" + e)) for e in ENGINES}
        dma_engs = sorted({op.eng for op in self.ops if op.dma is True})
        dma_slots = {e: [stack.enter_context(nc.semaphore("d_%s_%d" % (e, i)))
                         for i in range(self.n_dma_slots)] for e in dma_engs}
        cnt = {e: 0 for e in ENGINES}
        dcnt = {e: 0 for e in dma_engs}
        slot_uses = {e: [0] * self.n_dma_slots for e in dma_engs}
        slot_prev = {}
        ncc = 0
        for op in self.ops:
            if op.dma == "cc":
                op.sem = stack.enter_context(nc.semaphore("cc_%d" % ncc))
                ncc += 1
                op.val = 1
            elif op.dma:
                k = dcnt[op.eng] % self.n_dma_slots
                dcnt[op.eng] += 1
                slot_uses[op.eng][k] += 1
                op.sem = dma_slots[op.eng][k]
                op.val = 16 * slot_uses[op.eng][k]
                slot_prev[id(op)] = (op.sem, op.val - 16)
            elif op.sig:
                cnt[op.eng] += 1
                op.sem = eng_sem[op.eng]
                op.val = cnt[op.eng]
        block = stack.enter_context(nc.Block())
        handles = {"pe": block.tensor, "act": block.scalar, "dve": block.vector,
                   "pool": block.gpsimd, "sp": block.sync}
        all_async = [op for op in self.ops if op.dma]

        def run_engine(ename):
            def body(eng):
                waited = {}
                for op in self.ops:
                    if op.eng != ename:
                        continue
                    need = {}
                    for d in op.deps:
                        key = id(d.sem)
                        if key not in need or need[key][1] < d.val:
                            need[key] = (d.sem, d.val)
                    if op.dma is True:
                        s, v = slot_prev[id(op)]
                        if v > 0:
                            key = id(s)
                            if key not in need or need[key][1] < v:
                                need[key] = (s, v)
                    for key, (s, v) in need.items():
                        if waited.get(key, 0) >= v:
                            continue
                        eng.wait_ge(s, v)
                        waited[key] = v
                    ins = op.fn(eng)
                    if op.dma == "cc":
                        ins.then_inc(op.sem)
                    elif op.dma:
                        ins.then_inc(op.sem, 16)
                    elif op.sig:
                        ins.then_inc(op.sem, 1)
                if ename == "sp":
                    last = {}
                    for op in all_async:
                        last[id(op.sem)] = (op.sem, op.val)
                    for key, (s, v) in last.items():
                        if waited.get(key, 0) < v:
                            eng.wait_ge(s, v)
                    for e in ENGINES:
                        if cnt[e] > 0:
                            eng.wait_ge(eng_sem[e], cnt[e])
            handles[ename](body)

        for e in ENGINES:
            run_engine(e)


def build(dbg=(), nlayers=L, do_lat=True, stages=None):
    if stages is None:
        stages = {"mod", "norm1", "proj", "ctxattn", "conv", "latattn", "outproj", "norm2", "ffn", "final"}
    nc = bass.Bass("TRN2", target_bir_lowering=False)
    st = contextlib.ExitStack()
    S = Sched(nc)
    dbg_out = {}

    def din(name, shape, dt=F32):
        return nc.dram_tensor(name, list(shape), dt, kind="ExternalInput").ap()

    def dout(name, shape, dt=F32):
        return nc.dram_tensor(name, list(shape), dt, kind="ExternalOutput").ap()

    xin = din("xin", [2 * T, D])
    cvT = din("cvT", [128, 8, 2])
    bmodT = din("bmodT", [128, L, 48])
    gmixT = din("gmixT", [128, L, 8])
    gffT = din("gffT", [128, L, 8])
    gfinT = din("gfinT", [128, 8])
    w_mod = din("w_mod", [L, D, 6 * D])
    w_in = din("w_in", [L, D, 2464])
    w_inp = din("w_inp", [L, D, 608])
    w_out = din("w_out", [L, D, D])
    w_ff1 = din("w_ff1", [L, D, 4 * D])
    w_ff2 = din("w_ff2", [L, 4 * D, D])
    wqup = din("wqup", [L, 256, 384])
    wqupp = din("wqupp", [L, 256, 384])
    wkvup = din("wkvup", [L, 128, 512])
    gqT = din("gqT", [128, L, 2])
    gkvc = din("gkvc", [128, L])
    gkvr = din("gkvr", [128, L, 128])
    gsubc = din("gsubc", [128, L])
    lamv = din("lamv", [128, L, 4, 32])
    tz_all = din("tz_rep", [L * 4 * (A2 + 1), 64, TZL]).tensor
    dwT = din("dwT", [128, L, 2, 31])
    cbT = din("cbT", [128, L, 2])
    lngT = din("lngT", [128, L, 2])
    lnbT = din("lnbT", [128, L, 2])
    c_dak = din("c_dak", [L, 4, PAST, 64])
    c_dav = din("c_dav", [L, 4, PAST, 64])
    c_ckv = din("c_ckv", [L, PAST, 128])
    c_kr = din("c_kr", [L, PAST, 32])
    c_nak = din("c_nak", [L, 4, PAST, 64])
    c_nav = din("c_nav", [L, 4, PAST, 64])
    cosT_d = din("cosT", [128, T])
    sinT_d = din("sinT", [128, T])
    colmask_d = din("colmask", [128, 64])
    rowsel_d = din("rowsel", [32, T])
    rowind_d = din("rowind", [32, DEC_S])
    halsel_d = din("halsel", [128, 8])
    ident_d = din("ident", [128, 128])
    y_out = dout("y_out", [2 * T, D])
    o_dak = dout("o_dak", [2, L, 4, SEQ, 64])
    o_dav = dout("o_dav", [2, L, 4, SEQ, 64])
    o_ckv = dout("o_ckv", [2, L, SEQ, 128])
    o_kr = dout("o_kr", [2, L, SEQ, 32])
    o_nak = dout("o_nak", [2, L, 4, SEQ, 64])
    o_nav = dout("o_nav", [2, L, 4, SEQ, 64])
    LROWS = 6 * PA + 6 * PB
    pay_all = nc.dram_tensor("pay_all", [L * LROWS, T], BF16)
    O_AIN, O_AOUT, O_BIN, O_BOUT = 0, PA, 5 * PA, 5 * PA + PB

    class _Sub:
        def __init__(self, row0, nrows):
            self.row0, self.nrows = row0, nrows

        def ap(self):
            return pay_all.ap()[self.row0:self.row0 + self.nrows, :]

    payA_in = [_Sub(l * LROWS + O_AIN, PA) for l in range(L)]
    payA_out = [_Sub(l * LROWS + O_AOUT, 4 * PA) for l in range(L)]
    payB_in = [_Sub(l * LROWS + O_BIN, PB) for l in range(L)]
    payB_out = [_Sub(l * LROWS + O_BOUT, 4 * PB) for l in range(L)]
    TZ_SZ = 4 * (A2 + 1) * 64 * TZL

    def dram_ap(base, off, dims):
        if isinstance(base, _Sub):
            return bass.AP(pay_all, base.row0 * T + off, dims)
        return bass.AP(base, off, dims)

    def sb(name, shape, dt):
        return st.enter_context(nc.sbuf_tensor("sb_" + name, list(shape), dt))

    add = S.add

    banks = []
    for i in range(8):
        banks.append((st.enter_context(nc.psum_tensor("ps%d" % i, [128, 512], F32)), Buf("ps%d" % i, excl=True)))
    gp_i = [0]
    ac_i = [0]

    def gp():
        b = banks[(0, 1, 2, 7)[gp_i[0] % 4]]
        gp_i[0] += 1
        return b

    def acb():
        b = banks[4 + ac_i[0] % 3]
        ac_i[0] += 1
        return b

    ev_i = [0]

    def evac(dst, src, r, w, scale=None):
        ev_i[0] += 1
        if ev_i[0] % 2 == 0:
            if scale is None:
                add("act", lambda e: e.activation(out=dst, in_=src, func=AF.Copy), r, w)
            else:
                add("act", lambda e: e.activation(out=dst, in_=src, func=AF.Identity, scale=scale), r, w)
        else:
            if scale is None:
                add("dve", lambda e: e.tensor_copy(out=dst, in_=src), r, w)
            else:
                add("dve", lambda e: e.tensor_scalar(out=dst, in0=src, scalar1=scale, scalar2=None, op0=ALU.mult), r, w)

    ident = sb("ident", [128, 128], F32); b_ident = Buf("ident")
    identb = sb("identb", [128, 128], BF16); b_identb = Buf("identb")
    ones_f = sb("ones_f", [128, 128], F32); b_ones_f = Buf("ones_f")
    ones_b = sb("ones_b", [128, 128], BF16); b_ones_b = Buf("ones_b")
    epsc = sb("epsc", [128, 1], F32); b_epsc = Buf("epsc")
    add("sp", lambda e: e.dma_start(out=ident[:], in_=ident_d), (), [b_ident], dma=True)
    add("pool", lambda e: e.memset(ones_f[:], 1.0), (), [b_ones_f])
    add("pool", lambda e: e.memset(ones_b[:], 1.0), (), [b_ones_b])
    add("pool", lambda e: e.memset(epsc[:], EPS), (), [b_epsc])
    add("dve", lambda e: e.tensor_copy(out=identb[:], in_=ident[:]), [b_ident], [b_identb])

    prm = {}
    b_prm = Buf("prm")
    b_small = []

    def load_small(name, src, shape):
        t = sb(name, shape, F32)
        bt = Buf("ld_" + name)
        b_small.append(bt)
        add("sp", lambda e: e.dma_start(out=t[:], in_=src), (), [bt], dma=True)
        prm[name] = t
        return t

    cv_s = load_small("cv_s", cvT, [128, 8, 2])
    bmod_s = load_small("bmod_s", bmodT, [128, L, 48])
    gmix_s = load_small("gmix_s", gmixT, [128, L, 8])
    gff_s = load_small("gff_s", gffT, [128, L, 8])
    gfin_s = load_small("gfin_s", gfinT, [128, 8])
    gq_s = load_small("gq_s", gqT, [128, L, 2])
    gkvc_s = load_small("gkvc_s", gkvc, [128, L])
    gkvr_s = sb("gkvr_s", [128, L, 128], BF16)
    b_small.append(Buf("ld_gkvr"))
    add("pool", lambda e: e.dma_start(out=gkvr_s[:], in_=gkvr), (), [b_small[-1]], dma=True)
    gsub_s = load_small("gsub_s", gsubc, [128, L])
    lam_s = load_small("lam_s", lamv, [128, L, 4, 32])
    dw_s = load_small("dw_s", dwT, [128, L, 2, 31])
    cb_s = load_small("cb_s", cbT, [128, L, 2])
    lng_s = load_small("lng_s", lngT, [128, L, 2])
    lnb_s = load_small("lnb_s", lnbT, [128, L, 2])
    cos_s = load_small("cos_s", cosT_d, [128, T])
    sin_s = load_small("sin_s", sinT_d, [128, T])
    colm_s = load_small("colm_s", colmask_d, [128, 64])
    halsel_s = load_small("halsel_s", halsel_d, [128, 8])
    joinc = sb("joinc", [128, 1], F32)
    add("dve", lambda e: e.memset(joinc[:], 0.0), list(b_small), [b_prm])

    lams = sb("lams", [128, L, 2], F32)
    neglam = sb("neglam", [128, L], F32)
    gsub2 = sb("gsub2", [128, L], F32)
    for l in range(L):
        lam_init = 0.8 - 0.6 * math.exp(-0.3 * l)
        for m in range(2):
            add("dve", lambda e, l=l, m=m: e.tensor_tensor(out=lam_s[:, l, 2 * m, :], in0=lam_s[:, l, 2 * m, :],
                                                          in1=lam_s[:, l, 2 * m + 1, :], op=ALU.mult), [b_prm], [b_prm])
            add("dve", lambda e, l=l, m=m: e.reduce_sum(out=lams[:, l, m:m + 1], in_=lam_s[:, l, 2 * m, :],
                                                       axis=mybir.AxisListType.X), [b_prm], [b_prm])
        add("act", lambda e, l=l: e.activation(out=lams[:, l, :], in_=lams[:, l, :], func=AF.Exp), [b_prm], [b_prm])
        add("dve", lambda e, l=l: e.tensor_tensor(out=neglam[:, l:l + 1], in0=lams[:, l, 1:2], in1=lams[:, l, 0:1],
                                                 op=ALU.subtract), [b_prm], [b_prm])
        add("dve", lambda e, l=l, li=lam_init: e.tensor_scalar(out=neglam[:, l:l + 1], in0=neglam[:, l:l + 1],
                                                              scalar1=-li, scalar2=None, op0=ALU.add), [b_prm], [b_prm])
        add("dve", lambda e, l=l, li=lam_init: e.tensor_scalar(out=gsub2[:, l:l + 1], in0=gsub_s[:, l:l + 1],
                                                              scalar1=1.0 - li, scalar2=None, op0=ALU.mult), [b_prm], [b_prm])

    xT = [sb("xT%d" % g, [128, 8, T], F32) for g in range(2)]
    b_xT = [[Buf("xT%d_%d" % (g, j)) for j in range(8)] for g in range(2)]
    hT = [sb("hT%d" % g, [128, 8, T], BF16) for g in range(2)]
    b_hT = [Buf("hT%d" % g) for g in range(2)]
    mixT = [sb("mixT%d" % g, [128, 8, T], BF16) for g in range(2)]
    b_mix = [[Buf("mix%d_%d" % (g, j)) for j in range(8)] for g in range(2)]
    NW = 3
    wpool = [sb("wp%d" % i, [128, 8, 512], BF16) for i in range(NW)]
    b_wp = [Buf("wp%d" % i) for i in range(NW)]
    wp_i = [0]

    def wtile(src_ap, ncols, nk=8):
        i = wp_i[0] % NW
        wp_i[0] += 1
        t, b = wpool[i], b_wp[i]
        add("pool", lambda e: e.dma_start(out=t[:, 0:nk, 0:ncols], in_=src_ap.rearrange("(k p) c -> p k c", p=128)),
            (), [b], dma=True)
        return t, b

    qd = sb("qd", [128, 2, T], F32); b_qd = Buf("qd_cacc_stage")
    stage1 = qd[:, :, :].rearrange("p a b -> p (a b)")

    class _Stage:
        def __getitem__(self, key):
            p, sl, c = key
            return stage1[p, c]
    stage = _Stage()
    b_stage = [b_qd, b_qd]
    rstd = sb("rstd", [128, T], F32); b_rstd = Buf("rstd")
    tmpf = [sb("tmpf%d" % i, [128, T], F32) for i in range(4)]
    b_tmpf = [Buf("tmpf%d" % i) for i in range(4)]
    tf_i = [0]

    def tmp():
        i = tf_i[0] % 4
        tf_i[0] += 1
        return tmpf[i], b_tmpf[i]

    sqb = [sb("sqb%d" % i, [128, T], BF16) for i in range(2)]
    b_sqb = [Buf("sqb%d" % i) for i in range(2)]
    sq_i = [0]

    def sqt():
        i = sq_i[0] % 2
        sq_i[0] += 1
        return sqb[i], b_sqb[i]

    for g in range(2):
        for tb in range(4):
            for half in range(2):
                stg, bstg = tmp()
                add("sp", lambda e, g=g, tb=tb, half=half, stg=stg: e.dma_start(
                    out=stg[:, :], in_=xin[g * T + tb * 128: g * T + (tb + 1) * 128, half * 512:(half + 1) * 512]), (), [bstg], dma=True)
                pt, pb = gp()
                for jj in range(4):
                    add("pe", lambda e, pt=pt, stg=stg, jj=jj: e.transpose(out=pt[:, jj * 128:(jj + 1) * 128],
                                                                           in_=stg[:, jj * 128:(jj + 1) * 128], identity=ident[:]),
                        [bstg, b_ident], [pb])
                evac(xT[g][:, half * 4:half * 4 + 4, tb * 128:(tb + 1) * 128],
                     pt[:, :].rearrange("p (j t) -> p j t", j=4), [pb], [b_xT[g][half * 4 + jj] for jj in range(4)])

    sil = sb("sil", [128, 8, 2], BF16); b_sil = Buf("sil")
    add("act", lambda e: e.activation(out=sil[:], in_=cv_s[:], func=AF.Silu), [b_prm], [b_sil])
    modv = sb("modv", [128, L, 48, 2], F32)
    gsc = sb("gsc", [128, L, 2, 8, 2], F32)
    b_mod = [Buf("mod%d" % l) for l in range(L)]

    def modulation(l):
        pt, pb = banks[3]
        for ti in range(12):
            wt, wb = wtile(w_mod[l, :, ti * 512:(ti + 1) * 512], 512)
            for cc in range(4):
                n = ti * 4 + cc
                for k in range(8):
                    add("pe", lambda e, wt=wt, cc=cc, k=k, n=n, pt=pt: e.matmul(pt[:, n * 2:n * 2 + 2], lhsT=wt[:, k, cc * 128:(cc + 1) * 128],
                                                                             rhs=sil[:, k, :], start=(k == 0), stop=(k == 7)),
                        [wb, b_sil], [pb])
            yield
        add("dve", lambda e, pt=pt: e.tensor_tensor(out=modv[:, l, :, :], in0=pt[:, 0:96].rearrange("p (n v) -> p n v", v=2),
                                                   in1=bmod_s[:, l, :].unsqueeze(2).to_broadcast([128, 48, 2]), op=ALU.add),
            [pb, b_prm], [b_mod[l]])
        for ni, (gsrc, off) in enumerate(((gmix_s, 8), (gff_s, 32))):
            add("dve", lambda e, ni=ni, gsrc=gsrc, off=off: e.scalar_tensor_tensor(
                out=gsc[:, l, ni, :, :], in0=modv[:, l, off:off + 8, :], scalar=1.0,
                in1=gsrc[:, l, :].unsqueeze(2).to_broadcast([128, 8, 2]), op0=ALU.add, op1=ALU.mult),
                [b_mod[l], b_prm], [b_mod[l]])

    def mcol(l, which, j, v):
        off = {"sh1": 0, "sc1": 8, "g1": 16, "sh2": 24, "sc2": 32, "g2": 40}[which]
        return modv[:, l, off + j, v:v + 1]

    def rmsnorm_mod(l, g, ni):
        pt, pb = gp()
        for j in range(8):
            sq, bq = sqt()
            add("act", lambda e, sq=sq, j=j: e.activation(out=sq[:], in_=xT[g][:, j, :], func=AF.Square), [b_xT[g][j]], [bq])
            add("pe", lambda e, sq=sq, j=j, pt=pt: e.matmul(pt[:, :], lhsT=ones_b[:, :], rhs=sq[:], start=(j == 0), stop=(j == 7)),
                [bq, b_ones_b], [pb])
        add("act", lambda e, pt=pt: e.activation(out=rstd[:], in_=pt[:, :], func=AF.Ln, bias=epsc[:, 0:1], scale=1.0 / D),
            [pb, b_epsc], [b_rstd])
        add("act", lambda e: e.activation(out=rstd[:], in_=rstd[:], func=AF.Exp, scale=-0.5), [b_rstd], [b_rstd])
        for j in range(8):
            tt, tb_ = tmp()
            add("dve", lambda e, tt=tt, j=j: e.scalar_tensor_tensor(out=tt[:], in0=xT[g][:, j, :], scalar=gsc[:, l, ni, j, g:g + 1],
                                                                    in1=rstd[:], op0=ALU.mult, op1=ALU.mult),
                [b_xT[g][j], b_rstd, b_mod[l]], [tb_])
            add("act", lambda e, tt=tt, j=j: e.activation(out=hT[g][:, j, :], in_=tt[:], func=AF.Identity,
                                                          bias=mcol(l, "sh1" if ni == 0 else "sh2", j, g), scale=1.0),
                [tb_, b_mod[l]], [b_hT[g]])

    def stat_rstd(src_list, nfeat, dst, b_dst, reads, K=128, n=T):
        pt, pb = gp()
        for i, src in enumerate(src_list):
            tt, tb_ = tmp()
            add("act", lambda e, tt=tt, src=src: e.activation(out=tt[0:K, 0:n], in_=src, func=AF.Square), reads, [tb_])
            add("pe", lambda e, tt=tt, i=i, pt=pt: e.matmul(pt[0:K, 0:n], lhsT=ones_f[0:K, 0:K], rhs=tt[0:K, 0:n],
                                                          start=(i == 0), stop=(i == len(src_list) - 1)),
                [tb_, b_ones_f], [pb])
        add("act", lambda e, pt=pt: e.activation(out=dst, in_=pt[0:K, 0:n], func=AF.Ln, bias=epsc[0:K, 0:1], scale=1.0 / nfeat),
            [pb, b_epsc], [b_dst])
        add("act", lambda e: e.activation(out=dst, in_=dst, func=AF.Exp, scale=-0.5), [b_dst], [b_dst])

    KTb = sb("KTb", [128, 4, DEC_S + PAST], BF16); b_KTh = [Buf("KT%d" % h) for h in range(4)]
    VAb = sb("VAb", [128, 18, 4, 72], BF16); b_VAk = [Buf("VA%d" % k) for k in range(18)]
    KTc2 = [sb("KTc%d" % m, [128, 4, T], BF16) for m in range(2)]
    b_KTc = [Buf("KTc%d" % m) for m in range(3)]
    VAc = [sb("VAc%d" % m, [128, 4, 4, 72], BF16) for m in range(3)]; b_VAc = [Buf("VAc%d" % m) for m in range(3)]
    QT2 = [[sb("QT%d_%d" % (g, m), [128, 4, T], BF16) for m in range(2)] for g in range(2)]
    b_QT = [[Buf("QT%d_%d" % (g, m)) for m in range(3)] for g in range(2)]
    add("pool", lambda e: e.memset(VAb[:, :, :, 64:72], 1.0), (), b_VAk)
    add("pool", lambda e: e.memset(KTb[64:128, :, :], 0.0), (), b_KTh)
    add("pool", lambda e: e.memset(QT2[1][1][96:128, :, :], 0.0), (), [b_QT[1][1]])
    for m in range(3):
        add("pool", lambda e, m=m: e.memset(VAc[m][:, :, :, 64:72], 1.0), (), [b_VAc[m]])

    def q_ap(g, m, h, p_lo, p_hi, c0=0, n=T):
        if m == 0:
            return QT2[g][0][p_lo:p_hi, h, c0:c0 + n]
        if m == 1:
            return QT2[g][1][p_lo:p_hi, h, c0:c0 + n]
        return QT2[g][0][64 + p_lo:64 + p_hi, h, c0:c0 + n]

    def kc_ap(m, h, p_lo, p_hi, c0, n):
        if m == 0:
            return KTc2[0][p_lo:p_hi, h, c0:c0 + n]
        if m == 1:
            return KTc2[1][p_lo:p_hi, h, c0:c0 + n]
        return KTc2[0][64 + p_lo:64 + p_hi, h, c0:c0 + n]
    NE = 4
    Eb = [sb("E%d" % i, [128, T], BF16) for i in range(NE)]
    b_E = [Buf("E%d" % i) for i in range(NE)]
    e_i = [0]
    NSET = 2
    rsum_s = [sb("rsum0", [128, T], F32)] * 2; b_rsum_s = [Buf("rsum0")] * 2
    rsb0 = sb("rsb0", [128, T], BF16)
    rsb_row = [64, 64]
    b_rsb_s = [Buf("rsb0")] * 2
    dao = sb("dao", [128, T], F32); b_dao = Buf("dao")
    set_i = [0]
    ckvT = sb("ckvT", [128, DEC_S + PAST], BF16); b_ckvT = Buf("ckvT")
    ckvTc = ckvT; b_ckvTc = b_ckvT
    qdn = sb("qdn", [128, 2, T], BF16); b_qdn = Buf("qdn")
    wq_s = sb("wq_s", [128, 2, 384], BF16); wqp_s = sb("wqp_s", [128, 2, 384], BF16); wkv_s = sb("wkv_s", [128, 512], BF16)
    b_wsm = Buf("wsmall")
    gpad = [sb("gpad%d" % g, [128, 2, 2, 15 + 256 + 15], BF16) for g in range(2)]
    b_gpad = [Buf("gpad%d" % g) for g in range(2)]
    for g in range(2):
        add("pool", lambda e, g=g: e.memset(gpad[g][:], 0.0), (), [b_gpad[g]])
    cacc = qd; b_cacc = b_qd
    halo = sb("halo", [128, 2, 4, 30], BF16); b_halo = Buf("halo")
    NTBT = 2
    A2U = 39
    tbt = [sb("tbt%d" % i, [128, A2U, 64], BF16) for i in range(NTBT)] * (2 // NTBT)
    b_tbt = [Buf("tbt%d" % i) for i in range(NTBT)] * (2 // NTBT)
    b_payA = [Buf("payA%d" % l) for l in range(L)]
    b_payB = [Buf("payB%d" % l) for l in range(L)]
    b_payoA = [Buf("payoA%d" % l) for l in range(L)]
    b_payoB = [Buf("payoB%d" % l) for l in range(L)]
    b_tz = [Buf("tz%d" % l) for l in range(L)]

    pending_post = [None]
    import os as _os2
    WARM_LAT = int(_os2.environ.get("WARM_LAT", "0"))
    WARM_CTX = int(_os2.environ.get("WARM_CTX", "0"))
    WARM_N = int(_os2.environ.get("WARM_N", "256"))
    BURST = int(_os2.environ.get("BURST", "20"))

    def warm(n):
        for _ in range(n):
            add("pe", lambda e: e.matmul(banks[7][0][:, 512 - WARM_N:512], lhsT=identb[:, 0:128], rhs=identb[:, 0:128], start=True, stop=True), (), ())

    def flush_post():
        if pending_post[0] is not None:
            f = pending_post[0]
            pending_post[0] = None
            f()

    def attention(kt_fn, q_ap, va_fn, nkb, nq, scale, reads, extra_fn=None):
        at, ab = acb()
        pend = []

        def score(kb):
            pt, pb = gp()
            ex = extra_fn(kb) if extra_fn is not None else []
            add("pe", lambda e, pt=pt, kb=kb: e.matmul(pt[:, 0:nq], lhsT=kt_fn(kb), rhs=q_ap, start=True, stop=(len(ex) == 0)),
                reads, [pb])
            for i, (lh, rh, rd) in enumerate(ex):
                add("pe", lambda e, pt=pt, lh=lh, rh=rh, i=i: e.matmul(pt[:, 0:nq], lhsT=lh, rhs=rh, start=False, stop=(i == len(ex) - 1)),
                    rd, [pb])
            i = e_i[0] % NE
            e_i[0] += 1
            add("act", lambda e, pt=pt, i=i: e.activation(out=Eb[i][:, 0:nq], in_=pt[:, 0:nq], func=AF.Exp, scale=scale), [pb], [b_E[i]])
            return i

        for kb in range(min(2, nkb)):
            pend.append(score(kb))
        flush_post()
        for kb in range(nkb):
            if kb + 2 < nkb:
                pend.append(score(kb + 2))
            warm(WARM_LAT)
            i = pend[kb]
            add("pe", lambda e, at=at, kb=kb, i=i: e.matmul(at[0:65, 0:nq], lhsT=va_fn(kb), rhs=Eb[i][:, 0:nq], start=(kb == 0), stop=(kb == nkb - 1)),
                reads + [b_E[i]], [ab])
        return at, ab

    def normalize_rep(at, ab, nq, dst, b_dst_list):
        r, br = tmp()
        add("act", lambda e: e.activation(out=r[0:64, 0:nq], in_=at[0:64, 256:256 + nq], func=AF.Ln), [ab], [br])
        add("act", lambda e: e.activation(out=r[0:64, 0:nq], in_=r[0:64, 0:nq], func=AF.Exp, scale=-1.0), [br], [br])
        add("dve", lambda e: e.tensor_tensor(out=dst, in0=at[0:64, 0:nq], in1=r[0:64, 0:nq], op=ALU.mult), [ab, br], b_dst_list)

    def normalize(at, ab, nq, dst, b_dst_list, k=0, c0=0):
        rsum, b_rsum, b_rsb, rr = rsum_s[k], b_rsum_s[k], b_rsb_s[k], rsb_row[k]
        add("act", lambda e: e.activation(out=rsum[64:65, 0:nq], in_=at[64:65, c0:c0 + nq], func=AF.Ln), [ab], [b_rsum])
        add("act", lambda e: e.activation(out=rsb0[rr:rr + 1, 0:nq], in_=rsum[64:65, 0:nq], func=AF.Exp, scale=-1.0), [b_rsum], [b_rsb])
        pt, pb = gp()
        add("pe", lambda e: e.matmul(pt[0:64, 0:nq], lhsT=ones_b[rr:rr + 1, 0:64], rhs=rsb0[rr:rr + 1, 0:nq], start=True, stop=True),
            [b_rsb, b_ones_b], [pb])
        bc, bbc = tmp()
        add("act", lambda e: e.activation(out=bc[0:64, 0:nq], in_=pt[0:64, 0:nq], func=AF.Copy), [pb], [bbc])
        add("dve", lambda e: e.tensor_tensor(out=dst, in0=at[0:64, c0:c0 + nq], in1=bc[0:64, 0:nq], op=ALU.mult), [ab, bbc], b_dst_list)

    def stream_attention(jobs, nkb=18, nq=T, side_cb=None, burst=0):
        blocks = [(ji, kb) for ji in range(len(jobs)) for kb in range(nkb)]
        N = len(blocks)
        pend = {}
        posts = {}

        def do_score(idx):
            ji, kb = blocks[idx]
            J = jobs[ji]
            if kb == 0 and J.get("pre") is not None:
                J["pre"]()
            pt, pb = gp()
            ex = J["extra_fn"](kb) if J.get("extra_fn") is not None else []
            add("pe", lambda e, pt=pt, kb=kb, J=J: e.matmul(pt[:, 0:nq], lhsT=J["kt_fn"](kb), rhs=J["q_ap"], start=True, stop=(len(ex) == 0)),
                J["reads"], [pb])
            for i2, (lh, rh, rd) in enumerate(ex):
                add("pe", lambda e, pt=pt, lh=lh, rh=rh, i2=i2: e.matmul(pt[:, 0:nq], lhsT=lh, rhs=rh, start=False, stop=(i2 == len(ex) - 1)),
                    rd, [pb])
            i = e_i[0] % NE
            e_i[0] += 1
            add("act", lambda e, pt=pt, i=i, J=J: e.activation(out=Eb[i][:, 0:nq], in_=pt[:, 0:nq], func=AF.Exp, scale=J["scale"]), [pb], [b_E[i]])
            pend[idx] = i

        LA = 3
        for idx in range(min(LA, N)):
            do_score(idx)
        if burst:
            wpt, wpb = gp()
            for _ in range(burst):
                add("pe", lambda e, wpt=wpt: e.matmul(wpt[:, :], lhsT=identb[:, :], rhs=hT[1][:, 0, :], start=True, stop=True),
                    [b_identb, b_hT[1]], [wpb])
        for idx in range(N):
            if idx + LA < N:
                do_score(idx + LA)
            ji, kb = blocks[idx]
            J = jobs[ji]
            if kb == 0:
                J["acc"] = acb()
            at, ab = J["acc"]
            i = pend.pop(idx)
            add("pe", lambda e, at=at, kb=kb, i=i, J=J: e.matmul(at[0:65, 0:nq], lhsT=J["va_fn"](kb), rhs=Eb[i][:, 0:nq],
                                                              start=(kb == 0), stop=(kb == nkb - 1)), J["reads"] + [b_E[i]], [ab])
            if kb == nkb - 1:
                if J.get("post") is not None:
                    posts.setdefault(min(idx + 2, N - 1), []).append(J["post"])
                if side_cb is not None:
                    side_cb()
            for f in posts.pop(idx, []):
                f()

    def input_proj(l):
        res = {}
        add("pool", lambda e: e.dma_start(out=wq_s[:], in_=wqup[l].rearrange("(k p) c -> p k c", p=128)), (), [b_wsm], dma=True)
        add("pool", lambda e: e.dma_start(out=wqp_s[:], in_=wqupp[l].rearrange("(k p) c -> p k c", p=128)), (), [b_wsm], dma=True)
        add("pool", lambda e: e.dma_start(out=wkv_s[:], in_=wkvup[l]), (), [b_wsm], dma=True)

        def fm(wt, wb, c0, M, g, n0=0, n=T):
            pt, pb = gp()
            for k in range(8):
                add("pe", lambda e, pt=pt, k=k: e.matmul(pt[0:M, 0:n], lhsT=wt[:, k, c0:c0 + M], rhs=hT[g][:, k, n0:n0 + n],
                                                        start=(k == 0), stop=(k == 7)), [wb, b_hT[g]], [pb])
            return pt, pb

        def tm(wt, wb, c0, N, g, tb):
            pt, pb = gp()
            for k in range(8):
                add("pe", lambda e, pt=pt, k=k: e.matmul(pt[:, 0:N], lhsT=hT[g][:, k, tb * 128:(tb + 1) * 128], rhs=wt[:, k, c0:c0 + N],
                                                        start=(k == 0), stop=(k == 7)), [wb, b_hT[g]], [pb])
            return pt, pb

        def rope_evac(ptA, pbA, ptB, pbB, p0, p1, dst, wlist):
            t1, tb1 = tmp()
            t2, tb2 = tmp()
            add("dve", lambda e: e.tensor_tensor(out=t1[p0:p1, :], in0=ptA[p0:p1, :], in1=cos_s[p0:p1, :], op=ALU.mult), [pbA, b_prm], [tb1])
            add("dve", lambda e: e.tensor_tensor(out=t2[p0:p1, :], in0=ptB[p0:p1, :], in1=sin_s[p0:p1, :], op=ALU.mult), [pbB, b_prm], [tb2])
            add("dve", lambda e: e.tensor_tensor(out=dst, in0=t1[p0:p1, :], in1=t2[p0:p1, :], op=ALU.add), [tb1, tb2], wlist)

        groups = [0, 1] if do_lat else [0]
        wA, bA = wtile(w_in[l, :, 0:512], 512)
        if do_lat:
            wF1, bF1 = wtile(w_inp[l, :, 0:512], 512)
        for hp in range(2):
            pt, pb = fm(wA, bA, hp * 128, 128, 0)
            evac(QT2[0][0][0:64, 2 * hp, :], pt[0:64, :], [pb], [b_QT[0][0]])
            evac(QT2[0][0][0:64, 2 * hp + 1, :], pt[64:128, :], [pb], [b_QT[0][0]])
            pt, pb = fm(wA, bA, 256 + hp * 128, 128, 0)
            evac(KTc2[0][0:64, 2 * hp, :], pt[0:64, :], [pb], [b_KTc[0]])
            evac(KTc2[0][0:64, 2 * hp + 1, :], pt[64:128, :], [pb], [b_KTc[0]])
        if do_lat:
            def rope_pair(c0):
                ptA, pbA = fm(wA, bA, c0, 128, 1)
                ptB, pbB = fm(wF1, bF1, c0, 128, 1)
                t1, tb1 = tmp()
                t2, tb2 = tmp()
                add("dve", lambda e: e.tensor_tensor(out=t1[:, :], in0=ptA[:, :], in1=cos_s[:, :], op=ALU.mult), [pbA, b_prm], [tb1])
                add("dve", lambda e: e.tensor_tensor(out=t2[:, :], in0=ptB[:, :], in1=sin_s[:, :], op=ALU.mult), [pbB, b_prm], [tb2])
                return t1, tb1, t2, tb2
            for hp in range(2):
                t1, tb1, t2, tb2 = rope_pair(hp * 128)
                for i in range(2):
                    add("dve", lambda e, t1=t1, t2=t2, i=i, hp=hp: e.tensor_tensor(out=QT2[1][0][0:64, 2 * hp + i, :], in0=t1[i * 64:(i + 1) * 64, :],
                                                                                in1=t2[i * 64:(i + 1) * 64, :], op=ALU.add), [tb1, tb2], [b_QT[1][0]])
                t1, tb1, t2, tb2 = rope_pair(256 + hp * 128)
                tk, tkb = sqt()
                add("dve", lambda e, t1=t1, t2=t2, tk=tk: e.tensor_tensor(out=tk[:, :], in0=t1[:, :], in1=t2[:, :], op=ALU.add), [tb1, tb2], [tkb])
                add("sp", lambda e, tk=tk, hp=hp: e.dma_start(out=payA_in[l].ap()[R_DAK + hp * 128:R_DAK + (hp + 1) * 128, :], in_=tk[:, :]),
                    [tkb], [b_payA[l]], dma=True)
        wB, bB = wtile(w_in[l, :, 512:1024], 512)
        for g in groups:
            for c in range(2):
                pt, pb = fm(wB, bB, 256 + c * 128, 128, g)
                evac(qd[:, c, :], pt[:, :], [pb], [b_qd])
            stat_rstd([qd[:, 0, :], qd[:, 1, :]], 256, rstd[:], b_rstd, [b_qd])
            for c in range(2):
                add("dve", lambda e, c=c: e.scalar_tensor_tensor(out=qdn[:, c, :], in0=qd[:, c, :], scalar=gq_s[:, l, c:c + 1], in1=rstd[:],
                                                               op0=ALU.mult, op1=ALU.mult), [b_qd, b_rstd, b_prm], [b_qdn])
            for h in range(4):
                pt, pb = gp()
                for c in range(2):
                    add("pe", lambda e, pt=pt, c=c, h=h: e.matmul(pt[0:96, :], lhsT=wq_s[:, c, h * 96:(h + 1) * 96], rhs=qdn[:, c, :],
                                                                 start=(c == 0), stop=(c == 1)), [b_wsm, b_qdn], [pb])
                if g == 0:
                    evac(QT2[0][1][0:96, h, :], pt[0:96, :], [pb], [b_QT[0][1]])
                else:
                    pt2, pb2 = gp()
                    for c in range(2):
                        add("pe", lambda e, pt2=pt2, c=c, h=h: e.matmul(pt2[0:96, :], lhsT=wqp_s[:, c, h * 96:(h + 1) * 96], rhs=qdn[:, c, :],
                                                                       start=(c == 0), stop=(c == 1)), [b_wsm, b_qdn], [pb2])
                    evac(QT2[1][1][0:64, h, :], pt[0:64, :], [pb], [b_QT[1][1]])
                    rope_evac(pt, pb, pt2, pb2, 64, 96, QT2[1][1][64:96, h, :], [b_QT[1][1]])
        for g in groups:
            for tb in range(4):
                pt, pb = tm(wB, bB, 0, 256, g, tb)
                if g == 0:
                    ostage, b_ostage = tmp()
                    add("act", lambda e, pt=pt, ostage=ostage: e.activation(out=ostage[:, 0:256], in_=pt[:, 0:256], func=AF.Copy), [pb], [b_ostage])
                    s, tl = tb // 2, (tb % 2) * 128
                    add("sp", lambda e, s=s, tl=tl, ostage=ostage: e.dma_start(out=o_dav[s, l, :, tl:tl + 128, :].rearrange("h t d -> t h d"),
                                                               in_=ostage[:, 0:256].rearrange("p (h d) -> p h d", h=4)), [b_ostage], (), dma=True)
                    add("dve", lambda e, pt=pt, tb=tb: e.tensor_copy(out=VAc[0][:, tb, :, 0:64], in_=pt[:, 0:256].rearrange("p (h d) -> p h d", h=4)),
                        [pb], [b_VAc[0]])
                else:
                    tv, tvb = sqt()
                    evac(tv[:, 0:256], pt[:, 0:256], [pb], [tvb])
                    add("sp", lambda e, tv=tv, tb=tb: e.dma_start(out=payB_in[l].ap()[R_V + tb * 128:R_V + (tb + 1) * 128, 0:256], in_=tv[:, 0:256]),
                        [tvb], [b_payB[l]], dma=True)
        wC, bC = wtile(w_in[l, :, 1024:1440], 416)
        if do_lat:
            wF2, bF2 = wtile(w_inp[l, :, 512:608], 96)
        for g in groups:
            pt, pb = fm(wC, bC, 0, 128, g)
            kvd, b_kvd = tmp()
            evac(kvd[:, :], pt[:, :], [pb], [b_kvd])
            stat_rstd([kvd[:, :]], 128, rstd[:], b_rstd, [b_kvd])
            dstc = ckvTc if g == 0 else sqb[0]
            if g == 0:
                add("dve", lambda e, kvd=kvd: e.scalar_tensor_tensor(out=ckvTc[:, 0:T], in0=kvd[:, :], scalar=gkvc_s[:, l:l + 1], in1=rstd[:],
                                                            op0=ALU.mult, op1=ALU.mult), [b_kvd, b_rstd, b_prm], [b_ckvTc])
            else:
                tk, tkb = sqt()
                add("dve", lambda e, tk=tk, kvd=kvd: e.scalar_tensor_tensor(out=tk[:, :], in0=kvd[:, :], scalar=gkvc_s[:, l:l + 1], in1=rstd[:],
                                                                   op0=ALU.mult, op1=ALU.mult), [b_kvd, b_rstd, b_prm], [tkb])
                add("sp", lambda e, tk=tk: e.dma_start(out=payA_in[l].ap()[R_CKV:R_CKV + 128, :], in_=tk[:, :]), [tkb], [b_payA[l]], dma=True)
            pt, pb = fm(wC, bC, 64, 96, g)
            if g == 0:
                for h in range(4):
                    evac(KTc2[1][64:96, h, :], pt[64:96, :], [pb], [b_KTc[1]])
            else:
                ptB, pbB = fm(wF2, bF2, 0, 96, 1)
                tk, tkb = sqt()
                rope_evac(pt, pb, ptB, pbB, 64, 96, tk[64:96, :], [tkb])
                add("sp", lambda e, tk=tk: e.dma_start(out=payA_in[l].ap()[R_KR:R_KR + 32, :], in_=tk[64:96, :]), [tkb], [b_payA[l]], dma=True)
            for hp in range(2):
                pt, pb = fm(wC, bC, 160 + hp * 128, 128, g)
                evac(QT2[g][0][64:128, 2 * hp, :], pt[0:64, :], [pb], [b_QT[g][2]])
                evac(QT2[g][0][64:128, 2 * hp + 1, :], pt[64:128, :], [pb], [b_QT[g][2]])
        for tb in range(4):
            pt, pb = tm(wC, bC, 0, 160, 0, tb)
            s, tl = tb // 2, (tb % 2) * 128
            tt, tb_ = tmp()
            add("act", lambda e, pt=pt, tt=tt: e.activation(out=tt[:, 0:128], in_=pt[:, 0:128], func=AF.Square), [pb], [tb_])
            add("dve", lambda e, tt=tt: e.reduce_sum(out=tt[:, 200:201], in_=tt[:, 0:128], axis=mybir.AxisListType.X), [tb_], [tb_])
            add("act", lambda e, tt=tt: e.activation(out=tt[:, 201:202], in_=tt[:, 200:201], func=AF.Ln, bias=epsc[:, 0:1], scale=1.0 / 128),
                [tb_, b_epsc], [tb_])
            add("act", lambda e, tt=tt: e.activation(out=tt[:, 202:203], in_=tt[:, 201:202], func=AF.Exp, scale=-0.5), [tb_], [tb_])
            add("dve", lambda e, pt=pt, tt=tt: e.scalar_tensor_tensor(out=tt[:, 256:384], in0=pt[:, 0:128], scalar=tt[:, 202:203],
                                                                      in1=gkvr_s[:, l, :], op0=ALU.mult, op1=ALU.mult),
                [pb, tb_, b_prm], [tb_])
            add("act", lambda e, pt=pt, tt=tt: e.activation(out=tt[:, 384:416], in_=pt[:, 128:160], func=AF.Copy), [pb, tb_], [tb_])
            add("sp", lambda e, s=s, tl=tl, tt=tt: e.dma_start(out=o_ckv[s, l, tl:tl + 128, :], in_=tt[:, 256:384]), [tb_], (), dma=True)
            add("sp", lambda e, s=s, tl=tl, tt=tt: e.dma_start(out=o_kr[s, l, tl:tl + 128, :], in_=tt[:, 384:416]), [tb_], (), dma=True)
        for h in range(4):
            pt, pb = gp()
            add("pe", lambda e, pt=pt, h=h: e.matmul(pt[0:64, :], lhsT=wkv_s[:, h * 128:h * 128 + 64], rhs=ckvTc[:, 0:T], start=True, stop=True),
                [b_wsm, b_ckvTc], [pb])
            evac(KTc2[1][0:64, h, :], pt[0:64, :], [pb], [b_KTc[1]])
        for tb in range(4):
            pt, pb = gp()
            add("pe", lambda e, pt=pt, tb=tb: e.matmul(pt[:, 0:256], lhsT=ckvTc[:, tb * 128:(tb + 1) * 128],
                                                      rhs=wkv_s[:, :].rearrange("p (h c) -> p h c", h=4)[:, :, 64:128], start=True, stop=True),
                [b_wsm, b_ckvTc], [pb])
            evac(VAc[1][:, tb, :, 0:64], pt[:, 0:256].rearrange("p (h d) -> p h d", h=4), [pb], [b_VAc[1]])
        wD, bD = wtile(w_in[l, :, 1440:1952], 512)
        for hp in range(2):
            pt, pb = fm(wD, bD, hp * 128, 128, 0)
            evac(KTc2[0][64:128, 2 * hp, :], pt[0:64, :], [pb], [b_KTc[2]])
            evac(KTc2[0][64:128, 2 * hp + 1, :], pt[64:128, :], [pb], [b_KTc[2]])
            if do_lat:
                pt, pb = fm(wD, bD, hp * 128, 128, 1)
                tk, tkb = sqt()
                evac(tk[:, :], pt[:, :], [pb], [tkb])
                add("sp", lambda e, tk=tk, hp=hp: e.dma_start(out=payA_in[l].ap()[R_NAK + hp * 128:R_NAK + (hp + 1) * 128, :], in_=tk[:, :]),
                    [tkb], [b_payA[l]], dma=True)
        for tb in range(4):
            pt, pb = tm(wD, bD, 0, 512, 0, tb)
            s, tl = tb // 2, (tb % 2) * 128
            ostage, b_ostage = tmp()
            add("act", lambda e, pt=pt, ostage=ostage: e.activation(out=ostage[:, :], in_=pt[:, :], func=AF.Copy), [pb], [b_ostage])
            add("sp", lambda e, s=s, tl=tl, ostage=ostage: e.dma_start(out=o_nak[s, l, :, tl:tl + 128, :].rearrange("h t d -> t h d"),
                                                       in_=ostage[:, 0:256].rearrange("p (h d) -> p h d", h=4)), [b_ostage], (), dma=True)
            add("sp", lambda e, s=s, tl=tl, ostage=ostage: e.dma_start(out=o_nav[s, l, :, tl:tl + 128, :].rearrange("h t d -> t h d"),
                                                       in_=ostage[:, 256:512].rearrange("p (h d) -> p h d", h=4)), [b_ostage], (), dma=True)
            add("dve", lambda e, pt=pt, tb=tb: e.tensor_copy(out=VAc[2][:, tb, :, 0:64], in_=pt[:, 256:512].rearrange("p (h d) -> p h d", h=4)),
                [pb], [b_VAc[2]])
            if do_lat:
                pt, pb = tm(wD, bD, 256, 256, 1, tb)
                tv, tvb = sqt()
                evac(tv[:, 0:256], pt[:, 0:256], [pb], [tvb])
                add("sp", lambda e, tv=tv, tb=tb: e.dma_start(out=payB_in[l].ap()[R_V + tb * 128:R_V + (tb + 1) * 128, 256:512], in_=tv[:, 0:256]),
                    [tvb], [b_payB[l]], dma=True)
        wA2, bA2 = wtile(w_in[l, :, 256:512], 256)
        for tb in range(4):
            pt, pb = tm(wA2, bA2, 0, 256, 0, tb)
            s, tl = tb // 2, (tb % 2) * 128
            ostage, b_ostage = tmp()
            add("act", lambda e, pt=pt, ostage=ostage: e.activation(out=ostage[:, 0:256], in_=pt[:, 0:256], func=AF.Copy), [pb], [b_ostage])
            add("sp", lambda e, s=s, tl=tl, ostage=ostage: e.dma_start(out=o_dak[s, l, :, tl:tl + 128, :].rearrange("h t d -> t h d"),
                                                       in_=ostage[:, 0:256].rearrange("p (h d) -> p h d", h=4)), [b_ostage], (), dma=True)
        wE, bE = wtile(w_in[l, :, 1952:2464], 512)
        for g in groups:
            for c in range(2):
                pa, pba = fm(wE, bE, c * 128, 128, g)
                pbm, pbb = fm(wE, bE, 256 + c * 128, 128, g)
                tt, tb_ = tmp()
                add("act", lambda e, tt=tt, pbm=pbm: e.activation(out=tt[:], in_=pbm[:, :], func=AF.Sigmoid), [pbb], [tb_])
                add("dve", lambda e, tt=tt, pa=pa, c=c, g=g: e.tensor_tensor(out=gpad[g][:, c, :, 15:15 + 256],
                                                                           in0=pa[:, :].rearrange("p (s t) -> p s t", s=2),
                                                                           in1=tt[:].rearrange("p (s t) -> p s t", s=2), op=ALU.mult),
                    [pba, tb_], [b_gpad[g]])
        if do_lat:
            add("dve", lambda e: e.tensor_copy(out=halo[:, :, 0, 0:15], in_=gpad[1][:, :, 0, 15:30]), [b_gpad[1]], [b_halo])
            add("dve", lambda e: e.tensor_copy(out=halo[:, :, 0, 15:30], in_=gpad[1][:, :, 1, 256:271]), [b_gpad[1]], [b_halo])
            hdst = dram_ap(payB_in[l], R_HALO * T, [[30, 128], [128 * 30, 2], [1, 30]])
            add("sp", lambda e: e.dma_start(out=hdst, in_=halo[:, :, 0, :]), [b_halo], [b_payB[l]], dma=True)
            add("pool", lambda e: e.collective_compute("AllGather", ALU.bypass, replica_groups=[[0, 1, 2, 3], [4, 5, 6, 7]],
                                                       ins=[payA_in[l].ap().opt()], outs=[payA_out[l].ap().opt()]),
                [b_payA[l]], [b_payoA[l]], dma="cc")
            add("pool", lambda e: e.collective_compute("AllGather", ALU.bypass, replica_groups=[[0, 1, 2, 3], [4, 5, 6, 7]],
                                                       ins=[payB_in[l].ap().opt()], outs=[payB_out[l].ap().opt()]),
                [b_payB[l]], [b_payoB[l]], dma="cc")

    def da_post(l, g, h, at1, ab1, at2, ab2, nq=T, q0=0, repl=False, c1=0, c2=0):
        o1, bo1 = dao, b_dao
        o2, bo2 = rsum_s[0], b_rsum_s[0]
        if repl:
            normalize_rep(at1, ab1, nq, o1[0:64, 0:nq], [bo1])
            normalize_rep(at2, ab2, nq, o2[0:64, 0:nq], [bo2])
        else:
            normalize(at1, ab1, nq, o1[0:64, 0:nq], [bo1], 0, c1)
            normalize(at2, ab2, nq, o2[0:64, 0:nq], [bo2], 1 if nq <= 256 else 0, c2)
        add("dve", lambda e: e.scalar_tensor_tensor(out=o1[0:64, 0:nq], in0=o2[0:64, 0:nq], scalar=neglam[0:64, l:l + 1], in1=o1[0:64, 0:nq],
                                                    op0=ALU.mult, op1=ALU.add), [bo1, bo2, b_prm], [bo1])
        stat_rstd([o1[0:64, 0:nq]], 64, rstd[0:64, 0:nq], b_rstd, [bo1], K=64, n=nq)
        p0 = (h % 2) * 64
        add("dve", lambda e: e.scalar_tensor_tensor(out=mixT[g][p0:p0 + 64, h // 2, q0:q0 + nq], in0=o1[0:64, 0:nq], scalar=gsub2[0:64, l:l + 1],
                                                    in1=rstd[0:64, 0:nq], op0=ALU.mult, op1=ALU.mult),
            [bo1, b_rstd, b_prm], [b_mix[g][h // 2]])

    def plain_post(g, m, h, at, ab, nq=T, q0=0, repl=False):
        if repl:
            p0 = (h % 2) * 64
            ch = 2 * m + h // 2
            normalize_rep(at, ab, nq, mixT[g][p0:p0 + 64, ch, q0:q0 + nq], [b_mix[g][ch]])
            return
        k = (set_i[0] % NSET) if nq <= 256 else 0
        set_i[0] += 1
        o1, bo1 = tmp()
        normalize(at, ab, nq, o1[0:64, 0:nq], [bo1], k)
        p0 = (h % 2) * 64
        ch = 2 * m + h // 2
        add("act", lambda e: e.activation(out=mixT[g][p0:p0 + 64, ch, q0:q0 + nq], in_=o1[0:64, 0:nq], func=AF.Copy), [bo1], [b_mix[g][ch]])

    def ctx_attention(l, side_gen=None):
        scales = (32 ** -0.5, 96 ** -0.5, 64 ** -0.5)
        rows = {0: None, 1: (0, 96), 2: (0, 64)}
        jobs = []
        for s in range(2):
            for m in range(3):
                for h in range(4):
                    for mp in ((0, 1) if m == 0 else (0,)):
                        jobs.append(dict(s=s, m=m, h=h, mp=mp))
        n = len(jobs)

        def stage_a(j):
            s_, m, h, mp = j["s"], j["m"], j["h"], j["mp"]
            q0 = s_ * 256
            lo, hi = (mp * 32, mp * 32 + 32) if m == 0 else rows[m]
            pt, pb = gp()
            rd = [b_KTc[m], b_QT[0][m]]
            for kb in range(2):
                add("pe", lambda e, pt=pt, kb=kb, m=m, h=h, lo=lo, hi=hi, q0=q0: e.matmul(
                    pt[:, kb * 256:(kb + 1) * 256], lhsT=kc_ap(m, h, lo, hi, q0 + kb * 128, 128), rhs=q_ap(0, m, h, lo, hi, q0, 256),
                    start=True, stop=True), rd, [pb])
            i = e_i[0] % NE
            e_i[0] += 1
            add("act", lambda e, pt=pt, i=i, m=m: e.activation(out=Eb[i][:, :], in_=pt[:, :], func=AF.Exp, scale=scales[m]), [pb], [b_E[i]])
            j["e"] = i

        def stage_b(j):
            s_, m, h = j["s"], j["m"], j["h"]
            at, ab = acb()
            i = j["e"]
            for kb in range(2):
                add("pe", lambda e, at=at, kb=kb, i=i, m=m, h=h, s_=s_: e.matmul(at[0:65, 0:256], lhsT=VAc[m][:, s_ * 2 + kb, h, 0:65],
                                                                               rhs=Eb[i][:, kb * 256:(kb + 1) * 256], start=(kb == 0), stop=(kb == 1)),
                    [b_VAc[m], b_E[i]], [ab])
            for kb in range(2):
                add("pe", lambda e, at=at, kb=kb, i=i: e.matmul(at[0:64, 256:512], lhsT=ones_b[:, 0:64], rhs=Eb[i][:, kb * 256:(kb + 1) * 256],
                                                               start=(kb == 0), stop=(kb == 1)), [b_ones_b, b_E[i]], [ab])
            j["acc"] = (at, ab)

        def stage_c(idx):
            j = jobs[idx]
            s_, m, h, mp = j["s"], j["m"], j["h"], j["mp"]
            q0 = s_ * 256
            if m == 0:
                if mp == 1:
                    a1 = jobs[idx - 1]["acc"]
                    da_post(l, 0, h, a1[0], a1[1], j["acc"][0], j["acc"][1], nq=256, q0=q0, repl=True)
            else:
                plain_post(0, m, h, j["acc"][0], j["acc"][1], nq=256, q0=q0, repl=True)

        for i in range(n + 2):
            if i < n:
                stage_a(jobs[i])
            warm(WARM_CTX)
            if 0 <= i - 1 < n:
                stage_b(jobs[i - 1])
            if 0 <= i - 2 < n:
                stage_c(i - 2)
            if side_gen is not None and i % 2 == 1:
                next(side_gen, None)

    def cache_T(src_ap, ncol, dst_fn, wlist):
        stg, bstg = tmp()
        add("sp", lambda e: e.dma_start(out=stg[:, 0:2 * ncol].rearrange("p (t c) -> p t c", t=2),
                                        in_=src_ap.rearrange("(t p) c -> p t c", p=128)), (), [bstg], dma=True)
        for tb in range(2):
            pt, pb = gp()
            add("pe", lambda e, pt=pt, tb=tb: e.transpose(out=pt[0:ncol, 0:128], in_=stg[:, tb * ncol:(tb + 1) * ncol], identity=ident[:]),
                [bstg, b_ident], [pb])
            evac(dst_fn(tb), pt[0:ncol, 0:128], [pb], wlist)

    def load_V(l, c0, cache_ap):
        for rk in range(4):
            for tb in range(4):
                src = dram_ap(payB_out[l], (rk * PB + R_V + tb * 128) * T + c0, [[T, 128], [64, 4], [1, 64]])
                add("sp", lambda e, rk=rk, tb=tb, src=src: e.dma_start(out=VAb[:, rk * 4 + tb, :, 0:64], in_=src), [b_payoB[l]], [b_VAk[rk * 4 + tb]], dma=True)
        for tb in range(2):
            add("pool", lambda e, tb=tb: e.dma_start(out=VAb[:, 16 + tb, :, 0:64], in_=cache_ap[:, tb * 128:(tb + 1) * 128, :].rearrange("h t d -> t h d")),
                (), [b_VAk[16 + tb]], dma=True)

    def load_KT(l, row0, nrows, p0, heads=True):
        for h in range(4):
            r = row0 + (h * nrows if heads else 0)
            src = dram_ap(payA_out[l], r * T, [[T, nrows], [PA * T, 4], [1, T]])
            add("sp", lambda e, h=h, src=src: e.dma_start(out=KTb[p0:p0 + nrows, h, 0:DEC_S].rearrange("p (r t) -> p r t", r=4), in_=src),
                [b_payoA[l]], [b_KTh[h]], dma=True)

    def lat_attention(l, side_gen=None, mod_gen=None):
        rdv = list(b_VAk)

        def side_step(n=1):
            if side_gen is not None:
                for _ in range(n):
                    next(side_gen, None)
            if mod_gen is not None:
                next(mod_gen, None)

        load_KT(l, R_DAK, 64, 0)
        for h in range(4):
            cache_T(c_dak[l, h], 64, lambda tb, h=h: KTb[0:64, h, DEC_S + tb * 128:DEC_S + (tb + 1) * 128], [b_KTh[h]])
        load_V(l, 0, c_dav[l])
        jobs = []
        for h in range(4):
            for qh in range(2):
                idx = h * 2 + qh
                tq, btq = tbt[idx // 4], b_tbt[idx // 4]
                qb = tq[:, :, :].rearrange("p a w -> p (a w)")[:, (idx % 4) * 512:(idx % 4 + 1) * 512]
                add("pool", lambda e, qb=qb: e.memset(qb, 0.0), (), [btq])
                add("pool", lambda e, qb=qb, h=h, qh=qh: e.tensor_copy(out=qb[0:32, 0:256], in_=QT2[1][0][0:32, h, qh * 256:(qh + 1) * 256]),
                    [b_QT[1][0]], [btq])
                add("pool", lambda e, qb=qb, h=h, qh=qh: e.tensor_copy(out=qb[32:64, 256:512], in_=QT2[1][0][32:64, h, qh * 256:(qh + 1) * 256]),
                    [b_QT[1][0]], [btq])
                J = dict(kt_fn=(lambda kb, h=h: KTb[0:128, h, kb * 128:(kb + 1) * 128]), q_ap=qb,
                         va_fn=(lambda kb, h=h: VAb[:, kb, h, 0:65]), scale=32 ** -0.5, reads=rdv + [b_KTh[h], btq])
                J["post"] = (lambda h=h, qh=qh, J=J: da_post(l, 1, h, J["acc"][0], J["acc"][1], J["acc"][0], J["acc"][1],
                                                             nq=256, q0=qh * 256, c1=0, c2=256))
                jobs.append(J)
        stream_attention(jobs, side_cb=lambda: side_step(2), burst=BURST)
        src = dram_ap(payA_out[l], R_CKV * T, [[T, 128], [PA * T, 4], [1, T]])
        add("sp", lambda e, src=src: e.dma_start(out=ckvT[:, 0:DEC_S].rearrange("p (r t) -> p r t", r=4), in_=src), [b_payoA[l]], [b_ckvT], dma=True)
        cache_T(c_ckv[l], 128, lambda tb: ckvT[:, DEC_S + tb * 128:DEC_S + (tb + 1) * 128], [b_ckvT])
        load_KT(l, R_KR, 32, 64, heads=False)
        for h in range(4):
            cache_T(c_kr[l], 32, lambda tb, h=h: KTb[64:96, h, DEC_S + tb * 128:DEC_S + (tb + 1) * 128], [b_KTh[h]])
        for h in range(4):
            for cb in range(5):
                n0 = cb * 512
                n = min(512, DEC_S + PAST - n0)
                pt, pb = gp()
                add("pe", lambda e, pt=pt, h=h, n0=n0, n=n: e.matmul(pt[0:64, 0:n], lhsT=wkv_s[:, h * 128:h * 128 + 64], rhs=ckvT[:, n0:n0 + n],
                                                                    start=True, stop=True), [b_wsm, b_ckvT], [pb])
                evac(KTb[0:64, h, n0:n0 + n], pt[0:64, 0:n], [pb], [b_KTh[h]])
        for kb in range(18):
            pt, pb = gp()
            add("pe", lambda e, pt=pt, kb=kb: e.matmul(pt[:, 0:256], lhsT=ckvT[:, kb * 128:(kb + 1) * 128],
                                                      rhs=wkv_s[:, :].rearrange("p (h c) -> p h c", h=4)[:, :, 64:128], start=True, stop=True),
                [b_wsm, b_ckvT], [pb])
            evac(VAb[:, kb, :, 0:64], pt[:, 0:256].rearrange("p (h d) -> p h d", h=4), [pb], [b_VAk[kb]])
        jobs = []
        for h in range(4):
            J = dict(kt_fn=(lambda kb, h=h: KTb[0:128, h, kb * 128:(kb + 1) * 128]), q_ap=q_ap(1, 1, h, 0, 128),
                     va_fn=(lambda kb, h=h: VAb[:, kb, h, 0:65]), scale=96 ** -0.5, reads=rdv + [b_KTh[h], b_QT[1][1]])
            J["post"] = (lambda h=h, J=J: plain_post(1, 1, h, J["acc"][0], J["acc"][1]))
            jobs.append(J)
        stream_attention(jobs, side_cb=lambda: side_step(2))
        load_KT(l, R_NAK, 64, 0)
        for h in range(4):
            cache_T(c_nak[l, h], 64, lambda tb, h=h: KTb[0:64, h, DEC_S + tb * 128:DEC_S + (tb + 1) * 128], [b_KTh[h]])
            add("pool", lambda e, h=h: e.dma_start(out=KTb[64:96, h, 0:DEC_S], in_=rowind_d), (), [b_KTh[h]], dma=True)
            add("pool", lambda e, h=h: e.memset(KTb[64:96, h, DEC_S:DEC_S + PAST], 0.0), (), [b_KTh[h]])
            evac(QT2[1][0][0:64, h, :], QT2[1][0][64:128, h, :], [b_QT[1][2], b_QT[1][0]], [b_QT[1][0], b_QT[1][2]])
            add("pool", lambda e, h=h: e.dma_start(out=QT2[1][0][64:96, h, :], in_=rowsel_d), (), [b_QT[1][0], b_QT[1][2]], dma=True)
        load_V(l, 256, c_nav[l])
        jobs = []
        for h in range(4):
            ti = h % 2

            def pre(h=h, ti=ti):
                for j in range(2):
                    src = bass.AP(tz_all, l * TZ_SZ + (h * (A2 + 1) + (1 - j) + 7) * 64 * TZL + 63, [[TZL - 1, 64], [64 * TZL, A2U], [1, 64]])
                    add("pool", lambda e, j=j, src=src, ti=ti: e.dma_start(out=tbt[ti][j * 64:(j + 1) * 64, :, :], in_=src), (), [b_tbt[ti]], dma=True)
                add("dve", lambda e, ti=ti: e.scalar_tensor_tensor(out=tbt[ti][:], in0=tbt[ti][:], scalar=8.0,
                                                                   in1=colm_s[:, :].unsqueeze(1).to_broadcast([128, A2U, 64]),
                                                                   op0=ALU.mult, op1=ALU.add), [b_tbt[ti], b_prm], [b_tbt[ti]])

            def extra(kb, ti=ti):
                if kb >= 16:
                    return []
                a0 = 30 - 2 * kb
                return [(identb[:, :], tbt[ti][:, a0:a0 + 8, :], [b_identb, b_tbt[ti]])]

            J = dict(kt_fn=(lambda kb, h=h: KTb[0:128, h, kb * 128:(kb + 1) * 128]), q_ap=QT2[1][0][0:128, h, :],
                     va_fn=(lambda kb, h=h: VAb[:, kb, h, 0:65]), scale=64 ** -0.5, reads=rdv + [b_KTh[h], b_QT[1][2]],
                     extra_fn=extra, pre=pre)
            J["post"] = (lambda h=h, J=J: plain_post(1, 2, h, J["acc"][0], J["acc"][1]))
            jobs.append(J)
        stream_attention(jobs, side_cb=lambda: side_step(2))

    def conv_module(l, g):
        if g == 1:
            for c in range(2):
                src = dram_ap(payB_out[l], R_HALO * T + c * 128 * 30, [[30, 128], [PB * T, 4], [1, 30]])
                add("sp", lambda e, src=src, c=c: e.dma_start(out=halo[:, c, :, :], in_=src), [b_payoB[l]], [b_halo], dma=True)
            for side in range(2):
                dst = gpad[1][:, :, 0, 0:15] if side == 0 else gpad[1][:, :, 1, 271:286]
                for rk in range(4):
                    srcv = halo[:, :, rk, 15:30] if side == 0 else halo[:, :, rk, 0:15]
                    sc = halsel_s[:, side * 4 + rk:side * 4 + rk + 1]
                    if rk == 0:
                        add("dve", lambda e, dst=dst, srcv=srcv, sc=sc: e.tensor_scalar(out=dst, in0=srcv, scalar1=sc, scalar2=None, op0=ALU.mult),
                            [b_halo, b_prm, b_gpad[1]], [b_gpad[1]])
                    else:
                        add("dve", lambda e, dst=dst, srcv=srcv, sc=sc: e.scalar_tensor_tensor(out=dst, in0=srcv, scalar=sc, in1=dst,
                                                                                              op0=ALU.mult, op1=ALU.add),
                            [b_halo, b_prm, b_gpad[1]], [b_gpad[1]])
        for c in range(2):
            for j in range(31):
                if g == 0:
                    src = gpad[0][:, c, :, j:j + 256]
                    dst = cacc[:, c, :].rearrange("p (s t) -> p s t", s=2)
                    ops = [(src, dst)]
                else:
                    ops = []
                    n_a = max(0, min(256, 271 - j))
                    if n_a > 0:
                        ops.append((gpad[1][:, c, 0, j:j + n_a], cacc[:, c, 0:n_a]))
                    if n_a < 256:
                        ops.append((gpad[1][:, c, 1, 15 + (n_a + j - 271):15 + (256 + j - 271)], cacc[:, c, n_a:256]))
                    n_b = max(0, min(256, 271 - (256 + j)))
                    if n_b > 0:
                        ops.append((gpad[1][:, c, 0, 256 + j:256 + j + n_b], cacc[:, c, 256:256 + n_b]))
                    ops.append((gpad[1][:, c, 1, 15 + (256 + n_b + j - 271):15 + (512 + j - 271)], cacc[:, c, 256 + n_b:512]))
                if j % 4 == 3:
                    yield
                for (src, dst) in ops:
                    if j == 0:
                        add("dve", lambda e, src=src, dst=dst, c=c: e.tensor_scalar(out=dst, in0=src, scalar1=dw_s[:, l, c, 0:1], scalar2=cb_s[:, l, c:c + 1],
                                                                                    op0=ALU.mult, op1=ALU.add), [b_gpad[g], b_prm, b_cacc], [b_cacc])
                    else:
                        add("dve", lambda e, src=src, dst=dst, c=c, j=j: e.scalar_tensor_tensor(out=dst, in0=src, scalar=dw_s[:, l, c, j:j + 1], in1=dst,
                                                                                                op0=ALU.mult, op1=ALU.add), [b_gpad[g], b_prm, b_cacc], [b_cacc])
        p1, pb1 = gp()
        p2, pb2 = gp()
        for c in range(2):
            tt, tb_ = tmp()
            add("act", lambda e, tt=tt, c=c: e.activation(out=tt[:], in_=cacc[:, c, :], func=AF.Square), [b_cacc], [tb_])
            add("pe", lambda e, c=c: e.matmul(p1[:, :], lhsT=ones_f[:, :], rhs=cacc[:, c, :], start=(c == 0), stop=(c == 1)), [b_cacc, b_ones_f], [pb1])
            add("pe", lambda e, tt=tt, c=c: e.matmul(p2[:, :], lhsT=ones_f[:, :], rhs=tt[:], start=(c == 0), stop=(c == 1)), [tb_, b_ones_f], [pb2])
        mean, bmean = tmp()
        var, bvar = tmp()
        add("act", lambda e: e.activation(out=mean[:], in_=p1[:, :], func=AF.Identity, scale=1.0 / 256), [pb1], [bmean])
        add("dve", lambda e: e.tensor_tensor(out=var[:], in0=mean[:], in1=mean[:], op=ALU.mult), [bmean], [bvar])
        add("dve", lambda e: e.scalar_tensor_tensor(out=var[:], in0=p2[:, :], scalar=1.0 / 256, in1=var[:], op0=ALU.mult, op1=ALU.subtract),
            [pb2, bvar], [bvar])
        add("act", lambda e: e.activation(out=var[:], in_=var[:], func=AF.Ln, bias=epsc[:, 0:1], scale=1.0), [bvar, b_epsc], [bvar])
        add("act", lambda e: e.activation(out=var[:], in_=var[:], func=AF.Exp, scale=-0.5), [bvar], [bvar])
        for c in range(2):
            add("dve", lambda e, c=c: e.tensor_tensor(out=cacc[:, c, :], in0=cacc[:, c, :], in1=mean[:], op=ALU.subtract), [b_cacc, bmean], [b_cacc])
            add("dve", lambda e, c=c: e.tensor_tensor(out=cacc[:, c, :], in0=cacc[:, c, :], in1=var[:], op=ALU.mult), [b_cacc, bvar], [b_cacc])
            add("act", lambda e, c=c: e.activation(out=cacc[:, c, :], in_=cacc[:, c, :], func=AF.Identity, bias=lnb_s[:, l, c:c + 1],
                                                   scale=lng_s[:, l, c:c + 1]), [b_cacc, b_prm], [b_cacc])
            tt, tb_ = tmp()
            add("act", lambda e, tt=tt, c=c: e.activation(out=tt[:], in_=cacc[:, c, :], func=AF.Sigmoid), [b_cacc], [tb_])
            add("dve", lambda e, tt=tt, c=c: e.tensor_tensor(out=mixT[g][:, 6 + c, :], in0=cacc[:, c, :], in1=tt[:], op=ALU.mult),
                [b_cacc, tb_], [b_mix[g][6 + c]])

    def out_proj(l, groups):
        for ti in range(2):
            wt, wb = wtile(w_out[l, :, ti * 512:(ti + 1) * 512], 512)
            for cc in range(4):
                oc = ti * 4 + cc
                for g in groups:
                    pt, pb = gp()
                    for k in range(8):
                        add("pe", lambda e, pt=pt, k=k, cc=cc, wt=wt, g=g: e.matmul(pt[:, :], lhsT=wt[:, k, cc * 128:(cc + 1) * 128], rhs=mixT[g][:, k, :],
                                                                                   start=(k == 0), stop=(k == 7)), [wb, b_mix[g][k]], [pb])
                    add("dve", lambda e, pt=pt, g=g, oc=oc: e.scalar_tensor_tensor(out=xT[g][:, oc, :], in0=pt[:, :], scalar=mcol(l, "g1", oc, g),
                                                                                 in1=xT[g][:, oc, :], op0=ALU.mult, op1=ALU.add),
                        [pb, b_mod[l], b_xT[g][oc]], [b_xT[g][oc]])

    def ffn(l, groups):
        for blk in range(4):
            for ti in range(2):
                wt, wb = wtile(w_ff1[l, :, blk * 1024 + ti * 512: blk * 1024 + (ti + 1) * 512], 512)
                for cc in range(4):
                    fc = ti * 4 + cc
                    for g in groups:
                        pt, pb = gp()
                        for k in range(8):
                            add("pe", lambda e, pt=pt, k=k, cc=cc, wt=wt, g=g: e.matmul(pt[:, :], lhsT=wt[:, k, cc * 128:(cc + 1) * 128], rhs=hT[g][:, k, :],
                                                                                       start=(k == 0), stop=(k == 7)), [wb, b_hT[g]], [pb])
                        sq, bq = sqt()
                        add("act", lambda e, pt=pt, sq=sq: e.activation(out=sq[:], in_=pt[:, :], func=AF.Relu), [pb], [bq])
                        add("dve", lambda e, sq=sq, g=g, fc=fc: e.tensor_tensor(out=mixT[g][:, fc, :], in0=sq[:], in1=sq[:], op=ALU.mult),
                            [bq], [b_mix[g][fc]])
            for ti in range(2):
                wt, wb = wtile(w_ff2[l, blk * 1024:(blk + 1) * 1024, ti * 512:(ti + 1) * 512], 512)
                for cc in range(4):
                    oc = ti * 4 + cc
                    for g in groups:
                        pt, pb = gp()
                        for k in range(8):
                            add("pe", lambda e, pt=pt, k=k, cc=cc, wt=wt, g=g: e.matmul(pt[:, :], lhsT=wt[:, k, cc * 128:(cc + 1) * 128], rhs=mixT[g][:, k, :],
                                                                                       start=(k == 0), stop=(k == 7)), [wb, b_mix[g][k]], [pb])
                        add("dve", lambda e, pt=pt, g=g, oc=oc: e.scalar_tensor_tensor(out=xT[g][:, oc, :], in0=pt[:, :], scalar=mcol(l, "g2", oc, g),
                                                                                     in1=xT[g][:, oc, :], op0=ALU.mult, op1=ALU.add),
                            [pb, b_mod[l], b_xT[g][oc]], [b_xT[g][oc]])

    def dump8(name, t, bufs):
        if name not in dbg:
            return
        o = dout("dbg_" + name, [128, 8, T])
        for j in range(8):
            tt, tb_ = tmp()
            add("dve", lambda e, tt=tt, j=j: e.tensor_copy(out=tt[:], in_=t[:, j, :]), [bufs[j]], [tb_])
            add("sp", lambda e, tt=tt, j=j: e.dma_start(out=o[:, j, :], in_=tt[:]), [tb_], (), dma=True)
        dbg_out[name] = [128, 8, T]

    groups = [0, 1] if do_lat else [0]
    import os as _os
    _l1 = _os.environ.get("L1STAGES")
    _stages0 = stages
    for l in range(nlayers):
        stages = _stages0 if (l == 0 or _l1 is None) else set(_l1.split(","))
        if "mod" in stages and l == 0:
            for _ in modulation(0):
                pass
        if "norm1" in stages:
            for g in groups:
                rmsnorm_mod(l, g, 0)
        if "proj" in stages:
            input_proj(l)
        side = modulation(l + 1) if ("mod" in stages and l + 1 < nlayers) else None
        if "ctxattn" in stages:
            ctx_attention(l, None if (do_lat and "latattn" in stages) else side)
        lat_on = do_lat and "latattn" in stages
        if "conv" in stages and not lat_on:
            for _ in conv_module(l, 0):
                pass
        dump8("mix0%d" % l, mixT[0], b_mix[0])
        if do_lat:
            side2 = itertools.chain(conv_module(l, 0), conv_module(l, 1)) if "conv" in stages else None
            if "latattn" in stages:
                lat_attention(l, side2, side)
            if side2 is not None:
                for _ in side2:
                    pass
        if side is not None:
            for _ in side:
                pass
            dump8("mix1%d" % l, mixT[1], b_mix[1])
        if "outproj" in stages:
            out_proj(l, groups)
        for g in groups:
            dump8("xattn%d%d" % (g, l), xT[g], b_xT[g])
        if "norm2" in stages:
            for g in groups:
                rmsnorm_mod(l, g, 1)
        if "ffn" in stages:
            ffn(l, groups)
        for g in groups:
            dump8("xffn%d%d" % (g, l), xT[g], b_xT[g])
    for g in groups:
        if "final" not in stages:
            continue
        pt, pb = gp()
        for j in range(8):
            sq, bq = sqt()
            add("act", lambda e, sq=sq, j=j, g=g: e.activation(out=sq[:], in_=xT[g][:, j, :], func=AF.Square), [b_xT[g][j]], [bq])
            add("pe", lambda e, sq=sq, j=j, pt=pt: e.matmul(pt[:, :], lhsT=ones_b[:, :], rhs=sq[:], start=(j == 0), stop=(j == 7)), [bq, b_ones_b], [pb])
        add("act", lambda e, pt=pt: e.activation(out=rstd[:], in_=pt[:, :], func=AF.Ln, bias=epsc[:, 0:1], scale=1.0 / D), [pb, b_epsc], [b_rstd])
        add("act", lambda e: e.activation(out=rstd[:], in_=rstd[:], func=AF.Exp, scale=-0.5), [b_rstd], [b_rstd])
        for j in range(8):
            add("dve", lambda e, j=j, g=g: e.scalar_tensor_tensor(out=xT[g][:, j, :], in0=xT[g][:, j, :], scalar=gfin_s[:, j:j + 1], in1=rstd[:],
                                                                  op0=ALU.mult, op1=ALU.mult), [b_xT[g][j], b_rstd, b_prm], [b_xT[g][j]])
    for g in groups:
        for tb in range(4):
            for half in range(2):
                pt, pb = gp()
                for jj in range(4):
                    j = half * 4 + jj
                    add("pe", lambda e, pt=pt, j=j, jj=jj, tb=tb, g=g: e.transpose(out=pt[:, jj * 128:(jj + 1) * 128],
                                                                                   in_=xT[g][:, j, tb * 128:(tb + 1) * 128], identity=ident[:]),
                        [b_xT[g][j], b_ident], [pb])
                stg, bstg = tmp()
                evac(stg[:, :], pt[:, :], [pb], [bstg])
                add("sp", lambda e, g=g, tb=tb, half=half, stg=stg: e.dma_start(
                    out=y_out[g * T + tb * 128:g * T + (tb + 1) * 128, half * 512:(half + 1) * 512], in_=stg[:, :]), [bstg], (), dma=True)
    S.emit(st)
    st.close()
    return nc, dbg_out


def _colT(v, nchunk):
    return np.ascontiguousarray(v.reshape(nchunk, 128).T)


def prepare_inputs(inp):
    f = lambda a: np.ascontiguousarray(np.asarray(a, dtype=np.float32))
    x_prompt, x_sample, c = f(inp["x_prompt"]), f(inp["x_sample"]), f(inp["c"])
    w_in = f(inp["w_in"])
    shared = {}
    shared["bmodT"] = np.ascontiguousarray(np.stack([_colT(f(inp["b_mod"])[l], 48) for l in range(L)], 1))
    shared["gmixT"] = np.ascontiguousarray(np.stack([_colT(f(inp["g_norm_mix"])[l], 8) for l in range(L)], 1))
    shared["gffT"] = np.ascontiguousarray(np.stack([_colT(f(inp["g_norm_ff"])[l], 8) for l in range(L)], 1))
    shared["gfinT"] = _colT(f(inp["g_final"]), 8)
    shared["w_mod"] = f(inp["w_mod"])
    shared["w_in"] = w_in
    idx = []
    for base in (0, 256):
        for h in range(4):
            for m in range(2):
                idx += [base + h * 64 + m * 32 + P32[j] for j in range(32)]
    idx += list(range(1088, 1152))
    idx += [1152 + P32[j] for j in range(32)]
    shared["w_inp"] = np.ascontiguousarray(w_in[:, :, idx])
    shared["w_out"] = f(inp["w_out"])
    shared["w_ff1"] = f(inp["w_ff1"])
    shared["w_ff2"] = f(inp["w_ff2"])
    wq = f(inp["w_mla_qup"])
    shared["wqup"] = wq
    wqp = np.zeros_like(wq)
    for h in range(4):
        for j in range(32):
            wqp[:, :, h * 96 + 64 + j] = wq[:, :, h * 96 + 64 + P32[j]]
    shared["wqupp"] = wqp
    shared["wkvup"] = f(inp["w_mla_kvup"])
    shared["gqT"] = np.ascontiguousarray(np.stack([_colT(f(inp["g_mla_q"])[l], 2) for l in range(L)], 1))
    shared["gkvc"] = np.ascontiguousarray(f(inp["g_mla_kv"]).T)
    shared["gkvr"] = np.ascontiguousarray(np.broadcast_to(f(inp["g_mla_kv"])[None], (128, L, 128)))
    gs = f(inp["g_da_subln"])
    shared["gsubc"] = np.ascontiguousarray(np.concatenate([gs.T, gs.T], 0))
    lamv = np.stack([f(inp["da_lambda_q1"]), f(inp["da_lambda_k1"]), f(inp["da_lambda_q2"]), f(inp["da_lambda_k2"])], 1)
    shared["lamv"] = np.ascontiguousarray(np.broadcast_to(lamv[None], (128, L, 4, 32)))
    dw = f(inp["conv_dw"])
    shared["dwT"] = np.ascontiguousarray(dw.reshape(L, 31, 2, 128).transpose(3, 0, 2, 1))
    for nm, key in (("cbT", "conv_b"), ("lngT", "conv_ln_g"), ("lnbT", "conv_ln_b")):
        shared[nm] = np.ascontiguousarray(f(inp[key]).reshape(L, 2, 128).transpose(2, 0, 1))
    shared["ident"] = np.eye(128, dtype=np.float32)
    w = np.arange(GRID_W)
    cs = np.clip(w - 8, 0, GRID_W - 16)
    col_ok = (w[None, :] >= cs[:, None]) & (w[None, :] < cs[:, None] + 16)
    cm = np.where(col_ok.T, 0.0, -BIG * 8).astype(np.float32)
    shared["colmask"] = np.ascontiguousarray(np.concatenate([cm, cm], 0))
    ri = np.zeros((32, DEC_S), np.float32)
    ri[np.arange(DEC_S) // 64, np.arange(DEC_S)] = 1.0
    shared["rowind"] = ri
    rpb = f(inp["na_rpb"])
    half = 8
    freqs = (10000.0 ** (-np.arange(half, dtype=np.float32) * 2.0 / 16)).astype(np.float32)
    maps = []
    for core in range(NCORE):
        b, r = core // 4, core % 4
        m = dict(shared)
        m["xin"] = np.ascontiguousarray(np.concatenate([x_prompt[2 * core], x_prompt[2 * core + 1], x_sample[b, r * T:(r + 1) * T]], 0))
        cv = np.stack([f(inp["c_ctx"]), c[b]], 1)
        m["cvT"] = np.ascontiguousarray(cv.reshape(8, 128, 2).transpose(1, 0, 2))
        m["c_dak"] = f(inp["cache_da_k"])[b]
        m["c_dav"] = f(inp["cache_da_v"])[b]
        m["c_ckv"] = f(inp["cache_mla_ckv"])[b]
        m["c_kr"] = f(inp["cache_mla_krope"])[b]
        m["c_nak"] = f(inp["cache_na_k"])[b]
        m["c_nav"] = f(inp["cache_na_v"])[b]
        t = r * T + np.arange(T)
        rows = (t // GRID_W).astype(np.float32)
        cols = (t % GRID_W).astype(np.float32)
        ang = np.concatenate([rows[None, :] * freqs[:, None], rows[None, :] * freqs[:, None],
                              cols[None, :] * freqs[:, None], cols[None, :] * freqs[:, None]], 0)
        cos32 = np.cos(ang).astype(np.float32)
        sin32 = np.sin(ang).astype(np.float32)
        sgn = np.concatenate([-np.ones(8), np.ones(8), -np.ones(8), np.ones(8)]).astype(np.float32)[:, None]
        m["cosT"] = np.ascontiguousarray(np.tile(cos32, (4, 1)))
        m["sinT"] = np.ascontiguousarray(np.tile(sin32 * sgn, (4, 1)))
        r0 = r * 8
        qrow = r0 + np.arange(T) // 64
        start = np.clip(qrow - 4, 0, NROWS - 8)
        rk = np.arange(32)
        ok = (rk[:, None] >= start[None, :]) & (rk[:, None] < start[None, :] + 8)
        m["rowsel"] = np.where(ok, 0.0, -BIG * 8).astype(np.float32)
        P = np.zeros((L, 4, A2 + 1, TZL), np.float32)
        for a2 in range(A2):
            a = 45 - a2 - r0
            if 0 <= a <= 14:
                P[:, :, a2, 48:79] = rpb[:, :, a, ::-1]
        m["tz_rep"] = np.ascontiguousarray(np.broadcast_to(P.reshape(L * 4 * (A2 + 1), 1, TZL), (L * 4 * (A2 + 1), 64, TZL)))
        hs = np.zeros((128, 8), np.float32)
        if r > 0:
            hs[:, r - 1] = 1.0
        if r < 3:
            hs[:, 4 + r + 1] = 1.0
        m["halsel"] = hs
        maps.append(m)
    return maps


_NC_CACHE = {}


def kernel(**inputs):
    maps = prepare_inputs(inputs)
    if "nc" not in _NC_CACHE:
        _NC_CACHE["nc"] = build()[0]
    nc = _NC_CACHE["nc"]
    res = run_bass_kernel_spmd(nc, maps, core_ids=list(range(NCORE)))
    R = res.results
    y_prompt = np.zeros((NB, SEQ, D), np.float32)
    y_sample = np.zeros((DEC_B, DEC_S, D), np.float32)
    outs = {k: [] for k in ("o_dak", "o_dav", "o_ckv", "o_kr", "o_nak", "o_nav")}
    for core in range(NCORE):
        b, r = core // 4, core % 4
        y = R[core]["y_out"]
        y_prompt[2 * core] = y[0:256]
        y_prompt[2 * core + 1] = y[256:512]
        y_sample[b, r * T:(r + 1) * T] = y[512:1024]
        for k in outs:
            outs[k].append(R[core][k])
    cat = lambda k: np.ascontiguousarray(np.concatenate(outs[k], 0).astype(np.float32))
    return (y_prompt, y_sample, cat("o_dak"), cat("o_dav"), cat("o_ckv"), cat("o_kr"), cat("o_nak"), cat("o_nav"))
```

```python
import contextlib
import itertools
import math
import numpy as np
import concourse.bass as bass
import concourse.mybir as mybir
from concourse.bass_utils import run_bass_kernel_spmd

F32 = mybir.dt.float32
BF16 = mybir.dt.bfloat16
ALU = mybir.AluOpType
AF = mybir.ActivationFunctionType

ENGINES = ("pe", "act", "dve", "pool", "sp")

D = 1024
L = 2
NB = 16
SEQ = 256
DEC_B = 2
DEC_S = 2048
PAST = 256
GRID_W = 64
NROWS = DEC_S // GRID_W
EPS = 1e-6
T = 512
NCORE = 8
BIG = 30000.0
A2 = 46
TZL = 127
PA = 672
PB = 527
R_DAK, R_CKV, R_KR, R_NAK = 0, 256, 384, 416
R_V, R_HALO = 0, 512
P32 = [8, 9, 10, 11, 12, 13, 14, 15, 0, 1, 2, 3, 4, 5, 6, 7,
       24, 25, 26, 27, 28, 29, 30, 31, 16, 17, 18, 19, 20, 21, 22, 23]


class Buf:
    __slots__ = ("name", "last_w", "readers", "excl")

    def __init__(self, name, excl=False):
        self.name = name
        self.last_w = None
        self.readers = []
        self.excl = excl


class Op:
    __slots__ = ("eng", "fn", "deps", "dma", "sig", "sem", "val")

    def __init__(self, eng, fn, dma):
        self.eng = eng
        self.fn = fn
        self.dma = dma
        self.deps = []
        self.sig = False
        self.sem = None
        self.val = 0


class Sched:
    def __init__(self, nc, n_dma_slots=16):
        self.nc = nc
        self.ops = []
        self.n_dma_slots = n_dma_slots

    def add(self, eng, fn, reads=(), writes=(), dma=False):
        op = Op(eng, fn, dma)
        ex = [b for b in reads if b.excl]
        if ex:
            reads = [b for b in reads if not b.excl]
            writes = list(writes) + ex
        deps = {}
        for b in reads:
            if b.last_w is not None:
                deps[id(b.last_w)] = b.last_w
        for b in writes:
            if b.last_w is not None:
                deps[id(b.last_w)] = b.last_w
            for r in b.readers:
                deps[id(r)] = r
        for b in reads:
            b.readers.append(op)
        for b in writes:
            b.last_w = op
            b.readers = []
        for d in deps.values():
            if d.eng == "pe" and eng == "pe" and not d.dma and not dma:
                continue
            op.deps.append(d)
            d.sig = True
        self.ops.append(op)
        return op

    def emit(self, stack):
        nc = self.nc
        eng_sem = {e: stack.enter_context(nc.semaphore("s_" + e)) for e in ENGINES}
        dma_engs = sorted({op.eng for op in self.ops if op.dma is True})
        dma_slots = {e: [stack.enter_context(nc.semaphore("d_%s_%d" % (e, i)))
                         for i in range(self.n_dma_slots)] for e in dma_engs}
        cnt = {e: 0 for e in ENGINES}
        dcnt = {e: 0 for e in dma_engs}
        slot_uses = {e: [0] * self.n_dma_slots for e in dma_engs}
        slot_prev = {}
        ncc = 0
        for op in self.ops:
            if op.dma == "cc":
                op.sem = stack.enter_context(nc.semaphore("cc_%d" % ncc))
                ncc += 1
                op.val = 1
            elif op.dma:
                k = dcnt[op.eng] % self.n_dma_slots
                dcnt[op.eng] += 1
                slot_uses[op.eng][k] += 1
                op.sem = dma_slots[op.eng][k]
                op.val = 16 * slot_uses[op.eng][k]
                slot_prev[id(op)] = (op.sem, op.val - 16)
            elif op.sig:
                cnt[op.eng] += 1
                op.sem = eng_sem[op.eng]
                op.val = cnt[op.eng]
        block = stack.enter_context(nc.Block())
        handles = {"pe": block.tensor, "act": block.scalar, "dve": block.vector,
                   "pool": block.gpsimd, "sp": block.sync}
        all_async = [op for op in self.ops if op.dma]

        def run_engine(ename):
            def body(eng):
                waited = {}
                for op in self.ops:
                    if op.eng != ename:
                        continue
                    need = {}
                    for d in op.deps:
                        key = id(d.sem)
                        if key not in need or need[key][1] < d.val:
                            need[key] = (d.sem, d.val)
                    if op.dma is True:
                        s, v = slot_prev[id(op)]
                        if v > 0:
                            key = id(s)
                            if key not in need or need[key][1] < v:
                                need[key] = (s, v)
                    for key, (s, v) in need.items():
                        if waited.get(key, 0) >= v:
                            continue
                        eng.wait_ge(s, v)
                        waited[key] = v
                    ins = op.fn(eng)
                    if op.dma == "cc":
                        ins.then_inc(op.sem)
                    elif op.dma:
                        ins.then_inc(op.sem, 16)
                    elif op.sig:
                        ins.then_inc(op.sem, 1)
                if ename == "sp":
                    last = {}
                    for op in all_async:
                        last[id(op.sem)] = (op.sem, op.val)
                    for key, (s, v) in last.items():
                        if waited.get(key, 0) < v:
                            eng.wait_ge(s, v)
                    for e in ENGINES:
                        if cnt[e] > 0:
                            eng.wait_ge(eng_sem[e], cnt[e])
            handles[ename](body)

        for e in ENGINES:
            run_engine(e)


def build(dbg=(), nlayers=L, do_lat=True, stages=None):
    if stages is None:
        stages = {"mod", "norm1", "proj", "ctxattn", "conv", "latattn", "outproj", "norm2", "ffn", "final"}
    nc = bass.Bass("TRN2", target_bir_lowering=False)
    st = contextlib.ExitStack()
    S = Sched(nc)
    dbg_out = {}

    def din(name, shape, dt=F32):
        return nc.dram_tensor(name, list(shape), dt, kind="ExternalInput").ap()

    def dout(name, shape, dt=F32):
        return nc.dram_tensor(name, list(shape), dt, kind="ExternalOutput").ap()

    xin = din("xin", [2 * T, D])
    cvT = din("cvT", [128, 8, 2])
    bmodT = din("bmodT", [128, L, 48])
    gmixT = din("gmixT", [128, L, 8])
    gffT = din("gffT", [128, L, 8])
    gfinT = din("gfinT", [128, 8])
    w_mod = din("w_mod", [L, D, 6 * D])
    w_in = din("w_in", [L, D, 2464])
    w_inp = din("w_inp", [L, D, 608])
    w_out = din("w_out", [L, D, D])
    w_ff1 = din("w_ff1", [L, D, 4 * D])
    w_ff2 = din("w_ff2", [L, 4 * D, D])
    wqup = din("wqup", [L, 256, 384])
    wqupp = din("wqupp", [L, 256, 384])
    wkvup = din("wkvup", [L, 128, 512])
    gqT = din("gqT", [128, L, 2])
    gkvc = din("gkvc", [128, L])
    gkvr = din("gkvr", [128, L, 128])
    gsubc = din("gsubc", [128, L])
    lamv = din("lamv", [128, L, 4, 32])
    tz_all = din("tz_rep", [L * 4 * (A2 + 1), 64, TZL]).tensor
    dwT = din("dwT", [128, L, 2, 31])
    cbT = din("cbT", [128, L, 2])
    lngT = din("lngT", [128, L, 2])
    lnbT = din("lnbT", [128, L, 2])
    c_dak = din("c_dak", [L, 4, PAST, 64])
    c_dav = din("c_dav", [L, 4, PAST, 64])
    c_ckv = din("c_ckv", [L, PAST, 128])
    c_kr = din("c_kr", [L, PAST, 32])
    c_nak = din("c_nak", [L, 4, PAST, 64])
    c_nav = din("c_nav", [L, 4, PAST, 64])
    cosT_d = din("cosT", [128, T])
    sinT_d = din("sinT", [128, T])
    colmask_d = din("colmask", [128, 64])
    rowsel_d = din("rowsel", [32, T])
    rowind_d = din("rowind", [32, DEC_S])
    halsel_d = din("halsel", [128, 8])
    ident_d = din("ident", [128, 128])
    y_out = dout("y_out", [2 * T, D])
    o_dak = dout("o_dak", [2, L, 4, SEQ, 64])
    o_dav = dout("o_dav", [2, L, 4, SEQ, 64])
    o_ckv = dout("o_ckv", [2, L, SEQ, 128])
    o_kr = dout("o_kr", [2, L, SEQ, 32])
    o_nak = dout("o_nak", [2, L, 4, SEQ, 64])
    o_nav = dout("o_nav", [2, L, 4, SEQ, 64])
    LROWS = 6 * PA + 6 * PB
    pay_all = nc.dram_tensor("pay_all", [L * LROWS, T], BF16)
    O_AIN, O_AOUT, O_BIN, O_BOUT = 0, PA, 5 * PA, 5 * PA + PB

    class _Sub:
        def __init__(self, row0, nrows):
            self.row0, self.nrows = row0, nrows

        def ap(self):
            return pay_all.ap()[self.row0:self.row0 + self.nrows, :]

    payA_in = [_Sub(l * LROWS + O_AIN, PA) for l in range(L)]
    payA_out = [_Sub(l * LROWS + O_AOUT, 4 * PA) for l in range(L)]
    payB_in = [_Sub(l * LROWS + O_BIN, PB) for l in range(L)]
    payB_out = [_Sub(l * LROWS + O_BOUT, 4 * PB) for l in range(L)]
    TZ_SZ = 4 * (A2 + 1) * 64 * TZL

    def dram_ap(base, off, dims):
        if isinstance(base, _Sub):
            return bass.AP(pay_all, base.row0 * T + off, dims)
        return bass.AP(base, off, dims)

    def sb(name, shape, dt):
        return st.enter_context(nc.sbuf_tensor("sb_" + name, list(shape), dt))

    add = S.add

    banks = []
    for i in range(8):
        banks.append((st.enter_context(nc.psum_tensor("ps%d" % i, [128, 512], F32)), Buf("ps%d" % i, excl=True)))
    gp_i = [0]
    ac_i = [0]

    def gp():
        b = banks[(0, 1, 2, 7)[gp_i[0] % 4]]
        gp_i[0] += 1
        return b

    def acb():
        b = banks[4 + ac_i[0] % 3]
        ac_i[0] += 1
        return b

    ev_i = [0]

    def evac(dst, src, r, w, scale=None):
        ev_i[0] += 1
        if ev_i[0] % 2 == 0:
            if scale is None:
                add("act", lambda e: e.activation(out=dst, in_=src, func=AF.Copy), r, w)
            else:
                add("act", lambda e: e.activation(out=dst, in_=src, func=AF.Identity, scale=scale), r, w)
        else:
            if scale is None:
                add("dve", lambda e: e.tensor_copy(out=dst, in_=src), r, w)
            else:
                add("dve", lambda e: e.tensor_scalar(out=dst, in0=src, scalar1=scale, scalar2=None, op0=ALU.mult), r, w)

    ident = sb("ident", [128, 128], F32); b_ident = Buf("ident")
    identb = sb("identb", [128, 128], BF16); b_identb = Buf("identb")
    ones_f = sb("ones_f", [128, 128], F32); b_ones_f = Buf("ones_f")
    ones_b = sb("ones_b", [128, 128], BF16); b_ones_b = Buf("ones_b")
    epsc = sb("epsc", [128, 1], F32); b_epsc = Buf("epsc")
    add("sp", lambda e: e.dma_start(out=ident[:], in_=ident_d), (), [b_ident], dma=True)
    add("pool", lambda e: e.memset(ones_f[:], 1.0), (), [b_ones_f])
    add("pool", lambda e: e.memset(ones_b[:], 1.0), (), [b_ones_b])
    add("pool", lambda e: e.memset(epsc[:], EPS), (), [b_epsc])
    add("dve", lambda e: e.tensor_copy(out=identb[:], in_=ident[:]), [b_ident], [b_identb])

    prm = {}
    b_prm = Buf("prm")
    b_small = []

    def load_small(name, src, shape):
        t = sb(name, shape, F32)
        bt = Buf("ld_" + name)
        b_small.append(bt)
        add("sp", lambda e: e.dma_start(out=t[:], in_=src), (), [bt], dma=True)
        prm[name] = t
        return t

    cv_s = load_small("cv_s", cvT, [128, 8, 2])
    bmod_s = load_small("bmod_s", bmodT, [128, L, 48])
    gmix_s = load_small("gmix_s", gmixT, [128, L, 8])
    gff_s = load_small("gff_s", gffT, [128, L, 8])
    gfin_s = load_small("gfin_s", gfinT, [128, 8])
    gq_s = load_small("gq_s", gqT, [128, L, 2])
    gkvc_s = load_small("gkvc_s", gkvc, [128, L])
    gkvr_s = sb("gkvr_s", [128, L, 128], BF16)
    b_small.append(Buf("ld_gkvr"))
    add("pool", lambda e: e.dma_start(out=gkvr_s[:], in_=gkvr), (), [b_small[-1]], dma=True)
    gsub_s = load_small("gsub_s", gsubc, [128, L])
    lam_s = load_small("lam_s", lamv, [128, L, 4, 32])
    dw_s = load_small("dw_s", dwT, [128, L, 2, 31])
    cb_s = load_small("cb_s", cbT, [128, L, 2])
    lng_s = load_small("lng_s", lngT, [128, L, 2])
    lnb_s = load_small("lnb_s", lnbT, [128, L, 2])
    cos_s = load_small("cos_s", cosT_d, [128, T])
    sin_s = load_small("sin_s", sinT_d, [128, T])
    colm_s = load_small("colm_s", colmask_d, [128, 64])
    halsel_s = load_small("halsel_s", halsel_d, [128, 8])
    joinc = sb("joinc", [128, 1], F32)
    add("dve", lambda e: e.memset(joinc[:], 0.0), list(b_small), [b_prm])

    lams = sb("lams", [128, L, 2], F32)
    neglam = sb("neglam", [128, L], F32)
    gsub2 = sb("gsub2", [128, L], F32)
    for l in range(L):
        lam_init = 0.8 - 0.6 * math.exp(-0.3 * l)
        for m in range(2):
            add("dve", lambda e, l=l, m=m: e.tensor_tensor(out=lam_s[:, l, 2 * m, :], in0=lam_s[:, l, 2 * m, :],
                                                          in1=lam_s[:, l, 2 * m + 1, :], op=ALU.mult), [b_prm], [b_prm])
            add("dve", lambda e, l=l, m=m: e.reduce_sum(out=lams[:, l, m:m + 1], in_=lam_s[:, l, 2 * m, :],
                                                       axis=mybir.AxisListType.X), [b_prm], [b_prm])
        add("act", lambda e, l=l: e.activation(out=lams[:, l, :], in_=lams[:, l, :], func=AF.Exp), [b_prm], [b_prm])
        add("dve", lambda e, l=l: e.tensor_tensor(out=neglam[:, l:l + 1], in0=lams[:, l, 1:2], in1=lams[:, l, 0:1],
                                                 op=ALU.subtract), [b_prm], [b_prm])
        add("dve", lambda e, l=l, li=lam_init: e.tensor_scalar(out=neglam[:, l:l + 1], in0=neglam[:, l:l + 1],
                                                              scalar1=-li, scalar2=None, op0=ALU.add), [b_prm], [b_prm])
        add("dve", lambda e, l=l, li=lam_init: e.tensor_scalar(out=gsub2[:, l:l + 1], in0=gsub_s[:, l:l + 1],
                                                              scalar1=1.0 - li, scalar2=None, op0=ALU.mult), [b_prm], [b_prm])

    xT = [sb("xT%d" % g, [128, 8, T], F32) for g in range(2)]
    b_xT = [[Buf("xT%d_%d" % (g, j)) for j in range(8)] for g in range(2)]
    hT = [sb("hT%d" % g, [128, 8, T], BF16) for g in range(2)]
    b_hT = [Buf("hT%d" % g) for g in range(2)]
    mixT = [sb("mixT%d" % g, [128, 8, T], BF16) for g in range(2)]
    b_mix = [[Buf("mix%d_%d" % (g, j)) for j in range(8)] for g in range(2)]
    NW = 3
    wpool = [sb("wp%d" % i, [128, 8, 512], BF16) for i in range(NW)]
    b_wp = [Buf("wp%d" % i) for i in range(NW)]
    wp_i = [0]

    def wtile(src_ap, ncols, nk=8):
        i = wp_i[0] % NW
        wp_i[0] += 1
        t, b = wpool[i], b_wp[i]
        add("pool", lambda e: e.dma_start(out=t[:, 0:nk, 0:ncols], in_=src_ap.rearrange("(k p) c -> p k c", p=128)),
            (), [b], dma=True)
        return t, b

    qd = sb("qd", [128, 2, T], F32); b_qd = Buf("qd_cacc_stage")
    stage1 = qd[:, :, :].rearrange("p a b -> p (a b)")

    class _Stage:
        def __getitem__(self, key):
            p, sl, c = key
            return stage1[p, c]
    stage = _Stage()
    b_stage = [b_qd, b_qd]
    rstd = sb("rstd", [128, T], F32); b_rstd = Buf("rstd")
    tmpf = [sb("tmpf%d" % i, [128, T], F32) for i in range(4)]
    b_tmpf = [Buf("tmpf%d" % i) for i in range(4)]
    tf_i = [0]

    def tmp():
        i = tf_i[0] % 4
        tf_i[0] += 1
        return tmpf[i], b_tmpf[i]

    sqb = [sb("sqb%d" % i, [128, T], BF16) for i in range(2)]
    b_sqb = [Buf("sqb%d" % i) for i in range(2)]
    sq_i = [0]

    def sqt():
        i = sq_i[0] % 2
        sq_i[0] += 1
        return sqb[i], b_sqb[i]

    for g in range(2):
        for tb in range(4):
            for half in range(2):
                stg, bstg = tmp()
                add("sp", lambda e, g=g, tb=tb, half=half, stg=stg: e.dma_start(
                    out=stg[:, :], in_=xin[g * T + tb * 128: g * T + (tb + 1) * 128, half * 512:(half + 1) * 512]), (), [bstg], dma=True)
                pt, pb = gp()
                for jj in range(4):
                    add("pe", lambda e, pt=pt, stg=stg, jj=jj: e.transpose(out=pt[:, jj * 128:(jj + 1) * 128],
                                                                           in_=stg[:, jj * 128:(jj + 1) * 128], identity=ident[:]),
                        [bstg, b_ident], [pb])
                evac(xT[g][:, half * 4:half * 4 + 4, tb * 128:(tb + 1) * 128],
                     pt[:, :].rearrange("p (j t) -> p j t", j=4), [pb], [b_xT[g][half * 4 + jj] for jj in range(4)])

    sil = sb("sil", [128, 8, 2], BF16); b_sil = Buf("sil")
    add("act", lambda e: e.activation(out=sil[:], in_=cv_s[:], func=AF.Silu), [b_prm], [b_sil])
    modv = sb("modv", [128, L, 48, 2], F32)
    gsc = sb("gsc", [128, L, 2, 8, 2], F32)
    b_mod = [Buf("mod%d" % l) for l in range(L)]

    def modulation(l):
        pt, pb = banks[3]
        for ti in range(12):
            wt, wb = wtile(w_mod[l, :, ti * 512:(ti + 1) * 512], 512)
            for cc in range(4):
                n = ti * 4 + cc
                for k in range(8):
                    add("pe", lambda e, wt=wt, cc=cc, k=k, n=n, pt=pt: e.matmul(pt[:, n * 2:n * 2 + 2], lhsT=wt[:, k, cc * 128:(cc + 1) * 128],
                                                                             rhs=sil[:, k, :], start=(k == 0), stop=(k == 7)),
                        [wb, b_sil], [pb])
            yield
        add("dve", lambda e, pt=pt: e.tensor_tensor(out=modv[:, l, :, :], in0=pt[:, 0:96].rearrange("p (n v) -> p n v", v=2),
                                                   in1=bmod_s[:, l, :].unsqueeze(2).to_broadcast([128, 48, 2]), op=ALU.add),
            [pb, b_prm], [b_mod[l]])
        for ni, (gsrc, off) in enumerate(((gmix_s, 8), (gff_s, 32))):
            add("dve", lambda e, ni=ni, gsrc=gsrc, off=off: e.scalar_tensor_tensor(
                out=gsc[:, l, ni, :, :], in0=modv[:, l, off:off + 8, :], scalar=1.0,
                in1=gsrc[:, l, :].unsqueeze(2).to_broadcast([128, 8, 2]), op0=ALU.add, op1=ALU.mult),
                [b_mod[l], b_prm], [b_mod[l]])

    def mcol(l, which, j, v):
        off = {"sh1": 0, "sc1": 8, "g1": 16, "sh2": 24, "sc2": 32, "g2": 40}[which]
        return modv[:, l, off + j, v:v + 1]

    def rmsnorm_mod(l, g, ni):
        pt, pb = gp()
        for j in range(8):
            sq, bq = sqt()
            add("act", lambda e, sq=sq, j=j: e.activation(out=sq[:], in_=xT[g][:, j, :], func=AF.Square), [b_xT[g][j]], [bq])
            add("pe", lambda e, sq=sq, j=j, pt=pt: e.matmul(pt[:, :], lhsT=ones_b[:, :], rhs=sq[:], start=(j == 0), stop=(j == 7)),
                [bq, b_ones_b], [pb])
        add("act", lambda e, pt=pt: e.activation(out=rstd[:], in_=pt[:, :], func=AF.Ln, bias=epsc[:, 0:1], scale=1.0 / D),
            [pb, b_epsc], [b_rstd])
        add("act", lambda e: e.activation(out=rstd[:], in_=rstd[:], func=AF.Exp, scale=-0.5), [b_rstd], [b_rstd])
        for j in range(8):
            tt, tb_ = tmp()
            add("dve", lambda e, tt=tt, j=j: e.scalar_tensor_tensor(out=tt[:], in0=xT[g][:, j, :], scalar=gsc[:, l, ni, j, g:g + 1],
                                                                    in1=rstd[:], op0=ALU.mult, op1=ALU.mult),
                [b_xT[g][j], b_rstd, b_mod[l]], [tb_])
            add("act", lambda e, tt=tt, j=j: e.activation(out=hT[g][:, j, :], in_=tt[:], func=AF.Identity,
                                                          bias=mcol(l, "sh1" if ni == 0 else "sh2", j, g), scale=1.0),
                [tb_, b_mod[l]], [b_hT[g]])

    def stat_rstd(src_list, nfeat, dst, b_dst, reads, K=128, n=T):
        pt, pb = gp()
        for i, src in enumerate(src_list):
            tt, tb_ = tmp()
            add("act", lambda e, tt=tt, src=src: e.activation(out=tt[0:K, 0:n], in_=src, func=AF.Square), reads, [tb_])
            add("pe", lambda e, tt=tt, i=i, pt=pt: e.matmul(pt[0:K, 0:n], lhsT=ones_f[0:K, 0:K], rhs=tt[0:K, 0:n],
                                                          start=(i == 0), stop=(i == len(src_list) - 1)),
                [tb_, b_ones_f], [pb])
        add("act", lambda e, pt=pt: e.activation(out=dst, in_=pt[0:K, 0:n], func=AF.Ln, bias=epsc[0:K, 0:1], scale=1.0 / nfeat),
            [pb, b_epsc], [b_dst])
        add("act", lambda e: e.activation(out=dst, in_=dst, func=AF.Exp, scale=-0.5), [b_dst], [b_dst])

    KTb = sb("KTb", [128, 4, DEC_S + PAST], BF16); b_KTh = [Buf("KT%d" % h) for h in range(4)]
    VAb = sb("VAb", [128, 18, 4, 72], BF16); b_VAk = [Buf("VA%d" % k) for k in range(18)]
    KTc2 = [sb("KTc%d" % m, [128, 4, T], BF16) for m in range(2)]
    b_KTc = [Buf("KTc%d" % m) for m in range(3)]
    VAc = [sb("VAc%d" % m, [128, 4, 4, 72], BF16) for m in range(3)]; b_VAc = [Buf("VAc%d" % m) for m in range(3)]
    QT2 = [[sb("QT%d_%d" % (g, m), [128, 4, T], BF16) for m in range(2)] for g in range(2)]
    b_QT = [[Buf("QT%d_%d" % (g, m)) for m in range(3)] for g in range(2)]
    add("pool", lambda e: e.memset(VAb[:, :, :, 64:72], 1.0), (), b_VAk)
    add("pool", lambda e: e.memset(KTb[64:128, :, :], 0.0), (), b_KTh)
    add("pool", lambda e: e.memset(QT2[1][1][96:128, :, :], 0.0), (), [b_QT[1][1]])
    for m in range(3):
        add("pool", lambda e, m=m: e.memset(VAc[m][:, :, :, 64:72], 1.0), (), [b_VAc[m]])

    def q_ap(g, m, h, p_lo, p_hi, c0=0, n=T):
        if m == 0:
            return QT2[g][0][p_lo:p_hi, h, c0:c0 + n]
        if m == 1:
            return QT2[g][1][p_lo:p_hi, h, c0:c0 + n]
        return QT2[g][0][64 + p_lo:64 + p_hi, h, c0:c0 + n]

    def kc_ap(m, h, p_lo, p_hi, c0, n):
        if m == 0:
            return KTc2[0][p_lo:p_hi, h, c0:c0 + n]
        if m == 1:
            return KTc2[1][p_lo:p_hi, h, c0:c0 + n]
        return KTc2[0][64 + p_lo:64 + p_hi, h, c0:c0 + n]
    NE = 4
    Eb = [sb("E%d" % i, [128, T], BF16) for i in range(NE)]
    b_E = [Buf("E%d" % i) for i in range(NE)]
    e_i = [0]
    NSET = 2
    rsum_s = [sb("rsum0", [128, T], F32)] * 2; b_rsum_s = [Buf("rsum0")] * 2
    rsb0 = sb("rsb0", [128, T], BF16)
    rsb_row = [64, 64]
    b_rsb_s = [Buf("rsb0")] * 2
    dao = sb("dao", [128, T], F32); b_dao = Buf("dao")
    set_i = [0]
    ckvT = sb("ckvT", [128, DEC_S + PAST], BF16); b_ckvT = Buf("ckvT")
    ckvTc = ckvT; b_ckvTc = b_ckvT
    qdn = sb("qdn", [128, 2, T], BF16); b_qdn = Buf("qdn")
    wq_s = sb("wq_s", [128, 2, 384], BF16); wqp_s = sb("wqp_s", [128, 2, 384], BF16); wkv_s = sb("wkv_s", [128, 512], BF16)
    b_wsm = Buf("wsmall")
    gpad = [sb("gpad%d" % g, [128, 2, 2, 15 + 256 + 15], BF16) for g in range(2)]
    b_gpad = [Buf("gpad%d" % g) for g in range(2)]
    for g in range(2):
        add("pool", lambda e, g=g: e.memset(gpad[g][:], 0.0), (), [b_gpad[g]])
    cacc = qd; b_cacc = b_qd
    halo = sb("halo", [128, 2, 4, 30], BF16); b_halo = Buf("halo")
    NTBT = 2
    A2U = 39
    tbt = [sb("tbt%d" % i, [128, A2U, 64], BF16) for i in range(NTBT)] * (2 // NTBT)
    b_tbt = [Buf("tbt%d" % i) for i in range(NTBT)] * (2 // NTBT)
    b_payA = [Buf("payA%d" % l) for l in range(L)]
    b_payB = [Buf("payB%d" % l) for l in range(L)]
    b_payoA = [Buf("payoA%d" % l) for l in range(L)]
    b_payoB = [Buf("payoB%d" % l) for l in range(L)]
    b_tz = [Buf("tz%d" % l) for l in range(L)]

    pending_post = [None]
    import os as _os2
    WARM_LAT = int(_os2.environ.get("WARM_LAT", "0"))
    WARM_CTX = int(_os2.environ.get("WARM_CTX", "0"))
    WARM_N = int(_os2.environ.get("WARM_N", "256"))
    BURST = int(_os2.environ.get("BURST", "20"))

    def warm(n):
        for _ in range(n):
            add("pe", lambda e: e.matmul(banks[7][0][:, 512 - WARM_N:512], lhsT=identb[:, 0:128], rhs=identb[:, 0:128], start=True, stop=True), (), ())

    def flush_post():
        if pending_post[0] is not None:
            f = pending_post[0]
            pending_post[0] = None
            f()

    def attention(kt_fn, q_ap, va_fn, nkb, nq, scale, reads, extra_fn=None):
        at, ab = acb()
        pend = []

        def score(kb):
            pt, pb = gp()
            ex = extra_fn(kb) if extra_fn is not None else []
            add("pe", lambda e, pt=pt, kb=kb: e.matmul(pt[:, 0:nq], lhsT=kt_fn(kb), rhs=q_ap, start=True, stop=(len(ex) == 0)),
                reads, [pb])
            for i, (lh, rh, rd) in enumerate(ex):
                add("pe", lambda e, pt=pt, lh=lh, rh=rh, i=i: e.matmul(pt[:, 0:nq], lhsT=lh, rhs=rh, start=False, stop=(i == len(ex) - 1)),
                    rd, [pb])
            i = e_i[0] % NE
            e_i[0] += 1
            add("act", lambda e, pt=pt, i=i: e.activation(out=Eb[i][:, 0:nq], in_=pt[:, 0:nq], func=AF.Exp, scale=scale), [pb], [b_E[i]])
            return i

        for kb in range(min(2, nkb)):
            pend.append(score(kb))
        flush_post()
        for kb in range(nkb):
            if kb + 2 < nkb:
                pend.append(score(kb + 2))
            warm(WARM_LAT)
            i = pend[kb]
            add("pe", lambda e, at=at, kb=kb, i=i: e.matmul(at[0:65, 0:nq], lhsT=va_fn(kb), rhs=Eb[i][:, 0:nq], start=(kb == 0), stop=(kb == nkb - 1)),
                reads + [b_E[i]], [ab])
        return at, ab

    def normalize_rep(at, ab, nq, dst, b_dst_list):
        r, br = tmp()
        add("act", lambda e: e.activation(out=r[0:64, 0:nq], in_=at[0:64, 256:256 + nq], func=AF.Ln), [ab], [br])
        add("act", lambda e: e.activation(out=r[0:64, 0:nq], in_=r[0:64, 0:nq], func=AF.Exp, scale=-1.0), [br], [br])
        add("dve", lambda e: e.tensor_tensor(out=dst, in0=at[0:64, 0:nq], in1=r[0:64, 0:nq], op=ALU.mult), [ab, br], b_dst_list)

    def normalize(at, ab, nq, dst, b_dst_list, k=0, c0=0):
        rsum, b_rsum, b_rsb, rr = rsum_s[k], b_rsum_s[k], b_rsb_s[k], rsb_row[k]
        add("act", lambda e: e.activation(out=rsum[64:65, 0:nq], in_=at[64:65, c0:c0 + nq], func=AF.Ln), [ab], [b_rsum])
        add("act", lambda e: e.activation(out=rsb0[rr:rr + 1, 0:nq], in_=rsum[64:65, 0:nq], func=AF.Exp, scale=-1.0), [b_rsum], [b_rsb])
        pt, pb = gp()
        add("pe", lambda e: e.matmul(pt[0:64, 0:nq], lhsT=ones_b[rr:rr + 1, 0:64], rhs=rsb0[rr:rr + 1, 0:nq], start=True, stop=True),
            [b_rsb, b_ones_b], [pb])
        bc, bbc = tmp()
        add("act", lambda e: e.activation(out=bc[0:64, 0:nq], in_=pt[0:64, 0:nq], func=AF.Copy), [pb], [bbc])
        add("dve", lambda e: e.tensor_tensor(out=dst, in0=at[0:64, c0:c0 + nq], in1=bc[0:64, 0:nq], op=ALU.mult), [ab, bbc], b_dst_list)

    def stream_attention(jobs, nkb=18, nq=T, side_cb=None, burst=0):
        blocks = [(ji, kb) for ji in range(len(jobs)) for kb in range(nkb)]
        N = len(blocks)
        pend = {}
        posts = {}

        def do_score(idx):
            ji, kb = blocks[idx]
            J = jobs[ji]
            if kb == 0 and J.get("pre") is not None:
                J["pre"]()
            pt, pb = gp()
            ex = J["extra_fn"](kb) if J.get("extra_fn") is not None else []
            add("pe", lambda e, pt=pt, kb=kb, J=J: e.matmul(pt[:, 0:nq], lhsT=J["kt_fn"](kb), rhs=J["q_ap"], start=True, stop=(len(ex) == 0)),
                J["reads"], [pb])
            for i2, (lh, rh, rd) in enumerate(ex):
                add("pe", lambda e, pt=pt, lh=lh, rh=rh, i2=i2: e.matmul(pt[:, 0:nq], lhsT=lh, rhs=rh, start=False, stop=(i2 == len(ex) - 1)),
                    rd, [pb])
            i = e_i[0] % NE
            e_i[0] += 1
            add("act", lambda e, pt=pt, i=i, J=J: e.activation(out=Eb[i][:, 0:nq], in_=pt[:, 0:nq], func=AF.Exp, scale=J["scale"]), [pb], [b_E[i]])
            pend[idx] = i

        LA = 3
        for idx in range(min(LA, N)):
            do_score(idx)
        if burst:
            wpt, wpb = gp()
            for _ in range(burst):
                add("pe", lambda e, wpt=wpt: e.matmul(wpt[:, :], lhsT=identb[:, :], rhs=hT[1][:, 0, :], start=True, stop=True),
                    [b_identb, b_hT[1]], [wpb])
        for idx in range(N):
            if idx + LA < N:
                do_score(idx + LA)
            ji, kb = blocks[idx]
            J = jobs[ji]
            if kb == 0:
                J["acc"] = acb()
            at, ab = J["acc"]
            i = pend.pop(idx)
            add("pe", lambda e, at=at, kb=kb, i=i, J=J: e.matmul(at[0:65, 0:nq], lhsT=J["va_fn"](kb), rhs=Eb[i][:, 0:nq],
                                                              start=(kb == 0), stop=(kb == nkb - 1)), J["reads"] + [b_E[i]], [ab])
            if kb == nkb - 1:
                if J.get("post") is not None:
                    posts.setdefault(min(idx + 2, N - 1), []).append(J["post"])
                if side_cb is not None:
                    side_cb()
            for f in posts.pop(idx, []):
                f()

    def input_proj(l):
        res = {}
        add("pool", lambda e: e.dma_start(out=wq_s[:], in_=wqup[l].rearrange("(k p) c -> p k c", p=128)), (), [b_wsm], dma=True)
        add("pool", lambda e: e.dma_start(out=wqp_s[:], in_=wqupp[l].rearrange("(k p) c -> p k c", p=128)), (), [b_wsm], dma=True)
        add("pool", lambda e: e.dma_start(out=wkv_s[:], in_=wkvup[l]), (), [b_wsm], dma=True)

        def fm(wt, wb, c0, M, g, n0=0, n=T):
            pt, pb = gp()
            for k in range(8):
                add("pe", lambda e, pt=pt, k=k: e.matmul(pt[0:M, 0:n], lhsT=wt[:, k, c0:c0 + M], rhs=hT[g][:, k, n0:n0 + n],
                                                        start=(k == 0), stop=(k == 7)), [wb, b_hT[g]], [pb])
            return pt, pb

        def tm(wt, wb, c0, N, g, tb):
            pt, pb = gp()
            for k in range(8):
                add("pe", lambda e, pt=pt, k=k: e.matmul(pt[:, 0:N], lhsT=hT[g][:, k, tb * 128:(tb + 1) * 128], rhs=wt[:, k, c0:c0 + N],
                                                        start=(k == 0), stop=(k == 7)), [wb, b_hT[g]], [pb])
            return pt, pb

        def rope_evac(ptA, pbA, ptB, pbB, p0, p1, dst, wlist):
            t1, tb1 = tmp()
            t2, tb2 = tmp()
            add("dve", lambda e: e.tensor_tensor(out=t1[p0:p1, :], in0=ptA[p0:p1, :], in1=cos_s[p0:p1, :], op=ALU.mult), [pbA, b_prm], [tb1])
            add("dve", lambda e: e.tensor_tensor(out=t2[p0:p1, :], in0=ptB[p0:p1, :], in1=sin_s[p0:p1, :], op=ALU.mult), [pbB, b_prm], [tb2])
            add("dve", lambda e: e.tensor_tensor(out=dst, in0=t1[p0:p1, :], in1=t2[p0:p1, :], op=ALU.add), [tb1, tb2], wlist)

        groups = [0, 1] if do_lat else [0]
        wA, bA = wtile(w_in[l, :, 0:512], 512)
        if do_lat:
            wF1, bF1 = wtile(w_inp[l, :, 0:512], 512)
        for hp in range(2):
            pt, pb = fm(wA, bA, hp * 128, 128, 0)
            evac(QT2[0][0][0:64, 2 * hp, :], pt[0:64, :], [pb], [b_QT[0][0]])
            evac(QT2[0][0][0:64, 2 * hp + 1, :], pt[64:128, :], [pb], [b_QT[0][0]])
            pt, pb = fm(wA, bA, 256 + hp * 128, 128, 0)
            evac(KTc2[0][0:64, 2 * hp, :], pt[0:64, :], [pb], [b_KTc[0]])
            evac(KTc2[0][0:64, 2 * hp + 1, :], pt[64:128, :], [pb], [b_KTc[0]])
        if do_lat:
            def rope_pair(c0):
                ptA, pbA = fm(wA, bA, c0, 128, 1)
                ptB, pbB = fm(wF1, bF1, c0, 128, 1)
                t1, tb1 = tmp()
                t2, tb2 = tmp()
                add("dve", lambda e: e.tensor_tensor(out=t1[:, :], in0=ptA[:, :], in1=cos_s[:, :], op=ALU.mult), [pbA, b_prm], [tb1])
                add("dve", lambda e: e.tensor_tensor(out=t2[:, :], in0=ptB[:, :], in1=sin_s[:, :], op=ALU.mult), [pbB, b_prm], [tb2])
                return t1, tb1, t2, tb2
            for hp in range(2):
                t1, tb1, t2, tb2 = rope_pair(hp * 128)
                for i in range(2):
                    add("dve", lambda e, t1=t1, t2=t2, i=i, hp=hp: e.tensor_tensor(out=QT2[1][0][0:64, 2 * hp + i, :], in0=t1[i * 64:(i + 1) * 64, :],
                                                                                in1=t2[i * 64:(i + 1) * 64, :], op=ALU.add), [tb1, tb2], [b_QT[1][0]])
                t1, tb1, t2, tb2 = rope_pair(256 + hp * 128)
                tk, tkb = sqt()
                add("dve", lambda e, t1=t1, t2=t2, tk=tk: e.tensor_tensor(out=tk[:, :], in0=t1[:, :], in1=t2[:, :], op=ALU.add), [tb1, tb2], [tkb])
                add("sp", lambda e, tk=tk, hp=hp: e.dma_start(out=payA_in[l].ap()[R_DAK + hp * 128:R_DAK + (hp + 1) * 128, :], in_=tk[:, :]),
                    [tkb], [b_payA[l]], dma=True)
        wB, bB = wtile(w_in[l, :, 512:1024], 512)
        for g in groups:
            for c in range(2):
                pt, pb = fm(wB, bB, 256 + c * 128, 128, g)
                evac(qd[:, c, :], pt[:, :], [pb], [b_qd])
            stat_rstd([qd[:, 0, :], qd[:, 1, :]], 256, rstd[:], b_rstd, [b_qd])
            for c in range(2):
                add("dve", lambda e, c=c: e.scalar_tensor_tensor(out=qdn[:, c, :], in0=qd[:, c, :], scalar=gq_s[:, l, c:c + 1], in1=rstd[:],
                                                               op0=ALU.mult, op1=ALU.mult), [b_qd, b_rstd, b_prm], [b_qdn])
            for h in range(4):
                pt, pb = gp()
                for c in range(2):
                    add("pe", lambda e, pt=pt, c=c, h=h: e.matmul(pt[0:96, :], lhsT=wq_s[:, c, h * 96:(h + 1) * 96], rhs=qdn[:, c, :],
                                                                 start=(c == 0), stop=(c == 1)), [b_wsm, b_qdn], [pb])
                if g == 0:
                    evac(QT2[0][1][0:96, h, :], pt[0:96, :], [pb], [b_QT[0][1]])
                else:
                    pt2, pb2 = gp()
                    for c in range(2):
                        add("pe", lambda e, pt2=pt2, c=c, h=h: e.matmul(pt2[0:96, :], lhsT=wqp_s[:, c, h * 96:(h + 1) * 96], rhs=qdn[:, c, :],
                                                                       start=(c == 0), stop=(c == 1)), [b_wsm, b_qdn], [pb2])
                    evac(QT2[1][1][0:64, h, :], pt[0:64, :], [pb], [b_QT[1][1]])
                    rope_evac(pt, pb, pt2, pb2, 64, 96, QT2[1][1][64:96, h, :], [b_QT[1][1]])
        for g in groups:
            for tb in range(4):
                pt, pb = tm(wB, bB, 0, 256, g, tb)
                if g == 0:
                    ostage, b_ostage = tmp()
                    add("act", lambda e, pt=pt, ostage=ostage: e.activation(out=ostage[:, 0:256], in_=pt[:, 0:256], func=AF.Copy), [pb], [b_ostage])
                    s, tl = tb // 2, (tb % 2) * 128
                    add("sp", lambda e, s=s, tl=tl, ostage=ostage: e.dma_start(out=o_dav[s, l, :, tl:tl + 128, :].rearrange("h t d -> t h d"),
                                                               in_=ostage[:, 0:256].rearrange("p (h d) -> p h d", h=4)), [b_ostage], (), dma=True)
                    add("dve", lambda e, pt=pt, tb=tb: e.tensor_copy(out=VAc[0][:, tb, :, 0:64], in_=pt[:, 0:256].rearrange("p (h d) -> p h d", h=4)),
                        [pb], [b_VAc[0]])
                else:
                    tv, tvb = sqt()
                    evac(tv[:, 0:256], pt[:, 0:256], [pb], [tvb])
                    add("sp", lambda e, tv=tv, tb=tb: e.dma_start(out=payB_in[l].ap()[R_V + tb * 128:R_V + (tb + 1) * 128, 0:256], in_=tv[:, 0:256]),
                        [tvb], [b_payB[l]], dma=True)
        wC, bC = wtile(w_in[l, :, 1024:1440], 416)
        if do_lat:
            wF2, bF2 = wtile(w_inp[l, :, 512:608], 96)
        for g in groups:
            pt, pb = fm(wC, bC, 0, 128, g)
            kvd, b_kvd = tmp()
            evac(kvd[:, :], pt[:, :], [pb], [b_kvd])
            stat_rstd([kvd[:, :]], 128, rstd[:], b_rstd, [b_kvd])
            dstc = ckvTc if g == 0 else sqb[0]
            if g == 0:
                add("dve", lambda e, kvd=kvd: e.scalar_tensor_tensor(out=ckvTc[:, 0:T], in0=kvd[:, :], scalar=gkvc_s[:, l:l + 1], in1=rstd[:],
                                                            op0=ALU.mult, op1=ALU.mult), [b_kvd, b_rstd, b_prm], [b_ckvTc])
            else:
                tk, tkb = sqt()
                add("dve", lambda e, tk=tk, kvd=kvd: e.scalar_tensor_tensor(out=tk[:, :], in0=kvd[:, :], scalar=gkvc_s[:, l:l + 1], in1=rstd[:],
                                                                   op0=ALU.mult, op1=ALU.mult), [b_kvd, b_rstd, b_prm], [tkb])
                add("sp", lambda e, tk=tk: e.dma_start(out=payA_in[l].ap()[R_CKV:R_CKV + 128, :], in_=tk[:, :]), [tkb], [b_payA[l]], dma=True)
            pt, pb = fm(wC, bC, 64, 96, g)
            if g == 0:
                for h in range(4):
                    evac(KTc2[1][64:96, h, :], pt[64:96, :], [pb], [b_KTc[1]])
            else:
                ptB, pbB = fm(wF2, bF2, 0, 96, 1)
                tk, tkb = sqt()
                rope_evac(pt, pb, ptB, pbB, 64, 96, tk[64:96, :], [tkb])
                add("sp", lambda e, tk=tk: e.dma_start(out=payA_in[l].ap()[R_KR:R_KR + 32, :], in_=tk[64:96, :]), [tkb], [b_payA[l]], dma=True)
            for hp in range(2):
                pt, pb = fm(wC, bC, 160 + hp * 128, 128, g)
                evac(QT2[g][0][64:128, 2 * hp, :], pt[0:64, :], [pb], [b_QT[g][2]])
                evac(QT2[g][0][64:128, 2 * hp + 1, :], pt[64:128, :], [pb], [b_QT[g][2]])
        for tb in range(4):
            pt, pb = tm(wC, bC, 0, 160, 0, tb)
            s, tl = tb // 2, (tb % 2) * 128
            tt, tb_ = tmp()
            add("act", lambda e, pt=pt, tt=tt: e.activation(out=tt[:, 0:128], in_=pt[:, 0:128], func=AF.Square), [pb], [tb_])
            add("dve", lambda e, tt=tt: e.reduce_sum(out=tt[:, 200:201], in_=tt[:, 0:128], axis=mybir.AxisListType.X), [tb_], [tb_])
            add("act", lambda e, tt=tt: e.activation(out=tt[:, 201:202], in_=tt[:, 200:201], func=AF.Ln, bias=epsc[:, 0:1], scale=1.0 / 128),
                [tb_, b_epsc], [tb_])
            add("act", lambda e, tt=tt: e.activation(out=tt[:, 202:203], in_=tt[:, 201:202], func=AF.Exp, scale=-0.5), [tb_], [tb_])
            add("dve", lambda e, pt=pt, tt=tt: e.scalar_tensor_tensor(out=tt[:, 256:384], in0=pt[:, 0:128], scalar=tt[:, 202:203],
                                                                      in1=gkvr_s[:, l, :], op0=ALU.mult, op1=ALU.mult),
                [pb, tb_, b_prm], [tb_])
            add("act", lambda e, pt=pt, tt=tt: e.activation(out=tt[:, 384:416], in_=pt[:, 128:160], func=AF.Copy), [pb, tb_], [tb_])
            add("sp", lambda e, s=s, tl=tl, tt=tt: e.dma_start(out=o_ckv[s, l, tl:tl + 128, :], in_=tt[:, 256:384]), [tb_], (), dma=True)
            add("sp", lambda e, s=s, tl=tl, tt=tt: e.dma_start(out=o_kr[s, l, tl:tl + 128, :], in_=tt[:, 384:416]), [tb_], (), dma=True)
        for h in range(4):
            pt, pb = gp()
            add("pe", lambda e, pt=pt, h=h: e.matmul(pt[0:64, :], lhsT=wkv_s[:, h * 128:h * 128 + 64], rhs=ckvTc[:, 0:T], start=True, stop=True),
                [b_wsm, b_ckvTc], [pb])
            evac(KTc2[1][0:64, h, :], pt[0:64, :], [pb], [b_KTc[1]])
        for tb in range(4):
            pt, pb = gp()
            add("pe", lambda e, pt=pt, tb=tb: e.matmul(pt[:, 0:256], lhsT=ckvTc[:, tb * 128:(tb + 1) * 128],
                                                      rhs=wkv_s[:, :].rearrange("p (h c) -> p h c", h=4)[:, :, 64:128], start=True, stop=True),
                [b_wsm, b_ckvTc], [pb])
            evac(VAc[1][:, tb, :, 0:64], pt[:, 0:256].rearrange("p (h d) -> p h d", h=4), [pb], [b_VAc[1]])
        wD, bD = wtile(w_in[l, :, 1440:1952], 512)
        for hp in range(2):
            pt, pb = fm(wD, bD, hp * 128, 128, 0)
            evac(KTc2[0][64:128, 2 * hp, :], pt[0:64, :], [pb], [b_KTc[2]])
            evac(KTc2[0][64:128, 2 * hp + 1, :], pt[64:128, :], [pb], [b_KTc[2]])
            if do_lat:
                pt, pb = fm(wD, bD, hp * 128, 128, 1)
                tk, tkb = sqt()
                evac(tk[:, :], pt[:, :], [pb], [tkb])
                add("sp", lambda e, tk=tk, hp=hp: e.dma_start(out=payA_in[l].ap()[R_NAK + hp * 128:R_NAK + (hp + 1) * 128, :], in_=tk[:, :]),
                    [tkb], [b_payA[l]], dma=True)
        for tb in range(4):
            pt, pb = tm(wD, bD, 0, 512, 0, tb)
            s, tl = tb // 2, (tb % 2) * 128
            ostage, b_ostage = tmp()
            add("act", lambda e, pt=pt, ostage=ostage: e.activation(out=ostage[:, :], in_=pt[:, :], func=AF.Copy), [pb], [b_ostage])
            add("sp", lambda e, s=s, tl=tl, ostage=ostage: e.dma_start(out=o_nak[s, l, :, tl:tl + 128, :].rearrange("h t d -> t h d"),
                                                       in_=ostage[:, 0:256].rearrange("p (h d) -> p h d", h=4)), [b_ostage], (), dma=True)
            add("sp", lambda e, s=s, tl=tl, ostage=ostage: e.dma_start(out=o_nav[s, l, :, tl:tl + 128, :].rearrange("h t d -> t h d"),
                                                       in_=ostage[:, 256:512].rearrange("p (h d) -> p h d", h=4)), [b_ostage], (), dma=True)
            add("dve", lambda e, pt=pt, tb=tb: e.tensor_copy(out=VAc[2][:, tb, :, 0:64], in_=pt[:, 256:512].rearrange("p (h d) -> p h d", h=4)),
                [pb], [b_VAc[2]])
            if do_lat:
                pt, pb = tm(wD, bD, 256, 256, 1, tb)
                tv, tvb = sqt()
                evac(tv[:, 0:256], pt[:, 0:256], [pb], [tvb])
                add("sp", lambda e, tv=tv, tb=tb: e.dma_start(out=payB_in[l].ap()[R_V + tb * 128:R_V + (tb + 1) * 128, 256:512], in_=tv[:, 0:256]),
                    [tvb], [b_payB[l]], dma=True)
        wA2, bA2 = wtile(w_in[l, :, 256:512], 256)
        for tb in range(4):
            pt, pb = tm(wA2, bA2, 0, 256, 0, tb)
            s, tl = tb // 2, (tb % 2) * 128
            ostage, b_ostage = tmp()
            add("act", lambda e, pt=pt, ostage=ostage: e.activation(out=ostage[:, 0:256], in_=pt[:, 0:256], func=AF.Copy), [pb], [b_ostage])
            add("sp", lambda e, s=s, tl=tl, ostage=ostage: e.dma_start(out=o_dak[s, l, :, tl:tl + 128, :].rearrange("h t d -> t h d"),
                                                       in_=ostage[:, 0:256].rearrange("p (h d) -> p h d", h=4)), [b_ostage], (), dma=True)
        wE, bE = wtile(w_in[l, :, 1952:2464], 512)
        for g in groups:
            for c in range(2):
                pa, pba = fm(wE, bE, c * 128, 128, g)
                pbm, pbb = fm(wE, bE, 256 + c * 128, 128, g)
                tt, tb_ = tmp()
                add("act", lambda e, tt=tt, pbm=pbm: e.activation(out=tt[:], in_=pbm[:, :], func=AF.Sigmoid), [pbb], [tb_])
                add("dve", lambda e, tt=tt, pa=pa, c=c, g=g: e.tensor_tensor(out=gpad[g][:, c, :, 15:15 + 256],
                                                                           in0=pa[:, :].rearrange("p (s t) -> p s t", s=2),
                                                                           in1=tt[:].rearrange("p (s t) -> p s t", s=2), op=ALU.mult),
                    [pba, tb_], [b_gpad[g]])
        if do_lat:
            add("dve", lambda e: e.tensor_copy(out=halo[:, :, 0, 0:15], in_=gpad[1][:, :, 0, 15:30]), [b_gpad[1]], [b_halo])
            add("dve", lambda e: e.tensor_copy(out=halo[:, :, 0, 15:30], in_=gpad[1][:, :, 1, 256:271]), [b_gpad[1]], [b_halo])
            hdst = dram_ap(payB_in[l], R_HALO * T, [[30, 128], [128 * 30, 2], [1, 30]])
            add("sp", lambda e: e.dma_start(out=hdst, in_=halo[:, :, 0, :]), [b_halo], [b_payB[l]], dma=True)
            add("pool", lambda e: e.collective_compute("AllGather", ALU.bypass, replica_groups=[[0, 1, 2, 3], [4, 5, 6, 7]],
                                                       ins=[payA_in[l].ap().opt()], outs=[payA_out[l].ap().opt()]),
                [b_payA[l]], [b_payoA[l]], dma="cc")
            add("pool", lambda e: e.collective_compute("AllGather", ALU.bypass, replica_groups=[[0, 1, 2, 3], [4, 5, 6, 7]],
                                                       ins=[payB_in[l].ap().opt()], outs=[payB_out[l].ap().opt()]),
                [b_payB[l]], [b_payoB[l]], dma="cc")

    def da_post(l, g, h, at1, ab1, at2, ab2, nq=T, q0=0, repl=False, c1=0, c2=0):
        o1, bo1 = dao, b_dao
        o2, bo2 = rsum_s[0], b_rsum_s[0]
        if repl:
            normalize_rep(at1, ab1, nq, o1[0:64, 0:nq], [bo1])
            normalize_rep(at2, ab2, nq, o2[0:64, 0:nq], [bo2])
        else:
            normalize(at1, ab1, nq, o1[0:64, 0:nq], [bo1], 0, c1)
            normalize(at2, ab2, nq, o2[0:64, 0:nq], [bo2], 1 if nq <= 256 else 0, c2)
        add("dve", lambda e: e.scalar_tensor_tensor(out=o1[0:64, 0:nq], in0=o2[0:64, 0:nq], scalar=neglam[0:64, l:l + 1], in1=o1[0:64, 0:nq],
                                                    op0=ALU.mult, op1=ALU.add), [bo1, bo2, b_prm], [bo1])
        stat_rstd([o1[0:64, 0:nq]], 64, rstd[0:64, 0:nq], b_rstd, [bo1], K=64, n=nq)
        p0 = (h % 2) * 64
        add("dve", lambda e: e.scalar_tensor_tensor(out=mixT[g][p0:p0 + 64, h // 2, q0:q0 + nq], in0=o1[0:64, 0:nq], scalar=gsub2[0:64, l:l + 1],
                                                    in1=rstd[0:64, 0:nq], op0=ALU.mult, op1=ALU.mult),
            [bo1, b_rstd, b_prm], [b_mix[g][h // 2]])

    def plain_post(g, m, h, at, ab, nq=T, q0=0, repl=False):
        if repl:
            p0 = (h % 2) * 64
            ch = 2 * m + h // 2
            normalize_rep(at, ab, nq, mixT[g][p0:p0 + 64, ch, q0:q0 + nq], [b_mix[g][ch]])
            return
        k = (set_i[0] % NSET) if nq <= 256 else 0
        set_i[0] += 1
        o1, bo1 = tmp()
        normalize(at, ab, nq, o1[0:64, 0:nq], [bo1], k)
        p0 = (h % 2) * 64
        ch = 2 * m + h // 2
        add("act", lambda e: e.activation(out=mixT[g][p0:p0 + 64, ch, q0:q0 + nq], in_=o1[0:64, 0:nq], func=AF.Copy), [bo1], [b_mix[g][ch]])

    def ctx_attention(l, side_gen=None):
        scales = (32 ** -0.5, 96 ** -0.5, 64 ** -0.5)
        rows = {0: None, 1: (0, 96), 2: (0, 64)}
        jobs = []
        for s in range(2):
            for m in range(3):
                for h in range(4):
                    for mp in ((0, 1) if m == 0 else (0,)):
                        jobs.append(dict(s=s, m=m, h=h, mp=mp))
        n = len(jobs)

        def stage_a(j):
            s_, m, h, mp = j["s"], j["m"], j["h"], j["mp"]
            q0 = s_ * 256
            lo, hi = (mp * 32, mp * 32 + 32) if m == 0 else rows[m]
            pt, pb = gp()
            rd = [b_KTc[m], b_QT[0][m]]
            for kb in range(2):
                add("pe", lambda e, pt=pt, kb=kb, m=m, h=h, lo=lo, hi=hi, q0=q0: e.matmul(
                    pt[:, kb * 256:(kb + 1) * 256], lhsT=kc_ap(m, h, lo, hi, q0 + kb * 128, 128), rhs=q_ap(0, m, h, lo, hi, q0, 256),
                    start=True, stop=True), rd, [pb])
            i = e_i[0] % NE
            e_i[0] += 1
            add("act", lambda e, pt=pt, i=i, m=m: e.activation(out=Eb[i][:, :], in_=pt[:, :], func=AF.Exp, scale=scales[m]), [pb], [b_E[i]])
            j["e"] = i

        def stage_b(j):
            s_, m, h = j["s"], j["m"], j["h"]
            at, ab = acb()
            i = j["e"]
            for kb in range(2):
                add("pe", lambda e, at=at, kb=kb, i=i, m=m, h=h, s_=s_: e.matmul(at[0:65, 0:256], lhsT=VAc[m][:, s_ * 2 + kb, h, 0:65],
                                                                               rhs=Eb[i][:, kb * 256:(kb + 1) * 256], start=(kb == 0), stop=(kb == 1)),
                    [b_VAc[m], b_E[i]], [ab])
            for kb in range(2):
                add("pe", lambda e, at=at, kb=kb, i=i: e.matmul(at[0:64, 256:512], lhsT=ones_b[:, 0:64], rhs=Eb[i][:, kb * 256:(kb + 1) * 256],
                                                               start=(kb == 0), stop=(kb == 1)), [b_ones_b, b_E[i]], [ab])
            j["acc"] = (at, ab)

        def stage_c(idx):
            j = jobs[idx]
            s_, m, h, mp = j["s"], j["m"], j["h"], j["mp"]
            q0 = s_ * 256
            if m == 0:
                if mp == 1:
                    a1 = jobs[idx - 1]["acc"]
                    da_post(l, 0, h, a1[0], a1[1], j["acc"][0], j["acc"][1], nq=256, q0=q0, repl=True)
            else:
                plain_post(0, m, h, j["acc"][0], j["acc"][1], nq=256, q0=q0, repl=True)

        for i in range(n + 2):
            if i < n:
                stage_a(jobs[i])
            warm(WARM_CTX)
            if 0 <= i - 1 < n:
                stage_b(jobs[i - 1])
            if 0 <= i - 2 < n:
                stage_c(i - 2)
            if side_gen is not None and i % 2 == 1:
                next(side_gen, None)

    def cache_T(src_ap, ncol, dst_fn, wlist):
        stg, bstg = tmp()
        add("sp", lambda e: e.dma_start(out=stg[:, 0:2 * ncol].rearrange("p (t c) -> p t c", t=2),
                                        in_=src_ap.rearrange("(t p) c -> p t c", p=128)), (), [bstg], dma=True)
        for tb in range(2):
            pt, pb = gp()
            add("pe", lambda e, pt=pt, tb=tb: e.transpose(out=pt[0:ncol, 0:128], in_=stg[:, tb * ncol:(tb + 1) * ncol], identity=ident[:]),
                [bstg, b_ident], [pb])
            evac(dst_fn(tb), pt[0:ncol, 0:128], [pb], wlist)

    def load_V(l, c0, cache_ap):
        for rk in range(4):
            for tb in range(4):
                src = dram_ap(payB_out[l], (rk * PB + R_V + tb * 128) * T + c0, [[T, 128], [64, 4], [1, 64]])
                add("sp", lambda e, rk=rk, tb=tb, src=src: e.dma_start(out=VAb[:, rk * 4 + tb, :, 0:64], in_=src), [b_payoB[l]], [b_VAk[rk * 4 + tb]], dma=True)
        for tb in range(2):
            add("pool", lambda e, tb=tb: e.dma_start(out=VAb[:, 16 + tb, :, 0:64], in_=cache_ap[:, tb * 128:(tb + 1) * 128, :].rearrange("h t d -> t h d")),
                (), [b_VAk[16 + tb]], dma=True)

    def load_KT(l, row0, nrows, p0, heads=True):
        for h in range(4):
            r = row0 + (h * nrows if heads else 0)
            src = dram_ap(payA_out[l], r * T, [[T, nrows], [PA * T, 4], [1, T]])
            add("sp", lambda e, h=h, src=src: e.dma_start(out=KTb[p0:p0 + nrows, h, 0:DEC_S].rearrange("p (r t) -> p r t", r=4), in_=src),
                [b_payoA[l]], [b_KTh[h]], dma=True)

    def lat_attention(l, side_gen=None, mod_gen=None):
        rdv = list(b_VAk)

        def side_step(n=1):
            if side_gen is not None:
                for _ in range(n):
                    next(side_gen, None)
            if mod_gen is not None:
                next(mod_gen, None)

        load_KT(l, R_DAK, 64, 0)
        for h in range(4):
            cache_T(c_dak[l, h], 64, lambda tb, h=h: KTb[0:64, h, DEC_S + tb * 128:DEC_S + (tb + 1) * 128], [b_KTh[h]])
        load_V(l, 0, c_dav[l])
        jobs = []
        for h in range(4):
            for qh in range(2):
                idx = h * 2 + qh
                tq, btq = tbt[idx // 4], b_tbt[idx // 4]
                qb = tq[:, :, :].rearrange("p a w -> p (a w)")[:, (idx % 4) * 512:(idx % 4 + 1) * 512]
                add("pool", lambda e, qb=qb: e.memset(qb, 0.0), (), [btq])
                add("pool", lambda e, qb=qb, h=h, qh=qh: e.tensor_copy(out=qb[0:32, 0:256], in_=QT2[1][0][0:32, h, qh * 256:(qh + 1) * 256]),
                    [b_QT[1][0]], [btq])
                add("pool", lambda e, qb=qb, h=h, qh=qh: e.tensor_copy(out=qb[32:64, 256:512], in_=QT2[1][0][32:64, h, qh * 256:(qh + 1) * 256]),
                    [b_QT[1][0]], [btq])
                J = dict(kt_fn=(lambda kb, h=h: KTb[0:128, h, kb * 128:(kb + 1) * 128]), q_ap=qb,
                         va_fn=(lambda kb, h=h: VAb[:, kb, h, 0:65]), scale=32 ** -0.5, reads=rdv + [b_KTh[h], btq])
                J["post"] = (lambda h=h, qh=qh, J=J: da_post(l, 1, h, J["acc"][0], J["acc"][1], J["acc"][0], J["acc"][1],
                                                             nq=256, q0=qh * 256, c1=0, c2=256))
                jobs.append(J)
        stream_attention(jobs, side_cb=lambda: side_step(2), burst=BURST)
        src = dram_ap(payA_out[l], R_CKV * T, [[T, 128], [PA * T, 4], [1, T]])
        add("sp", lambda e, src=src: e.dma_start(out=ckvT[:, 0:DEC_S].rearrange("p (r t) -> p r t", r=4), in_=src), [b_payoA[l]], [b_ckvT], dma=True)
        cache_T(c_ckv[l], 128, lambda tb: ckvT[:, DEC_S + tb * 128:DEC_S + (tb + 1) * 128], [b_ckvT])
        load_KT(l, R_KR, 32, 64, heads=False)
        for h in range(4):
            cache_T(c_kr[l], 32, lambda tb, h=h: KTb[64:96, h, DEC_S + tb * 128:DEC_S + (tb + 1) * 128], [b_KTh[h]])
        for h in range(4):
            for cb in range(5):
                n0 = cb * 512
                n = min(512, DEC_S + PAST - n0)
                pt, pb = gp()
                add("pe", lambda e, pt=pt, h=h, n0=n0, n=n: e.matmul(pt[0:64, 0:n], lhsT=wkv_s[:, h * 128:h * 128 + 64], rhs=ckvT[:, n0:n0 + n],
                                                                    start=True, stop=True), [b_wsm, b_ckvT], [pb])
                evac(KTb[0:64, h, n0:n0 + n], pt[0:64, 0:n], [pb], [b_KTh[h]])
        for kb in range(18):
            pt, pb = gp()
            add("pe", lambda e, pt=pt, kb=kb: e.matmul(pt[:, 0:256], lhsT=ckvT[:, kb * 128:(kb + 1) * 128],
                                                      rhs=wkv_s[:, :].rearrange("p (h c) -> p h c", h=4)[:, :, 64:128], start=True, stop=True),
                [b_wsm, b_ckvT], [pb])
            evac(VAb[:, kb, :, 0:64], pt[:, 0:256].rearrange("p (h d) -> p h d", h=4), [pb], [b_VAk[kb]])
        jobs = []
        for h in range(4):
            J = dict(kt_fn=(lambda kb, h=h: KTb[0:128, h, kb * 128:(kb + 1) * 128]), q_ap=q_ap(1, 1, h, 0, 128),
                     va_fn=(lambda kb, h=h: VAb[:, kb, h, 0:65]), scale=96 ** -0.5, reads=rdv + [b_KTh[h], b_QT[1][1]])
            J["post"] = (lambda h=h, J=J: plain_post(1, 1, h, J["acc"][0], J["acc"][1]))
            jobs.append(J)
        stream_attention(jobs, side_cb=lambda: side_step(2))
        load_KT(l, R_NAK, 64, 0)
        for h in range(4):
            cache_T(c_nak[l, h], 64, lambda tb, h=h: KTb[0:64, h, DEC_S + tb * 128:DEC_S + (tb + 1) * 128], [b_KTh[h]])
            add("pool", lambda e, h=h: e.dma_start(out=KTb[64:96, h, 0:DEC_S], in_=rowind_d), (), [b_KTh[h]], dma=True)
            add("pool", lambda e, h=h: e.memset(KTb[64:96, h, DEC_S:DEC_S + PAST], 0.0), (), [b_KTh[h]])
            evac(QT2[1][0][0:64, h, :], QT2[1][0][64:128, h, :], [b_QT[1][2], b_QT[1][0]], [b_QT[1][0], b_QT[1][2]])
            add("pool", lambda e, h=h: e.dma_start(out=QT2[1][0][64:96, h, :], in_=rowsel_d), (), [b_QT[1][0], b_QT[1][2]], dma=True)
        load_V(l, 256, c_nav[l])
        jobs = []
        for h in range(4):
            ti = h % 2

            def pre(h=h, ti=ti):
                for j in range(2):
                    src = bass.AP(tz_all, l * TZ_SZ + (h * (A2 + 1) + (1 - j) + 7) * 64 * TZL + 63, [[TZL - 1, 64], [64 * TZL, A2U], [1, 64]])
                    add("pool", lambda e, j=j, src=src, ti=ti: e.dma_start(out=tbt[ti][j * 64:(j + 1) * 64, :, :], in_=src), (), [b_tbt[ti]], dma=True)
                add("dve", lambda e, ti=ti: e.scalar_tensor_tensor(out=tbt[ti][:], in0=tbt[ti][:], scalar=8.0,
                                                                   in1=colm_s[:, :].unsqueeze(1).to_broadcast([128, A2U, 64]),
                                                                   op0=ALU.mult, op1=ALU.add), [b_tbt[ti], b_prm], [b_tbt[ti]])

            def extra(kb, ti=ti):
                if kb >= 16:
                    return []
                a0 = 30 - 2 * kb
                return [(identb[:, :], tbt[ti][:, a0:a0 + 8, :], [b_identb, b_tbt[ti]])]

            J = dict(kt_fn=(lambda kb, h=h: KTb[0:128, h, kb * 128:(kb + 1) * 128]), q_ap=QT2[1][0][0:128, h, :],
                     va_fn=(lambda kb, h=h: VAb[:, kb, h, 0:65]), scale=64 ** -0.5, reads=rdv + [b_KTh[h], b_QT[1][2]],
                     extra_fn=extra, pre=pre)
            J["post"] = (lambda h=h, J=J: plain_post(1, 2, h, J["acc"][0], J["acc"][1]))
            jobs.append(J)
        stream_attention(jobs, side_cb=lambda: side_step(2))

    def conv_module(l, g):
        if g == 1:
            for c in range(2):
                src = dram_ap(payB_out[l], R_HALO * T + c * 128 * 30, [[30, 128], [PB * T, 4], [1, 30]])
                add("sp", lambda e, src=src, c=c: e.dma_start(out=halo[:, c, :, :], in_=src), [b_payoB[l]], [b_halo], dma=True)
            for side in range(2):
                dst = gpad[1][:, :, 0, 0:15] if side == 0 else gpad[1][:, :, 1, 271:286]
                for rk in range(4):
                    srcv = halo[:, :, rk, 15:30] if side == 0 else halo[:, :, rk, 0:15]
                    sc = halsel_s[:, side * 4 + rk:side * 4 + rk + 1]
                    if rk == 0:
                        add("dve", lambda e, dst=dst, srcv=srcv, sc=sc: e.tensor_scalar(out=dst, in0=srcv, scalar1=sc, scalar2=None, op0=ALU.mult),
                            [b_halo, b_prm, b_gpad[1]], [b_gpad[1]])
                    else:
                        add("dve", lambda e, dst=dst, srcv=srcv, sc=sc: e.scalar_tensor_tensor(out=dst, in0=srcv, scalar=sc, in1=dst,
                                                                                              op0=ALU.mult, op1=ALU.add),
                            [b_halo, b_prm, b_gpad[1]], [b_gpad[1]])
        for c in range(2):
            for j in range(31):
                if g == 0:
                    src = gpad[0][:, c, :, j:j + 256]
                    dst = cacc[:, c, :].rearrange("p (s t) -> p s t", s=2)
                    ops = [(src, dst)]
                else:
                    ops = []
                    n_a = max(0, min(256, 271 - j))
                    if n_a > 0:
                        ops.append((gpad[1][:, c, 0, j:j + n_a], cacc[:, c, 0:n_a]))
                    if n_a < 256:
                        ops.append((gpad[1][:, c, 1, 15 + (n_a + j - 271):15 + (256 + j - 271)], cacc[:, c, n_a:256]))
                    n_b = max(0, min(256, 271 - (256 + j)))
                    if n_b > 0:
                        ops.append((gpad[1][:, c, 0, 256 + j:256 + j + n_b], cacc[:, c, 256:256 + n_b]))
                    ops.append((gpad[1][:, c, 1, 15 + (256 + n_b + j - 271):15 + (512 + j - 271)], cacc[:, c, 256 + n_b:512]))
                if j % 4 == 3:
                    yield
                for (src, dst) in ops:
                    if j == 0:
                        add("dve", lambda e, src=src, dst=dst, c=c: e.tensor_scalar(out=dst, in0=src, scalar1=dw_s[:, l, c, 0:1], scalar2=cb_s[:, l, c:c + 1],
                                                                                    op0=ALU.mult, op1=ALU.add), [b_gpad[g], b_prm, b_cacc], [b_cacc])
                    else:
                        add("dve", lambda e, src=src, dst=dst, c=c, j=j: e.scalar_tensor_tensor(out=dst, in0=src, scalar=dw_s[:, l, c, j:j + 1], in1=dst,
                                                                                                op0=ALU.mult, op1=ALU.add), [b_gpad[g], b_prm, b_cacc], [b_cacc])
        p1, pb1 = gp()
        p2, pb2 = gp()
        for c in range(2):
            tt, tb_ = tmp()
            add("act", lambda e, tt=tt, c=c: e.activation(out=tt[:], in_=cacc[:, c, :], func=AF.Square), [b_cacc], [tb_])
            add("pe", lambda e, c=c: e.matmul(p1[:, :], lhsT=ones_f[:, :], rhs=cacc[:, c, :], start=(c == 0), stop=(c == 1)), [b_cacc, b_ones_f], [pb1])
            add("pe", lambda e, tt=tt, c=c: e.matmul(p2[:, :], lhsT=ones_f[:, :], rhs=tt[:], start=(c == 0), stop=(c == 1)), [tb_, b_ones_f], [pb2])
        mean, bmean = tmp()
        var, bvar = tmp()
        add("act", lambda e: e.activation(out=mean[:], in_=p1[:, :], func=AF.Identity, scale=1.0 / 256), [pb1], [bmean])
        add("dve", lambda e: e.tensor_tensor(out=var[:], in0=mean[:], in1=mean[:], op=ALU.mult), [bmean], [bvar])
        add("dve", lambda e: e.scalar_tensor_tensor(out=var[:], in0=p2[:, :], scalar=1.0 / 256, in1=var[:], op0=ALU.mult, op1=ALU.subtract),
            [pb2, bvar], [bvar])
        add("act", lambda e: e.activation(out=var[:], in_=var[:], func=AF.Ln, bias=epsc[:, 0:1], scale=1.0), [bvar, b_epsc], [bvar])
        add("act", lambda e: e.activation(out=var[:], in_=var[:], func=AF.Exp, scale=-0.5), [bvar], [bvar])
        for c in range(2):
            add("dve", lambda e, c=c: e.tensor_tensor(out=cacc[:, c, :], in0=cacc[:, c, :], in1=mean[:], op=ALU.subtract), [b_cacc, bmean], [b_cacc])
            add("dve", lambda e, c=c: e.tensor_tensor(out=cacc[:, c, :], in0=cacc[:, c, :], in1=var[:], op=ALU.mult), [b_cacc, bvar], [b_cacc])
            add("act", lambda e, c=c: e.activation(out=cacc[:, c, :], in_=cacc[:, c, :], func=AF.Identity, bias=lnb_s[:, l, c:c + 1],
                                                   scale=lng_s[:, l, c:c + 1]), [b_cacc, b_prm], [b_cacc])
            tt, tb_ = tmp()
            add("act", lambda e, tt=tt, c=c: e.activation(out=tt[:], in_=cacc[:, c, :], func=AF.Sigmoid), [b_cacc], [tb_])
            add("dve", lambda e, tt=tt, c=c: e.tensor_tensor(out=mixT[g][:, 6 + c, :], in0=cacc[:, c, :], in1=tt[:], op=ALU.mult),
                [b_cacc, tb_], [b_mix[g][6 + c]])

    def out_proj(l, groups):
        for ti in range(2):
            wt, wb = wtile(w_out[l, :, ti * 512:(ti + 1) * 512], 512)
            for cc in range(4):
                oc = ti * 4 + cc
                for g in groups:
                    pt, pb = gp()
                    for k in range(8):
                        add("pe", lambda e, pt=pt, k=k, cc=cc, wt=wt, g=g: e.matmul(pt[:, :], lhsT=wt[:, k, cc * 128:(cc + 1) * 128], rhs=mixT[g][:, k, :],
                                                                                   start=(k == 0), stop=(k == 7)), [wb, b_mix[g][k]], [pb])
                    add("dve", lambda e, pt=pt, g=g, oc=oc: e.scalar_tensor_tensor(out=xT[g][:, oc, :], in0=pt[:, :], scalar=mcol(l, "g1", oc, g),
                                                                                 in1=xT[g][:, oc, :], op0=ALU.mult, op1=ALU.add),
                        [pb, b_mod[l], b_xT[g][oc]], [b_xT[g][oc]])

    def ffn(l, groups):
        for blk in range(4):
            for ti in range(2):
                wt, wb = wtile(w_ff1[l, :, blk * 1024 + ti * 512: blk * 1024 + (ti + 1) * 512], 512)
                for cc in range(4):
                    fc = ti * 4 + cc
                    for g in groups:
                        pt, pb = gp()
                        for k in range(8):
                            add("pe", lambda e, pt=pt, k=k, cc=cc, wt=wt, g=g: e.matmul(pt[:, :], lhsT=wt[:, k, cc * 128:(cc + 1) * 128], rhs=hT[g][:, k, :],
                                                                                       start=(k == 0), stop=(k == 7)), [wb, b_hT[g]], [pb])
                        sq, bq = sqt()
                        add("act", lambda e, pt=pt, sq=sq: e.activation(out=sq[:], in_=pt[:, :], func=AF.Relu), [pb], [bq])
                        add("dve", lambda e, sq=sq, g=g, fc=fc: e.tensor_tensor(out=mixT[g][:, fc, :], in0=sq[:], in1=sq[:], op=ALU.mult),
                            [bq], [b_mix[g][fc]])
            for ti in range(2):
                wt, wb = wtile(w_ff2[l, blk * 1024:(blk + 1) * 1024, ti * 512:(ti + 1) * 512], 512)
                for cc in range(4):
                    oc = ti * 4 + cc
                    for g in groups:
                        pt, pb = gp()
                        for k in range(8):
                            add("pe", lambda e, pt=pt, k=k, cc=cc, wt=wt, g=g: e.matmul(pt[:, :], lhsT=wt[:, k, cc * 128:(cc + 1) * 128], rhs=mixT[g][:, k, :],
                                                                                       start=(k == 0), stop=(k == 7)), [wb, b_mix[g][k]], [pb])
                        add("dve", lambda e, pt=pt, g=g, oc=oc: e.scalar_tensor_tensor(out=xT[g][:, oc, :], in0=pt[:, :], scalar=mcol(l, "g2", oc, g),
                                                                                     in1=xT[g][:, oc, :], op0=ALU.mult, op1=ALU.add),
                            [pb, b_mod[l], b_xT[g][oc]], [b_xT[g][oc]])

    def dump8(name, t, bufs):
        if name not in dbg:
            return
        o = dout("dbg_" + name, [128, 8, T])
        for j in range(8):
            tt, tb_ = tmp()
            add("dve", lambda e, tt=tt, j=j: e.tensor_copy(out=tt[:], in_=t[:, j, :]), [bufs[j]], [tb_])
            add("sp", lambda e, tt=tt, j=j: e.dma_start(out=o[:, j, :], in_=tt[:]), [tb_], (), dma=True)
        dbg_out[name] = [128, 8, T]

    groups = [0, 1] if do_lat else [0]
    import os as _os
    _l1 = _os.environ.get("L1STAGES")
    _stages0 = stages
    for l in range(nlayers):
        stages = _stages0 if (l == 0 or _l1 is None) else set(_l1.split(","))
        if "mod" in stages and l == 0:
            for _ in modulation(0):
                pass
        if "norm1" in stages:
            for g in groups:
                rmsnorm_mod(l, g, 0)
        if "proj" in stages:
            input_proj(l)
        side = modulation(l + 1) if ("mod" in stages and l + 1 < nlayers) else None
        if "ctxattn" in stages:
            ctx_attention(l, None if (do_lat and "latattn" in stages) else side)
        lat_on = do_lat and "latattn" in stages
        if "conv" in stages and not lat_on:
            for _ in conv_module(l, 0):
                pass
        dump8("mix0%d" % l, mixT[0], b_mix[0])
        if do_lat:
            side2 = itertools.chain(conv_module(l, 0), conv_module(l, 1)) if "conv" in stages else None
            if "latattn" in stages:
                lat_attention(l, side2, side)
            if side2 is not None:
                for _ in side2:
                    pass
        if side is not None:
            for _ in side:
                pass
            dump8("mix1%d" % l, mixT[1], b_mix[1])
        if "outproj" in stages:
            out_proj(l, groups)
        for g in groups:
            dump8("xattn%d%d" % (g, l), xT[g], b_xT[g])
        if "norm2" in stages:
            for g in groups:
                rmsnorm_mod(l, g, 1)
        if "ffn" in stages:
            ffn(l, groups)
        for g in groups:
            dump8("xffn%d%d" % (g, l), xT[g], b_xT[g])
    for g in groups:
        if "final" not in stages:
            continue
        pt, pb = gp()
        for j in range(8):
            sq, bq = sqt()
            add("act", lambda e, sq=sq, j=j, g=g: e.activation(out=sq[:], in_=xT[g][:, j, :], func=AF.Square), [b_xT[g][j]], [bq])
            add("pe", lambda e, sq=sq, j=j, pt=pt: e.matmul(pt[:, :], lhsT=ones_b[:, :], rhs=sq[:], start=(j == 0), stop=(j == 7)), [bq, b_ones_b], [pb])
        add("act", lambda e, pt=pt: e.activation(out=rstd[:], in_=pt[:, :], func=AF.Ln, bias=epsc[:, 0:1], scale=1.0 / D), [pb, b_epsc], [b_rstd])
        add("act", lambda e: e.activation(out=rstd[:], in_=rstd[:], func=AF.Exp, scale=-0.5), [b_rstd], [b_rstd])
        for j in range(8):
            add("dve", lambda e, j=j, g=g: e.scalar_tensor_tensor(out=xT[g][:, j, :], in0=xT[g][:, j, :], scalar=gfin_s[:, j:j + 1], in1=rstd[:],
                                                                  op0=ALU.mult, op1=ALU.mult), [b_xT[g][j], b_rstd, b_prm], [b_xT[g][j]])
    for g in groups:
        for tb in range(4):
            for half in range(2):
                pt, pb = gp()
                for jj in range(4):
                    j = half * 4 + jj
                    add("pe", lambda e, pt=pt, j=j, jj=jj, tb=tb, g=g: e.transpose(out=pt[:, jj * 128:(jj + 1) * 128],
                                                                                   in_=xT[g][:, j, tb * 128:(tb + 1) * 128], identity=ident[:]),
                        [b_xT[g][j], b_ident], [pb])
                stg, bstg = tmp()
                evac(stg[:, :], pt[:, :], [pb], [bstg])
                add("sp", lambda e, g=g, tb=tb, half=half, stg=stg: e.dma_start(
                    out=y_out[g * T + tb * 128:g * T + (tb + 1) * 128, half * 512:(half + 1) * 512], in_=stg[:, :]), [bstg], (), dma=True)
    S.emit(st)
    st.close()
    return nc, dbg_out


def _colT(v, nchunk):
    return np.ascontiguousarray(v.reshape(nchunk, 128).T)


def prepare_inputs(inp):
    f = lambda a: np.ascontiguousarray(np.asarray(a, dtype=np.float32))
    x_prompt, x_sample, c = f(inp["x_prompt"]), f(inp["x_sample"]), f(inp["c"])
    w_in = f(inp["w_in"])
    shared = {}
    shared["bmodT"] = np.ascontiguousarray(np.stack([_colT(f(inp["b_mod"])[l], 48) for l in range(L)], 1))
    shared["gmixT"] = np.ascontiguousarray(np.stack([_colT(f(inp["g_norm_mix"])[l], 8) for l in range(L)], 1))
    shared["gffT"] = np.ascontiguousarray(np.stack([_colT(f(inp["g_norm_ff"])[l], 8) for l in range(L)], 1))
    shared["gfinT"] = _colT(f(inp["g_final"]), 8)
    shared["w_mod"] = f(inp["w_mod"])
    shared["w_in"] = w_in
    idx = []
    for base in (0, 256):
        for h in range(4):
            for m in range(2):
                idx += [base + h * 64 + m * 32 + P32[j] for j in range(32)]
    idx += list(range(1088, 1152))
    idx += [1152 + P32[j] for j in range(32)]
    shared["w_inp"] = np.ascontiguousarray(w_in[:, :, idx])
    shared["w_out"] = f(inp["w_out"])
    shared["w_ff1"] = f(inp["w_ff1"])
    shared["w_ff2"] = f(inp["w_ff2"])
    wq = f(inp["w_mla_qup"])
    shared["wqup"] = wq
    wqp = np.zeros_like(wq)
    for h in range(4):
        for j in range(32):
            wqp[:, :, h * 96 + 64 + j] = wq[:, :, h * 96 + 64 + P32[j]]
    shared["wqupp"] = wqp
    shared["wkvup"] = f(inp["w_mla_kvup"])
    shared["gqT"] = np.ascontiguousarray(np.stack([_colT(f(inp["g_mla_q"])[l], 2) for l in range(L)], 1))
    shared["gkvc"] = np.ascontiguousarray(f(inp["g_mla_kv"]).T)
    shared["gkvr"] = np.ascontiguousarray(np.broadcast_to(f(inp["g_mla_kv"])[None], (128, L, 128)))
    gs = f(inp["g_da_subln"])
    shared["gsubc"] = np.ascontiguousarray(np.concatenate([gs.T, gs.T], 0))
    lamv = np.stack([f(inp["da_lambda_q1"]), f(inp["da_lambda_k1"]), f(inp["da_lambda_q2"]), f(inp["da_lambda_k2"])], 1)
    shared["lamv"] = np.ascontiguousarray(np.broadcast_to(lamv[None], (128, L, 4, 32)))
    dw = f(inp["conv_dw"])
    shared["dwT"] = np.ascontiguousarray(dw.reshape(L, 31, 2, 128).transpose(3, 0, 2, 1))
    for nm, key in (("cbT", "conv_b"), ("lngT", "conv_ln_g"), ("lnbT", "conv_ln_b")):
        shared[nm] = np.ascontiguousarray(f(inp[key]).reshape(L, 2, 128).transpose(2, 0, 1))
    shared["ident"] = np.eye(128, dtype=np.float32)
    w = np.arange(GRID_W)
    cs = np.clip(w - 8, 0, GRID_W - 16)
    col_ok = (w[None, :] >= cs[:, None]) & (w[None, :] < cs[:, None] + 16)
    cm = np.where(col_ok.T, 0.0, -BIG * 8).astype(np.float32)
    shared["colmask"] = np.ascontiguousarray(np.concatenate([cm, cm], 0))
    ri = np.zeros((32, DEC_S), np.float32)
    ri[np.arange(DEC_S) // 64, np.arange(DEC_S)] = 1.0
    shared["rowind"] = ri
    rpb = f(inp["na_rpb"])
    half = 8
    freqs = 10000.0 ** (-np.arange(half, dtype=np.float64) * 2.0 / 16)
    maps = []
    for core in range(NCORE):
        b, r = core // 4, core % 4
        m = dict(shared)
        m["xin"] = np.ascontiguousarray(np.concatenate([x_prompt[2 * core], x_prompt[2 * core + 1], x_sample[b, r * T:(r + 1) * T]], 0))
        cv = np.stack([f(inp["c_ctx"]), c[b]], 1)
        m["cvT"] = np.ascontiguousarray(cv.reshape(8, 128, 2).transpose(1, 0, 2))
        m["c_dak"] = f(inp["cache_da_k"])[b]
        m["c_dav"] = f(inp["cache_da_v"])[b]
        m["c_ckv"] = f(inp["cache_mla_ckv"])[b]
        m["c_kr"] = f(inp["cache_mla_krope"])[b]
        m["c_nak"] = f(inp["cache_na_k"])[b]
        m["c_nav"] = f(inp["cache_na_v"])[b]
        t = r * T + np.arange(T)
        rows = (t // GRID_W).astype(np.float64)
        cols = (t % GRID_W).astype(np.float64)
        ang = np.concatenate([rows[None, :] * freqs[:, None], rows[None, :] * freqs[:, None],
                              cols[None, :] * freqs[:, None], cols[None, :] * freqs[:, None]], 0)
        cos32 = np.cos(ang).astype(np.float32)
        sin32 = np.sin(ang).astype(np.float32)
        sgn = np.concatenate([-np.ones(8), np.ones(8), -np.ones(8), np.ones(8)]).astype(np.float32)[:, None]
        m["cosT"] = np.ascontiguousarray(np.tile(cos32, (4, 1)))
        m["sinT"] = np.ascontiguousarray(np.tile(sin32 * sgn, (4, 1)))
        r0 = r * 8
        qrow = r0 + np.arange(T) // 64
        start = np.clip(qrow - 4, 0, NROWS - 8)
        rk = np.arange(32)
        ok = (rk[:, None] >= start[None, :]) & (rk[:, None] < start[None, :] + 8)
        m["rowsel"] = np.where(ok, 0.0, -BIG * 8).astype(np.float32)
        P = np.zeros((L, 4, A2 + 1, TZL), np.float32)
        for a2 in range(A2):
            a = 45 - a2 - r0
            if 0 <= a <= 14:
                P[:, :, a2, 48:79] = rpb[:, :, a, ::-1]
        m["tz_rep"] = np.ascontiguousarray(np.broadcast_to(P.reshape(L * 4 * (A2 + 1), 1, TZL), (L * 4 * (A2 + 1), 64, TZL)))
        hs = np.zeros((128, 8), np.float32)
        if r > 0:
            hs[:, r - 1] = 1.0
        if r < 3:
            hs[:, 4 + r + 1] = 1.0
        m["halsel"] = hs
        maps.append(m)
    return maps


_NC_CACHE = {}


def kernel(**inputs):
    maps = prepare_inputs(inputs)
    if "nc" not in _NC_CACHE:
        _NC_CACHE["nc"] = build()[0]
    nc = _NC_CACHE["nc"]
    res = run_bass_kernel_spmd(nc, maps, core_ids=list(range(NCORE)))
    R = res.results
    y_prompt = np.zeros((NB, SEQ, D), np.float32)
    y_sample = np.zeros((DEC_B, DEC_S, D), np.float32)
    outs = {k: [] for k in ("o_dak", "o_dav", "o_ckv", "o_kr", "o_nak", "o_nav")}
    for core in range(NCORE):
        b, r = core // 4, core % 4
        y = R[core]["y_out"]
        y_prompt[2 * core] = y[0:256]
        y_prompt[2 * core + 1] = y[256:512]
        y_sample[b, r * T:(r + 1) * T] = y[512:1024]
        for k in outs:
            outs[k].append(R[core][k])
    cat = lambda k: np.ascontiguousarray(np.concatenate(outs[k], 0).astype(np.float32))
    return (y_prompt, y_sample, cat("o_dak"), cat("o_dav"), cat("o_ckv"), cat("o_kr"), cat("o_nak"), cat("o_nav"))
```
